# Optimizing a Trainium2 kernel written in Bass

```python
import jax, jax.numpy as jnp
from jax import lax
import numpy as np

D_MODEL = 1024
BATCH = 16
SEQ = 2048
DEPTH = 2

N_META = 16
BLOCK = 128
N_PAD = BLOCK - N_META
NORM_EPS = 1e-6
NEG = -1e30

SB_HEADS = 8
SB_DIM = 64
SB_WIDTH = SB_HEADS * SB_DIM

MLA_HEADS = 8
MLA_Q_LORA = 256
MLA_KV_LORA = 128
MLA_NOPE = 64
MLA_ROPE = 32
MLA_V = 64
MLA_WIDTH = MLA_HEADS * MLA_V
ROPE_BASE = 10000.0

SWA_HEADS = 16
SWA_KV_HEADS = 2
SWA_DIM = 64
SWA_WINDOW = 128
SWA_WIDTH = SWA_HEADS * SWA_DIM

EVEN_SPLITS = [SB_WIDTH, SB_WIDTH, SB_WIDTH, SB_WIDTH,
               MLA_Q_LORA, MLA_KV_LORA, MLA_ROPE, MLA_WIDTH]
EVEN_IN = sum(EVEN_SPLITS)
EVEN_OUT = SB_WIDTH + MLA_WIDTH
ODD_SPLITS = [SWA_WIDTH, SWA_KV_HEADS * SWA_DIM, SWA_KV_HEADS * SWA_DIM, SWA_WIDTH]
ODD_IN = sum(ODD_SPLITS)
ODD_OUT = SWA_WIDTH

kernel_name = "hybrid_stickbreak_mla_swa_meta"


def _offsets(sizes):
    return [int(o) for o in np.cumsum(sizes)[:-1]]


def rmsnorm(x, g):
    xf = x.astype(jnp.float32)
    y = xf * lax.rsqrt(jnp.mean(xf * xf, axis=-1, keepdims=True) + NORM_EPS)
    return (y * g.astype(jnp.float32)).astype(x.dtype)


def apply_rope(x, pos):
    half = x.shape[-1] // 2
    inv = ROPE_BASE ** (-jnp.arange(half, dtype=jnp.float32) / half)
    ang = pos.astype(jnp.float32)[:, None] * inv[None, :]
    cos = jnp.cos(ang)[:, None, :]
    sin = jnp.sin(ang)[:, None, :]
    x1 = x[..., :half].astype(jnp.float32)
    x2 = x[..., half:].astype(jnp.float32)
    return jnp.concatenate([x1 * cos - x2 * sin, x1 * sin + x2 * cos], axis=-1).astype(x.dtype)


def alibi_slopes(n_heads):
    return 2.0 ** (-8.0 * (jnp.arange(n_heads, dtype=jnp.float32) + 1.0) / n_heads)


def stick_breaking_attention(q, k, v):
    Lp = q.shape[1]
    pos = jnp.arange(Lp)
    scale = SB_DIM ** -0.5
    outs = []
    for i in range(Lp // BLOCK):
        q0, q1 = i * BLOCK, (i + 1) * BLOCK
        z = jnp.einsum('bthd,bshd->bhts', q[:, q0:q1], k[:, :q1]).astype(jnp.float32) * scale
        t_pos = pos[q0:q1][:, None]
        s_pos = pos[:q1][None, :]
        mask = (s_pos < t_pos) & (s_pos >= N_PAD)
        log_beta = jax.nn.log_sigmoid(z)
        log_1m = jnp.where(mask, log_beta - z, 0.0)
        suffix = lax.cumsum(log_1m, axis=3, reverse=True) - log_1m
        a = jnp.where(mask, jnp.exp(log_beta + suffix), 0.0)
        outs.append(jnp.einsum('bhts,bshd->bthd', a.astype(v.dtype), v[:, :q1]))
    return jnp.concatenate(outs, axis=1)


def causal_block_softmax_attention(q, k, v, scale):
    Lp = q.shape[1]
    pos = jnp.arange(Lp)
    outs = []
    for i in range(Lp // BLOCK):
        q0, q1 = i * BLOCK, (i + 1) * BLOCK
        s = jnp.einsum('bthd,bshd->bhts', q[:, q0:q1], k[:, :q1]).astype(jnp.float32) * scale
        mask = (pos[None, :q1] <= pos[q0:q1, None]) & (pos[None, :q1] >= N_PAD)
        p = jax.nn.softmax(jnp.where(mask, s, NEG), axis=-1)
        outs.append(jnp.einsum('bhts,bshd->bthd', p.astype(v.dtype), v[:, :q1]))
    return jnp.concatenate(outs, axis=1)


def sliding_window_sink_attention(q, k, v, sinks):
    B, Lp = q.shape[0], q.shape[1]
    nb = Lp // BLOCK
    G = SWA_HEADS // SWA_KV_HEADS
    K = SWA_KV_HEADS
    qb = q.reshape(B, nb, BLOCK, K, G, SWA_DIM)
    kb = k.reshape(B, nb, BLOCK, K, SWA_DIM)
    vb = v.reshape(B, nb, BLOCK, K, SWA_DIM)
    shift = ((0, 0), (1, 0), (0, 0), (0, 0), (0, 0))
    k_band = jnp.concatenate([jnp.pad(kb[:, :-1], shift), kb], axis=2)
    v_band = jnp.concatenate([jnp.pad(vb[:, :-1], shift), vb], axis=2)
    k_meta = k[:, N_PAD:BLOCK]
    v_meta = v[:, N_PAD:BLOCK]
    blk = jnp.arange(nb)[:, None] * BLOCK
    t_pos = blk + jnp.arange(BLOCK)[None, :]
    s_pos = blk - BLOCK + jnp.arange(2 * BLOCK)[None, :]
    m_pos = N_PAD + jnp.arange(N_META)
    d_band = t_pos[:, :, None] - s_pos[:, None, :]
    d_meta = t_pos[:, :, None] - m_pos[None, None, :]
    band_ok = (d_band >= 0) & (d_band < SWA_WINDOW) & (s_pos[:, None, :] >= BLOCK)
    meta_ok = d_meta >= 0
    slopes = alibi_slopes(SWA_HEADS).reshape(K, G)[:, :, None, None]
    scale = SWA_DIM ** -0.5
    s_band = (jnp.einsum('bnqkgd,bnskd->bnkgqs', qb, k_band).astype(jnp.float32) * scale
              - slopes * d_band.astype(jnp.float32)[:, None, None])
    s_band = jnp.where(band_ok[:, None, None], s_band, NEG)
    s_meta = (jnp.einsum('bnqkgd,bmkd->bnkgqm', qb, k_meta).astype(jnp.float32) * scale
              - slopes * d_meta.astype(jnp.float32)[:, None, None])
    s_meta = jnp.where(meta_ok[:, None, None], s_meta, NEG)
    sink = jnp.broadcast_to(sinks.astype(jnp.float32).reshape(K, G, 1, 1),
                            s_band.shape[:-1] + (1,))
    p = jax.nn.softmax(jnp.concatenate([s_band, s_meta, sink], axis=-1), axis=-1)
    S = 2 * BLOCK
    p_band = p[..., :S].astype(v.dtype)
    p_meta = p[..., S:S + N_META].astype(v.dtype)
    o = (jnp.einsum('bnkgqs,bnskd->bnqkgd', p_band, v_band)
         + jnp.einsum('bnkgqm,bmkd->bnqkgd', p_meta, v_meta))
    return o.reshape(B, Lp, SWA_WIDTH)


def even_layer(h, pos, w_in, q_norm_g, kv_norm_g, w_uq, w_ukv, w_out):
    B, Lp = h.shape[0], h.shape[1]
    proj = h @ w_in
    q_sb, k_sb, v_sb, g_sb, c_q, c_kv, k_r, g_mla = jnp.split(proj, _offsets(EVEN_SPLITS), axis=-1)
    shp = (B, Lp, SB_HEADS, SB_DIM)
    o_sb = stick_breaking_attention(q_sb.reshape(shp), k_sb.reshape(shp), v_sb.reshape(shp))
    o_sb = o_sb.reshape(B, Lp, SB_WIDTH) * jax.nn.silu(g_sb)
    qh = (rmsnorm(c_q, q_norm_g) @ w_uq).reshape(B, Lp, MLA_HEADS, MLA_NOPE + MLA_ROPE)
    q_nope, q_rope = qh[..., :MLA_NOPE], qh[..., MLA_NOPE:]
    kvh = (rmsnorm(c_kv, kv_norm_g) @ w_ukv).reshape(B, Lp, MLA_HEADS, MLA_NOPE + MLA_V)
    k_nope, v_mla = kvh[..., :MLA_NOPE], kvh[..., MLA_NOPE:]
    k_rope = apply_rope(k_r[:, :, None, :], pos)
    q_full = jnp.concatenate([q_nope, apply_rope(q_rope, pos)], axis=-1)
    k_full = jnp.concatenate(
        [k_nope, jnp.broadcast_to(k_rope, (B, Lp, MLA_HEADS, MLA_ROPE))], axis=-1)
    o_mla = causal_block_softmax_attention(q_full, k_full, v_mla, (MLA_NOPE + MLA_ROPE) ** -0.5)
    o_mla = o_mla.reshape(B, Lp, MLA_WIDTH) * jax.nn.silu(g_mla)
    return jnp.concatenate([o_sb, o_mla], axis=-1) @ w_out


def odd_layer(h, w_in, sinks, w_out):
    B, Lp = h.shape[0], h.shape[1]
    proj = h @ w_in
    q, k, v, g = jnp.split(proj, _offsets(ODD_SPLITS), axis=-1)
    o = sliding_window_sink_attention(
        q.reshape(B, Lp, SWA_HEADS, SWA_DIM),
        k.reshape(B, Lp, SWA_KV_HEADS, SWA_DIM),
        v.reshape(B, Lp, SWA_KV_HEADS, SWA_DIM), sinks)
    return (o * jax.nn.silu(g)) @ w_out


def setup_inputs(seed: int = 0) -> dict:
    key = jax.random.key(seed)
    ks = jax.random.split(key, 13)
    ne = (DEPTH + 1) // 2
    no = DEPTH // 2
    f32 = jnp.float32

    def w(k, shape, fan_in):
        return jax.random.normal(k, shape, f32) * (fan_in ** -0.5)

    def gain(k, shape):
        return 1.0 + 0.05 * jax.random.normal(k, shape, f32)

    return {
        "x": jax.random.normal(ks[0], (BATCH, SEQ, D_MODEL), f32),
        "meta": jax.random.normal(ks[1], (N_META, D_MODEL), f32),
        "norm_g": gain(ks[2], (DEPTH, D_MODEL)),
        "final_g": gain(ks[3], (D_MODEL,)),
        "ev_w_in": w(ks[4], (ne, D_MODEL, EVEN_IN), D_MODEL),
        "ev_q_norm_g": gain(ks[5], (ne, MLA_Q_LORA)),
        "ev_kv_norm_g": gain(ks[6], (ne, MLA_KV_LORA)),
        "ev_w_uq": w(ks[7], (ne, MLA_Q_LORA, MLA_HEADS * (MLA_NOPE + MLA_ROPE)), MLA_Q_LORA),
        "ev_w_ukv": w(ks[8], (ne, MLA_KV_LORA, MLA_HEADS * (MLA_NOPE + MLA_V)), MLA_KV_LORA),
        "ev_w_out": w(ks[9], (ne, EVEN_OUT, D_MODEL), EVEN_OUT),
        "od_w_in": w(ks[10], (no, D_MODEL, ODD_IN), D_MODEL),
        "od_sinks": 0.5 * jax.random.normal(ks[11], (no, SWA_HEADS), f32),
        "od_w_out": w(ks[12], (no, ODD_OUT, D_MODEL), ODD_OUT),
    }


def reference(x, meta, norm_g, final_g, ev_w_in, ev_q_norm_g, ev_kv_norm_g, ev_w_uq,
              ev_w_ukv, ev_w_out, od_w_in, od_sinks, od_w_out):
    B = x.shape[0]
    meta_b = jnp.broadcast_to(meta.astype(x.dtype)[None], (B, N_META, D_MODEL))
    pad = jnp.zeros((B, N_PAD, D_MODEL), x.dtype)
    h = jnp.concatenate([pad, meta_b, x], axis=1)
    pos = jnp.arange(h.shape[1]) - N_PAD
    for layer in range(DEPTH):
        hn = rmsnorm(h, norm_g[layer])
        if layer % 2 == 0:
            i = layer // 2
            h = h + even_layer(hn, pos, ev_w_in[i], ev_q_norm_g[i], ev_kv_norm_g[i],
                               ev_w_uq[i], ev_w_ukv[i], ev_w_out[i])
        else:
            i = layer // 2
            h = h + odd_layer(hn, od_w_in[i], od_sinks[i], od_w_out[i])
    return rmsnorm(h, final_g)[:, BLOCK:]
```

```python
import numpy as np
from contextlib import ExitStack
import concourse.bass as bass
import concourse.mybir as mybir
from concourse.bass_utils import run_bass_kernel_spmd

F32 = mybir.dt.float32
BF16 = mybir.dt.bfloat16
AF = mybir.ActivationFunctionType
ALU = mybir.AluOpType

ENGS = ["pe", "act", "dve", "pool", "sp"]

D = 1024
LP = 2176
NB = 17
NPAD = 112
EPS = 1e-6
SEQ_PER_CORE = 2
EV_IN = 2976
OD_IN = 2304


class Buf:
    __slots__ = ("name", "last_w", "readers", "excl")

    def __init__(self, name, excl=False):
        self.name = name
        self.last_w = None
        self.readers = []
        self.excl = excl


class Op:
    __slots__ = ("eng", "fn", "deps", "signal", "is_dma", "dsem", "dval", "cnt", "prev_dma")

    def __init__(self, eng, fn, is_dma):
        self.eng = eng
        self.fn = fn
        self.deps = []
        self.signal = False
        self.is_dma = is_dma
        self.dsem = None
        self.dval = 0
        self.cnt = 0
        self.prev_dma = None


class Sched:
    def __init__(self, n_dma_sems=8):
        self.ops = {e: [] for e in ENGS}
        self.n_dma_sems = n_dma_sems
        self.dma_rr = {e: 0 for e in ENGS}
        self.dma_cnt = {}
        self.dma_last = {}

    def op(self, eng, fn, reads=(), writes=(), dma=False):
        o = Op(eng, fn, dma)
        deps = {}
        for b in reads:
            if b.last_w is not None:
                deps[id(b.last_w)] = b.last_w
            if b.excl:
                for r in b.readers:
                    if r.eng != eng:
                        deps[id(r)] = r
        for b in writes:
            if b.last_w is not None:
                deps[id(b.last_w)] = b.last_w
            for r in b.readers:
                deps[id(r)] = r
        final = []
        for d in deps.values():
            if d is o:
                continue
            if d.eng == "pe" and eng == "pe" and (not d.is_dma) and (not dma):
                continue
            final.append(d)
            d.signal = True
        o.deps = final
        for b in reads:
            if not dma:
                b.readers = [r for r in b.readers if r.is_dma or r.eng != eng]
            b.readers.append(o)
        for b in writes:
            b.last_w = o
            b.readers = []
        if dma:
            k = self.dma_rr[eng]
            self.dma_rr[eng] = (k + 1) % self.n_dma_sems
            key = (eng, k)
            self.dma_cnt[key] = self.dma_cnt.get(key, 0) + 1
            o.dsem = key
            o.dval = 16 * self.dma_cnt[key]
            o.prev_dma = self.dma_last.get(key)
            self.dma_last[key] = o
        self.ops[eng].append(o)
        return o

    def barrier(self):
        lasts = []
        for e in ENGS:
            for o in reversed(self.ops[e]):
                if (not o.is_dma) and o.fn is not None:
                    lasts.append(o)
                    break
        dmas = list(self.dma_last.values())
        for e in ENGS:
            m = Op(e, None, False)
            m.deps = [d for d in lasts if d.eng != e] + dmas
            for d in m.deps:
                d.signal = True
            self.ops[e].append(m)

    def finalize(self):
        for e in ENGS:
            c = 0
            for o in self.ops[e]:
                if o.is_dma or o.fn is None:
                    continue
                if o.signal:
                    c += 1
                    o.cnt = c

    def emit_engine(self, eng_name, e, sems, dma_sems, final_wait=False):
        waited = {}

        def wait(key, sem, val):
            if val <= 0:
                return
            if waited.get(key, 0) < val:
                e.wait_ge(sem, val)
                waited[key] = val

        for o in self.ops[eng_name]:
            for d in o.deps:
                if d.is_dma:
                    wait(d.dsem, dma_sems[d.dsem], d.dval)
                else:
                    wait(d.eng, sems[d.eng], d.cnt)
            if o.is_dma and o.prev_dma is not None:
                wait(o.dsem, dma_sems[o.dsem], o.prev_dma.dval)
            if o.fn is None:
                continue
            ins = o.fn(e)
            if o.is_dma:
                ins.then_inc(dma_sems[o.dsem], 16)
            elif o.signal:
                ins.then_inc(sems[eng_name], 1)
        if final_wait:
            for key, o in self.dma_last.items():
                wait(key, dma_sems[key], o.dval)


def _const_tables():
    r = np.arange(128)
    s = r[:, None]
    t = r[None, :]
    ident = (s == t).astype(np.float32)
    negU = -(s >= t).astype(np.float32)
    negU0 = negU * (s >= NPAD)
    negOnes = -np.ones((128, 128), np.float32)
    negOnes0 = negOnes * (s >= NPAD)
    ones = np.ones((128, 128), np.float32)
    mstrict = (s < t).astype(np.float32)
    mincl = (s <= t).astype(np.float32)
    onesA = np.concatenate([np.ones((128, 64)), np.zeros((128, 64))], 1).astype(np.float32)
    onesB = np.concatenate([np.zeros((128, 64)), np.ones((128, 64))], 1).astype(np.float32)
    zeros = np.zeros((128, 128), np.float32)
    cb = np.concatenate([ident, negU, negU0, negOnes, negOnes0, ones, mstrict, mincl,
                         onesA, onesB, zeros], axis=1)
    half = 16
    inv = 10000.0 ** (-np.arange(half, dtype=np.float64) / half)
    pos = (np.arange(LP) - NPAD).astype(np.float64)
    ang = inv[:, None] * pos[None, :]
    cos = np.concatenate([np.cos(ang), np.cos(ang)], 0).astype(np.float32)
    sin = np.concatenate([np.sin(ang), np.sin(ang)], 0).astype(np.float32)
    rope = np.stack([cos, sin], 0)
    H = 16
    slopes = 2.0 ** (-8.0 * (np.arange(H, dtype=np.float64) + 1.0) / H)
    sl = slopes[None, :, None]
    dprev = (128 + r[None, None, :] - r[:, None, None]).astype(np.float64)
    eprev = np.where(dprev < 128, np.exp(-sl * dprev), 0.0)
    dcur = (r[None, None, :] - r[:, None, None]).astype(np.float64)
    ecur = np.where(dcur >= 0, np.exp(-sl * np.maximum(dcur, 0)), 0.0)
    m = np.arange(16)
    dm = (16 + r[None, None, :] - m[:, None, None]).astype(np.float64)
    em = np.exp(-sl * dm)
    n = np.arange(NB)
    cm = np.exp(-slopes[None, None, :] * 128.0 * np.maximum(n - 1, 0)[None, :, None])
    cm = np.broadcast_to(cm, (16, NB, H))
    swa_e = np.stack([eprev, ecur], 0).astype(np.float32)
    return (cb.astype(np.float32), rope, swa_e, em.astype(np.float32),
            np.ascontiguousarray(cm).astype(np.float32))


def build_nc(debug_h1=False, n_seq=SEQ_PER_CORE):
    nc = bass.Bass("TRN2", target_bir_lowering=False)
    dt = nc.dram_tensor
    x_d = dt("x", [n_seq, 2048, D], F32, kind="ExternalInput").ap()
    meta_d = dt("meta", [16, D], F32, kind="ExternalInput").ap()
    gains_d = dt("gains", [3, D], F32, kind="ExternalInput").ap()
    w0in_d = dt("w0in", [D, EV_IN], F32, kind="ExternalInput").ap()
    qng_d = dt("qng", [256], F32, kind="ExternalInput").ap()
    kvng_d = dt("kvng", [128], F32, kind="ExternalInput").ap()
    wuq_d = dt("wuq", [256, 768], F32, kind="ExternalInput").ap()
    wukv_d = dt("wukv", [128, 1024], F32, kind="ExternalInput").ap()
    w0out_d = dt("w0out", [D, D], F32, kind="ExternalInput").ap()
    w1in_d = dt("w1in", [D, OD_IN], F32, kind="ExternalInput").ap()
    sinks_d = dt("sinks", [16], F32, kind="ExternalInput").ap()
    w1out_d = dt("w1out", [D, D], F32, kind="ExternalInput").ap()
    cb_d = dt("cb", [128, 11 * 128], F32, kind="ExternalInput").ap()
    rope_d = dt("rope", [2, 32, LP], F32, kind="ExternalInput").ap()
    swae_d = dt("swae", [2, 128, 16, 128], F32, kind="ExternalInput").ap()
    em_d = dt("em", [16, 16, 128], F32, kind="ExternalInput").ap()
    cm_d = dt("cm", [16, NB, 16], F32, kind="ExternalInput").ap()
    out_d = dt("out", [n_seq, 2048, D], F32, kind="ExternalOutput").ap()
    if debug_h1:
        dbg_d = dt("dbg", [n_seq, LP, D], F32, kind="ExternalOutput").ap()
        dbg2_d = dt("dbg2", [n_seq, 128, 8 * LP], F32, kind="ExternalOutput").ap()

    S = Sched()
    with ExitStack() as es:
        ARENA_F32 = 53000
        arena = es.enter_context(nc.sbuf_tensor("arena", [128, ARENA_F32], F32))
        psq = [es.enter_context(nc.psum_tensor(f"psq{i}", [128, 1024], F32)) for i in range(4)]
        ps = [psq[i // 2][:, (i % 2) * 512:(i % 2 + 1) * 512] for i in range(8)]
        psP = [q_.rearrange("p (h n) -> p h n", h=2) for q_ in psq]
        sems = {e: es.enter_context(nc.semaphore(f"s_{e}")) for e in ENGS}
        dma_sems = {}
        for e in ["sp", "pool"]:
            for k in range(S.n_dma_sems):
                dma_sems[(e, k)] = es.enter_context(nc.semaphore(f"d_{e}{k}"))
        P = [Buf(f"ps{i}", excl=True) for i in range(8)]

        class Arena:
            def __init__(self, start=0):
                self.off = start

            def f32(self, n):
                a = arena[:, self.off:self.off + n]
                self.off += n
                return a

            def bf16(self, n):
                assert n % 2 == 0
                a = arena[:, self.off:self.off + n // 2].bitcast(BF16)
                self.off += n // 2
                return a

        A = Arena(0)
        CB = A.bf16(11 * 128)
        cb_v = lambda i: CB[:, i * 128:(i + 1) * 128]
        IDENT, NEGU, NEGU0, NEGONES, NEGONES0, ONES, MSTRICT, MINCL, ONESA, ONESB, ZEROS = [cb_v(i) for i in range(11)]
        G = A.f32(3 * D)
        NG = A.f32(4)
        FIN = A.bf16(512)
        OG = A.bf16(8 * LP)
        OGv = OG.rearrange("p (c t) -> p c t", c=8)
        pers_end = A.off
        B_cb, B_g, B_ng, B_fin, B_og = Buf("cb"), Buf("g"), Buf("ng"), Buf("fin"), Buf("og")

        def mm(out, lhsT, rhs, start, stop, reads, writes):
            S.op("pe", lambda e: e.matmul(out, lhsT=lhsT, rhs=rhs, start=start, stop=stop),
                 reads=reads, writes=writes)

        def tr(out, in_, reads, writes):
            S.op("pe", lambda e: e.transpose(out, in_, IDENT), reads=list(reads) + [B_cb], writes=writes)

        def act(out, in_, func, reads, writes, scale=1.0, bias=0.0, accum_out=None, eng="act"):
            if accum_out is None:
                S.op("act", lambda e: e.activation(out=out, in_=in_, func=func, bias=bias, scale=scale),
                     reads=reads, writes=writes)
            else:
                S.op("act", lambda e: e.activation(out=out, in_=in_, func=func, bias=bias, scale=scale,
                                                   accum_out=accum_out), reads=reads, writes=writes)

        def tt(eng, out, in0, in1, op, reads, writes):
            S.op(eng, lambda e: e.tensor_tensor(out=out, in0=in0, in1=in1, op=op), reads=reads, writes=writes)

        def tsc(eng, out, in0, s1, op0, reads, writes, s2=None, op1=None):
            if op1 is None:
                S.op(eng, lambda e: e.tensor_scalar(out=out, in0=in0, scalar1=s1, scalar2=None, op0=op0),
                     reads=reads, writes=writes)
            else:
                S.op(eng, lambda e: e.tensor_scalar(out=out, in0=in0, scalar1=s1, scalar2=s2, op0=op0, op1=op1),
                     reads=reads, writes=writes)

        def stt(eng, out, in0, scalar, in1, op0, op1, reads, writes):
            S.op(eng, lambda e: e.scalar_tensor_tensor(out=out, in0=in0, scalar=scalar, in1=in1, op0=op0, op1=op1),
                 reads=reads, writes=writes)

        def cp(eng, out, in_, reads, writes):
            if eng == "act":
                S.op("act", lambda e: e.copy(out=out, in_=in_), reads=reads, writes=writes)
            else:
                S.op(eng, lambda e: e.tensor_copy(out=out, in_=in_), reads=reads, writes=writes)

        def memset(eng, ap, val, writes):
            S.op(eng, lambda e: e.memset(ap, val), writes=writes)

        def dma(eng, out, in_, reads, writes):
            S.op(eng, lambda e: e.dma_start(out=out, in_=in_), reads=reads, writes=writes, dma=True)

        def recip(out, in_, reads, writes):
            S.op("dve", lambda e: e.reciprocal(out=out, in_=in_), reads=reads, writes=writes)

        def wload(dst3, src2, c0, ncols, writes, reads=()):
            kc = src2.shape[0] // 128
            srcv = src2.rearrange("(kc p) n -> p kc n", p=128)
            dma("pool", dst3, srcv[:, :, c0:c0 + ncols], reads, writes)

        dma("pool", CB, cb_d, [], [B_cb])
        for i in range(3):
            dma("sp", G[:, i * D:(i + 1) * D], gains_d[i:i + 1, :].broadcast_to([128, D]), [], [B_g])
        for c in range(2):
            dma("sp", NG[:, c:c + 1], qng_d[c * 128:(c + 1) * 128].rearrange("(p o) -> p o", o=1), [], [B_ng])
        dma("sp", NG[:, 2:3], kvng_d.rearrange("(p o) -> p o", o=1), [], [B_ng])
        memset("dve", FIN, 1.0, [B_fin])

        A0 = Arena(pers_end)
        HNT = A0.bf16(8 * LP); HNTv = HNT.rearrange("p (c t) -> p c t", c=8); B_hnt = Buf("hnt")
        WS = [A0.bf16(8 * 512) for _ in range(2)]; B_ws = [Buf("ws0"), Buf("ws1")]
        WSv = [w.rearrange("p (c n) -> p c n", c=8) for w in WS]
        WM = A0.bf16(8 * 416); WMv = WM.rearrange("p (c n) -> p c n", c=8); B_wm = Buf("wm")
        WKRR = A0.bf16(8 * 32); WKRRv = WKRR.rearrange("p (c n) -> p c n", c=8); B_wkrr = Buf("wkrr")
        WUQ = A0.bf16(2 * 768); WUQv = WUQ.rearrange("p (c n) -> p c n", c=2); B_wuq = Buf("wuq")
        WUQR = A0.bf16(2 * 256); WUQRv = WUQR.rearrange("p (c n) -> p c n", c=2); B_wuqr = Buf("wuqr")
        WUKV = A0.bf16(1024); B_wukv = Buf("wukv")
        A0_KT_START = A0.off
        KT = [A0.bf16(LP) for _ in range(2)]; B_kt = [Buf("kt0"), Buf("kt1")]
        QT = [A0.bf16(LP) for _ in range(2)]; B_qt = [Buf("qt0"), Buf("qt1")]
        VP = A0.bf16(NB * 256); VPv = VP.rearrange("p (b v n) -> p b v n", b=NB, v=2); B_vp = Buf("vp")
        SG = A0.bf16(LP); B_sg = Buf("sg")
        CQN = A0.bf16(2 * LP); CQNv = CQN.rearrange("p (c t) -> p c t", c=2); B_cqn = Buf("cqn")
        CKVN = A0.bf16(LP); B_ckvn = Buf("ckvn")
        KR = A0.bf16(LP); B_kr = Buf("kr")
        ROPE = A0.f32(2 * LP); ROPEv = ROPE.rearrange("p (c t) -> p c t", c=2); B_rope = Buf("rope")
        def pairbuf(ap):
            return ap.rearrange("p (h n) -> p h n", h=2), [ap[:, 0:512], ap[:, 512:1024]]
        E32P = A0.f32(1024); e32v, E32 = pairbuf(E32P); B_e32p = Buf("e32p"); B_e32 = [B_e32p, B_e32p]
        SPBP = A0.bf16(1024); spbv, SPB = pairbuf(SPBP); B_spbp = Buf("spbp"); B_spb = [B_spbp, B_spbp]
        SPBP2 = A0.bf16(1024); spbv2, SPB2 = pairbuf(SPBP2); B_spbp2 = Buf("spbp2")
        C32P = A0.f32(1024); c32v, C32 = pairbuf(C32P); B_c32p = Buf("c32p")
        C16P = A0.bf16(1024); c16v, C16 = pairbuf(C16P); B_c16p = Buf("c16p")
        ATBP = [A0.bf16(1024) for _ in range(2)]
        atbv = [pairbuf(a_)[0] for a_ in ATBP]
        ATB = [pairbuf(ATBP[i // 2])[1][i % 2] for i in range(4)]
        B_atbp = [Buf("atbp0"), Buf("atbp1")]; B_atb = [B_atbp[i // 2] for i in range(4)]
        XS = A0.f32(D); B_xs = Buf("xs")
        T32 = [XS[:, 0:512], XS[:, 512:1024]]; B_t32 = [Buf("t32a"), Buf("t32b")]
        HN = A0.bf16(D); B_hn = Buf("hn")
        ST = A0.f32(8); B_st = Buf("st"); B_stb = Buf("stb")
        JK = HN; B_jk = B_hn
        l0_end = A0.off
        assert l0_end <= ARENA_F32, l0_end

        A1 = Arena(pers_end)
        W0O = A1.bf16(8 * D); W0Ov = W0O.rearrange("p (c n) -> p c n", c=8); B_w0o = Buf("w0o")
        W1I = A1.bf16(8 * OD_IN); W1Iv = W1I.rearrange("p (c n) -> p c n", c=8); B_w1i = Buf("w1i")
        W1O = A1.bf16(8 * D); W1Ov = W1O.rearrange("p (c n) -> p c n", c=8); B_w1o = Buf("w1o")
        SWE = A1.f32(2 * 16 * 128); SWEv = SWE.rearrange("p (a h r) -> p a h r", a=2, h=16); B_swe = Buf("swe")
        EM = A1.f32(16 * 128); EMv = EM.rearrange("p (h r) -> p h r", h=16); B_em = Buf("em")
        CM = A1.f32(NB * 16); CMv = CM.rearrange("p (n h) -> p n h", n=NB); B_cm = Buf("cm")
        ESK = A1.f32(8); B_esk = Buf("esk")
        KT1 = A1.bf16(LP); B_kt1 = [Buf(f"kt1l{i}") for i in range(NB)]
        VR = A1.bf16(3 * 4 * 128); VRv = VR.rearrange("p (b v n) -> p b v n", b=3, v=4); B_vr = [Buf("vr0"), Buf("vr1"), Buf("vr2")]
        VM = A1.bf16(4 * 128); VMv = VM.rearrange("p (v n) -> p v n", v=4); B_vm = Buf("vm")
        H1 = [A1.f32(D) for _ in range(2)]; B_h1 = [Buf("h1a"), Buf("h1b")]
        XS1 = A1.f32(D); B_xs1 = Buf("xs1")
        HN1 = A1.bf16(D); B_hn1 = Buf("hn1")
        HNT1 = [A1.bf16(8 * 128) for _ in range(2)]; HNT1v = [h.rearrange("p (c t) -> p c t", c=8) for h in HNT1]
        B_hnt1 = [Buf("hnt1a"), Buf("hnt1b")]
        QG = [[A1.bf16(8 * 128) for _ in range(2)] for _ in range(2)]
        QGv = [[q.rearrange("p (h t) -> p h t", h=8) for q in qq] for qq in QG]
        B_qg = [[Buf(f"qg{i}{j}") for j in range(2)] for i in range(2)]
        SG1 = [A1.bf16(8 * 128) for _ in range(2)]; SG1v = [x_.rearrange("p (c t) -> p c t", c=8) for x_ in SG1]
        B_sg1 = [Buf("sg1a"), Buf("sg1b")]
        OG1 = A1.bf16(8 * 128); OG1v = OG1.rearrange("p (c t) -> p c t", c=8); B_og1 = Buf("og1")
        EX = [A1.f32(1024) for _ in range(2)]; EXv = [x_.rearrange("p (h t) -> p h t", h=8) for x_ in EX]
        B_ex = [Buf("exa"), Buf("exb")]
        PB = [A1.bf16(1024) for _ in range(2)]; PBv = [x_.rearrange("p (h t) -> p h t", h=8) for x_ in PB]
        B_pb = [Buf("pba"), Buf("pbb")]
        B_exh = [[Buf(f"exh{i}{j}") for j in range(2)] for i in range(2)]
        B_pbh = [[Buf(f"pbh{i}{j}") for j in range(2)] for i in range(2)]
        R32 = A1.f32(512); B_r32 = Buf("r32")
        U32 = A1.f32(512); B_u32 = Buf("u32")
        ST1 = A1.f32(8); B_st1 = Buf("st1")
        ST2 = A1.f32(8); B_st2 = Buf("st2")
        JK1 = HN1; B_jk1 = B_hn1
        OUTB = A1.f32(D); B_outb = Buf("outb")
        assert A1.off <= ARENA_F32, A1.off

        PT = ps[7].bitcast(BF16)

        TCH = [(c * 512, min(512, LP - c * 512)) for c in range(5)]
        QCH = [(0, 1), (1, 5), (5, 9), (9, 13), (13, 17)]

        def rmsnorm_block(src32, B_src, gidx, dst_bf, B_dst, st, B_st_, jk, B_jk_):
            act(jk, src32, AF.Square, [B_src], [B_jk_, B_st_], accum_out=st[:, 0:1])
            act(st[:, 1:2], st[:, 0:1], AF.Ln, [B_st_], [B_st_], scale=1.0 / D, bias=EPS)
            act(st[:, 2:3], st[:, 1:2], AF.Exp, [B_st_], [B_st_], scale=-0.5)
            stt("dve", dst_bf, src32, st[:, 2:3], G[:, gidx * D:(gidx + 1) * D], ALU.mult, ALU.mult,
                [B_src, B_st_, B_g], [B_dst])

        def transpose_block(hn_bf, B_hn_, dst3, B_dst):
            for kc in range(8):
                tr(PT[:, kc * 128:(kc + 1) * 128], hn_bf[:, kc * 128:(kc + 1) * 128], [B_hn_], [P[7]])
            cp("dve", dst3, PT.rearrange("p (c t) -> p c t", c=8), [P[7]], [B_dst])

        def proj_fm(psb, Pb, wv, c0, m, t0, n, B_w, hnt_v, B_h):
            for kc in range(8):
                mm(psb[0:m, 0:n], wv[:, kc, c0:c0 + m], hnt_v[:, kc, t0:t0 + n], kc == 0, kc == 7,
                   [B_w, B_h], [Pb])

        for sq in range(n_seq):
            S.barrier()
            wload(WMv, w0in_d, 2048, 416, [B_wm])
            wload(WUQv, wuq_d, 0, 768, [B_wuq])
            dma("pool", WUKV, wukv_d, [], [B_wukv])
            dma("sp", ROPEv[0:32], rope_d.rearrange("c p t -> p c t"), [], [B_rope])
            for i in range(2):
                memset("pool", KT[i], 0.0, [B_kt[i]])
                memset("pool", QT[i], 0.0, [B_qt[i]])
            memset("pool", VP, 0.0, [B_vp])
            for kc in range(8):
                tsc("pool", WKRRv[:, kc, 0:16], WMv[:, kc, 400:416], -1.0, ALU.mult, [B_wm], [B_wkrr])
                cp("pool", WKRRv[:, kc, 16:32], WMv[:, kc, 384:400], [B_wm], [B_wkrr])
            for kc in range(2):
                src = WUQv[:, kc, :].rearrange("p (h d) -> p h d", h=8)
                dst = WUQRv[:, kc, :].rearrange("p (h d) -> p h d", h=8)
                tsc("pool", dst[:, :, 0:16], src[:, :, 80:96], -1.0, ALU.mult, [B_wuq], [B_wuqr])
                cp("pool", dst[:, :, 16:32], src[:, :, 64:80], [B_wuq], [B_wuqr])

            XSs = [XS, E32P]; B_xss = [B_xs, B_e32p]
            HNs = [HN, C32P.bitcast(BF16)[:, 0:D]]; B_hns = [B_hn, B_c32p]
            STs = [ST[:, 0:4], ST[:, 4:8]]; B_sts = [B_st, B_stb]
            for b in range(NB):
                i = b % 2
                xs_, Bx_ = XSs[i], B_xss[i]
                if b == 0:
                    memset("dve", xs_, 0.0, [Bx_])
                    dma("sp", xs_[NPAD:128, :], meta_d, [], [Bx_])
                else:
                    dma("sp", xs_, x_d[sq, (b - 1) * 128:b * 128, :], [], [Bx_])
                rmsnorm_block(xs_, Bx_, 0, HNs[i], B_hns[i], STs[i], B_sts[i], HNs[i], B_hns[i])
                pk = 6 + i
                ptv = ps[pk].bitcast(BF16)
                for kc in range(8):
                    tr(ptv[:, kc * 128:(kc + 1) * 128], HNs[i][:, kc * 128:(kc + 1) * 128], [B_hns[i]], [P[pk]])
                cp("dve", HNTv[:, :, b * 128:(b + 1) * 128], ptv.rearrange("p (c t) -> p c t", c=8), [P[pk]], [B_hnt])

            def attention_pair(kts, B_ks, qts, B_qs, pair_chunk, sb, escale):
                M2 = lambda m_: m_.unsqueeze(1).to_broadcast([128, 2, 128])
                for ci, (ba, bz) in enumerate(QCH):
                    q0, q1 = ba * 128, bz * 128
                    N = q1 - q0
                    if sb:
                        psO, PO = ps[4 + (ci % 2)], P[4 + (ci % 2)]
                    else:
                        ob = 4 if ci % 2 == 0 else 6
                        psO, PO = ps[ob], P[ob]
                        psD, PD = ps[ob + 1], P[ob + 1]
                    mm(psO[:, 0:N], ZEROS, FIN[:, 0:N], True, False, [B_cb, B_fin], [PO])
                    if not sb:
                        mm(psD[:, 0:N], ZEROS, FIN[:, 0:N], True, False, [B_cb, B_fin], [PD])
                    steps = list(range(bz - 1, -1, -1))

                    def geom(j):
                        tq0 = max(q0, j * 128)
                        return tq0, q1 - tq0, tq0 - q0, j >= ba

                    def qk(si):
                        j = steps[si]
                        tq0, n, c0, diag = geom(j)
                        for hh in range(2):
                            bank = hh if sb else 2 * (si % 2) + hh
                            mm(ps[bank][:, 0:n], kts[hh][:, j * 128:(j + 1) * 128], qts[hh][:, tq0:q1], True, True,
                               [B_ks[hh], B_qs[hh]], [P[bank]])

                    if sb:
                        SPV = [(spbv, SPB, B_spbp), (spbv2, SPB2, B_spbp2)]
                        CSB = [(2, 3), (6, 7)]

                        def stage1(si):
                            j = steps[si]
                            tq0, n, c0, diag = geom(j)
                            sv, _, Bs = SPV[si % 2]
                            act(e32v[:, :, 0:n], psP[0][:, :, 0:n], AF.Exp, [P[0], P[1]], [B_e32p])
                            act(sv[:, :, 0:n], e32v[:, :, 0:n], AF.Ln, [B_e32p], [Bs], bias=1.0)
                            if diag:
                                tt("pool", sv[:, :, 0:128], sv[:, :, 0:128], M2(MSTRICT), ALU.mult, [Bs, B_cb], [Bs])

                        def cumsum(si):
                            j = steps[si]
                            tq0, n, c0, diag = geom(j)
                            _, sp2, Bs = SPV[si % 2]
                            cb_ = CSB[si % 2]
                            first = si == 0
                            for hh in range(2):
                                bk = cb_[hh]
                                mm(ps[bk][:, 0:n], kts[hh][:, j * 128:(j + 1) * 128], qts[hh][:, tq0:q1], True, False,
                                   [B_ks[hh], B_qs[hh]], [P[bk]])
                            for hh in range(2):
                                bk = cb_[hh]
                                mm(ps[bk][:, 0:n], NEGU0 if j == 0 else NEGU, sp2[hh][:, 0:n], False, first,
                                   [B_cb, Bs], [P[bk]])
                                if not first:
                                    mm(ps[bk][:, 0:n], NEGONES, C16[hh][:, c0:c0 + n], False, True,
                                       [B_cb, B_c16p], [P[bk]])

                        def carry(si):
                            j = steps[si]
                            tq0, n, c0, diag = geom(j)
                            sv, _, Bs = SPV[si % 2]
                            if j == 0:
                                return
                            if si == 0:
                                memset("pool", C32P, 0.0, [B_c32p])
                            tt("dve", c32v[:, :, c0:c0 + n], c32v[:, :, c0:c0 + n], sv[:, :, 0:n], ALU.add,
                               [B_c32p, Bs], [B_c32p])
                            cp("dve", c16v[:, :, 0:N], c32v[:, :, 0:N], [B_c32p], [B_c16p])

                        def exp2(si):
                            j = steps[si]
                            tq0, n, c0, diag = geom(j)
                            cb_ = CSB[si % 2]
                            e = si % 2
                            act(atbv[e][:, :, 0:n], psP[cb_[0] // 2][:, :, 0:n], AF.Exp, [P[cb_[0]], P[cb_[1]]], [B_atbp[e]])
                            if diag:
                                tt("pool", atbv[e][:, :, 0:128], atbv[e][:, :, 0:128], M2(MSTRICT), ALU.mult,
                                   [B_atbp[e], B_cb], [B_atbp[e]])

                        def av(si):
                            j = steps[si]
                            tq0, n, c0, diag = geom(j)
                            e = si % 2
                            for hh in range(2):
                                mm(psO[:, c0:c0 + n], VPv[:, j, hh, :], ATB[2 * e + hh][:, 0:n], False, j == 0 and hh == 1,
                                   [B_vp, B_atbp[e]], [PO])

                        ns = len(steps)
                        qk(0)
                        stage1(0)
                        if ns > 1:
                            qk(1)
                        for si in range(ns):
                            cumsum(si)
                            carry(si)
                            if si + 1 < ns:
                                stage1(si + 1)
                            if si + 2 < ns:
                                qk(si + 2)
                            exp2(si)
                            av(si)
                    else:
                        qk(0)
                    for si, j in (enumerate(steps) if not sb else []):
                        tq0, n, c0, diag = geom(j)
                        first = si == 0
                        last = j == 0
                        if True:
                            e = si % 2
                            act(atbv[e][:, :, 0:n], psP[e][:, :, 0:n], AF.Exp, [P[2 * e], P[2 * e + 1]], [B_atbp[e]],
                                scale=escale)
                            if diag:
                                tt("pool", atbv[e][:, :, 0:128], atbv[e][:, :, 0:128], M2(MINCL), ALU.mult,
                                   [B_atbp[e], B_cb], [B_atbp[e]])
                            if not last:
                                qk(si + 1)
                            for hh in range(2):
                                ai = 2 * e + hh
                                mm(psO[:, c0:c0 + n], VPv[:, j, hh, :], ATB[ai][:, 0:n], False, last and hh == 1,
                                   [B_vp, B_atbp[e]], [PO])
                                mm(psD[:, c0:c0 + n], ONESA if hh == 0 else ONESB, ATB[ai][:, 0:n], False,
                                   last and hh == 1, [B_cb, B_atbp[e]], [PD])
                    if sb:
                        tt("dve", OGv[:, pair_chunk, q0:q1], psO[:, 0:N], SG[:, q0:q1], ALU.mult,
                           [PO, B_sg], [B_og])
                    else:
                        t32, B_t = T32[ci % 2], B_t32[ci % 2]
                        act(t32[:, 0:N], psD[:, 0:N], AF.Ln, [PD], [B_t], bias=1e-30)
                        act(t32[:, 0:N], t32[:, 0:N], AF.Exp, [B_t], [B_t], scale=-1.0)
                        tt("dve", t32[:, 0:N], t32[:, 0:N], SG[:, q0:q1], ALU.mult, [B_t, B_sg], [B_t])
                        tt("dve", OGv[:, pair_chunk, q0:q1], psO[:, 0:N], t32[:, 0:N], ALU.mult,
                           [PO, B_t], [B_og])

            bk_rot = [0]
            BK_ORDER = [6, 7]

            def nbk():
                bk_rot[0] = (bk_rot[0] + 1) % len(BK_ORDER)
                return BK_ORDER[bk_rot[0]]

            def build_v(lhs_fn, rhs_fn, nk, reads):
                for b0 in range(0, NB, 4):
                    nb4 = min(4, NB - b0)
                    k_ = nbk()
                    for i in range(nb4):
                        b = b0 + i
                        for kc in range(nk):
                            mm(ps[k_][:, i * 128:(i + 1) * 128], lhs_fn(kc, b), rhs_fn(kc), kc == 0, kc == nk - 1,
                               reads, [P[k_]])
                    pv = ps[k_][:, 0:nb4 * 128].rearrange("p (b n) -> p b n", b=nb4)
                    cp("dve", VPv[:, b0:b0 + nb4, 0, 0:64], pv[:, :, 0:64], [P[k_]], [B_vp])
                    cp("dve", VPv[:, b0:b0 + nb4, 1, 64:128], pv[:, :, 64:128], [P[k_]], [B_vp])

            def pj(wv, c0, m, t0, n, B_w, src_v, B_src, nkc=8):
                k_ = nbk()
                for kc in range(nkc):
                    mm(ps[k_][0:m, 0:n], wv[:, kc, c0:c0 + m], src_v[:, kc, t0:t0 + n], kc == 0, kc == nkc - 1,
                       [B_w, B_src], [P[k_]])
                return ps[k_], P[k_]

            for p in range(4):
                ws, wsv, B_w = WS[p % 2], WSv[p % 2], B_ws[p % 2]
                wv4 = ws.rearrange("p (c f n) -> p c f n", c=8, f=4)
                srcv = w0in_d.rearrange("(kc p) n -> p kc n", p=128)
                for f in range(4):
                    dma("pool", wv4[:, :, f, :], srcv[:, :, f * 512 + p * 128:f * 512 + (p + 1) * 128], [], [B_w])
                for (t0, n) in TCH:
                    pa, Pa = pj(wsv, 128, 128, t0, n, B_w, HNTv, B_hnt)
                    cp("act", KT[0][:, t0:t0 + n], pa[:, 0:n], [Pa], [B_kt[0]])
                    pa, Pa = pj(wsv, 0, 128, t0, n, B_w, HNTv, B_hnt)
                    tsc("dve", QT[0][0:64, t0:t0 + n], pa[0:64, 0:n], 0.125, ALU.mult, [Pa], [B_qt[0]])
                    tsc("dve", QT[1][64:128, t0:t0 + n], pa[64:128, 0:n], 0.125, ALU.mult, [Pa], [B_qt[1]])
                    pa, Pa = pj(wsv, 384, 128, t0, n, B_w, HNTv, B_hnt)
                    act(SG[:, t0:t0 + n], pa[:, 0:n], AF.Silu, [Pa], [B_sg])
                build_v(lambda kc, b: HNTv[:, kc, b * 128:(b + 1) * 128], lambda kc, wsv=wsv: wsv[:, kc, 256:384], 8,
                        [B_hnt, B_w])
                attention_pair([KT[0], KT[0]], [B_kt[0], B_kt[0]], QT, B_qt, p, True, 1.0)

            for i in range(2):
                memset("pool", KT[i], 0.0, [B_kt[i]])
                memset("pool", QT[i], 0.0, [B_qt[i]])
                memset("pool", KT[i][96:97, 0:NPAD], -30000.0, [B_kt[i]])
                memset("pool", QT[i][96:97, :], 1.0, [B_qt[i]])
            CKVNv1 = CKVN.rearrange("p (c t) -> p c t", c=1)
            for (t0, n) in TCH:
                for cc in range(2):
                    pa, Pa = pj(WMv, cc * 128, 128, t0, n, B_wm, HNTv, B_hnt)
                    cp("dve", T32[cc][:, 0:n], pa[:, 0:n], [Pa], [B_t32[cc]])
                    act(ATB[cc][:, 0:n], pa[:, 0:n], AF.Square, [Pa], [B_atb[cc]])
                k_ = nbk()
                for cc in range(2):
                    mm(ps[k_][:, 0:n], ONES, ATB[cc][:, 0:n], cc == 0, cc == 1, [B_cb, B_atb[cc]], [P[k_]])
                act(E32[0][:, 0:n], ps[k_][:, 0:n], AF.Ln, [P[k_]], [B_e32[0]], scale=1.0 / 256, bias=EPS)
                act(E32[0][:, 0:n], E32[0][:, 0:n], AF.Exp, [B_e32[0]], [B_e32[0]], scale=-0.5)
                for cc in range(2):
                    stt("dve", CQNv[:, cc, t0:t0 + n], T32[cc][:, 0:n], NG[:, cc:cc + 1], E32[0][:, 0:n],
                        ALU.mult, ALU.mult, [B_t32[cc], B_ng, B_e32[0]], [B_cqn])
                pa, Pa = pj(WMv, 256, 128, t0, n, B_wm, HNTv, B_hnt)
                cp("dve", T32[0][:, 0:n], pa[:, 0:n], [Pa], [B_t32[0]])
                act(ATB[2][:, 0:n], pa[:, 0:n], AF.Square, [Pa], [B_atb[2]])
                k_ = nbk()
                mm(ps[k_][:, 0:n], ONES, ATB[2][:, 0:n], True, True, [B_cb, B_atb[2]], [P[k_]])
                act(E32[1][:, 0:n], ps[k_][:, 0:n], AF.Ln, [P[k_]], [B_e32[1]], scale=1.0 / 128, bias=EPS)
                act(E32[1][:, 0:n], E32[1][:, 0:n], AF.Exp, [B_e32[1]], [B_e32[1]], scale=-0.5)
                stt("dve", CKVN[:, t0:t0 + n], T32[0][:, 0:n], NG[:, 2:3], E32[1][:, 0:n],
                    ALU.mult, ALU.mult, [B_t32[0], B_ng, B_e32[1]], [B_ckvn])
                pa, Pa = pj(WMv, 384, 32, t0, n, B_wm, HNTv, B_hnt)
                pb_, Pb_ = pj(WKRRv, 0, 32, t0, n, B_wkrr, HNTv, B_hnt)
                tt("dve", T32[0][0:32, 0:n], pa[0:32, 0:n], ROPEv[0:32, 0, t0:t0 + n], ALU.mult,
                   [Pa, B_rope], [B_t32[0]])
                tt("dve", T32[1][0:32, 0:n], pb_[0:32, 0:n], ROPEv[0:32, 1, t0:t0 + n], ALU.mult,
                   [Pb_, B_rope], [B_t32[1]])
                tt("dve", KR[0:32, t0:t0 + n], T32[0][0:32, 0:n], T32[1][0:32, 0:n], ALU.add,
                   [B_t32[0], B_t32[1]], [B_kr])

            WUKVv = WUKV.rearrange("p (h a d) -> p h a d", h=8, a=2)
            for p in range(4):
                ws, wsv, B_w = WS[p % 2], WSv[p % 2], B_ws[p % 2]
                wload(wsv[:, :, 0:128], w0in_d, 2464 + p * 128, 128, [B_w])
                for (t0, n) in TCH:
                    pa, Pa = pj(wsv, 0, 128, t0, n, B_w, HNTv, B_hnt)
                    act(SG[:, t0:t0 + n], pa[:, 0:n], AF.Silu, [Pa], [B_sg])
                    for hh in range(2):
                        h = 2 * p + hh
                        k_ = nbk()
                        mm(ps[k_][0:64, 0:n], WUKVv[:, h, 0, :], CKVN[:, t0:t0 + n], True, True,
                           [B_wukv, B_ckvn], [P[k_]])
                        cp("act", KT[hh][0:64, t0:t0 + n], ps[k_][0:64, 0:n], [P[k_]], [B_kt[hh]])
                        cp("pool", KT[hh][64:96, t0:t0 + n], KR[0:32, t0:t0 + n], [B_kr], [B_kt[hh]])
                        pa, Pa = pj(WUQv, h * 96, 64, t0, n, B_wuq, CQNv, B_cqn, nkc=2)
                        cp("act", QT[hh][0:64, t0:t0 + n], pa[0:64, 0:n], [Pa], [B_qt[hh]])
                        px, Px = pj(WUQv, h * 96 + 64, 32, t0, n, B_wuq, CQNv, B_cqn, nkc=2)
                        pr_, Pr_ = pj(WUQRv, h * 32, 32, t0, n, B_wuqr, CQNv, B_cqn, nkc=2)
                        tt("dve", T32[0][0:32, 0:n], px[0:32, 0:n], ROPEv[0:32, 0, t0:t0 + n], ALU.mult,
                           [Px, B_rope], [B_t32[0]])
                        tt("dve", T32[1][0:32, 0:n], pr_[0:32, 0:n], ROPEv[0:32, 1, t0:t0 + n], ALU.mult,
                           [Pr_, B_rope], [B_t32[1]])
                        tt("dve", QT[hh][64:96, t0:t0 + n], T32[0][0:32, 0:n], T32[1][0:32, 0:n], ALU.add,
                           [B_t32[0], B_t32[1]], [B_qt[hh]])
                build_v(lambda kc, b: CKVN[:, b * 128:(b + 1) * 128], lambda kc, p=p: WUKVv[:, 2 * p:2 * p + 2, 1, :], 1,
                        [B_ckvn, B_wukv])
                if p == 3:
                    assert pers_end + 4096 + 9216 <= A0_KT_START - (128 + 768 + 256 + 512)
                    dead = [B_hnt, B_ws[0], B_ws[1], B_wm]
                    for c in range(0, 1024, 512):
                        wload(W0Ov[:, :, c:c + 512], w0out_d, c, 512, [B_w0o] + dead)
                    for c in range(0, OD_IN, 576):
                        wload(W1Iv[:, :, c:c + 576], w1in_d, c, 576, [B_w1i] + dead)
                attention_pair(KT, B_kt, QT, B_qt, 4 + p, False, 96.0 ** -0.5)

            if debug_h1:
                for c in range(8):
                    dma("pool", dbg2_d[sq, :, c * LP:(c + 1) * LP], OG[:, c * LP:(c + 1) * LP], [B_og], [])
            S.barrier()
            for c in range(0, 1024, 512):
                wload(W1Ov[:, :, c:c + 512], w1out_d, c, 512, [B_w1o])
            dma("sp", SWEv, swae_d.rearrange("a p h r -> p a h r"), [], [B_swe])
            dma("sp", EMv[0:16], em_d, [], [B_em])
            dma("sp", CMv[0:16], cm_d, [], [B_cm])
            sk2 = sinks_d.rearrange("(p two) -> two p", two=2)
            S.op("sp", lambda e: e.dma_start(out=ESK[0:64, :], in_=sk2[0:1, :].broadcast_to([64, 8]),
                                             allow_slow_non_contiguous=True), writes=[B_esk], dma=True)
            S.op("sp", lambda e: e.dma_start(out=ESK[64:128, :], in_=sk2[1:2, :].broadcast_to([64, 8]),
                                             allow_slow_non_contiguous=True), writes=[B_esk], dma=True)
            act(ESK, ESK, AF.Exp, [B_esk], [B_esk])
            for par in range(2):
                for g in range(2):
                    memset("pool", QG[par][g], 0.0, [B_qg[par][g]])
            memset("pool", VM, 0.0, [B_vm])
            memset("pool", VR, 0.0, B_vr)

            rot = [0]

            def pbank():
                rot[0] ^= 1
                return 6 + rot[0]

            def front(b):
                par = b % 2
                h1, Bh1 = H1[par], B_h1[par]
                hv, Bhv = HNT1v[par], B_hnt1[par]
                if b == 0:
                    memset("dve", XS1, 0.0, [B_xs1])
                    dma("sp", XS1[NPAD:128, :], meta_d, [], [B_xs1])
                else:
                    dma("sp", XS1, x_d[sq, (b - 1) * 128:b * 128, :], [], [B_xs1])
                for hf in range(2):
                    for kc in range(8):
                        mm(ps[4 + hf][:, :], OGv[:, kc, b * 128:(b + 1) * 128], W0Ov[:, kc, hf * 512:(hf + 1) * 512],
                           kc == 0, kc == 7, [B_og, B_w0o], [P[4 + hf]])
                    tt("dve", h1[:, hf * 512:(hf + 1) * 512], ps[4 + hf][:, :], XS1[:, hf * 512:(hf + 1) * 512],
                       ALU.add, [P[4 + hf], B_xs1], [Bh1])
                if debug_h1:
                    dma("sp", dbg_d[sq, b * 128:(b + 1) * 128, :], h1, [Bh1], [])
                yield
                rmsnorm_block(h1, Bh1, 1, HN1, B_hn1, ST1, B_st1, JK1, B_jk1)
                pk = pbank()
                ptv = ps[pk].bitcast(BF16)
                for kc in range(8):
                    tr(ptv[:, kc * 128:(kc + 1) * 128], HN1[:, kc * 128:(kc + 1) * 128], [B_hn1], [P[pk]])
                cp("dve", hv, ptv.rearrange("p (c t) -> p c t", c=8), [P[pk]], [Bhv])
                yield
                pk = pbank()
                for kc in range(8):
                    mm(ps[pk][:, 0:128], W1Iv[:, kc, 1024:1152], hv[:, kc, :], kc == 0, kc == 7,
                       [B_w1i, Bhv], [P[pk]])
                cp("act", KT1[:, b * 128:(b + 1) * 128], ps[pk][:, 0:128], [P[pk]], [B_kt1[b]])
                slot = b % 3
                pk = pbank()
                if b == 0:
                    for kc in range(8):
                        mm(ps[pk][0:16, 0:128], hv[:, kc, NPAD:128], W1Iv[:, kc, 1152:1280], kc == 0, kc == 7,
                           [Bhv, B_w1i], [P[pk]])
                    for kh in range(2):
                        cp("dve", VMv[0:16, 2 * kh, 0:64], ps[pk][0:16, kh * 64:(kh + 1) * 64], [P[pk]], [B_vm])
                        cp("dve", VMv[0:16, 2 * kh + 1, 64:128], ps[pk][0:16, kh * 64:(kh + 1) * 64], [P[pk]], [B_vm])
                    return
                for kc in range(8):
                    mm(ps[pk][:, 0:128], hv[:, kc, :], W1Iv[:, kc, 1152:1280], kc == 0, kc == 7,
                       [Bhv, B_w1i], [P[pk]])
                for kh in range(2):
                    cp("dve", VRv[:, slot, 2 * kh, 0:64], ps[pk][:, kh * 64:(kh + 1) * 64], [P[pk]], [B_vr[slot]])
                    cp("dve", VRv[:, slot, 2 * kh + 1, 64:128], ps[pk][:, kh * 64:(kh + 1) * 64], [P[pk]], [B_vr[slot]])
                yield
                for q4 in range(2):
                    pk = pbank()
                    for i in range(4):
                        pr = q4 * 4 + i
                        for kc in range(8):
                            mm(ps[pk][:, i * 128:(i + 1) * 128], W1Iv[:, kc, pr * 128:(pr + 1) * 128], hv[:, kc, :],
                               kc == 0, kc == 7, [B_w1i, Bhv], [P[pk]])
                    g = q4
                    gs = slice(g * 64, g * 64 + 64)
                    pv = ps[pk].rearrange("p (i t) -> p i t", i=4)
                    qv = QGv[par][g][gs].rearrange("p (i two) t -> p two i t", two=2)
                    cp("act" if g == 0 else "dve", qv[:, 0], pv[0:64], [P[pk]], [B_qg[par][g]])
                    cp("dve" if g == 0 else "act", qv[:, 1], pv[64:128], [P[pk]], [B_qg[par][g]])
                    yield
                for c4 in range(2):
                    pk = pbank()
                    for i in range(4):
                        cc = c4 * 4 + i
                        for kc in range(8):
                            mm(ps[pk][:, i * 128:(i + 1) * 128], W1Iv[:, kc, 1280 + cc * 128:1280 + (cc + 1) * 128],
                               hv[:, kc, :], kc == 0, kc == 7, [B_w1i, Bhv], [P[pk]])
                    act(SG1v[par][:, c4 * 4:(c4 + 1) * 4, :], ps[pk].rearrange("p (i t) -> p i t", i=4), AF.Silu,
                        [P[pk]], [B_sg1[par]])
                    yield

            def back_parts(b):
                par = b % 2
                slot = b % 3
                h1, Bh1 = H1[par], B_h1[par]
                tiles = []
                for g in range(2):
                    if b >= 2:
                        tiles.append((g, "prev", KT1[:, (b - 1) * 128:b * 128], 128, (b - 1) % 3, B_kt1[b - 1]))
                    tiles.append((g, "cur", KT1[:, b * 128:(b + 1) * 128], 128, slot, B_kt1[b]))
                    tiles.append((g, "meta", KT1[:, NPAD:128], 16, None, B_kt1[0]))
                nt = len(tiles)

                def qk(ti):
                    g, kind, kk, nk, vs, Bk = tiles[ti]
                    for hf in range(2):
                        mm(ps[hf][0:nk, :], kk, QGv[par][g][:, hf * 4:(hf + 1) * 4, :], True, True,
                           [Bk, B_qg[par][g]], [P[hf]])

                def soft(ti):
                    g, kind, kk, nk, vs, Bk = tiles[ti]
                    e = ti % 2
                    exv, pbv = EXv[e], PBv[e]
                    for hf in range(2):
                        act(EX[e][0:nk, hf * 512:(hf + 1) * 512], ps[hf][0:nk, :], AF.Exp, [P[hf]], [B_exh[e][hf]],
                            scale=0.125)
                        if kind != "meta":
                            a = 0 if kind == "prev" else 1
                            hs = slice(hf * 4, (hf + 1) * 4)
                            tt("pool" if hf == 0 else "dve", pbv[:, hs, :], exv[:, hs, :],
                               SWEv[:, a, g * 8 + hf * 4:g * 8 + (hf + 1) * 4, :], ALU.mult,
                               [B_exh[e][hf], B_swe], [B_pbh[e][hf]])
                    if kind == "meta":
                        tt("dve", exv[0:16], exv[0:16], EMv[0:16, g * 8:(g + 1) * 8, :], ALU.mult,
                           B_exh[e] + [B_em], B_exh[e])
                        tt("dve", pbv[0:16], exv[0:16],
                           CMv[0:16, b, g * 8:(g + 1) * 8].unsqueeze(2).to_broadcast([16, 8, 128]), ALU.mult,
                           B_exh[e] + [B_cm], B_pbh[e])

                def prologue():
                    qk(0)
                    soft(0)
                    if nt > 1:
                        qk(1)

                def tile_gen():
                    for ti, (g, kind, kk, nk, vs, Bk) in enumerate(tiles):
                        e = ti % 2
                        pbv, B_p = PBv[e], B_pbh[e]
                        if kind == "meta":
                            va, vb2 = VMv[0:16, 2 * g, :], VMv[0:16, 2 * g + 1, :]
                            B_v = B_vm
                        else:
                            va, vb2 = VRv[:, vs, 2 * g, :], VRv[:, vs, 2 * g + 1, :]
                            B_v = B_vr[vs]
                        pe_ = pbv[0:nk].rearrange("p (q two) t -> p two q t", two=2)
                        gfirst = kind == ("prev" if b >= 2 else "cur")
                        glast = kind == "meta"
                        mm(ps[2][:, :], va, pe_[:, 0], gfirst, False, [B_v] + B_p, [P[2]])
                        mm(ps[2][:, :], vb2, pe_[:, 1], False, glast, [B_v] + B_p, [P[2]])
                        mm(ps[3][:, :], ONESA[0:nk, :], pe_[:, 0], gfirst, False, [B_cb] + B_p, [P[3]])
                        mm(ps[3][:, :], ONESB[0:nk, :], pe_[:, 1], False, glast, [B_cb] + B_p, [P[3]])
                        if ti + 1 < nt:
                            soft(ti + 1)
                        if ti + 2 < nt:
                            qk(ti + 2)
                        if glast:
                            R3 = R32.rearrange("p (q t) -> p q t", q=4)
                            U3 = U32.rearrange("p (q t) -> p q t", q=4)
                            tt("dve", R3, ps[3].rearrange("p (q t) -> p q t", q=4),
                               ESK[:, g * 4:(g + 1) * 4].unsqueeze(2).to_broadcast([128, 4, 128]), ALU.add,
                               [P[3], B_esk], [B_r32])
                            act(R32, R32, AF.Ln, [B_r32], [B_r32])
                            act(R32, R32, AF.Exp, [B_r32], [B_r32], scale=-1.0)
                            tt("pool", U3, R3, SG1v[par][:, g * 4:(g + 1) * 4, :], ALU.mult, [B_r32, B_sg1[par]], [B_u32])
                            tt("dve", OG1v[:, g * 4:(g + 1) * 4, :], ps[2].rearrange("p (q t) -> p q t", q=4), U3,
                               ALU.mult, [P[2], B_u32], [B_og1])
                        yield

                def tail():
                    for hf in range(2):
                        for kc in range(8):
                            mm(ps[4 + hf][:, :], OG1v[:, kc, :], W1Ov[:, kc, hf * 512:(hf + 1) * 512],
                               kc == 0, kc == 7, [B_og1, B_w1o], [P[4 + hf]])
                        tt("dve", h1[:, hf * 512:(hf + 1) * 512], ps[4 + hf][:, :], h1[:, hf * 512:(hf + 1) * 512],
                           ALU.add, [P[4 + hf], Bh1], [Bh1])
                    rmsnorm_block(h1, Bh1, 2, OUTB, B_outb, ST2, B_st2, JK1, B_jk1)
                    dma("sp", out_d[sq, (b - 1) * 128:b * 128, :], OUTB, [B_outb], [])

                return prologue, tile_gen, tail

            def drain(gen):
                for _ in gen:
                    pass

            drain(front(0))
            drain(front(1))
            parts = back_parts(1)
            parts[0]()
            for b in range(1, NB):
                prologue, tile_gen, tail = parts
                alive = [tile_gen()]
                if b + 1 < NB:
                    alive.append(front(b + 1))
                while alive:
                    for gq in list(alive):
                        try:
                            next(gq)
                        except StopIteration:
                            alive.remove(gq)
                if b + 1 < NB:
                    parts = back_parts(b + 1)
                    parts[0]()
                tail()

        S.finalize()
        with nc.Block() as block:
            @block.tensor
            def _(e):
                S.emit_engine("pe", e, sems, dma_sems)

            @block.scalar
            def _(e):
                S.emit_engine("act", e, sems, dma_sems)

            @block.vector
            def _(e):
                S.emit_engine("dve", e, sems, dma_sems)

            @block.gpsimd
            def _(e):
                S.emit_engine("pool", e, sems, dma_sems)

            @block.sync
            def _(e):
                S.emit_engine("sp", e, sems, dma_sems, final_wait=True)
    return nc


_NC_CACHE = {}


def _common_inputs(meta, norm_g, final_g, ev_w_in, ev_q_norm_g, ev_kv_norm_g, ev_w_uq, ev_w_ukv,
                   ev_w_out, od_w_in, od_sinks, od_w_out):
    cb, rope, swae, em, cm = _const_tables()
    f = lambda a: np.ascontiguousarray(np.asarray(a, dtype=np.float32))
    return {
        "meta": f(meta),
        "gains": f(np.concatenate([np.asarray(norm_g), np.asarray(final_g)[None, :]], 0)),
        "w0in": f(ev_w_in[0]), "qng": f(ev_q_norm_g[0]), "kvng": f(ev_kv_norm_g[0]),
        "wuq": f(ev_w_uq[0]), "wukv": f(ev_w_ukv[0]), "w0out": f(ev_w_out[0]),
        "w1in": f(od_w_in[0]), "sinks": f(od_sinks[0]), "w1out": f(od_w_out[0]),
        "cb": cb, "rope": f(rope), "swae": f(swae), "em": f(em), "cm": f(cm),
    }


def kernel(x, meta, norm_g, final_g, ev_w_in, ev_q_norm_g, ev_kv_norm_g, ev_w_uq, ev_w_ukv,
           ev_w_out, od_w_in, od_sinks, od_w_out):
    n = 8
    x = np.asarray(x, dtype=np.float32)
    common = _common_inputs(meta, norm_g, final_g, ev_w_in, ev_q_norm_g, ev_kv_norm_g, ev_w_uq,
                            ev_w_ukv, ev_w_out, od_w_in, od_sinks, od_w_out)
    if "nc" not in _NC_CACHE:
        _NC_CACHE["nc"] = build_nc()
    nc = _NC_CACHE["nc"]
    in_maps = []
    for c in range(n):
        m = dict(common)
        m["x"] = np.ascontiguousarray(x[c * SEQ_PER_CORE:(c + 1) * SEQ_PER_CORE])
        in_maps.append(m)
    res = run_bass_kernel_spmd(nc, in_maps, core_ids=list(range(n)))
    return np.concatenate([r["out"] for r in res.results], axis=0)
```

```python
import numpy as np
from contextlib import ExitStack
import concourse.bass as bass
import concourse.mybir as mybir
from concourse.bass_utils import run_bass_kernel_spmd

F32 = mybir.dt.float32
BF16 = mybir.dt.bfloat16
AF = mybir.ActivationFunctionType
ALU = mybir.AluOpType

ENGS = ["pe", "act", "dve", "pool", "sp"]

D = 1024
LP = 2176
NB = 17
NPAD = 112
EPS = 1e-6
SEQ_PER_CORE = 2
EV_IN = 2976
OD_IN = 2304


class Buf:
    __slots__ = ("name", "last_w", "readers", "excl")

    def __init__(self, name, excl=False):
        self.name = name
        self.last_w = None
        self.readers = []
        self.excl = excl


class Op:
    __slots__ = ("eng", "fn", "deps", "signal", "is_dma", "dsem", "dval", "cnt", "prev_dma")

    def __init__(self, eng, fn, is_dma):
        self.eng = eng
        self.fn = fn
        self.deps = []
        self.signal = False
        self.is_dma = is_dma
        self.dsem = None
        self.dval = 0
        self.cnt = 0
        self.prev_dma = None


class Sched:
    def __init__(self, n_dma_sems=8):
        self.ops = {e: [] for e in ENGS}
        self.n_dma_sems = n_dma_sems
        self.dma_rr = {e: 0 for e in ENGS}
        self.dma_cnt = {}
        self.dma_last = {}

    def op(self, eng, fn, reads=(), writes=(), dma=False):
        o = Op(eng, fn, dma)
        deps = {}
        for b in reads:
            if b.last_w is not None:
                deps[id(b.last_w)] = b.last_w
            if b.excl:
                for r in b.readers:
                    if r.eng != eng:
                        deps[id(r)] = r
        for b in writes:
            if b.last_w is not None:
                deps[id(b.last_w)] = b.last_w
            for r in b.readers:
                deps[id(r)] = r
        final = []
        for d in deps.values():
            if d is o:
                continue
            if d.eng == "pe" and eng == "pe" and (not d.is_dma) and (not dma):
                continue
            final.append(d)
            d.signal = True
        o.deps = final
        for b in reads:
            if not dma:
                b.readers = [r for r in b.readers if r.is_dma or r.eng != eng]
            b.readers.append(o)
        for b in writes:
            b.last_w = o
            b.readers = []
        if dma:
            k = self.dma_rr[eng]
            self.dma_rr[eng] = (k + 1) % self.n_dma_sems
            key = (eng, k)
            self.dma_cnt[key] = self.dma_cnt.get(key, 0) + 1
            o.dsem = key
            o.dval = 16 * self.dma_cnt[key]
            o.prev_dma = self.dma_last.get(key)
            self.dma_last[key] = o
        self.ops[eng].append(o)
        return o

    def barrier(self):
        lasts = []
        for e in ENGS:
            for o in reversed(self.ops[e]):
                if (not o.is_dma) and o.fn is not None:
                    lasts.append(o)
                    break
        dmas = list(self.dma_last.values())
        for e in ENGS:
            m = Op(e, None, False)
            m.deps = [d for d in lasts if d.eng != e] + dmas
            for d in m.deps:
                d.signal = True
            self.ops[e].append(m)

    def finalize(self):
        for e in ENGS:
            c = 0
            for o in self.ops[e]:
                if o.is_dma or o.fn is None:
                    continue
                if o.signal:
                    c += 1
                    o.cnt = c

    def emit_engine(self, eng_name, e, sems, dma_sems, final_wait=False):
        waited = {}

        def wait(key, sem, val):
            if val <= 0:
                return
            if waited.get(key, 0) < val:
                e.wait_ge(sem, val)
                waited[key] = val

        for o in self.ops[eng_name]:
            for d in o.deps:
                if d.is_dma:
                    wait(d.dsem, dma_sems[d.dsem], d.dval)
                else:
                    wait(d.eng, sems[d.eng], d.cnt)
            if o.is_dma and o.prev_dma is not None:
                wait(o.dsem, dma_sems[o.dsem], o.prev_dma.dval)
            if o.fn is None:
                continue
            ins = o.fn(e)
            if o.is_dma:
                ins.then_inc(dma_sems[o.dsem], 16)
            elif o.signal:
                ins.then_inc(sems[eng_name], 1)
        if final_wait:
            for key, o in self.dma_last.items():
                wait(key, dma_sems[key], o.dval)


def _const_tables():
    r = np.arange(128)
    s = r[:, None]
    t = r[None, :]
    ident = (s == t).astype(np.float32)
    negU = -(s >= t).astype(np.float32)
    negU0 = negU * (s >= NPAD)
    negOnes = -np.ones((128, 128), np.float32)
    negOnes0 = negOnes * (s >= NPAD)
    ones = np.ones((128, 128), np.float32)
    mstrict = (s < t).astype(np.float32)
    mincl = (s <= t).astype(np.float32)
    onesA = np.concatenate([np.ones((128, 64)), np.zeros((128, 64))], 1).astype(np.float32)
    onesB = np.concatenate([np.zeros((128, 64)), np.ones((128, 64))], 1).astype(np.float32)
    zeros = np.zeros((128, 128), np.float32)
    cb = np.concatenate([ident, negU, negU0, negOnes, negOnes0, ones, mstrict, mincl,
                         onesA, onesB, zeros], axis=1)
    half = 16
    inv = 10000.0 ** (-np.arange(half, dtype=np.float64) / half)
    pos = (np.arange(LP) - NPAD).astype(np.float64)
    ang = inv[:, None] * pos[None, :]
    cos = np.concatenate([np.cos(ang), np.cos(ang)], 0).astype(np.float32)
    sin = np.concatenate([np.sin(ang), np.sin(ang)], 0).astype(np.float32)
    rope = np.stack([cos, sin], 0)
    H = 16
    slopes = 2.0 ** (-8.0 * (np.arange(H, dtype=np.float64) + 1.0) / H)
    sl = slopes[None, :, None]
    dprev = (128 + r[None, None, :] - r[:, None, None]).astype(np.float64)
    eprev = np.where(dprev < 128, np.exp(-sl * dprev), 0.0)
    dcur = (r[None, None, :] - r[:, None, None]).astype(np.float64)
    ecur = np.where(dcur >= 0, np.exp(-sl * np.maximum(dcur, 0)), 0.0)
    m = np.arange(16)
    dm = (16 + r[None, None, :] - m[:, None, None]).astype(np.float64)
    em = np.exp(-sl * dm)
    n = np.arange(NB)
    cm = np.exp(-slopes[None, None, :] * 128.0 * np.maximum(n - 1, 0)[None, :, None])
    cm = np.broadcast_to(cm, (16, NB, H))
    swa_e = np.stack([eprev, ecur], 0).astype(np.float32)
    return (cb.astype(np.float32), rope, swa_e, em.astype(np.float32),
            np.ascontiguousarray(cm).astype(np.float32))


def build_nc(debug_h1=False, n_seq=SEQ_PER_CORE):
    nc = bass.Bass("TRN2", target_bir_lowering=False)
    dt = nc.dram_tensor
    x_d = dt("x", [n_seq, 2048, D], F32, kind="ExternalInput").ap()
    meta_d = dt("meta", [16, D], F32, kind="ExternalInput").ap()
    gains_d = dt("gains", [3, D], F32, kind="ExternalInput").ap()
    w0in_d = dt("w0in", [D, EV_IN], F32, kind="ExternalInput").ap()
    qng_d = dt("qng", [256], F32, kind="ExternalInput").ap()
    kvng_d = dt("kvng", [128], F32, kind="ExternalInput").ap()
    wuq_d = dt("wuq", [256, 768], F32, kind="ExternalInput").ap()
    wukv_d = dt("wukv", [128, 1024], F32, kind="ExternalInput").ap()
    w0out_d = dt("w0out", [D, D], F32, kind="ExternalInput").ap()
    w1in_d = dt("w1in", [D, OD_IN], F32, kind="ExternalInput").ap()
    sinks_d = dt("sinks", [16], F32, kind="ExternalInput").ap()
    w1out_d = dt("w1out", [D, D], F32, kind="ExternalInput").ap()
    cb_d = dt("cb", [128, 11 * 128], F32, kind="ExternalInput").ap()
    rope_d = dt("rope", [2, 32, LP], F32, kind="ExternalInput").ap()
    swae_d = dt("swae", [2, 128, 16, 128], F32, kind="ExternalInput").ap()
    em_d = dt("em", [16, 16, 128], F32, kind="ExternalInput").ap()
    cm_d = dt("cm", [16, NB, 16], F32, kind="ExternalInput").ap()
    out_d = dt("out", [n_seq, 2048, D], F32, kind="ExternalOutput").ap()
    if debug_h1:
        dbg_d = dt("dbg", [n_seq, LP, D], F32, kind="ExternalOutput").ap()
        dbg2_d = dt("dbg2", [n_seq, 128, 8 * LP], F32, kind="ExternalOutput").ap()

    S = Sched()
    with ExitStack() as es:
        ARENA_F32 = 53000
        arena = es.enter_context(nc.sbuf_tensor("arena", [128, ARENA_F32], F32))
        psq = [es.enter_context(nc.psum_tensor(f"psq{i}", [128, 1024], F32)) for i in range(4)]
        ps = [psq[i // 2][:, (i % 2) * 512:(i % 2 + 1) * 512] for i in range(8)]
        psP = [q_.rearrange("p (h n) -> p h n", h=2) for q_ in psq]
        sems = {e: es.enter_context(nc.semaphore(f"s_{e}")) for e in ENGS}
        dma_sems = {}
        for e in ["sp", "pool"]:
            for k in range(S.n_dma_sems):
                dma_sems[(e, k)] = es.enter_context(nc.semaphore(f"d_{e}{k}"))
        P = [Buf(f"ps{i}", excl=True) for i in range(8)]

        class Arena:
            def __init__(self, start=0):
                self.off = start

            def f32(self, n):
                a = arena[:, self.off:self.off + n]
                self.off += n
                return a

            def bf16(self, n):
                assert n % 2 == 0
                a = arena[:, self.off:self.off + n // 2].bitcast(BF16)
                self.off += n // 2
                return a

        A = Arena(0)
        CB = A.bf16(11 * 128)
        cb_v = lambda i: CB[:, i * 128:(i + 1) * 128]
        IDENT, NEGU, NEGU0, NEGONES, NEGONES0, ONES, MSTRICT, MINCL, ONESA, ONESB, ZEROS = [cb_v(i) for i in range(11)]
        G = A.f32(3 * D)
        NG = A.f32(4)
        FIN = A.bf16(512)
        OG = A.bf16(8 * LP)
        OGv = OG.rearrange("p (c t) -> p c t", c=8)
        pers_end = A.off
        B_cb, B_g, B_ng, B_fin, B_og = Buf("cb"), Buf("g"), Buf("ng"), Buf("fin"), Buf("og")

        def mm(out, lhsT, rhs, start, stop, reads, writes):
            S.op("pe", lambda e: e.matmul(out, lhsT=lhsT, rhs=rhs, start=start, stop=stop),
                 reads=reads, writes=writes)

        def tr(out, in_, reads, writes):
            S.op("pe", lambda e: e.transpose(out, in_, IDENT), reads=list(reads) + [B_cb], writes=writes)

        def act(out, in_, func, reads, writes, scale=1.0, bias=0.0, accum_out=None, eng="act"):
            if accum_out is None:
                S.op("act", lambda e: e.activation(out=out, in_=in_, func=func, bias=bias, scale=scale),
                     reads=reads, writes=writes)
            else:
                S.op("act", lambda e: e.activation(out=out, in_=in_, func=func, bias=bias, scale=scale,
                                                   accum_out=accum_out), reads=reads, writes=writes)

        def tt(eng, out, in0, in1, op, reads, writes):
            S.op(eng, lambda e: e.tensor_tensor(out=out, in0=in0, in1=in1, op=op), reads=reads, writes=writes)

        def tsc(eng, out, in0, s1, op0, reads, writes, s2=None, op1=None):
            if op1 is None:
                S.op(eng, lambda e: e.tensor_scalar(out=out, in0=in0, scalar1=s1, scalar2=None, op0=op0),
                     reads=reads, writes=writes)
            else:
                S.op(eng, lambda e: e.tensor_scalar(out=out, in0=in0, scalar1=s1, scalar2=s2, op0=op0, op1=op1),
                     reads=reads, writes=writes)

        def stt(eng, out, in0, scalar, in1, op0, op1, reads, writes):
            S.op(eng, lambda e: e.scalar_tensor_tensor(out=out, in0=in0, scalar=scalar, in1=in1, op0=op0, op1=op1),
                 reads=reads, writes=writes)

        def cp(eng, out, in_, reads, writes):
            if eng == "act":
                S.op("act", lambda e: e.copy(out=out, in_=in_), reads=reads, writes=writes)
            else:
                S.op(eng, lambda e: e.tensor_copy(out=out, in_=in_), reads=reads, writes=writes)

        def memset(eng, ap, val, writes):
            S.op(eng, lambda e: e.memset(ap, val), writes=writes)

        def dma(eng, out, in_, reads, writes):
            S.op(eng, lambda e: e.dma_start(out=out, in_=in_), reads=reads, writes=writes, dma=True)

        def recip(out, in_, reads, writes):
            S.op("dve", lambda e: e.reciprocal(out=out, in_=in_), reads=reads, writes=writes)

        def wload(dst3, src2, c0, ncols, writes, reads=()):
            kc = src2.shape[0] // 128
            srcv = src2.rearrange("(kc p) n -> p kc n", p=128)
            dma("pool", dst3, srcv[:, :, c0:c0 + ncols], reads, writes)

        dma("pool", CB, cb_d, [], [B_cb])
        for i in range(3):
            dma("sp", G[:, i * D:(i + 1) * D], gains_d[i:i + 1, :].broadcast_to([128, D]), [], [B_g])
        for c in range(2):
            dma("sp", NG[:, c:c + 1], qng_d[c * 128:(c + 1) * 128].rearrange("(p o) -> p o", o=1), [], [B_ng])
        dma("sp", NG[:, 2:3], kvng_d.rearrange("(p o) -> p o", o=1), [], [B_ng])
        memset("dve", FIN, 1.0, [B_fin])

        A0 = Arena(pers_end)
        HNT = A0.bf16(8 * LP); HNTv = HNT.rearrange("p (c t) -> p c t", c=8); B_hnt = Buf("hnt")
        WS = [A0.bf16(8 * 512) for _ in range(2)]; B_ws = [Buf("ws0"), Buf("ws1")]
        WSv = [w.rearrange("p (c n) -> p c n", c=8) for w in WS]
        WM = A0.bf16(8 * 416); WMv = WM.rearrange("p (c n) -> p c n", c=8); B_wm = Buf("wm")
        WKRR = A0.bf16(8 * 32); WKRRv = WKRR.rearrange("p (c n) -> p c n", c=8); B_wkrr = Buf("wkrr")
        WUQ = A0.bf16(2 * 768); WUQv = WUQ.rearrange("p (c n) -> p c n", c=2); B_wuq = Buf("wuq")
        WUQR = A0.bf16(2 * 256); WUQRv = WUQR.rearrange("p (c n) -> p c n", c=2); B_wuqr = Buf("wuqr")
        WUKV = A0.bf16(1024); B_wukv = Buf("wukv")
        A0_KT_START = A0.off
        KT = [A0.bf16(LP) for _ in range(2)]; B_kt = [Buf("kt0"), Buf("kt1")]
        QT = [A0.bf16(LP) for _ in range(2)]; B_qt = [Buf("qt0"), Buf("qt1")]
        VP = A0.bf16(NB * 256); VPv = VP.rearrange("p (b v n) -> p b v n", b=NB, v=2); B_vp = Buf("vp")
        SG = A0.bf16(LP); B_sg = Buf("sg")
        CQN = A0.bf16(2 * LP); CQNv = CQN.rearrange("p (c t) -> p c t", c=2); B_cqn = Buf("cqn")
        CKVN = A0.bf16(LP); B_ckvn = Buf("ckvn")
        KR = A0.bf16(LP); B_kr = Buf("kr")
        ROPE = A0.f32(2 * LP); ROPEv = ROPE.rearrange("p (c t) -> p c t", c=2); B_rope = Buf("rope")
        def pairbuf(ap):
            return ap.rearrange("p (h n) -> p h n", h=2), [ap[:, 0:512], ap[:, 512:1024]]
        E32P = A0.f32(1024); e32v, E32 = pairbuf(E32P); B_e32p = Buf("e32p"); B_e32 = [B_e32p, B_e32p]
        SPBP = A0.bf16(1024); spbv, SPB = pairbuf(SPBP); B_spbp = Buf("spbp"); B_spb = [B_spbp, B_spbp]
        SPBP2 = A0.bf16(1024); spbv2, SPB2 = pairbuf(SPBP2); B_spbp2 = Buf("spbp2")
        C32P = A0.f32(1024); c32v, C32 = pairbuf(C32P); B_c32p = Buf("c32p")
        C16P = A0.bf16(1024); c16v, C16 = pairbuf(C16P); B_c16p = Buf("c16p")
        ATBP = [A0.bf16(1024) for _ in range(2)]
        atbv = [pairbuf(a_)[0] for a_ in ATBP]
        ATB = [pairbuf(ATBP[i // 2])[1][i % 2] for i in range(4)]
        B_atbp = [Buf("atbp0"), Buf("atbp1")]; B_atb = [B_atbp[i // 2] for i in range(4)]
        XS = A0.f32(D); B_xs = Buf("xs")
        T32 = [XS[:, 0:512], XS[:, 512:1024]]; B_t32 = [Buf("t32a"), Buf("t32b")]
        HN = A0.bf16(D); B_hn = Buf("hn")
        ST = A0.f32(8); B_st = Buf("st"); B_stb = Buf("stb")
        JK = HN; B_jk = B_hn
        l0_end = A0.off
        assert l0_end <= ARENA_F32, l0_end

        A1 = Arena(pers_end)
        W0O = A1.bf16(8 * D); W0Ov = W0O.rearrange("p (c n) -> p c n", c=8); B_w0o = Buf("w0o")
        W1I = A1.bf16(8 * OD_IN); W1Iv = W1I.rearrange("p (c n) -> p c n", c=8); B_w1i = Buf("w1i")
        W1O = A1.bf16(8 * D); W1Ov = W1O.rearrange("p (c n) -> p c n", c=8); B_w1o = Buf("w1o")
        SWE = A1.f32(2 * 16 * 128); SWEv = SWE.rearrange("p (a h r) -> p a h r", a=2, h=16); B_swe = Buf("swe")
        EM = A1.f32(16 * 128); EMv = EM.rearrange("p (h r) -> p h r", h=16); B_em = Buf("em")
        CM = A1.f32(NB * 16); CMv = CM.rearrange("p (n h) -> p n h", n=NB); B_cm = Buf("cm")
        ESK = A1.f32(8); B_esk = Buf("esk")
        KT1 = A1.bf16(LP); B_kt1 = [Buf(f"kt1l{i}") for i in range(NB)]
        VR = A1.bf16(3 * 4 * 128); VRv = VR.rearrange("p (b v n) -> p b v n", b=3, v=4); B_vr = [Buf("vr0"), Buf("vr1"), Buf("vr2")]
        VM = A1.bf16(4 * 128); VMv = VM.rearrange("p (v n) -> p v n", v=4); B_vm = Buf("vm")
        H1 = [A1.f32(D) for _ in range(2)]; B_h1 = [Buf("h1a"), Buf("h1b")]
        XS1 = A1.f32(D); B_xs1 = Buf("xs1")
        HN1 = A1.bf16(D); B_hn1 = Buf("hn1")
        HNT1 = [A1.bf16(8 * 128) for _ in range(2)]; HNT1v = [h.rearrange("p (c t) -> p c t", c=8) for h in HNT1]
        B_hnt1 = [Buf("hnt1a"), Buf("hnt1b")]
        QG = [[A1.bf16(8 * 128) for _ in range(2)] for _ in range(2)]
        QGv = [[q.rearrange("p (h t) -> p h t", h=8) for q in qq] for qq in QG]
        B_qg = [[Buf(f"qg{i}{j}") for j in range(2)] for i in range(2)]
        SG1 = [A1.bf16(8 * 128) for _ in range(2)]; SG1v = [x_.rearrange("p (c t) -> p c t", c=8) for x_ in SG1]
        B_sg1 = [Buf("sg1a"), Buf("sg1b")]
        OG1 = A1.bf16(8 * 128); OG1v = OG1.rearrange("p (c t) -> p c t", c=8); B_og1 = Buf("og1")
        EX = [A1.f32(1024) for _ in range(2)]; EXv = [x_.rearrange("p (h t) -> p h t", h=8) for x_ in EX]
        B_ex = [Buf("exa"), Buf("exb")]
        PB = [A1.bf16(1024) for _ in range(2)]; PBv = [x_.rearrange("p (h t) -> p h t", h=8) for x_ in PB]
        B_pb = [Buf("pba"), Buf("pbb")]
        B_exh = [[Buf(f"exh{i}{j}") for j in range(2)] for i in range(2)]
        B_pbh = [[Buf(f"pbh{i}{j}") for j in range(2)] for i in range(2)]
        R32 = A1.f32(512); B_r32 = Buf("r32")
        U32 = A1.f32(512); B_u32 = Buf("u32")
        ST1 = A1.f32(8); B_st1 = Buf("st1")
        ST2 = A1.f32(8); B_st2 = Buf("st2")
        JK1 = HN1; B_jk1 = B_hn1
        OUTB = A1.f32(D); B_outb = Buf("outb")
        assert A1.off <= ARENA_F32, A1.off

        PT = ps[7].bitcast(BF16)

        TCH = [(c * 512, min(512, LP - c * 512)) for c in range(5)]
        QCH = [(0, 1), (1, 5), (5, 9), (9, 13), (13, 17)]

        def rmsnorm_block(src32, B_src, gidx, dst_bf, B_dst, st, B_st_, jk, B_jk_):
            act(jk, src32, AF.Square, [B_src], [B_jk_, B_st_], accum_out=st[:, 0:1])
            act(st[:, 1:2], st[:, 0:1], AF.Ln, [B_st_], [B_st_], scale=1.0 / D, bias=EPS)
            act(st[:, 2:3], st[:, 1:2], AF.Exp, [B_st_], [B_st_], scale=-0.5)
            stt("dve", dst_bf, src32, st[:, 2:3], G[:, gidx * D:(gidx + 1) * D], ALU.mult, ALU.mult,
                [B_src, B_st_, B_g], [B_dst])

        def transpose_block(hn_bf, B_hn_, dst3, B_dst):
            for kc in range(8):
                tr(PT[:, kc * 128:(kc + 1) * 128], hn_bf[:, kc * 128:(kc + 1) * 128], [B_hn_], [P[7]])
            cp("dve", dst3, PT.rearrange("p (c t) -> p c t", c=8), [P[7]], [B_dst])

        def proj_fm(psb, Pb, wv, c0, m, t0, n, B_w, hnt_v, B_h):
            for kc in range(8):
                mm(psb[0:m, 0:n], wv[:, kc, c0:c0 + m], hnt_v[:, kc, t0:t0 + n], kc == 0, kc == 7,
                   [B_w, B_h], [Pb])

        for sq in range(n_seq):
            S.barrier()
            wload(WMv, w0in_d, 2048, 416, [B_wm])
            wload(WUQv, wuq_d, 0, 768, [B_wuq])
            dma("pool", WUKV, wukv_d, [], [B_wukv])
            dma("sp", ROPEv[0:32], rope_d.rearrange("c p t -> p c t"), [], [B_rope])
            for i in range(2):
                memset("pool", KT[i], 0.0, [B_kt[i]])
                memset("pool", QT[i], 0.0, [B_qt[i]])
            memset("pool", VP, 0.0, [B_vp])
            for kc in range(8):
                tsc("pool", WKRRv[:, kc, 0:16], WMv[:, kc, 400:416], -1.0, ALU.mult, [B_wm], [B_wkrr])
                cp("pool", WKRRv[:, kc, 16:32], WMv[:, kc, 384:400], [B_wm], [B_wkrr])
            for kc in range(2):
                src = WUQv[:, kc, :].rearrange("p (h d) -> p h d", h=8)
                dst = WUQRv[:, kc, :].rearrange("p (h d) -> p h d", h=8)
                tsc("pool", dst[:, :, 0:16], src[:, :, 80:96], -1.0, ALU.mult, [B_wuq], [B_wuqr])
                cp("pool", dst[:, :, 16:32], src[:, :, 64:80], [B_wuq], [B_wuqr])

            XSs = [XS, E32P]; B_xss = [B_xs, B_e32p]
            HNs = [HN, C32P.bitcast(BF16)[:, 0:D]]; B_hns = [B_hn, B_c32p]
            STs = [ST[:, 0:4], ST[:, 4:8]]; B_sts = [B_st, B_stb]
            for b in range(NB):
                i = b % 2
                xs_, Bx_ = XSs[i], B_xss[i]
                if b == 0:
                    memset("dve", xs_, 0.0, [Bx_])
                    dma("sp", xs_[NPAD:128, :], meta_d, [], [Bx_])
                else:
                    dma("sp", xs_, x_d[sq, (b - 1) * 128:b * 128, :], [], [Bx_])
                rmsnorm_block(xs_, Bx_, 0, HNs[i], B_hns[i], STs[i], B_sts[i], HNs[i], B_hns[i])
                pk = 6 + i
                ptv = ps[pk].bitcast(BF16)
                for kc in range(8):
                    tr(ptv[:, kc * 128:(kc + 1) * 128], HNs[i][:, kc * 128:(kc + 1) * 128], [B_hns[i]], [P[pk]])
                cp("dve", HNTv[:, :, b * 128:(b + 1) * 128], ptv.rearrange("p (c t) -> p c t", c=8), [P[pk]], [B_hnt])

            def attention_pair(kts, B_ks, qts, B_qs, pair_chunk, sb, escale):
                M2 = lambda m_: m_.unsqueeze(1).to_broadcast([128, 2, 128])
                for ci, (ba, bz) in enumerate(QCH):
                    q0, q1 = ba * 128, bz * 128
                    N = q1 - q0
                    if sb:
                        psO, PO = ps[4 + (ci % 2)], P[4 + (ci % 2)]
                    else:
                        ob = 4 if ci % 2 == 0 else 6
                        psO, PO = ps[ob], P[ob]
                        psD, PD = ps[ob + 1], P[ob + 1]
                    mm(psO[:, 0:N], ZEROS, FIN[:, 0:N], True, False, [B_cb, B_fin], [PO])
                    if not sb:
                        mm(psD[:, 0:N], ZEROS, FIN[:, 0:N], True, False, [B_cb, B_fin], [PD])
                    steps = list(range(bz - 1, -1, -1))

                    def geom(j):
                        tq0 = max(q0, j * 128)
                        return tq0, q1 - tq0, tq0 - q0, j >= ba

                    def qk(si):
                        j = steps[si]
                        tq0, n, c0, diag = geom(j)
                        for hh in range(2):
                            bank = hh if sb else 2 * (si % 2) + hh
                            mm(ps[bank][:, 0:n], kts[hh][:, j * 128:(j + 1) * 128], qts[hh][:, tq0:q1], True, True,
                               [B_ks[hh], B_qs[hh]], [P[bank]])

                    if sb:
                        SPV = [(spbv, SPB, B_spbp), (spbv2, SPB2, B_spbp2)]
                        CSB = [(2, 3), (6, 7)]

                        def stage1(si):
                            j = steps[si]
                            tq0, n, c0, diag = geom(j)
                            sv, _, Bs = SPV[si % 2]
                            act(e32v[:, :, 0:n], psP[0][:, :, 0:n], AF.Exp, [P[0], P[1]], [B_e32p])
                            act(sv[:, :, 0:n], e32v[:, :, 0:n], AF.Ln, [B_e32p], [Bs], bias=1.0)
                            if diag:
                                tt("pool", sv[:, :, 0:128], sv[:, :, 0:128], M2(MSTRICT), ALU.mult, [Bs, B_cb], [Bs])

                        def cumsum(si):
                            j = steps[si]
                            tq0, n, c0, diag = geom(j)
                            _, sp2, Bs = SPV[si % 2]
                            cb_ = CSB[si % 2]
                            first = si == 0
                            for hh in range(2):
                                bk = cb_[hh]
                                mm(ps[bk][:, 0:n], kts[hh][:, j * 128:(j + 1) * 128], qts[hh][:, tq0:q1], True, False,
                                   [B_ks[hh], B_qs[hh]], [P[bk]])
                            for hh in range(2):
                                bk = cb_[hh]
                                mm(ps[bk][:, 0:n], NEGU0 if j == 0 else NEGU, sp2[hh][:, 0:n], False, first,
                                   [B_cb, Bs], [P[bk]])
                                if not first:
                                    mm(ps[bk][:, 0:n], NEGONES, C16[hh][:, c0:c0 + n], False, True,
                                       [B_cb, B_c16p], [P[bk]])

                        def carry(si):
                            j = steps[si]
                            tq0, n, c0, diag = geom(j)
                            sv, _, Bs = SPV[si % 2]
                            if j == 0:
                                return
                            if si == 0:
                                memset("pool", C32P, 0.0, [B_c32p])
                            tt("dve", c32v[:, :, c0:c0 + n], c32v[:, :, c0:c0 + n], sv[:, :, 0:n], ALU.add,
                               [B_c32p, Bs], [B_c32p])
                            cp("dve", c16v[:, :, 0:N], c32v[:, :, 0:N], [B_c32p], [B_c16p])

                        def exp2(si):
                            j = steps[si]
                            tq0, n, c0, diag = geom(j)
                            cb_ = CSB[si % 2]
                            e = si % 2
                            act(atbv[e][:, :, 0:n], psP[cb_[0] // 2][:, :, 0:n], AF.Exp, [P[cb_[0]], P[cb_[1]]], [B_atbp[e]])
                            if diag:
                                tt("pool", atbv[e][:, :, 0:128], atbv[e][:, :, 0:128], M2(MSTRICT), ALU.mult,
                                   [B_atbp[e], B_cb], [B_atbp[e]])

                        def av(si):
                            j = steps[si]
                            tq0, n, c0, diag = geom(j)
                            e = si % 2
                            for hh in range(2):
                                mm(psO[:, c0:c0 + n], VPv[:, j, hh, :], ATB[2 * e + hh][:, 0:n], False, j == 0 and hh == 1,
                                   [B_vp, B_atbp[e]], [PO])

                        ns = len(steps)
                        qk(0)
                        stage1(0)
                        if ns > 1:
                            qk(1)
                        for si in range(ns):
                            cumsum(si)
                            carry(si)
                            if si + 1 < ns:
                                stage1(si + 1)
                            if si + 2 < ns:
                                qk(si + 2)
                            exp2(si)
                            av(si)
                    else:
                        qk(0)
                    for si, j in (enumerate(steps) if not sb else []):
                        tq0, n, c0, diag = geom(j)
                        first = si == 0
                        last = j == 0
                        if True:
                            e = si % 2
                            act(atbv[e][:, :, 0:n], psP[e][:, :, 0:n], AF.Exp, [P[2 * e], P[2 * e + 1]], [B_atbp[e]],
                                scale=escale)
                            if diag:
                                tt("pool", atbv[e][:, :, 0:128], atbv[e][:, :, 0:128], M2(MINCL), ALU.mult,
                                   [B_atbp[e], B_cb], [B_atbp[e]])
                            if not last:
                                qk(si + 1)
                            for hh in range(2):
                                ai = 2 * e + hh
                                mm(psO[:, c0:c0 + n], VPv[:, j, hh, :], ATB[ai][:, 0:n], False, last and hh == 1,
                                   [B_vp, B_atbp[e]], [PO])
                                mm(psD[:, c0:c0 + n], ONESA if hh == 0 else ONESB, ATB[ai][:, 0:n], False,
                                   last and hh == 1, [B_cb, B_atbp[e]], [PD])
                    if sb:
                        tt("dve", OGv[:, pair_chunk, q0:q1], psO[:, 0:N], SG[:, q0:q1], ALU.mult,
                           [PO, B_sg], [B_og])
                    else:
                        t32, B_t = T32[ci % 2], B_t32[ci % 2]
                        act(t32[:, 0:N], psD[:, 0:N], AF.Ln, [PD], [B_t], bias=1e-30)
                        act(t32[:, 0:N], t32[:, 0:N], AF.Exp, [B_t], [B_t], scale=-1.0)
                        tt("dve", t32[:, 0:N], t32[:, 0:N], SG[:, q0:q1], ALU.mult, [B_t, B_sg], [B_t])
                        tt("dve", OGv[:, pair_chunk, q0:q1], psO[:, 0:N], t32[:, 0:N], ALU.mult,
                           [PO, B_t], [B_og])

            def attention_pair_sb(kts, B_ks, qts, B_qs, pair_chunk):
                M2 = lambda m_: m_.unsqueeze(1).to_broadcast([128, 2, 128])
                SPV = [(spbv, SPB, B_spbp), (spbv2, SPB2, B_spbp2)]
                CSB = [(2, 3), (6, 7)]

                class Chunk:
                    pass

                def mk_chunk(ci):
                    ba, bz = QCH[ci]
                    q0, q1 = ba * 128, bz * 128
                    N = q1 - q0
                    psO, PO = ps[4 + (ci % 2)], P[4 + (ci % 2)]
                    steps = list(range(bz - 1, -1, -1))
                    ns = len(steps)

                    def geom(si):
                        j = steps[si]
                        tq0 = max(q0, j * 128)
                        return j, tq0, q1 - tq0, tq0 - q0, j >= ba

                    def qk(si):
                        j, tq0, n, c0, diag = geom(si)
                        for hh in range(2):
                            mm(ps[hh][:, 0:n], kts[hh][:, j * 128:(j + 1) * 128], qts[hh][:, tq0:q1], True, True,
                               [B_ks[hh], B_qs[hh]], [P[hh]])

                    def stage1(si):
                        j, tq0, n, c0, diag = geom(si)
                        sv, _, Bs = SPV[si % 2]
                        act(e32v[:, :, 0:n], psP[0][:, :, 0:n], AF.Exp, [P[0], P[1]], [B_e32p])
                        act(sv[:, :, 0:n], e32v[:, :, 0:n], AF.Ln, [B_e32p], [Bs], bias=1.0)
                        if diag:
                            tt("pool", sv[:, :, 0:128], sv[:, :, 0:128], M2(MSTRICT), ALU.mult, [Bs, B_cb], [Bs])

                    def cumsum(si):
                        j, tq0, n, c0, diag = geom(si)
                        _, sp2, Bs = SPV[si % 2]
                        cb_ = CSB[si % 2]
                        first = si == 0
                        for hh in range(2):
                            bk = cb_[hh]
                            mm(ps[bk][:, 0:n], kts[hh][:, j * 128:(j + 1) * 128], qts[hh][:, tq0:q1], True, False,
                               [B_ks[hh], B_qs[hh]], [P[bk]])
                        for hh in range(2):
                            bk = cb_[hh]
                            mm(ps[bk][:, 0:n], NEGU0 if j == 0 else NEGU, sp2[hh][:, 0:n], False, first,
                               [B_cb, Bs], [P[bk]])
                            if not first:
                                mm(ps[bk][:, 0:n], NEGONES, C16[hh][:, c0:c0 + n], False, True,
                                   [B_cb, B_c16p], [P[bk]])

                    def carry(si):
                        j, tq0, n, c0, diag = geom(si)
                        sv, _, Bs = SPV[si % 2]
                        if j == 0:
                            return
                        if si == 0:
                            memset("pool", C32P, 0.0, [B_c32p])
                        tt("dve", c32v[:, :, c0:c0 + n], c32v[:, :, c0:c0 + n], sv[:, :, 0:n], ALU.add,
                           [B_c32p, Bs], [B_c32p])
                        cp("dve", c16v[:, :, 0:N], c32v[:, :, 0:N], [B_c32p], [B_c16p])

                    def exp2(si):
                        j, tq0, n, c0, diag = geom(si)
                        cb_ = CSB[si % 2]
                        e = si % 2
                        act(atbv[e][:, :, 0:n], psP[cb_[0] // 2][:, :, 0:n], AF.Exp, [P[cb_[0]], P[cb_[1]]], [B_atbp[e]])
                        if diag:
                            tt("pool", atbv[e][:, :, 0:128], atbv[e][:, :, 0:128], M2(MSTRICT), ALU.mult,
                               [B_atbp[e], B_cb], [B_atbp[e]])

                    def av(si):
                        j, tq0, n, c0, diag = geom(si)
                        e = si % 2
                        for hh in range(2):
                            mm(psO[:, c0:c0 + n], VPv[:, j, hh, :], ATB[2 * e + hh][:, 0:n], False, j == 0 and hh == 1,
                               [B_vp, B_atbp[e]], [PO])

                    def prologue():
                        mm(psO[:, 0:N], ZEROS, FIN[:, 0:N], True, False, [B_cb, B_fin], [PO])
                        qk(0)
                        stage1(0)
                        if ns > 1:
                            qk(1)

                    def finalize():
                        tt("dve", OGv[:, pair_chunk, q0:q1], psO[:, 0:N], SG[:, q0:q1], ALU.mult,
                           [PO, B_sg], [B_og])

                    c = Chunk()
                    c.ns, c.qk, c.stage1, c.cumsum, c.carry, c.exp2, c.av = ns, qk, stage1, cumsum, carry, exp2, av
                    c.prologue, c.finalize = prologue, finalize
                    return c

                chunks = [mk_chunk(ci) for ci in range(len(QCH))]
                chunks[0].prologue()
                chunks[0].cumsum(0)
                for ci, ch in enumerate(chunks):
                    nxt = chunks[ci + 1] if ci + 1 < len(chunks) else None
                    for si in range(ch.ns):
                        lastst = si == ch.ns - 1
                        ch.carry(si)
                        if si + 1 < ch.ns:
                            ch.stage1(si + 1)
                        if si + 2 < ch.ns:
                            ch.qk(si + 2)
                        if lastst and nxt is not None:
                            nxt.prologue()
                        ch.exp2(si)
                        if not lastst:
                            ch.cumsum(si + 1)
                        elif nxt is not None:
                            nxt.cumsum(0)
                        ch.av(si)
                    ch.finalize()

            bk_rot = [0]
            BK_ORDER = [6, 7]

            def nbk():
                bk_rot[0] = (bk_rot[0] + 1) % len(BK_ORDER)
                return BK_ORDER[bk_rot[0]]

            def build_v(lhs_fn, rhs_fn, nk, reads):
                for b0 in range(0, NB, 4):
                    nb4 = min(4, NB - b0)
                    k_ = nbk()
                    for i in range(nb4):
                        b = b0 + i
                        for kc in range(nk):
                            mm(ps[k_][:, i * 128:(i + 1) * 128], lhs_fn(kc, b), rhs_fn(kc), kc == 0, kc == nk - 1,
                               reads, [P[k_]])
                    pv = ps[k_][:, 0:nb4 * 128].rearrange("p (b n) -> p b n", b=nb4)
                    cp("dve", VPv[:, b0:b0 + nb4, 0, 0:64], pv[:, :, 0:64], [P[k_]], [B_vp])
                    cp("dve", VPv[:, b0:b0 + nb4, 1, 64:128], pv[:, :, 64:128], [P[k_]], [B_vp])

            def pj(wv, c0, m, t0, n, B_w, src_v, B_src, nkc=8):
                k_ = nbk()
                for kc in range(nkc):
                    mm(ps[k_][0:m, 0:n], wv[:, kc, c0:c0 + m], src_v[:, kc, t0:t0 + n], kc == 0, kc == nkc - 1,
                       [B_w, B_src], [P[k_]])
                return ps[k_], P[k_]

            for p in range(4):
                ws, wsv, B_w = WS[p % 2], WSv[p % 2], B_ws[p % 2]
                wv4 = ws.rearrange("p (c f n) -> p c f n", c=8, f=4)
                srcv = w0in_d.rearrange("(kc p) n -> p kc n", p=128)
                for f in range(4):
                    dma("pool", wv4[:, :, f, :], srcv[:, :, f * 512 + p * 128:f * 512 + (p + 1) * 128], [], [B_w])
                for (t0, n) in TCH:
                    pa, Pa = pj(wsv, 128, 128, t0, n, B_w, HNTv, B_hnt)
                    cp("act", KT[0][:, t0:t0 + n], pa[:, 0:n], [Pa], [B_kt[0]])
                    pa, Pa = pj(wsv, 0, 128, t0, n, B_w, HNTv, B_hnt)
                    tsc("dve", QT[0][0:64, t0:t0 + n], pa[0:64, 0:n], 0.125, ALU.mult, [Pa], [B_qt[0]])
                    tsc("dve", QT[1][64:128, t0:t0 + n], pa[64:128, 0:n], 0.125, ALU.mult, [Pa], [B_qt[1]])
                    pa, Pa = pj(wsv, 384, 128, t0, n, B_w, HNTv, B_hnt)
                    act(SG[:, t0:t0 + n], pa[:, 0:n], AF.Silu, [Pa], [B_sg])
                build_v(lambda kc, b: HNTv[:, kc, b * 128:(b + 1) * 128], lambda kc, wsv=wsv: wsv[:, kc, 256:384], 8,
                        [B_hnt, B_w])
                attention_pair_sb([KT[0], KT[0]], [B_kt[0], B_kt[0]], QT, B_qt, p)

            for i in range(2):
                memset("pool", KT[i], 0.0, [B_kt[i]])
                memset("pool", QT[i], 0.0, [B_qt[i]])
                memset("pool", KT[i][96:97, 0:NPAD], -30000.0, [B_kt[i]])
                memset("pool", QT[i][96:97, :], 1.0, [B_qt[i]])
            CKVNv1 = CKVN.rearrange("p (c t) -> p c t", c=1)
            for (t0, n) in TCH:
                for cc in range(2):
                    pa, Pa = pj(WMv, cc * 128, 128, t0, n, B_wm, HNTv, B_hnt)
                    cp("dve", T32[cc][:, 0:n], pa[:, 0:n], [Pa], [B_t32[cc]])
                    act(ATB[cc][:, 0:n], pa[:, 0:n], AF.Square, [Pa], [B_atb[cc]])
                k_ = nbk()
                for cc in range(2):
                    mm(ps[k_][:, 0:n], ONES, ATB[cc][:, 0:n], cc == 0, cc == 1, [B_cb, B_atb[cc]], [P[k_]])
                act(E32[0][:, 0:n], ps[k_][:, 0:n], AF.Ln, [P[k_]], [B_e32[0]], scale=1.0 / 256, bias=EPS)
                act(E32[0][:, 0:n], E32[0][:, 0:n], AF.Exp, [B_e32[0]], [B_e32[0]], scale=-0.5)
                for cc in range(2):
                    stt("dve", CQNv[:, cc, t0:t0 + n], T32[cc][:, 0:n], NG[:, cc:cc + 1], E32[0][:, 0:n],
                        ALU.mult, ALU.mult, [B_t32[cc], B_ng, B_e32[0]], [B_cqn])
                pa, Pa = pj(WMv, 256, 128, t0, n, B_wm, HNTv, B_hnt)
                cp("dve", T32[0][:, 0:n], pa[:, 0:n], [Pa], [B_t32[0]])
                act(ATB[2][:, 0:n], pa[:, 0:n], AF.Square, [Pa], [B_atb[2]])
                k_ = nbk()
                mm(ps[k_][:, 0:n], ONES, ATB[2][:, 0:n], True, True, [B_cb, B_atb[2]], [P[k_]])
                act(E32[1][:, 0:n], ps[k_][:, 0:n], AF.Ln, [P[k_]], [B_e32[1]], scale=1.0 / 128, bias=EPS)
                act(E32[1][:, 0:n], E32[1][:, 0:n], AF.Exp, [B_e32[1]], [B_e32[1]], scale=-0.5)
                stt("dve", CKVN[:, t0:t0 + n], T32[0][:, 0:n], NG[:, 2:3], E32[1][:, 0:n],
                    ALU.mult, ALU.mult, [B_t32[0], B_ng, B_e32[1]], [B_ckvn])
                pa, Pa = pj(WMv, 384, 32, t0, n, B_wm, HNTv, B_hnt)
                pb_, Pb_ = pj(WKRRv, 0, 32, t0, n, B_wkrr, HNTv, B_hnt)
                tt("dve", T32[0][0:32, 0:n], pa[0:32, 0:n], ROPEv[0:32, 0, t0:t0 + n], ALU.mult,
                   [Pa, B_rope], [B_t32[0]])
                tt("dve", T32[1][0:32, 0:n], pb_[0:32, 0:n], ROPEv[0:32, 1, t0:t0 + n], ALU.mult,
                   [Pb_, B_rope], [B_t32[1]])
                tt("dve", KR[0:32, t0:t0 + n], T32[0][0:32, 0:n], T32[1][0:32, 0:n], ALU.add,
                   [B_t32[0], B_t32[1]], [B_kr])

            WUKVv = WUKV.rearrange("p (h a d) -> p h a d", h=8, a=2)
            for p in range(4):
                ws, wsv, B_w = WS[p % 2], WSv[p % 2], B_ws[p % 2]
                wload(wsv[:, :, 0:128], w0in_d, 2464 + p * 128, 128, [B_w])
                for (t0, n) in TCH:
                    pa, Pa = pj(wsv, 0, 128, t0, n, B_w, HNTv, B_hnt)
                    act(SG[:, t0:t0 + n], pa[:, 0:n], AF.Silu, [Pa], [B_sg])
                    for hh in range(2):
                        h = 2 * p + hh
                        k_ = nbk()
                        mm(ps[k_][0:64, 0:n], WUKVv[:, h, 0, :], CKVN[:, t0:t0 + n], True, True,
                           [B_wukv, B_ckvn], [P[k_]])
                        cp("act", KT[hh][0:64, t0:t0 + n], ps[k_][0:64, 0:n], [P[k_]], [B_kt[hh]])
                        cp("pool", KT[hh][64:96, t0:t0 + n], KR[0:32, t0:t0 + n], [B_kr], [B_kt[hh]])
                        pa, Pa = pj(WUQv, h * 96, 64, t0, n, B_wuq, CQNv, B_cqn, nkc=2)
                        cp("act", QT[hh][0:64, t0:t0 + n], pa[0:64, 0:n], [Pa], [B_qt[hh]])
                        px, Px = pj(WUQv, h * 96 + 64, 32, t0, n, B_wuq, CQNv, B_cqn, nkc=2)
                        pr_, Pr_ = pj(WUQRv, h * 32, 32, t0, n, B_wuqr, CQNv, B_cqn, nkc=2)
                        tt("dve", T32[0][0:32, 0:n], px[0:32, 0:n], ROPEv[0:32, 0, t0:t0 + n], ALU.mult,
                           [Px, B_rope], [B_t32[0]])
                        tt("dve", T32[1][0:32, 0:n], pr_[0:32, 0:n], ROPEv[0:32, 1, t0:t0 + n], ALU.mult,
                           [Pr_, B_rope], [B_t32[1]])
                        tt("dve", QT[hh][64:96, t0:t0 + n], T32[0][0:32, 0:n], T32[1][0:32, 0:n], ALU.add,
                           [B_t32[0], B_t32[1]], [B_qt[hh]])
                build_v(lambda kc, b: CKVN[:, b * 128:(b + 1) * 128], lambda kc, p=p: WUKVv[:, 2 * p:2 * p + 2, 1, :], 1,
                        [B_ckvn, B_wukv])
                if p == 3:
                    assert pers_end + 4096 + 9216 <= A0_KT_START - (128 + 768 + 256 + 512)
                    dead = [B_hnt, B_ws[0], B_ws[1], B_wm]
                    for c in range(0, 1024, 512):
                        wload(W0Ov[:, :, c:c + 512], w0out_d, c, 512, [B_w0o] + dead)
                    for c in range(0, OD_IN, 576):
                        wload(W1Iv[:, :, c:c + 576], w1in_d, c, 576, [B_w1i] + dead)
                attention_pair(KT, B_kt, QT, B_qt, 4 + p, False, 96.0 ** -0.5)

            if debug_h1:
                for c in range(8):
                    dma("pool", dbg2_d[sq, :, c * LP:(c + 1) * LP], OG[:, c * LP:(c + 1) * LP], [B_og], [])
            S.barrier()
            for c in range(0, 1024, 512):
                wload(W1Ov[:, :, c:c + 512], w1out_d, c, 512, [B_w1o])
            dma("sp", SWEv, swae_d.rearrange("a p h r -> p a h r"), [], [B_swe])
            dma("sp", EMv[0:16], em_d, [], [B_em])
            dma("sp", CMv[0:16], cm_d, [], [B_cm])
            sk2 = sinks_d.rearrange("(p two) -> two p", two=2)
            S.op("sp", lambda e: e.dma_start(out=ESK[0:64, :], in_=sk2[0:1, :].broadcast_to([64, 8]),
                                             allow_slow_non_contiguous=True), writes=[B_esk], dma=True)
            S.op("sp", lambda e: e.dma_start(out=ESK[64:128, :], in_=sk2[1:2, :].broadcast_to([64, 8]),
                                             allow_slow_non_contiguous=True), writes=[B_esk], dma=True)
            act(ESK, ESK, AF.Exp, [B_esk], [B_esk])
            for par in range(2):
                for g in range(2):
                    memset("pool", QG[par][g], 0.0, [B_qg[par][g]])
            memset("pool", VM, 0.0, [B_vm])
            memset("pool", VR, 0.0, B_vr)

            rot = [0]

            def pbank():
                rot[0] ^= 1
                return 6 + rot[0]

            def front(b):
                par = b % 2
                h1, Bh1 = H1[par], B_h1[par]
                hv, Bhv = HNT1v[par], B_hnt1[par]
                if b == 0:
                    memset("dve", XS1, 0.0, [B_xs1])
                    dma("sp", XS1[NPAD:128, :], meta_d, [], [B_xs1])
                else:
                    dma("sp", XS1, x_d[sq, (b - 1) * 128:b * 128, :], [], [B_xs1])
                for hf in range(2):
                    for kc in range(8):
                        mm(ps[4 + hf][:, :], OGv[:, kc, b * 128:(b + 1) * 128], W0Ov[:, kc, hf * 512:(hf + 1) * 512],
                           kc == 0, kc == 7, [B_og, B_w0o], [P[4 + hf]])
                    tt("dve", h1[:, hf * 512:(hf + 1) * 512], ps[4 + hf][:, :], XS1[:, hf * 512:(hf + 1) * 512],
                       ALU.add, [P[4 + hf], B_xs1], [Bh1])
                if debug_h1:
                    dma("sp", dbg_d[sq, b * 128:(b + 1) * 128, :], h1, [Bh1], [])
                yield
                rmsnorm_block(h1, Bh1, 1, HN1, B_hn1, ST1, B_st1, JK1, B_jk1)
                pk = pbank()
                ptv = ps[pk].bitcast(BF16)
                for kc in range(8):
                    tr(ptv[:, kc * 128:(kc + 1) * 128], HN1[:, kc * 128:(kc + 1) * 128], [B_hn1], [P[pk]])
                cp("dve", hv, ptv.rearrange("p (c t) -> p c t", c=8), [P[pk]], [Bhv])
                yield
                pk = pbank()
                for kc in range(8):
                    mm(ps[pk][:, 0:128], W1Iv[:, kc, 1024:1152], hv[:, kc, :], kc == 0, kc == 7,
                       [B_w1i, Bhv], [P[pk]])
                cp("act", KT1[:, b * 128:(b + 1) * 128], ps[pk][:, 0:128], [P[pk]], [B_kt1[b]])
                slot = b % 3
                pk = pbank()
                if b == 0:
                    for kc in range(8):
                        mm(ps[pk][0:16, 0:128], hv[:, kc, NPAD:128], W1Iv[:, kc, 1152:1280], kc == 0, kc == 7,
                           [Bhv, B_w1i], [P[pk]])
                    for kh in range(2):
                        cp("dve", VMv[0:16, 2 * kh, 0:64], ps[pk][0:16, kh * 64:(kh + 1) * 64], [P[pk]], [B_vm])
                        cp("dve", VMv[0:16, 2 * kh + 1, 64:128], ps[pk][0:16, kh * 64:(kh + 1) * 64], [P[pk]], [B_vm])
                    return
                for kc in range(8):
                    mm(ps[pk][:, 0:128], hv[:, kc, :], W1Iv[:, kc, 1152:1280], kc == 0, kc == 7,
                       [Bhv, B_w1i], [P[pk]])
                for kh in range(2):
                    cp("dve", VRv[:, slot, 2 * kh, 0:64], ps[pk][:, kh * 64:(kh + 1) * 64], [P[pk]], [B_vr[slot]])
                    cp("dve", VRv[:, slot, 2 * kh + 1, 64:128], ps[pk][:, kh * 64:(kh + 1) * 64], [P[pk]], [B_vr[slot]])
                yield
                for q4 in range(2):
                    pk = pbank()
                    for i in range(4):
                        pr = q4 * 4 + i
                        for kc in range(8):
                            mm(ps[pk][:, i * 128:(i + 1) * 128], W1Iv[:, kc, pr * 128:(pr + 1) * 128], hv[:, kc, :],
                               kc == 0, kc == 7, [B_w1i, Bhv], [P[pk]])
                    g = q4
                    gs = slice(g * 64, g * 64 + 64)
                    pv = ps[pk].rearrange("p (i t) -> p i t", i=4)
                    qv = QGv[par][g][gs].rearrange("p (i two) t -> p two i t", two=2)
                    cp("act" if g == 0 else "dve", qv[:, 0], pv[0:64], [P[pk]], [B_qg[par][g]])
                    cp("dve" if g == 0 else "act", qv[:, 1], pv[64:128], [P[pk]], [B_qg[par][g]])
                    yield
                for c4 in range(2):
                    pk = pbank()
                    for i in range(4):
                        cc = c4 * 4 + i
                        for kc in range(8):
                            mm(ps[pk][:, i * 128:(i + 1) * 128], W1Iv[:, kc, 1280 + cc * 128:1280 + (cc + 1) * 128],
                               hv[:, kc, :], kc == 0, kc == 7, [B_w1i, Bhv], [P[pk]])
                    act(SG1v[par][:, c4 * 4:(c4 + 1) * 4, :], ps[pk].rearrange("p (i t) -> p i t", i=4), AF.Silu,
                        [P[pk]], [B_sg1[par]])
                    yield

            def back_parts(b):
                par = b % 2
                slot = b % 3
                h1, Bh1 = H1[par], B_h1[par]
                tiles = []
                for g in range(2):
                    if b >= 2:
                        tiles.append((g, "prev", KT1[:, (b - 1) * 128:b * 128], 128, (b - 1) % 3, B_kt1[b - 1]))
                    tiles.append((g, "cur", KT1[:, b * 128:(b + 1) * 128], 128, slot, B_kt1[b]))
                    tiles.append((g, "meta", KT1[:, NPAD:128], 16, None, B_kt1[0]))
                nt = len(tiles)

                def qk(ti):
                    g, kind, kk, nk, vs, Bk = tiles[ti]
                    for hf in range(2):
                        mm(ps[hf][0:nk, :], kk, QGv[par][g][:, hf * 4:(hf + 1) * 4, :], True, True,
                           [Bk, B_qg[par][g]], [P[hf]])

                def soft(ti):
                    g, kind, kk, nk, vs, Bk = tiles[ti]
                    e = ti % 2
                    exv, pbv = EXv[e], PBv[e]
                    for hf in range(2):
                        act(EX[e][0:nk, hf * 512:(hf + 1) * 512], ps[hf][0:nk, :], AF.Exp, [P[hf]], [B_exh[e][hf]],
                            scale=0.125)
                        if kind != "meta":
                            a = 0 if kind == "prev" else 1
                            hs = slice(hf * 4, (hf + 1) * 4)
                            tt("pool" if hf == 0 else "dve", pbv[:, hs, :], exv[:, hs, :],
                               SWEv[:, a, g * 8 + hf * 4:g * 8 + (hf + 1) * 4, :], ALU.mult,
                               [B_exh[e][hf], B_swe], [B_pbh[e][hf]])
                    if kind == "meta":
                        tt("dve", exv[0:16], exv[0:16], EMv[0:16, g * 8:(g + 1) * 8, :], ALU.mult,
                           B_exh[e] + [B_em], B_exh[e])
                        tt("dve", pbv[0:16], exv[0:16],
                           CMv[0:16, b, g * 8:(g + 1) * 8].unsqueeze(2).to_broadcast([16, 8, 128]), ALU.mult,
                           B_exh[e] + [B_cm], B_pbh[e])

                def prologue():
                    qk(0)
                    soft(0)
                    if nt > 1:
                        qk(1)

                def tile_gen():
                    for ti, (g, kind, kk, nk, vs, Bk) in enumerate(tiles):
                        e = ti % 2
                        pbv, B_p = PBv[e], B_pbh[e]
                        if kind == "meta":
                            va, vb2 = VMv[0:16, 2 * g, :], VMv[0:16, 2 * g + 1, :]
                            B_v = B_vm
                        else:
                            va, vb2 = VRv[:, vs, 2 * g, :], VRv[:, vs, 2 * g + 1, :]
                            B_v = B_vr[vs]
                        pe_ = pbv[0:nk].rearrange("p (q two) t -> p two q t", two=2)
                        gfirst = kind == ("prev" if b >= 2 else "cur")
                        glast = kind == "meta"
                        mm(ps[2][:, :], va, pe_[:, 0], gfirst, False, [B_v] + B_p, [P[2]])
                        mm(ps[2][:, :], vb2, pe_[:, 1], False, glast, [B_v] + B_p, [P[2]])
                        mm(ps[3][:, :], ONESA[0:nk, :], pe_[:, 0], gfirst, False, [B_cb] + B_p, [P[3]])
                        mm(ps[3][:, :], ONESB[0:nk, :], pe_[:, 1], False, glast, [B_cb] + B_p, [P[3]])
                        if ti + 1 < nt:
                            soft(ti + 1)
                        if ti + 2 < nt:
                            qk(ti + 2)
                        if glast:
                            R3 = R32.rearrange("p (q t) -> p q t", q=4)
                            U3 = U32.rearrange("p (q t) -> p q t", q=4)
                            tt("dve", R3, ps[3].rearrange("p (q t) -> p q t", q=4),
                               ESK[:, g * 4:(g + 1) * 4].unsqueeze(2).to_broadcast([128, 4, 128]), ALU.add,
                               [P[3], B_esk], [B_r32])
                            act(R32, R32, AF.Ln, [B_r32], [B_r32])
                            act(R32, R32, AF.Exp, [B_r32], [B_r32], scale=-1.0)
                            tt("pool", U3, R3, SG1v[par][:, g * 4:(g + 1) * 4, :], ALU.mult, [B_r32, B_sg1[par]], [B_u32])
                            tt("dve", OG1v[:, g * 4:(g + 1) * 4, :], ps[2].rearrange("p (q t) -> p q t", q=4), U3,
                               ALU.mult, [P[2], B_u32], [B_og1])
                        yield

                def tail():
                    for hf in range(2):
                        for kc in range(8):
                            mm(ps[4 + hf][:, :], OG1v[:, kc, :], W1Ov[:, kc, hf * 512:(hf + 1) * 512],
                               kc == 0, kc == 7, [B_og1, B_w1o], [P[4 + hf]])
                        tt("dve", h1[:, hf * 512:(hf + 1) * 512], ps[4 + hf][:, :], h1[:, hf * 512:(hf + 1) * 512],
                           ALU.add, [P[4 + hf], Bh1], [Bh1])
                    rmsnorm_block(h1, Bh1, 2, OUTB, B_outb, ST2, B_st2, JK1, B_jk1)
                    dma("sp", out_d[sq, (b - 1) * 128:b * 128, :], OUTB, [B_outb], [])

                return prologue, tile_gen, tail

            def drain(gen):
                for _ in gen:
                    pass

            drain(front(0))
            drain(front(1))
            parts = back_parts(1)
            parts[0]()
            for b in range(1, NB):
                prologue, tile_gen, tail = parts
                alive = [tile_gen()]
                if b + 1 < NB:
                    alive.append(front(b + 1))
                while alive:
                    for gq in list(alive):
                        try:
                            next(gq)
                        except StopIteration:
                            alive.remove(gq)
                if b + 1 < NB:
                    parts = back_parts(b + 1)
                    parts[0]()
                tail()

        S.finalize()
        with nc.Block() as block:
            @block.tensor
            def _(e):
                S.emit_engine("pe", e, sems, dma_sems)

            @block.scalar
            def _(e):
                S.emit_engine("act", e, sems, dma_sems)

            @block.vector
            def _(e):
                S.emit_engine("dve", e, sems, dma_sems)

            @block.gpsimd
            def _(e):
                S.emit_engine("pool", e, sems, dma_sems)

            @block.sync
            def _(e):
                S.emit_engine("sp", e, sems, dma_sems, final_wait=True)
    return nc


_NC_CACHE = {}


def _common_inputs(meta, norm_g, final_g, ev_w_in, ev_q_norm_g, ev_kv_norm_g, ev_w_uq, ev_w_ukv,
                   ev_w_out, od_w_in, od_sinks, od_w_out):
    cb, rope, swae, em, cm = _const_tables()
    f = lambda a: np.ascontiguousarray(np.asarray(a, dtype=np.float32))
    return {
        "meta": f(meta),
        "gains": f(np.concatenate([np.asarray(norm_g), np.asarray(final_g)[None, :]], 0)),
        "w0in": f(ev_w_in[0]), "qng": f(ev_q_norm_g[0]), "kvng": f(ev_kv_norm_g[0]),
        "wuq": f(ev_w_uq[0]), "wukv": f(ev_w_ukv[0]), "w0out": f(ev_w_out[0]),
        "w1in": f(od_w_in[0]), "sinks": f(od_sinks[0]), "w1out": f(od_w_out[0]),
        "cb": cb, "rope": f(rope), "swae": f(swae), "em": f(em), "cm": f(cm),
    }


def kernel(x, meta, norm_g, final_g, ev_w_in, ev_q_norm_g, ev_kv_norm_g, ev_w_uq, ev_w_ukv,
           ev_w_out, od_w_in, od_sinks, od_w_out):
    n = 8
    x = np.asarray(x, dtype=np.float32)
    common = _common_inputs(meta, norm_g, final_g, ev_w_in, ev_q_norm_g, ev_kv_norm_g, ev_w_uq,
                            ev_w_ukv, ev_w_out, od_w_in, od_sinks, od_w_out)
    if "nc" not in _NC_CACHE:
        _NC_CACHE["nc"] = build_nc()
    nc = _NC_CACHE["nc"]
    in_maps = []
    for c in range(n):
        m = dict(common)
        m["x"] = np.ascontiguousarray(x[c * SEQ_PER_CORE:(c + 1) * SEQ_PER_CORE])
        in_maps.append(m)
    res = run_bass_kernel_spmd(nc, in_maps, core_ids=list(range(n)))
    return np.concatenate([r["out"] for r in res.results], axis=0)
```

```python
import numpy as np
from contextlib import ExitStack
import concourse.bass as bass
import concourse.mybir as mybir
from concourse.bass_utils import run_bass_kernel_spmd

F32 = mybir.dt.float32
BF16 = mybir.dt.bfloat16
AF = mybir.ActivationFunctionType
ALU = mybir.AluOpType

ENGS = ["pe", "act", "dve", "pool", "sp"]

D = 1024
LP = 2176
NB = 17
NPAD = 112
EPS = 1e-6
SEQ_PER_CORE = 2
EV_IN = 2976
OD_IN = 2304


class Buf:
    __slots__ = ("name", "last_w", "readers", "excl")

    def __init__(self, name, excl=False):
        self.name = name
        self.last_w = None
        self.readers = []
        self.excl = excl


class Op:
    __slots__ = ("eng", "fn", "deps", "signal", "is_dma", "dsem", "dval", "cnt", "prev_dma")

    def __init__(self, eng, fn, is_dma):
        self.eng = eng
        self.fn = fn
        self.deps = []
        self.signal = False
        self.is_dma = is_dma
        self.dsem = None
        self.dval = 0
        self.cnt = 0
        self.prev_dma = None


class Sched:
    def __init__(self, n_dma_sems=8):
        self.ops = {e: [] for e in ENGS}
        self.n_dma_sems = n_dma_sems
        self.dma_rr = {e: 0 for e in ENGS}
        self.dma_cnt = {}
        self.dma_last = {}

    def op(self, eng, fn, reads=(), writes=(), dma=False):
        o = Op(eng, fn, dma)
        deps = {}
        for b in reads:
            if b.last_w is not None:
                deps[id(b.last_w)] = b.last_w
            if b.excl:
                for r in b.readers:
                    if r.eng != eng:
                        deps[id(r)] = r
        for b in writes:
            if b.last_w is not None:
                deps[id(b.last_w)] = b.last_w
            for r in b.readers:
                deps[id(r)] = r
        final = []
        for d in deps.values():
            if d is o:
                continue
            if d.eng == "pe" and eng == "pe" and (not d.is_dma) and (not dma):
                continue
            final.append(d)
            d.signal = True
        o.deps = final
        for b in reads:
            if not dma:
                b.readers = [r for r in b.readers if r.is_dma or r.eng != eng]
            b.readers.append(o)
        for b in writes:
            b.last_w = o
            b.readers = []
        if dma:
            k = self.dma_rr[eng]
            self.dma_rr[eng] = (k + 1) % self.n_dma_sems
            key = (eng, k)
            self.dma_cnt[key] = self.dma_cnt.get(key, 0) + 1
            o.dsem = key
            o.dval = 16 * self.dma_cnt[key]
            o.prev_dma = self.dma_last.get(key)
            self.dma_last[key] = o
        self.ops[eng].append(o)
        return o

    def barrier(self):
        lasts = []
        for e in ENGS:
            for o in reversed(self.ops[e]):
                if (not o.is_dma) and o.fn is not None:
                    lasts.append(o)
                    break
        dmas = list(self.dma_last.values())
        for e in ENGS:
            m = Op(e, None, False)
            m.deps = [d for d in lasts if d.eng != e] + dmas
            for d in m.deps:
                d.signal = True
            self.ops[e].append(m)

    def finalize(self):
        for e in ENGS:
            c = 0
            for o in self.ops[e]:
                if o.is_dma or o.fn is None:
                    continue
                if o.signal:
                    c += 1
                    o.cnt = c

    def emit_engine(self, eng_name, e, sems, dma_sems, final_wait=False):
        waited = {}

        def wait(key, sem, val):
            if val <= 0:
                return
            if waited.get(key, 0) < val:
                e.wait_ge(sem, val)
                waited[key] = val

        for o in self.ops[eng_name]:
            for d in o.deps:
                if d.is_dma:
                    wait(d.dsem, dma_sems[d.dsem], d.dval)
                else:
                    wait(d.eng, sems[d.eng], d.cnt)
            if o.is_dma and o.prev_dma is not None:
                wait(o.dsem, dma_sems[o.dsem], o.prev_dma.dval)
            if o.fn is None:
                continue
            ins = o.fn(e)
            if o.is_dma:
                ins.then_inc(dma_sems[o.dsem], 16)
            elif o.signal:
                ins.then_inc(sems[eng_name], 1)
        if final_wait:
            for key, o in self.dma_last.items():
                wait(key, dma_sems[key], o.dval)


def _const_tables():
    r = np.arange(128)
    s = r[:, None]
    t = r[None, :]
    ident = (s == t).astype(np.float32)
    negU = -(s >= t).astype(np.float32)
    negU0 = negU * (s >= NPAD)
    negOnes = -np.ones((128, 128), np.float32)
    negOnes0 = negOnes * (s >= NPAD)
    ones = np.ones((128, 128), np.float32)
    mstrict = (s < t).astype(np.float32)
    mincl = (s <= t).astype(np.float32)
    onesA = np.concatenate([np.ones((128, 64)), np.zeros((128, 64))], 1).astype(np.float32)
    onesB = np.concatenate([np.zeros((128, 64)), np.ones((128, 64))], 1).astype(np.float32)
    zeros = np.zeros((128, 128), np.float32)
    cb = np.concatenate([ident, negU, negU0, negOnes, negOnes0, ones, mstrict, mincl,
                         onesA, onesB, zeros], axis=1)
    half = 16
    inv = 10000.0 ** (-np.arange(half, dtype=np.float64) / half)
    pos = (np.arange(LP) - NPAD).astype(np.float64)
    ang = inv[:, None] * pos[None, :]
    cos = np.concatenate([np.cos(ang), np.cos(ang)], 0).astype(np.float32)
    sin = np.concatenate([np.sin(ang), np.sin(ang)], 0).astype(np.float32)
    rope = np.stack([cos, sin], 0)
    H = 16
    slopes = 2.0 ** (-8.0 * (np.arange(H, dtype=np.float64) + 1.0) / H)
    sl = slopes[None, :, None]
    dprev = (128 + r[None, None, :] - r[:, None, None]).astype(np.float64)
    eprev = np.where(dprev < 128, np.exp(-sl * dprev), 0.0)
    dcur = (r[None, None, :] - r[:, None, None]).astype(np.float64)
    ecur = np.where(dcur >= 0, np.exp(-sl * np.maximum(dcur, 0)), 0.0)
    m = np.arange(16)
    dm = (16 + r[None, None, :] - m[:, None, None]).astype(np.float64)
    em = np.exp(-sl * dm)
    n = np.arange(NB)
    cm = np.exp(-slopes[None, None, :] * 128.0 * np.maximum(n - 1, 0)[None, :, None])
    cm = np.broadcast_to(cm, (16, NB, H))
    swa_e = np.stack([eprev, ecur], 0).astype(np.float32)
    return (cb.astype(np.float32), rope, swa_e, em.astype(np.float32),
            np.ascontiguousarray(cm).astype(np.float32))


def build_nc(debug_h1=False, n_seq=SEQ_PER_CORE):
    nc = bass.Bass("TRN2", target_bir_lowering=False)
    dt = nc.dram_tensor
    x_d = dt("x", [n_seq, 2048, D], F32, kind="ExternalInput").ap()
    meta_d = dt("meta", [16, D], F32, kind="ExternalInput").ap()
    gains_d = dt("gains", [3, D], F32, kind="ExternalInput").ap()
    w0in_d = dt("w0in", [D, EV_IN], F32, kind="ExternalInput").ap()
    qng_d = dt("qng", [256], F32, kind="ExternalInput").ap()
    kvng_d = dt("kvng", [128], F32, kind="ExternalInput").ap()
    wuq_d = dt("wuq", [256, 768], F32, kind="ExternalInput").ap()
    wukv_d = dt("wukv", [128, 1024], F32, kind="ExternalInput").ap()
    w0out_d = dt("w0out", [D, D], F32, kind="ExternalInput").ap()
    w1in_d = dt("w1in", [D, OD_IN], F32, kind="ExternalInput").ap()
    sinks_d = dt("sinks", [16], F32, kind="ExternalInput").ap()
    w1out_d = dt("w1out", [D, D], F32, kind="ExternalInput").ap()
    cb_d = dt("cb", [128, 11 * 128], F32, kind="ExternalInput").ap()
    rope_d = dt("rope", [2, 32, LP], F32, kind="ExternalInput").ap()
    swae_d = dt("swae", [2, 128, 16, 128], F32, kind="ExternalInput").ap()
    em_d = dt("em", [16, 16, 128], F32, kind="ExternalInput").ap()
    cm_d = dt("cm", [16, NB, 16], F32, kind="ExternalInput").ap()
    out_d = dt("out", [n_seq, 2048, D], F32, kind="ExternalOutput").ap()
    if debug_h1:
        dbg_d = dt("dbg", [n_seq, LP, D], F32, kind="ExternalOutput").ap()
        dbg2_d = dt("dbg2", [n_seq, 128, 8 * LP], F32, kind="ExternalOutput").ap()

    S = Sched()
    with ExitStack() as es:
        ARENA_F32 = 53000
        arena = es.enter_context(nc.sbuf_tensor("arena", [128, ARENA_F32], F32))
        psq = [es.enter_context(nc.psum_tensor(f"psq{i}", [128, 1024], F32)) for i in range(4)]
        ps = [psq[i // 2][:, (i % 2) * 512:(i % 2 + 1) * 512] for i in range(8)]
        psP = [q_.rearrange("p (h n) -> p h n", h=2) for q_ in psq]
        sems = {e: es.enter_context(nc.semaphore(f"s_{e}")) for e in ENGS}
        dma_sems = {}
        for e in ["sp", "pool"]:
            for k in range(S.n_dma_sems):
                dma_sems[(e, k)] = es.enter_context(nc.semaphore(f"d_{e}{k}"))
        P = [Buf(f"ps{i}", excl=True) for i in range(8)]

        class Arena:
            def __init__(self, start=0):
                self.off = start

            def f32(self, n):
                a = arena[:, self.off:self.off + n]
                self.off += n
                return a

            def bf16(self, n):
                assert n % 2 == 0
                a = arena[:, self.off:self.off + n // 2].bitcast(BF16)
                self.off += n // 2
                return a

        A = Arena(0)
        CB = A.bf16(11 * 128)
        cb_v = lambda i: CB[:, i * 128:(i + 1) * 128]
        IDENT, NEGU, NEGU0, NEGONES, NEGONES0, ONES, MSTRICT, MINCL, ONESA, ONESB, ZEROS = [cb_v(i) for i in range(11)]
        G = A.f32(3 * D)
        NG = A.f32(4)
        FIN = A.bf16(512)
        OG = A.bf16(8 * LP)
        OGv = OG.rearrange("p (c t) -> p c t", c=8)
        pers_end = A.off
        B_cb, B_g, B_ng, B_fin, B_og = Buf("cb"), Buf("g"), Buf("ng"), Buf("fin"), Buf("og")

        def mm(out, lhsT, rhs, start, stop, reads, writes):
            S.op("pe", lambda e: e.matmul(out, lhsT=lhsT, rhs=rhs, start=start, stop=stop),
                 reads=reads, writes=writes)

        def tr(out, in_, reads, writes):
            S.op("pe", lambda e: e.transpose(out, in_, IDENT), reads=list(reads) + [B_cb], writes=writes)

        def act(out, in_, func, reads, writes, scale=1.0, bias=0.0, accum_out=None, eng="act"):
            if accum_out is None:
                S.op("act", lambda e: e.activation(out=out, in_=in_, func=func, bias=bias, scale=scale),
                     reads=reads, writes=writes)
            else:
                S.op("act", lambda e: e.activation(out=out, in_=in_, func=func, bias=bias, scale=scale,
                                                   accum_out=accum_out), reads=reads, writes=writes)

        def tt(eng, out, in0, in1, op, reads, writes):
            S.op(eng, lambda e: e.tensor_tensor(out=out, in0=in0, in1=in1, op=op), reads=reads, writes=writes)

        def tsc(eng, out, in0, s1, op0, reads, writes, s2=None, op1=None):
            if op1 is None:
                S.op(eng, lambda e: e.tensor_scalar(out=out, in0=in0, scalar1=s1, scalar2=None, op0=op0),
                     reads=reads, writes=writes)
            else:
                S.op(eng, lambda e: e.tensor_scalar(out=out, in0=in0, scalar1=s1, scalar2=s2, op0=op0, op1=op1),
                     reads=reads, writes=writes)

        def stt(eng, out, in0, scalar, in1, op0, op1, reads, writes):
            S.op(eng, lambda e: e.scalar_tensor_tensor(out=out, in0=in0, scalar=scalar, in1=in1, op0=op0, op1=op1),
                 reads=reads, writes=writes)

        def cp(eng, out, in_, reads, writes):
            if eng == "act":
                S.op("act", lambda e: e.copy(out=out, in_=in_), reads=reads, writes=writes)
            else:
                S.op(eng, lambda e: e.tensor_copy(out=out, in_=in_), reads=reads, writes=writes)

        def memset(eng, ap, val, writes):
            S.op(eng, lambda e: e.memset(ap, val), writes=writes)

        def dma(eng, out, in_, reads, writes):
            S.op(eng, lambda e: e.dma_start(out=out, in_=in_), reads=reads, writes=writes, dma=True)

        def recip(out, in_, reads, writes):
            S.op("dve", lambda e: e.reciprocal(out=out, in_=in_), reads=reads, writes=writes)

        def wload(dst3, src2, c0, ncols, writes, reads=()):
            kc = src2.shape[0] // 128
            srcv = src2.rearrange("(kc p) n -> p kc n", p=128)
            dma("pool", dst3, srcv[:, :, c0:c0 + ncols], reads, writes)

        dma("pool", CB, cb_d, [], [B_cb])
        for i in range(3):
            dma("sp", G[:, i * D:(i + 1) * D], gains_d[i:i + 1, :].broadcast_to([128, D]), [], [B_g])
        for c in range(2):
            dma("sp", NG[:, c:c + 1], qng_d[c * 128:(c + 1) * 128].rearrange("(p o) -> p o", o=1), [], [B_ng])
        dma("sp", NG[:, 2:3], kvng_d.rearrange("(p o) -> p o", o=1), [], [B_ng])
        memset("dve", FIN, 1.0, [B_fin])

        A0 = Arena(pers_end)
        HNT = A0.bf16(8 * LP); HNTv = HNT.rearrange("p (c t) -> p c t", c=8); B_hnt = Buf("hnt")
        WS = [A0.bf16(8 * 512) for _ in range(2)]; B_ws = [Buf("ws0"), Buf("ws1")]
        WSv = [w.rearrange("p (c n) -> p c n", c=8) for w in WS]
        WM = A0.bf16(8 * 416); WMv = WM.rearrange("p (c n) -> p c n", c=8); B_wm = Buf("wm")
        WKRR = A0.bf16(8 * 32); WKRRv = WKRR.rearrange("p (c n) -> p c n", c=8); B_wkrr = Buf("wkrr")
        WUQ = A0.bf16(2 * 768); WUQv = WUQ.rearrange("p (c n) -> p c n", c=2); B_wuq = Buf("wuq")
        WUQR = A0.bf16(2 * 256); WUQRv = WUQR.rearrange("p (c n) -> p c n", c=2); B_wuqr = Buf("wuqr")
        WUKV = A0.bf16(1024); B_wukv = Buf("wukv")
        A0_KT_START = A0.off
        KT = [A0.bf16(LP) for _ in range(2)]; B_kt = [Buf("kt0"), Buf("kt1")]
        QT = [A0.bf16(LP) for _ in range(2)]; B_qt = [Buf("qt0"), Buf("qt1")]
        VP = A0.bf16(NB * 256); VPv = VP.rearrange("p (b v n) -> p b v n", b=NB, v=2); B_vp = Buf("vp")
        SG = A0.bf16(LP); B_sg = Buf("sg")
        CQN = A0.bf16(2 * LP); CQNv = CQN.rearrange("p (c t) -> p c t", c=2); B_cqn = Buf("cqn")
        CKVN = A0.bf16(LP); B_ckvn = Buf("ckvn")
        KR = A0.bf16(LP); B_kr = Buf("kr")
        ROPE = A0.f32(2 * LP); ROPEv = ROPE.rearrange("p (c t) -> p c t", c=2); B_rope = Buf("rope")
        def pairbuf(ap):
            return ap.rearrange("p (h n) -> p h n", h=2), [ap[:, 0:512], ap[:, 512:1024]]
        E32P = A0.f32(1024); e32v, E32 = pairbuf(E32P); B_e32p = Buf("e32p"); B_e32 = [B_e32p, B_e32p]
        SPBP = A0.bf16(1024); spbv, SPB = pairbuf(SPBP); B_spbp = Buf("spbp"); B_spb = [B_spbp, B_spbp]
        SPBP2 = A0.bf16(1024); spbv2, SPB2 = pairbuf(SPBP2); B_spbp2 = Buf("spbp2")
        C32P = A0.f32(1024); c32v, C32 = pairbuf(C32P); B_c32p = Buf("c32p")
        C16P = A0.bf16(1024); c16v, C16 = pairbuf(C16P); B_c16p = Buf("c16p")
        ATBP = [A0.bf16(1024) for _ in range(2)]
        atbv = [pairbuf(a_)[0] for a_ in ATBP]
        ATB = [pairbuf(ATBP[i // 2])[1][i % 2] for i in range(4)]
        B_atbp = [Buf("atbp0"), Buf("atbp1")]; B_atb = [B_atbp[i // 2] for i in range(4)]
        XS = A0.f32(D); B_xs = Buf("xs")
        T32 = [XS[:, 0:512], XS[:, 512:1024]]; B_t32 = [Buf("t32a"), Buf("t32b")]
        HN = A0.bf16(D); B_hn = Buf("hn")
        ST = A0.f32(8); B_st = Buf("st"); B_stb = Buf("stb")
        JK = HN; B_jk = B_hn
        l0_end = A0.off
        assert l0_end <= ARENA_F32, l0_end

        A1 = Arena(pers_end)
        W0O = A1.bf16(8 * D); W0Ov = W0O.rearrange("p (c n) -> p c n", c=8); B_w0o = Buf("w0o")
        W1I = A1.bf16(8 * OD_IN); W1Iv = W1I.rearrange("p (c n) -> p c n", c=8); B_w1i = Buf("w1i")
        W1O = A1.bf16(8 * D); W1Ov = W1O.rearrange("p (c n) -> p c n", c=8); B_w1o = Buf("w1o")
        SWE = A1.f32(2 * 16 * 128); SWEv = SWE.rearrange("p (a h r) -> p a h r", a=2, h=16); B_swe = Buf("swe")
        EM = A1.f32(16 * 128); EMv = EM.rearrange("p (h r) -> p h r", h=16); B_em = Buf("em")
        CM = A1.f32(NB * 16); CMv = CM.rearrange("p (n h) -> p n h", n=NB); B_cm = Buf("cm")
        ESK = A1.f32(8); B_esk = Buf("esk")
        KT1 = A1.bf16(LP); B_kt1 = [Buf(f"kt1l{i}") for i in range(NB)]
        VR = A1.bf16(3 * 4 * 128); VRv = VR.rearrange("p (b v n) -> p b v n", b=3, v=4); B_vr = [Buf("vr0"), Buf("vr1"), Buf("vr2")]
        VM = A1.bf16(4 * 128); VMv = VM.rearrange("p (v n) -> p v n", v=4); B_vm = Buf("vm")
        H1 = [A1.f32(D) for _ in range(2)]; B_h1 = [Buf("h1a"), Buf("h1b")]
        XS1 = A1.f32(D); B_xs1 = Buf("xs1")
        HN1 = A1.bf16(D); B_hn1 = Buf("hn1")
        HNT1 = [A1.bf16(8 * 128) for _ in range(2)]; HNT1v = [h.rearrange("p (c t) -> p c t", c=8) for h in HNT1]
        B_hnt1 = [Buf("hnt1a"), Buf("hnt1b")]
        QG = [[A1.bf16(8 * 128) for _ in range(2)] for _ in range(2)]
        QGv = [[q.rearrange("p (h t) -> p h t", h=8) for q in qq] for qq in QG]
        B_qg = [[Buf(f"qg{i}{j}") for j in range(2)] for i in range(2)]
        SG1 = [A1.bf16(8 * 128) for _ in range(2)]; SG1v = [x_.rearrange("p (c t) -> p c t", c=8) for x_ in SG1]
        B_sg1 = [Buf("sg1a"), Buf("sg1b")]
        OG1 = A1.bf16(8 * 128); OG1v = OG1.rearrange("p (c t) -> p c t", c=8); B_og1 = Buf("og1")
        EX = [A1.f32(1024) for _ in range(2)]; EXv = [x_.rearrange("p (h t) -> p h t", h=8) for x_ in EX]
        B_ex = [Buf("exa"), Buf("exb")]
        PB = [A1.bf16(1024) for _ in range(2)]; PBv = [x_.rearrange("p (h t) -> p h t", h=8) for x_ in PB]
        B_pb = [Buf("pba"), Buf("pbb")]
        B_exh = [[Buf(f"exh{i}{j}") for j in range(2)] for i in range(2)]
        B_pbh = [[Buf(f"pbh{i}{j}") for j in range(2)] for i in range(2)]
        R32 = A1.f32(512); B_r32 = Buf("r32")
        U32 = A1.f32(512); B_u32 = Buf("u32")
        ST1 = A1.f32(8); B_st1 = Buf("st1")
        ST2 = A1.f32(8); B_st2 = Buf("st2")
        JK1 = HN1; B_jk1 = B_hn1
        OUTB = A1.f32(D); B_outb = Buf("outb")
        assert A1.off <= ARENA_F32, A1.off

        PT = ps[7].bitcast(BF16)

        TCH = [(c * 512, min(512, LP - c * 512)) for c in range(5)]
        QCH = [(0, 1), (1, 5), (5, 9), (9, 13), (13, 17)]

        def rmsnorm_block(src32, B_src, gidx, dst_bf, B_dst, st, B_st_, jk, B_jk_):
            act(jk, src32, AF.Square, [B_src], [B_jk_, B_st_], accum_out=st[:, 0:1])
            act(st[:, 1:2], st[:, 0:1], AF.Ln, [B_st_], [B_st_], scale=1.0 / D, bias=EPS)
            act(st[:, 2:3], st[:, 1:2], AF.Exp, [B_st_], [B_st_], scale=-0.5)
            stt("dve", dst_bf, src32, st[:, 2:3], G[:, gidx * D:(gidx + 1) * D], ALU.mult, ALU.mult,
                [B_src, B_st_, B_g], [B_dst])

        def transpose_block(hn_bf, B_hn_, dst3, B_dst):
            for kc in range(8):
                tr(PT[:, kc * 128:(kc + 1) * 128], hn_bf[:, kc * 128:(kc + 1) * 128], [B_hn_], [P[7]])
            cp("dve", dst3, PT.rearrange("p (c t) -> p c t", c=8), [P[7]], [B_dst])

        def proj_fm(psb, Pb, wv, c0, m, t0, n, B_w, hnt_v, B_h):
            for kc in range(8):
                mm(psb[0:m, 0:n], wv[:, kc, c0:c0 + m], hnt_v[:, kc, t0:t0 + n], kc == 0, kc == 7,
                   [B_w, B_h], [Pb])

        for sq in range(n_seq):
            S.barrier()
            wload(WMv, w0in_d, 2048, 416, [B_wm])
            wload(WUQv, wuq_d, 0, 768, [B_wuq])
            dma("pool", WUKV, wukv_d, [], [B_wukv])
            dma("sp", ROPEv[0:32], rope_d.rearrange("c p t -> p c t"), [], [B_rope])
            for i in range(2):
                memset("pool", KT[i], 0.0, [B_kt[i]])
                memset("pool", QT[i], 0.0, [B_qt[i]])
            memset("pool", VP, 0.0, [B_vp])
            for kc in range(8):
                tsc("pool", WKRRv[:, kc, 0:16], WMv[:, kc, 400:416], -1.0, ALU.mult, [B_wm], [B_wkrr])
                cp("pool", WKRRv[:, kc, 16:32], WMv[:, kc, 384:400], [B_wm], [B_wkrr])
            for kc in range(2):
                src = WUQv[:, kc, :].rearrange("p (h d) -> p h d", h=8)
                dst = WUQRv[:, kc, :].rearrange("p (h d) -> p h d", h=8)
                tsc("pool", dst[:, :, 0:16], src[:, :, 80:96], -1.0, ALU.mult, [B_wuq], [B_wuqr])
                cp("pool", dst[:, :, 16:32], src[:, :, 64:80], [B_wuq], [B_wuqr])

            XSs = [XS, E32P]; B_xss = [B_xs, B_e32p]
            HNs = [HN, C32P.bitcast(BF16)[:, 0:D]]; B_hns = [B_hn, B_c32p]
            STs = [ST[:, 0:4], ST[:, 4:8]]; B_sts = [B_st, B_stb]
            for b in range(NB):
                i = b % 2
                xs_, Bx_ = XSs[i], B_xss[i]
                if b == 0:
                    memset("dve", xs_, 0.0, [Bx_])
                    dma("sp", xs_[NPAD:128, :], meta_d, [], [Bx_])
                else:
                    dma("sp", xs_, x_d[sq, (b - 1) * 128:b * 128, :], [], [Bx_])
                rmsnorm_block(xs_, Bx_, 0, HNs[i], B_hns[i], STs[i], B_sts[i], HNs[i], B_hns[i])
                pk = 6 + i
                ptv = ps[pk].bitcast(BF16)
                for kc in range(8):
                    tr(ptv[:, kc * 128:(kc + 1) * 128], HNs[i][:, kc * 128:(kc + 1) * 128], [B_hns[i]], [P[pk]])
                cp("dve", HNTv[:, :, b * 128:(b + 1) * 128], ptv.rearrange("p (c t) -> p c t", c=8), [P[pk]], [B_hnt])

            def attention_pair(kts, B_ks, qts, B_qs, pair_chunk, sb, escale):
                M2 = lambda m_: m_.unsqueeze(1).to_broadcast([128, 2, 128])
                for ci, (ba, bz) in enumerate(QCH):
                    q0, q1 = ba * 128, bz * 128
                    N = q1 - q0
                    if sb:
                        psO, PO = ps[4 + (ci % 2)], P[4 + (ci % 2)]
                    else:
                        ob = 4 if ci % 2 == 0 else 6
                        psO, PO = ps[ob], P[ob]
                        psD, PD = ps[ob + 1], P[ob + 1]
                    mm(psO[:, 0:N], ZEROS, FIN[:, 0:N], True, False, [B_cb, B_fin], [PO])
                    if not sb:
                        mm(psD[:, 0:N], ZEROS, FIN[:, 0:N], True, False, [B_cb, B_fin], [PD])
                    steps = list(range(bz - 1, -1, -1))

                    def geom(j):
                        tq0 = max(q0, j * 128)
                        return tq0, q1 - tq0, tq0 - q0, j >= ba

                    def qk(si):
                        j = steps[si]
                        tq0, n, c0, diag = geom(j)
                        for hh in range(2):
                            bank = hh if sb else 2 * (si % 2) + hh
                            mm(ps[bank][:, 0:n], kts[hh][:, j * 128:(j + 1) * 128], qts[hh][:, tq0:q1], True, True,
                               [B_ks[hh], B_qs[hh]], [P[bank]])

                    if sb:
                        SPV = [(spbv, SPB, B_spbp), (spbv2, SPB2, B_spbp2)]
                        CSB = [(2, 3), (6, 7)]

                        def stage1(si):
                            j = steps[si]
                            tq0, n, c0, diag = geom(j)
                            sv, _, Bs = SPV[si % 2]
                            act(e32v[:, :, 0:n], psP[0][:, :, 0:n], AF.Exp, [P[0], P[1]], [B_e32p])
                            act(sv[:, :, 0:n], e32v[:, :, 0:n], AF.Ln, [B_e32p], [Bs], bias=1.0)
                            if diag:
                                tt("pool", sv[:, :, 0:128], sv[:, :, 0:128], M2(MSTRICT), ALU.mult, [Bs, B_cb], [Bs])

                        def cumsum(si):
                            j = steps[si]
                            tq0, n, c0, diag = geom(j)
                            _, sp2, Bs = SPV[si % 2]
                            cb_ = CSB[si % 2]
                            first = si == 0
                            for hh in range(2):
                                bk = cb_[hh]
                                mm(ps[bk][:, 0:n], kts[hh][:, j * 128:(j + 1) * 128], qts[hh][:, tq0:q1], True, False,
                                   [B_ks[hh], B_qs[hh]], [P[bk]])
                            for hh in range(2):
                                bk = cb_[hh]
                                mm(ps[bk][:, 0:n], NEGU0 if j == 0 else NEGU, sp2[hh][:, 0:n], False, first,
                                   [B_cb, Bs], [P[bk]])
                                if not first:
                                    mm(ps[bk][:, 0:n], NEGONES, C16[hh][:, c0:c0 + n], False, True,
                                       [B_cb, B_c16p], [P[bk]])

                        def carry(si):
                            j = steps[si]
                            tq0, n, c0, diag = geom(j)
                            sv, _, Bs = SPV[si % 2]
                            if j == 0:
                                return
                            if si == 0:
                                memset("pool", C32P, 0.0, [B_c32p])
                            tt("dve", c32v[:, :, c0:c0 + n], c32v[:, :, c0:c0 + n], sv[:, :, 0:n], ALU.add,
                               [B_c32p, Bs], [B_c32p])
                            cp("dve", c16v[:, :, 0:N], c32v[:, :, 0:N], [B_c32p], [B_c16p])

                        def exp2(si):
                            j = steps[si]
                            tq0, n, c0, diag = geom(j)
                            cb_ = CSB[si % 2]
                            e = si % 2
                            act(atbv[e][:, :, 0:n], psP[cb_[0] // 2][:, :, 0:n], AF.Exp, [P[cb_[0]], P[cb_[1]]], [B_atbp[e]])
                            if diag:
                                tt("pool", atbv[e][:, :, 0:128], atbv[e][:, :, 0:128], M2(MSTRICT), ALU.mult,
                                   [B_atbp[e], B_cb], [B_atbp[e]])

                        def av(si):
                            j = steps[si]
                            tq0, n, c0, diag = geom(j)
                            e = si % 2
                            for hh in range(2):
                                mm(psO[:, c0:c0 + n], VPv[:, j, hh, :], ATB[2 * e + hh][:, 0:n], False, j == 0 and hh == 1,
                                   [B_vp, B_atbp[e]], [PO])

                        ns = len(steps)
                        qk(0)
                        stage1(0)
                        if ns > 1:
                            qk(1)
                        for si in range(ns):
                            cumsum(si)
                            carry(si)
                            if si + 1 < ns:
                                stage1(si + 1)
                            if si + 2 < ns:
                                qk(si + 2)
                            exp2(si)
                            av(si)
                    else:
                        qk(0)
                    for si, j in (enumerate(steps) if not sb else []):
                        tq0, n, c0, diag = geom(j)
                        first = si == 0
                        last = j == 0
                        if True:
                            e = si % 2
                            act(atbv[e][:, :, 0:n], psP[e][:, :, 0:n], AF.Exp, [P[2 * e], P[2 * e + 1]], [B_atbp[e]],
                                scale=escale)
                            if diag:
                                tt("pool", atbv[e][:, :, 0:128], atbv[e][:, :, 0:128], M2(MINCL), ALU.mult,
                                   [B_atbp[e], B_cb], [B_atbp[e]])
                            if not last:
                                qk(si + 1)
                            for hh in range(2):
                                ai = 2 * e + hh
                                mm(psO[:, c0:c0 + n], VPv[:, j, hh, :], ATB[ai][:, 0:n], False, last and hh == 1,
                                   [B_vp, B_atbp[e]], [PO])
                                mm(psD[:, c0:c0 + n], ONESA if hh == 0 else ONESB, ATB[ai][:, 0:n], False,
                                   last and hh == 1, [B_cb, B_atbp[e]], [PD])
                    if sb:
                        tt("dve", OGv[:, pair_chunk, q0:q1], psO[:, 0:N], SG[:, q0:q1], ALU.mult,
                           [PO, B_sg], [B_og])
                    else:
                        t32, B_t = T32[ci % 2], B_t32[ci % 2]
                        act(t32[:, 0:N], psD[:, 0:N], AF.Ln, [PD], [B_t], bias=1e-30)
                        act(t32[:, 0:N], t32[:, 0:N], AF.Exp, [B_t], [B_t], scale=-1.0)
                        tt("dve", t32[:, 0:N], t32[:, 0:N], SG[:, q0:q1], ALU.mult, [B_t, B_sg], [B_t])
                        tt("dve", OGv[:, pair_chunk, q0:q1], psO[:, 0:N], t32[:, 0:N], ALU.mult,
                           [PO, B_t], [B_og])

            def attention_pair_sb(kts, B_ks, qts, B_qs, pair_chunk):
                M2 = lambda m_: m_.unsqueeze(1).to_broadcast([128, 2, 128])
                SPV = [(spbv, SPB, B_spbp), (spbv2, SPB2, B_spbp2)]
                CSB = [(2, 3), (6, 7)]

                class Chunk:
                    pass

                def mk_chunk(ci):
                    ba, bz = QCH[ci]
                    q0, q1 = ba * 128, bz * 128
                    N = q1 - q0
                    psO, PO = ps[4 + (ci % 2)], P[4 + (ci % 2)]
                    steps = list(range(bz - 1, -1, -1))
                    ns = len(steps)

                    def geom(si):
                        j = steps[si]
                        tq0 = max(q0, j * 128)
                        return j, tq0, q1 - tq0, tq0 - q0, j >= ba

                    def qk(si):
                        j, tq0, n, c0, diag = geom(si)
                        for hh in range(2):
                            mm(ps[hh][:, 0:n], kts[hh][:, j * 128:(j + 1) * 128], qts[hh][:, tq0:q1], True, True,
                               [B_ks[hh], B_qs[hh]], [P[hh]])

                    def stage1(si):
                        j, tq0, n, c0, diag = geom(si)
                        sv, _, Bs = SPV[si % 2]
                        act(e32v[:, :, 0:n], psP[0][:, :, 0:n], AF.Exp, [P[0], P[1]], [B_e32p])
                        act(sv[:, :, 0:n], e32v[:, :, 0:n], AF.Ln, [B_e32p], [Bs], bias=1.0)
                        if diag:
                            tt("pool", sv[:, :, 0:128], sv[:, :, 0:128], M2(MSTRICT), ALU.mult, [Bs, B_cb], [Bs])

                    def cumsum(si):
                        j, tq0, n, c0, diag = geom(si)
                        _, sp2, Bs = SPV[si % 2]
                        cb_ = CSB[si % 2]
                        first = si == 0
                        for hh in range(2):
                            bk = cb_[hh]
                            mm(ps[bk][:, 0:n], kts[hh][:, j * 128:(j + 1) * 128], qts[hh][:, tq0:q1], True, False,
                               [B_ks[hh], B_qs[hh]], [P[bk]])
                        for hh in range(2):
                            bk = cb_[hh]
                            mm(ps[bk][:, 0:n], NEGU0 if j == 0 else NEGU, sp2[hh][:, 0:n], False, first,
                               [B_cb, Bs], [P[bk]])
                            if not first:
                                mm(ps[bk][:, 0:n], NEGONES, C16[hh][:, c0:c0 + n], False, True,
                                   [B_cb, B_c16p], [P[bk]])

                    def carry(si):
                        j, tq0, n, c0, diag = geom(si)
                        sv, _, Bs = SPV[si % 2]
                        if j == 0:
                            return
                        if si == 0:
                            memset("pool", C32P, 0.0, [B_c32p])
                        tt("dve", c32v[:, :, c0:c0 + n], c32v[:, :, c0:c0 + n], sv[:, :, 0:n], ALU.add,
                           [B_c32p, Bs], [B_c32p])
                        cp("dve", c16v[:, :, 0:N], c32v[:, :, 0:N], [B_c32p], [B_c16p])

                    def exp2(si):
                        j, tq0, n, c0, diag = geom(si)
                        cb_ = CSB[si % 2]
                        e = si % 2
                        act(atbv[e][:, :, 0:n], psP[cb_[0] // 2][:, :, 0:n], AF.Exp, [P[cb_[0]], P[cb_[1]]], [B_atbp[e]])
                        if diag:
                            tt("pool", atbv[e][:, :, 0:128], atbv[e][:, :, 0:128], M2(MSTRICT), ALU.mult,
                               [B_atbp[e], B_cb], [B_atbp[e]])

                    def av(si):
                        j, tq0, n, c0, diag = geom(si)
                        e = si % 2
                        for hh in range(2):
                            mm(psO[:, c0:c0 + n], VPv[:, j, hh, :], ATB[2 * e + hh][:, 0:n], False, j == 0 and hh == 1,
                               [B_vp, B_atbp[e]], [PO])

                    def prologue():
                        mm(psO[:, 0:N], ZEROS, FIN[:, 0:N], True, False, [B_cb, B_fin], [PO])
                        qk(0)
                        stage1(0)
                        if ns > 1:
                            qk(1)

                    def finalize():
                        tt("dve", OGv[:, pair_chunk, q0:q1], psO[:, 0:N], SG[:, q0:q1], ALU.mult,
                           [PO, B_sg], [B_og])

                    c = Chunk()
                    c.ns, c.qk, c.stage1, c.cumsum, c.carry, c.exp2, c.av = ns, qk, stage1, cumsum, carry, exp2, av
                    c.prologue, c.finalize = prologue, finalize
                    return c

                chunks = [mk_chunk(ci) for ci in range(len(QCH))]
                chunks[0].prologue()
                chunks[0].cumsum(0)
                for ci, ch in enumerate(chunks):
                    nxt = chunks[ci + 1] if ci + 1 < len(chunks) else None
                    for si in range(ch.ns):
                        lastst = si == ch.ns - 1
                        ch.carry(si)
                        if si + 1 < ch.ns:
                            ch.stage1(si + 1)
                        if si + 2 < ch.ns:
                            ch.qk(si + 2)
                        if lastst and nxt is not None:
                            nxt.prologue()
                        ch.exp2(si)
                        if not lastst:
                            ch.cumsum(si + 1)
                        elif nxt is not None:
                            nxt.cumsum(0)
                        ch.av(si)
                    ch.finalize()

            bk_rot = [0]
            BK_ORDER = [[6, 7]]

            def nbk():
                bk_rot[0] = (bk_rot[0] + 1) % len(BK_ORDER[0])
                return BK_ORDER[0][bk_rot[0]]

            def build_v(lhs_fn, rhs_fn, nk, reads):
                for b0 in range(0, NB, 4):
                    nb4 = min(4, NB - b0)
                    k_ = nbk()
                    for i in range(nb4):
                        b = b0 + i
                        for kc in range(nk):
                            mm(ps[k_][:, i * 128:(i + 1) * 128], lhs_fn(kc, b), rhs_fn(kc), kc == 0, kc == nk - 1,
                               reads, [P[k_]])
                    pv = ps[k_][:, 0:nb4 * 128].rearrange("p (b n) -> p b n", b=nb4)
                    cp("dve", VPv[:, b0:b0 + nb4, 0, 0:64], pv[:, :, 0:64], [P[k_]], [B_vp])
                    cp("dve", VPv[:, b0:b0 + nb4, 1, 64:128], pv[:, :, 64:128], [P[k_]], [B_vp])

            BK_ORDER[0] = [6, 7, 0, 1, 2, 3]

            def pj(wv, c0, m, t0, n, B_w, src_v, B_src, nkc=8):
                k_ = nbk()
                for kc in range(nkc):
                    mm(ps[k_][0:m, 0:n], wv[:, kc, c0:c0 + m], src_v[:, kc, t0:t0 + n], kc == 0, kc == nkc - 1,
                       [B_w, B_src], [P[k_]])
                return ps[k_], P[k_]

            for p in range(4):
                ws, wsv, B_w = WS[p % 2], WSv[p % 2], B_ws[p % 2]
                wv4 = ws.rearrange("p (c f n) -> p c f n", c=8, f=4)
                srcv = w0in_d.rearrange("(kc p) n -> p kc n", p=128)
                for f in range(4):
                    dma("pool", wv4[:, :, f, :], srcv[:, :, f * 512 + p * 128:f * 512 + (p + 1) * 128], [], [B_w])
                for (t0, n) in TCH:
                    pa, Pa = pj(wsv, 128, 128, t0, n, B_w, HNTv, B_hnt)
                    cp("act", KT[0][:, t0:t0 + n], pa[:, 0:n], [Pa], [B_kt[0]])
                    pa, Pa = pj(wsv, 0, 128, t0, n, B_w, HNTv, B_hnt)
                    tsc("dve", QT[0][0:64, t0:t0 + n], pa[0:64, 0:n], 0.125, ALU.mult, [Pa], [B_qt[0]])
                    tsc("dve", QT[1][64:128, t0:t0 + n], pa[64:128, 0:n], 0.125, ALU.mult, [Pa], [B_qt[1]])
                    pa, Pa = pj(wsv, 384, 128, t0, n, B_w, HNTv, B_hnt)
                    act(SG[:, t0:t0 + n], pa[:, 0:n], AF.Silu, [Pa], [B_sg])
                build_v(lambda kc, b: HNTv[:, kc, b * 128:(b + 1) * 128], lambda kc, wsv=wsv: wsv[:, kc, 256:384], 8,
                        [B_hnt, B_w])
                attention_pair_sb([KT[0], KT[0]], [B_kt[0], B_kt[0]], QT, B_qt, p)

            BK_ORDER[0] = [6, 7, 0, 1, 2, 3]
            for i in range(2):
                memset("pool", KT[i], 0.0, [B_kt[i]])
                memset("pool", QT[i], 0.0, [B_qt[i]])
                memset("pool", KT[i][96:97, 0:NPAD], -30000.0, [B_kt[i]])
                memset("pool", QT[i][96:97, :], 1.0, [B_qt[i]])
            CKVNv1 = CKVN.rearrange("p (c t) -> p c t", c=1)
            for (t0, n) in TCH:
                for cc in range(2):
                    pa, Pa = pj(WMv, cc * 128, 128, t0, n, B_wm, HNTv, B_hnt)
                    cp("dve", T32[cc][:, 0:n], pa[:, 0:n], [Pa], [B_t32[cc]])
                    act(ATB[cc][:, 0:n], pa[:, 0:n], AF.Square, [Pa], [B_atb[cc]])
                k_ = nbk()
                for cc in range(2):
                    mm(ps[k_][:, 0:n], ONES, ATB[cc][:, 0:n], cc == 0, cc == 1, [B_cb, B_atb[cc]], [P[k_]])
                act(E32[0][:, 0:n], ps[k_][:, 0:n], AF.Ln, [P[k_]], [B_e32[0]], scale=1.0 / 256, bias=EPS)
                act(E32[0][:, 0:n], E32[0][:, 0:n], AF.Exp, [B_e32[0]], [B_e32[0]], scale=-0.5)
                for cc in range(2):
                    stt("dve", CQNv[:, cc, t0:t0 + n], T32[cc][:, 0:n], NG[:, cc:cc + 1], E32[0][:, 0:n],
                        ALU.mult, ALU.mult, [B_t32[cc], B_ng, B_e32[0]], [B_cqn])
                pa, Pa = pj(WMv, 256, 128, t0, n, B_wm, HNTv, B_hnt)
                cp("dve", T32[0][:, 0:n], pa[:, 0:n], [Pa], [B_t32[0]])
                act(ATB[2][:, 0:n], pa[:, 0:n], AF.Square, [Pa], [B_atb[2]])
                k_ = nbk()
                mm(ps[k_][:, 0:n], ONES, ATB[2][:, 0:n], True, True, [B_cb, B_atb[2]], [P[k_]])
                act(E32[1][:, 0:n], ps[k_][:, 0:n], AF.Ln, [P[k_]], [B_e32[1]], scale=1.0 / 128, bias=EPS)
                act(E32[1][:, 0:n], E32[1][:, 0:n], AF.Exp, [B_e32[1]], [B_e32[1]], scale=-0.5)
                stt("dve", CKVN[:, t0:t0 + n], T32[0][:, 0:n], NG[:, 2:3], E32[1][:, 0:n],
                    ALU.mult, ALU.mult, [B_t32[0], B_ng, B_e32[1]], [B_ckvn])
                pa, Pa = pj(WMv, 384, 32, t0, n, B_wm, HNTv, B_hnt)
                pb_, Pb_ = pj(WKRRv, 0, 32, t0, n, B_wkrr, HNTv, B_hnt)
                tt("dve", T32[0][0:32, 0:n], pa[0:32, 0:n], ROPEv[0:32, 0, t0:t0 + n], ALU.mult,
                   [Pa, B_rope], [B_t32[0]])
                tt("dve", T32[1][0:32, 0:n], pb_[0:32, 0:n], ROPEv[0:32, 1, t0:t0 + n], ALU.mult,
                   [Pb_, B_rope], [B_t32[1]])
                tt("dve", KR[0:32, t0:t0 + n], T32[0][0:32, 0:n], T32[1][0:32, 0:n], ALU.add,
                   [B_t32[0], B_t32[1]], [B_kr])

            WUKVv = WUKV.rearrange("p (h a d) -> p h a d", h=8, a=2)
            for p in range(4):
                ws, wsv, B_w = WS[p % 2], WSv[p % 2], B_ws[p % 2]
                wload(wsv[:, :, 0:128], w0in_d, 2464 + p * 128, 128, [B_w])
                for (t0, n) in TCH:
                    pa, Pa = pj(wsv, 0, 128, t0, n, B_w, HNTv, B_hnt)
                    act(SG[:, t0:t0 + n], pa[:, 0:n], AF.Silu, [Pa], [B_sg])
                    for hh in range(2):
                        h = 2 * p + hh
                        k_ = nbk()
                        mm(ps[k_][0:64, 0:n], WUKVv[:, h, 0, :], CKVN[:, t0:t0 + n], True, True,
                           [B_wukv, B_ckvn], [P[k_]])
                        cp("act", KT[hh][0:64, t0:t0 + n], ps[k_][0:64, 0:n], [P[k_]], [B_kt[hh]])
                        cp("pool", KT[hh][64:96, t0:t0 + n], KR[0:32, t0:t0 + n], [B_kr], [B_kt[hh]])
                        pa, Pa = pj(WUQv, h * 96, 64, t0, n, B_wuq, CQNv, B_cqn, nkc=2)
                        cp("act", QT[hh][0:64, t0:t0 + n], pa[0:64, 0:n], [Pa], [B_qt[hh]])
                        px, Px = pj(WUQv, h * 96 + 64, 32, t0, n, B_wuq, CQNv, B_cqn, nkc=2)
                        pr_, Pr_ = pj(WUQRv, h * 32, 32, t0, n, B_wuqr, CQNv, B_cqn, nkc=2)
                        tt("dve", T32[0][0:32, 0:n], px[0:32, 0:n], ROPEv[0:32, 0, t0:t0 + n], ALU.mult,
                           [Px, B_rope], [B_t32[0]])
                        tt("dve", T32[1][0:32, 0:n], pr_[0:32, 0:n], ROPEv[0:32, 1, t0:t0 + n], ALU.mult,
                           [Pr_, B_rope], [B_t32[1]])
                        tt("dve", QT[hh][64:96, t0:t0 + n], T32[0][0:32, 0:n], T32[1][0:32, 0:n], ALU.add,
                           [B_t32[0], B_t32[1]], [B_qt[hh]])
                build_v(lambda kc, b: CKVN[:, b * 128:(b + 1) * 128], lambda kc, p=p: WUKVv[:, 2 * p:2 * p + 2, 1, :], 1,
                        [B_ckvn, B_wukv])
                if p == 3:
                    assert pers_end + 4096 + 9216 <= A0_KT_START - (128 + 768 + 256 + 512)
                    dead = [B_hnt, B_ws[0], B_ws[1], B_wm]
                    for c in range(0, 1024, 512):
                        wload(W0Ov[:, :, c:c + 512], w0out_d, c, 512, [B_w0o] + dead)
                    for c in range(0, OD_IN, 576):
                        wload(W1Iv[:, :, c:c + 576], w1in_d, c, 576, [B_w1i] + dead)
                attention_pair(KT, B_kt, QT, B_qt, 4 + p, False, 96.0 ** -0.5)

            if debug_h1:
                for c in range(8):
                    dma("pool", dbg2_d[sq, :, c * LP:(c + 1) * LP], OG[:, c * LP:(c + 1) * LP], [B_og], [])
            S.barrier()
            for c in range(0, 1024, 512):
                wload(W1Ov[:, :, c:c + 512], w1out_d, c, 512, [B_w1o])
            dma("sp", SWEv, swae_d.rearrange("a p h r -> p a h r"), [], [B_swe])
            dma("sp", EMv[0:16], em_d, [], [B_em])
            dma("sp", CMv[0:16], cm_d, [], [B_cm])
            sk2 = sinks_d.rearrange("(p two) -> two p", two=2)
            S.op("sp", lambda e: e.dma_start(out=ESK[0:64, :], in_=sk2[0:1, :].broadcast_to([64, 8]),
                                             allow_slow_non_contiguous=True), writes=[B_esk], dma=True)
            S.op("sp", lambda e: e.dma_start(out=ESK[64:128, :], in_=sk2[1:2, :].broadcast_to([64, 8]),
                                             allow_slow_non_contiguous=True), writes=[B_esk], dma=True)
            act(ESK, ESK, AF.Exp, [B_esk], [B_esk])
            for par in range(2):
                for g in range(2):
                    memset("pool", QG[par][g], 0.0, [B_qg[par][g]])
            memset("pool", VM, 0.0, [B_vm])
            memset("pool", VR, 0.0, B_vr)

            rot = [0]

            def pbank():
                rot[0] ^= 1
                return 6 + rot[0]

            def front(b):
                par = b % 2
                h1, Bh1 = H1[par], B_h1[par]
                hv, Bhv = HNT1v[par], B_hnt1[par]
                if b == 0:
                    memset("dve", XS1, 0.0, [B_xs1])
                    dma("sp", XS1[NPAD:128, :], meta_d, [], [B_xs1])
                else:
                    dma("sp", XS1, x_d[sq, (b - 1) * 128:b * 128, :], [], [B_xs1])
                for hf in range(2):
                    for kc in range(8):
                        mm(ps[4 + hf][:, :], OGv[:, kc, b * 128:(b + 1) * 128], W0Ov[:, kc, hf * 512:(hf + 1) * 512],
                           kc == 0, kc == 7, [B_og, B_w0o], [P[4 + hf]])
                    tt("dve", h1[:, hf * 512:(hf + 1) * 512], ps[4 + hf][:, :], XS1[:, hf * 512:(hf + 1) * 512],
                       ALU.add, [P[4 + hf], B_xs1], [Bh1])
                if debug_h1:
                    dma("sp", dbg_d[sq, b * 128:(b + 1) * 128, :], h1, [Bh1], [])
                yield
                rmsnorm_block(h1, Bh1, 1, HN1, B_hn1, ST1, B_st1, JK1, B_jk1)
                pk = pbank()
                ptv = ps[pk].bitcast(BF16)
                for kc in range(8):
                    tr(ptv[:, kc * 128:(kc + 1) * 128], HN1[:, kc * 128:(kc + 1) * 128], [B_hn1], [P[pk]])
                cp("dve", hv, ptv.rearrange("p (c t) -> p c t", c=8), [P[pk]], [Bhv])
                yield
                pk = pbank()
                for kc in range(8):
                    mm(ps[pk][:, 0:128], W1Iv[:, kc, 1024:1152], hv[:, kc, :], kc == 0, kc == 7,
                       [B_w1i, Bhv], [P[pk]])
                cp("act", KT1[:, b * 128:(b + 1) * 128], ps[pk][:, 0:128], [P[pk]], [B_kt1[b]])
                slot = b % 3
                pk = pbank()
                if b == 0:
                    for kc in range(8):
                        mm(ps[pk][0:16, 0:128], hv[:, kc, NPAD:128], W1Iv[:, kc, 1152:1280], kc == 0, kc == 7,
                           [Bhv, B_w1i], [P[pk]])
                    for kh in range(2):
                        cp("dve", VMv[0:16, 2 * kh, 0:64], ps[pk][0:16, kh * 64:(kh + 1) * 64], [P[pk]], [B_vm])
                        cp("dve", VMv[0:16, 2 * kh + 1, 64:128], ps[pk][0:16, kh * 64:(kh + 1) * 64], [P[pk]], [B_vm])
                    return
                for kc in range(8):
                    mm(ps[pk][:, 0:128], hv[:, kc, :], W1Iv[:, kc, 1152:1280], kc == 0, kc == 7,
                       [Bhv, B_w1i], [P[pk]])
                for kh in range(2):
                    cp("dve", VRv[:, slot, 2 * kh, 0:64], ps[pk][:, kh * 64:(kh + 1) * 64], [P[pk]], [B_vr[slot]])
                    cp("dve", VRv[:, slot, 2 * kh + 1, 64:128], ps[pk][:, kh * 64:(kh + 1) * 64], [P[pk]], [B_vr[slot]])
                yield
                for q4 in range(2):
                    pk = pbank()
                    for i in range(4):
                        pr = q4 * 4 + i
                        for kc in range(8):
                            mm(ps[pk][:, i * 128:(i + 1) * 128], W1Iv[:, kc, pr * 128:(pr + 1) * 128], hv[:, kc, :],
                               kc == 0, kc == 7, [B_w1i, Bhv], [P[pk]])
                    g = q4
                    gs = slice(g * 64, g * 64 + 64)
                    pv = ps[pk].rearrange("p (i t) -> p i t", i=4)
                    qv = QGv[par][g][gs].rearrange("p (i two) t -> p two i t", two=2)
                    cp("act" if g == 0 else "dve", qv[:, 0], pv[0:64], [P[pk]], [B_qg[par][g]])
                    cp("dve" if g == 0 else "act", qv[:, 1], pv[64:128], [P[pk]], [B_qg[par][g]])
                    yield
                for c4 in range(2):
                    pk = pbank()
                    for i in range(4):
                        cc = c4 * 4 + i
                        for kc in range(8):
                            mm(ps[pk][:, i * 128:(i + 1) * 128], W1Iv[:, kc, 1280 + cc * 128:1280 + (cc + 1) * 128],
                               hv[:, kc, :], kc == 0, kc == 7, [B_w1i, Bhv], [P[pk]])
                    act(SG1v[par][:, c4 * 4:(c4 + 1) * 4, :], ps[pk].rearrange("p (i t) -> p i t", i=4), AF.Silu,
                        [P[pk]], [B_sg1[par]])
                    yield

            def back_parts(b):
                par = b % 2
                slot = b % 3
                h1, Bh1 = H1[par], B_h1[par]
                tiles = []
                for g in range(2):
                    if b >= 2:
                        tiles.append((g, "prev", KT1[:, (b - 1) * 128:b * 128], 128, (b - 1) % 3, B_kt1[b - 1]))
                    tiles.append((g, "cur", KT1[:, b * 128:(b + 1) * 128], 128, slot, B_kt1[b]))
                    tiles.append((g, "meta", KT1[:, NPAD:128], 16, None, B_kt1[0]))
                nt = len(tiles)

                def qk(ti):
                    g, kind, kk, nk, vs, Bk = tiles[ti]
                    for hf in range(2):
                        mm(ps[hf][0:nk, :], kk, QGv[par][g][:, hf * 4:(hf + 1) * 4, :], True, True,
                           [Bk, B_qg[par][g]], [P[hf]])

                def soft(ti):
                    g, kind, kk, nk, vs, Bk = tiles[ti]
                    e = ti % 2
                    exv, pbv = EXv[e], PBv[e]
                    for hf in range(2):
                        act(EX[e][0:nk, hf * 512:(hf + 1) * 512], ps[hf][0:nk, :], AF.Exp, [P[hf]], [B_exh[e][hf]],
                            scale=0.125)
                        if kind != "meta":
                            a = 0 if kind == "prev" else 1
                            hs = slice(hf * 4, (hf + 1) * 4)
                            tt("pool" if hf == 0 else "dve", pbv[:, hs, :], exv[:, hs, :],
                               SWEv[:, a, g * 8 + hf * 4:g * 8 + (hf + 1) * 4, :], ALU.mult,
                               [B_exh[e][hf], B_swe], [B_pbh[e][hf]])
                    if kind == "meta":
                        tt("dve", exv[0:16], exv[0:16], EMv[0:16, g * 8:(g + 1) * 8, :], ALU.mult,
                           B_exh[e] + [B_em], B_exh[e])
                        tt("dve", pbv[0:16], exv[0:16],
                           CMv[0:16, b, g * 8:(g + 1) * 8].unsqueeze(2).to_broadcast([16, 8, 128]), ALU.mult,
                           B_exh[e] + [B_cm], B_pbh[e])

                def prologue():
                    qk(0)
                    soft(0)
                    if nt > 1:
                        qk(1)

                def tile_gen():
                    for ti, (g, kind, kk, nk, vs, Bk) in enumerate(tiles):
                        e = ti % 2
                        pbv, B_p = PBv[e], B_pbh[e]
                        if kind == "meta":
                            va, vb2 = VMv[0:16, 2 * g, :], VMv[0:16, 2 * g + 1, :]
                            B_v = B_vm
                        else:
                            va, vb2 = VRv[:, vs, 2 * g, :], VRv[:, vs, 2 * g + 1, :]
                            B_v = B_vr[vs]
                        pe_ = pbv[0:nk].rearrange("p (q two) t -> p two q t", two=2)
                        gfirst = kind == ("prev" if b >= 2 else "cur")
                        glast = kind == "meta"
                        mm(ps[2][:, :], va, pe_[:, 0], gfirst, False, [B_v] + B_p, [P[2]])
                        mm(ps[2][:, :], vb2, pe_[:, 1], False, glast, [B_v] + B_p, [P[2]])
                        mm(ps[3][:, :], ONESA[0:nk, :], pe_[:, 0], gfirst, False, [B_cb] + B_p, [P[3]])
                        mm(ps[3][:, :], ONESB[0:nk, :], pe_[:, 1], False, glast, [B_cb] + B_p, [P[3]])
                        if ti + 1 < nt:
                            soft(ti + 1)
                        if ti + 2 < nt:
                            qk(ti + 2)
                        if glast:
                            R3 = R32.rearrange("p (q t) -> p q t", q=4)
                            U3 = U32.rearrange("p (q t) -> p q t", q=4)
                            tt("dve", R3, ps[3].rearrange("p (q t) -> p q t", q=4),
                               ESK[:, g * 4:(g + 1) * 4].unsqueeze(2).to_broadcast([128, 4, 128]), ALU.add,
                               [P[3], B_esk], [B_r32])
                            act(R32, R32, AF.Ln, [B_r32], [B_r32])
                            act(R32, R32, AF.Exp, [B_r32], [B_r32], scale=-1.0)
                            tt("pool", U3, R3, SG1v[par][:, g * 4:(g + 1) * 4, :], ALU.mult, [B_r32, B_sg1[par]], [B_u32])
                            tt("dve", OG1v[:, g * 4:(g + 1) * 4, :], ps[2].rearrange("p (q t) -> p q t", q=4), U3,
                               ALU.mult, [P[2], B_u32], [B_og1])
                        yield

                def tail():
                    for hf in range(2):
                        for kc in range(8):
                            mm(ps[4 + hf][:, :], OG1v[:, kc, :], W1Ov[:, kc, hf * 512:(hf + 1) * 512],
                               kc == 0, kc == 7, [B_og1, B_w1o], [P[4 + hf]])
                        tt("dve", h1[:, hf * 512:(hf + 1) * 512], ps[4 + hf][:, :], h1[:, hf * 512:(hf + 1) * 512],
                           ALU.add, [P[4 + hf], Bh1], [Bh1])
                    rmsnorm_block(h1, Bh1, 2, OUTB, B_outb, ST2, B_st2, JK1, B_jk1)
                    dma("sp", out_d[sq, (b - 1) * 128:b * 128, :], OUTB, [B_outb], [])

                return prologue, tile_gen, tail

            def drain(gen):
                for _ in gen:
                    pass

            drain(front(0))
            drain(front(1))
            parts = back_parts(1)
            parts[0]()
            for b in range(1, NB):
                prologue, tile_gen, tail = parts
                alive = [tile_gen()]
                if b + 1 < NB:
                    alive.append(front(b + 1))
                while alive:
                    for gq in list(alive):
                        try:
                            next(gq)
                        except StopIteration:
                            alive.remove(gq)
                if b + 1 < NB:
                    parts = back_parts(b + 1)
                    parts[0]()
                tail()

        S.finalize()
        with nc.Block() as block:
            @block.tensor
            def _(e):
                S.emit_engine("pe", e, sems, dma_sems)

            @block.scalar
            def _(e):
                S.emit_engine("act", e, sems, dma_sems)

            @block.vector
            def _(e):
                S.emit_engine("dve", e, sems, dma_sems)

            @block.gpsimd
            def _(e):
                S.emit_engine("pool", e, sems, dma_sems)

            @block.sync
            def _(e):
                S.emit_engine("sp", e, sems, dma_sems, final_wait=True)
    return nc


_NC_CACHE = {}


def _common_inputs(meta, norm_g, final_g, ev_w_in, ev_q_norm_g, ev_kv_norm_g, ev_w_uq, ev_w_ukv,
                   ev_w_out, od_w_in, od_sinks, od_w_out):
    cb, rope, swae, em, cm = _const_tables()
    f = lambda a: np.ascontiguousarray(np.asarray(a, dtype=np.float32))
    return {
        "meta": f(meta),
        "gains": f(np.concatenate([np.asarray(norm_g), np.asarray(final_g)[None, :]], 0)),
        "w0in": f(ev_w_in[0]), "qng": f(ev_q_norm_g[0]), "kvng": f(ev_kv_norm_g[0]),
        "wuq": f(ev_w_uq[0]), "wukv": f(ev_w_ukv[0]), "w0out": f(ev_w_out[0]),
        "w1in": f(od_w_in[0]), "sinks": f(od_sinks[0]), "w1out": f(od_w_out[0]),
        "cb": cb, "rope": f(rope), "swae": f(swae), "em": f(em), "cm": f(cm),
    }


def kernel(x, meta, norm_g, final_g, ev_w_in, ev_q_norm_g, ev_kv_norm_g, ev_w_uq, ev_w_ukv,
           ev_w_out, od_w_in, od_sinks, od_w_out):
    n = 8
    x = np.asarray(x, dtype=np.float32)
    common = _common_inputs(meta, norm_g, final_g, ev_w_in, ev_q_norm_g, ev_kv_norm_g, ev_w_uq,
                            ev_w_ukv, ev_w_out, od_w_in, od_sinks, od_w_out)
    if "nc" not in _NC_CACHE:
        _NC_CACHE["nc"] = build_nc()
    nc = _NC_CACHE["nc"]
    in_maps = []
    for c in range(n):
        m = dict(common)
        m["x"] = np.ascontiguousarray(x[c * SEQ_PER_CORE:(c + 1) * SEQ_PER_CORE])
        in_maps.append(m)
    res = run_bass_kernel_spmd(nc, in_maps, core_ids=list(range(n)))
    return np.concatenate([r["out"] for r in res.results], axis=0)
```

```python
import numpy as np
from contextlib import ExitStack
import concourse.bass as bass
import concourse.mybir as mybir
from concourse.bass_utils import run_bass_kernel_spmd

F32 = mybir.dt.float32
BF16 = mybir.dt.bfloat16
AF = mybir.ActivationFunctionType
ALU = mybir.AluOpType

ENGS = ["pe", "act", "dve", "pool", "sp"]

D = 1024
LP = 2176
NB = 17
NPAD = 112
EPS = 1e-6
SEQ_PER_CORE = 2
EV_IN = 2976
OD_IN = 2304


class Buf:
    __slots__ = ("name", "last_w", "readers", "excl")

    def __init__(self, name, excl=False):
        self.name = name
        self.last_w = None
        self.readers = []
        self.excl = excl


class Op:
    __slots__ = ("eng", "fn", "deps", "signal", "is_dma", "dsem", "dval", "cnt", "prev_dma")

    def __init__(self, eng, fn, is_dma):
        self.eng = eng
        self.fn = fn
        self.deps = []
        self.signal = False
        self.is_dma = is_dma
        self.dsem = None
        self.dval = 0
        self.cnt = 0
        self.prev_dma = None


class Sched:
    def __init__(self, n_dma_sems=8):
        self.ops = {e: [] for e in ENGS}
        self.n_dma_sems = n_dma_sems
        self.dma_rr = {e: 0 for e in ENGS}
        self.dma_cnt = {}
        self.dma_last = {}

    def op(self, eng, fn, reads=(), writes=(), dma=False):
        o = Op(eng, fn, dma)
        deps = {}
        for b in reads:
            if b.last_w is not None:
                deps[id(b.last_w)] = b.last_w
            if b.excl:
                for r in b.readers:
                    if r.eng != eng:
                        deps[id(r)] = r
        for b in writes:
            if b.last_w is not None:
                deps[id(b.last_w)] = b.last_w
            for r in b.readers:
                deps[id(r)] = r
        final = []
        for d in deps.values():
            if d is o:
                continue
            if d.eng == "pe" and eng == "pe" and (not d.is_dma) and (not dma):
                continue
            final.append(d)
            d.signal = True
        o.deps = final
        for b in reads:
            if not dma:
                b.readers = [r for r in b.readers if r.is_dma or r.eng != eng]
            b.readers.append(o)
        for b in writes:
            b.last_w = o
            b.readers = []
        if dma:
            k = self.dma_rr[eng]
            self.dma_rr[eng] = (k + 1) % self.n_dma_sems
            key = (eng, k)
            self.dma_cnt[key] = self.dma_cnt.get(key, 0) + 1
            o.dsem = key
            o.dval = 16 * self.dma_cnt[key]
            o.prev_dma = self.dma_last.get(key)
            self.dma_last[key] = o
        self.ops[eng].append(o)
        return o

    def barrier(self):
        lasts = []
        for e in ENGS:
            for o in reversed(self.ops[e]):
                if (not o.is_dma) and o.fn is not None:
                    lasts.append(o)
                    break
        dmas = list(self.dma_last.values())
        for e in ENGS:
            m = Op(e, None, False)
            m.deps = [d for d in lasts if d.eng != e] + dmas
            for d in m.deps:
                d.signal = True
            self.ops[e].append(m)

    def finalize(self):
        for e in ENGS:
            c = 0
            for o in self.ops[e]:
                if o.is_dma or o.fn is None:
                    continue
                if o.signal:
                    c += 1
                    o.cnt = c

    def emit_engine(self, eng_name, e, sems, dma_sems, final_wait=False):
        waited = {}

        def wait(key, sem, val):
            if val <= 0:
                return
            if waited.get(key, 0) < val:
                e.wait_ge(sem, val)
                waited[key] = val

        for o in self.ops[eng_name]:
            for d in o.deps:
                if d.is_dma:
                    wait(d.dsem, dma_sems[d.dsem], d.dval)
                else:
                    wait(d.eng, sems[d.eng], d.cnt)
            if o.is_dma and o.prev_dma is not None:
                wait(o.dsem, dma_sems[o.dsem], o.prev_dma.dval)
            if o.fn is None:
                continue
            ins = o.fn(e)
            if o.is_dma:
                ins.then_inc(dma_sems[o.dsem], 16)
            elif o.signal:
                ins.then_inc(sems[eng_name], 1)
        if final_wait:
            for key, o in self.dma_last.items():
                wait(key, dma_sems[key], o.dval)


def _const_tables():
    r = np.arange(128)
    s = r[:, None]
    t = r[None, :]
    ident = (s == t).astype(np.float32)
    negU = -(s >= t).astype(np.float32)
    negU0 = negU * (s >= NPAD)
    negOnes = -np.ones((128, 128), np.float32)
    negOnes0 = negOnes * (s >= NPAD)
    ones = np.ones((128, 128), np.float32)
    mstrict = (s < t).astype(np.float32)
    mincl = (s <= t).astype(np.float32)
    onesA = np.concatenate([np.ones((128, 64)), np.zeros((128, 64))], 1).astype(np.float32)
    onesB = np.concatenate([np.zeros((128, 64)), np.ones((128, 64))], 1).astype(np.float32)
    zeros = np.zeros((128, 128), np.float32)
    cb = np.concatenate([ident, negU, negU0, negOnes, negOnes0, ones, mstrict, mincl,
                         onesA, onesB, zeros], axis=1)
    half = 16
    inv = 10000.0 ** (-np.arange(half, dtype=np.float64) / half)
    pos = (np.arange(LP) - NPAD).astype(np.float64)
    ang = inv[:, None] * pos[None, :]
    cos = np.concatenate([np.cos(ang), np.cos(ang)], 0).astype(np.float32)
    sin = np.concatenate([np.sin(ang), np.sin(ang)], 0).astype(np.float32)
    rope = np.stack([cos, sin], 0)
    H = 16
    slopes = 2.0 ** (-8.0 * (np.arange(H, dtype=np.float64) + 1.0) / H)
    sl = slopes[None, :, None]
    dprev = (128 + r[None, None, :] - r[:, None, None]).astype(np.float64)
    eprev = np.where(dprev < 128, np.exp(-sl * dprev), 0.0)
    dcur = (r[None, None, :] - r[:, None, None]).astype(np.float64)
    ecur = np.where(dcur >= 0, np.exp(-sl * np.maximum(dcur, 0)), 0.0)
    m = np.arange(16)
    dm = (16 + r[None, None, :] - m[:, None, None]).astype(np.float64)
    em = np.exp(-sl * dm)
    n = np.arange(NB)
    cm = np.exp(-slopes[None, None, :] * 128.0 * np.maximum(n - 1, 0)[None, :, None])
    cm = np.broadcast_to(cm, (16, NB, H))
    swa_e = np.stack([eprev, ecur], 0).astype(np.float32)
    return (cb.astype(np.float32), rope, swa_e, em.astype(np.float32),
            np.ascontiguousarray(cm).astype(np.float32))


def build_nc(debug_h1=False, n_seq=SEQ_PER_CORE):
    nc = bass.Bass("TRN2", target_bir_lowering=False)
    dt = nc.dram_tensor
    x_d = dt("x", [n_seq, 2048, D], F32, kind="ExternalInput").ap()
    meta_d = dt("meta", [16, D], F32, kind="ExternalInput").ap()
    gains_d = dt("gains", [3, D], F32, kind="ExternalInput").ap()
    w0in_d = dt("w0in", [D, EV_IN], F32, kind="ExternalInput").ap()
    qng_d = dt("qng", [256], F32, kind="ExternalInput").ap()
    kvng_d = dt("kvng", [128], F32, kind="ExternalInput").ap()
    wuq_d = dt("wuq", [256, 768], F32, kind="ExternalInput").ap()
    wukv_d = dt("wukv", [128, 1024], F32, kind="ExternalInput").ap()
    w0out_d = dt("w0out", [D, D], F32, kind="ExternalInput").ap()
    w1in_d = dt("w1in", [D, OD_IN], F32, kind="ExternalInput").ap()
    sinks_d = dt("sinks", [16], F32, kind="ExternalInput").ap()
    w1out_d = dt("w1out", [D, D], F32, kind="ExternalInput").ap()
    cb_d = dt("cb", [128, 11 * 128], F32, kind="ExternalInput").ap()
    rope_d = dt("rope", [2, 32, LP], F32, kind="ExternalInput").ap()
    swae_d = dt("swae", [2, 128, 16, 128], F32, kind="ExternalInput").ap()
    em_d = dt("em", [16, 16, 128], F32, kind="ExternalInput").ap()
    cm_d = dt("cm", [16, NB, 16], F32, kind="ExternalInput").ap()
    out_d = dt("out", [n_seq, 2048, D], F32, kind="ExternalOutput").ap()
    if debug_h1:
        dbg_d = dt("dbg", [n_seq, LP, D], F32, kind="ExternalOutput").ap()
        dbg2_d = dt("dbg2", [n_seq, 128, 8 * LP], F32, kind="ExternalOutput").ap()

    S = Sched()
    with ExitStack() as es:
        ARENA_F32 = 53000
        arena = es.enter_context(nc.sbuf_tensor("arena", [128, ARENA_F32], F32))
        psq = [es.enter_context(nc.psum_tensor(f"psq{i}", [128, 1024], F32)) for i in range(4)]
        ps = [psq[i // 2][:, (i % 2) * 512:(i % 2 + 1) * 512] for i in range(8)]
        psP = [q_.rearrange("p (h n) -> p h n", h=2) for q_ in psq]
        sems = {e: es.enter_context(nc.semaphore(f"s_{e}")) for e in ENGS}
        dma_sems = {}
        for e in ["sp", "pool"]:
            for k in range(S.n_dma_sems):
                dma_sems[(e, k)] = es.enter_context(nc.semaphore(f"d_{e}{k}"))
        P = [Buf(f"ps{i}", excl=True) for i in range(8)]

        class Arena:
            def __init__(self, start=0):
                self.off = start

            def f32(self, n):
                a = arena[:, self.off:self.off + n]
                self.off += n
                return a

            def bf16(self, n):
                assert n % 2 == 0
                a = arena[:, self.off:self.off + n // 2].bitcast(BF16)
                self.off += n // 2
                return a

        A = Arena(0)
        CB = A.bf16(11 * 128)
        cb_v = lambda i: CB[:, i * 128:(i + 1) * 128]
        IDENT, NEGU, NEGU0, NEGONES, NEGONES0, ONES, MSTRICT, MINCL, ONESA, ONESB, ZEROS = [cb_v(i) for i in range(11)]
        G = A.f32(3 * D)
        NG = A.f32(4)
        FIN = A.bf16(512)
        OG = A.bf16(8 * LP)
        OGv = OG.rearrange("p (c t) -> p c t", c=8)
        pers_end = A.off
        B_cb, B_g, B_ng, B_fin, B_og = Buf("cb"), Buf("g"), Buf("ng"), Buf("fin"), Buf("og")

        def mm(out, lhsT, rhs, start, stop, reads, writes):
            S.op("pe", lambda e: e.matmul(out, lhsT=lhsT, rhs=rhs, start=start, stop=stop),
                 reads=reads, writes=writes)

        def tr(out, in_, reads, writes):
            S.op("pe", lambda e: e.transpose(out, in_, IDENT), reads=list(reads) + [B_cb], writes=writes)

        def act(out, in_, func, reads, writes, scale=1.0, bias=0.0, accum_out=None, eng="act"):
            if accum_out is None:
                S.op("act", lambda e: e.activation(out=out, in_=in_, func=func, bias=bias, scale=scale),
                     reads=reads, writes=writes)
            else:
                S.op("act", lambda e: e.activation(out=out, in_=in_, func=func, bias=bias, scale=scale,
                                                   accum_out=accum_out), reads=reads, writes=writes)

        def tt(eng, out, in0, in1, op, reads, writes):
            S.op(eng, lambda e: e.tensor_tensor(out=out, in0=in0, in1=in1, op=op), reads=reads, writes=writes)

        def tsc(eng, out, in0, s1, op0, reads, writes, s2=None, op1=None):
            if op1 is None:
                S.op(eng, lambda e: e.tensor_scalar(out=out, in0=in0, scalar1=s1, scalar2=None, op0=op0),
                     reads=reads, writes=writes)
            else:
                S.op(eng, lambda e: e.tensor_scalar(out=out, in0=in0, scalar1=s1, scalar2=s2, op0=op0, op1=op1),
                     reads=reads, writes=writes)

        def stt(eng, out, in0, scalar, in1, op0, op1, reads, writes):
            S.op(eng, lambda e: e.scalar_tensor_tensor(out=out, in0=in0, scalar=scalar, in1=in1, op0=op0, op1=op1),
                 reads=reads, writes=writes)

        def cp(eng, out, in_, reads, writes):
            if eng == "act":
                S.op("act", lambda e: e.copy(out=out, in_=in_), reads=reads, writes=writes)
            else:
                S.op(eng, lambda e: e.tensor_copy(out=out, in_=in_), reads=reads, writes=writes)

        def memset(eng, ap, val, writes):
            S.op(eng, lambda e: e.memset(ap, val), writes=writes)

        def dma(eng, out, in_, reads, writes):
            S.op(eng, lambda e: e.dma_start(out=out, in_=in_), reads=reads, writes=writes, dma=True)

        def recip(out, in_, reads, writes):
            S.op("dve", lambda e: e.reciprocal(out=out, in_=in_), reads=reads, writes=writes)

        def wload(dst3, src2, c0, ncols, writes, reads=()):
            kc = src2.shape[0] // 128
            srcv = src2.rearrange("(kc p) n -> p kc n", p=128)
            dma("pool", dst3, srcv[:, :, c0:c0 + ncols], reads, writes)

        dma("pool", CB, cb_d, [], [B_cb])
        for i in range(3):
            dma("sp", G[:, i * D:(i + 1) * D], gains_d[i:i + 1, :].broadcast_to([128, D]), [], [B_g])
        for c in range(2):
            dma("sp", NG[:, c:c + 1], qng_d[c * 128:(c + 1) * 128].rearrange("(p o) -> p o", o=1), [], [B_ng])
        dma("sp", NG[:, 2:3], kvng_d.rearrange("(p o) -> p o", o=1), [], [B_ng])
        memset("dve", FIN, 1.0, [B_fin])

        A0 = Arena(pers_end)
        HNT = A0.bf16(8 * LP); HNTv = HNT.rearrange("p (c t) -> p c t", c=8); B_hnt = Buf("hnt")
        WS = [A0.bf16(8 * 512) for _ in range(2)]; B_ws = [Buf("ws0"), Buf("ws1")]
        WSv = [w.rearrange("p (c n) -> p c n", c=8) for w in WS]
        WM = A0.bf16(8 * 416); WMv = WM.rearrange("p (c n) -> p c n", c=8); B_wm = Buf("wm")
        WKRR = A0.bf16(8 * 32); WKRRv = WKRR.rearrange("p (c n) -> p c n", c=8); B_wkrr = Buf("wkrr")
        WUQ = A0.bf16(2 * 768); WUQv = WUQ.rearrange("p (c n) -> p c n", c=2); B_wuq = Buf("wuq")
        WUQR = A0.bf16(2 * 256); WUQRv = WUQR.rearrange("p (c n) -> p c n", c=2); B_wuqr = Buf("wuqr")
        WUKV = A0.bf16(1024); B_wukv = Buf("wukv")
        A0_KT_START = A0.off
        KT = [A0.bf16(LP) for _ in range(2)]; B_kt = [Buf("kt0"), Buf("kt1")]
        QT = [A0.bf16(LP) for _ in range(2)]; B_qt = [Buf("qt0"), Buf("qt1")]
        VP = A0.bf16(NB * 256); VPv = VP.rearrange("p (b v n) -> p b v n", b=NB, v=2); B_vp = Buf("vp")
        SG = A0.bf16(LP); B_sg = Buf("sg")
        CQN = A0.bf16(2 * LP); CQNv = CQN.rearrange("p (c t) -> p c t", c=2); B_cqn = Buf("cqn")
        CKVN = A0.bf16(LP); B_ckvn = Buf("ckvn")
        KR = A0.bf16(LP); B_kr = Buf("kr")
        ROPE = A0.f32(2 * LP); ROPEv = ROPE.rearrange("p (c t) -> p c t", c=2); B_rope = Buf("rope")
        def pairbuf(ap):
            return ap.rearrange("p (h n) -> p h n", h=2), [ap[:, 0:512], ap[:, 512:1024]]
        E32P = A0.f32(1024); e32v, E32 = pairbuf(E32P); B_e32p = Buf("e32p"); B_e32 = [B_e32p, B_e32p]
        SPBP = A0.bf16(1024); spbv, SPB = pairbuf(SPBP); B_spbp = Buf("spbp"); B_spb = [B_spbp, B_spbp]
        SPBP2 = A0.bf16(1024); spbv2, SPB2 = pairbuf(SPBP2); B_spbp2 = Buf("spbp2")
        C32P = A0.f32(1024); c32v, C32 = pairbuf(C32P); B_c32p = Buf("c32p")
        C16P = A0.bf16(1024); c16v, C16 = pairbuf(C16P); B_c16p = Buf("c16p")
        ATBP = [A0.bf16(1024) for _ in range(2)]
        atbv = [pairbuf(a_)[0] for a_ in ATBP]
        ATB = [pairbuf(ATBP[i // 2])[1][i % 2] for i in range(4)]
        B_atbp = [Buf("atbp0"), Buf("atbp1")]; B_atb = [B_atbp[i // 2] for i in range(4)]
        XS = A0.f32(D); B_xs = Buf("xs")
        T32 = [XS[:, 0:512], XS[:, 512:1024]]; B_t32 = [Buf("t32a"), Buf("t32b")]
        HN = A0.bf16(D); B_hn = Buf("hn")
        ST = A0.f32(8); B_st = Buf("st"); B_stb = Buf("stb")
        JK = HN; B_jk = B_hn
        l0_end = A0.off
        assert l0_end <= ARENA_F32, l0_end

        A1 = Arena(pers_end)
        W0O = A1.bf16(8 * D); W0Ov = W0O.rearrange("p (c n) -> p c n", c=8); B_w0o = Buf("w0o")
        W1I = A1.bf16(8 * OD_IN); W1Iv = W1I.rearrange("p (c n) -> p c n", c=8); B_w1i = Buf("w1i")
        W1O = A1.bf16(8 * D); W1Ov = W1O.rearrange("p (c n) -> p c n", c=8); B_w1o = Buf("w1o")
        SWE = A1.f32(2 * 16 * 128); SWEv = SWE.rearrange("p (a h r) -> p a h r", a=2, h=16); B_swe = Buf("swe")
        EM = A1.f32(16 * 128); EMv = EM.rearrange("p (h r) -> p h r", h=16); B_em = Buf("em")
        CM = A1.f32(NB * 16); CMv = CM.rearrange("p (n h) -> p n h", n=NB); B_cm = Buf("cm")
        ESK = A1.f32(8); B_esk = Buf("esk")
        KT1 = A1.bf16(LP); B_kt1 = [Buf(f"kt1l{i}") for i in range(NB)]
        VR = A1.bf16(3 * 4 * 128); VRv = VR.rearrange("p (b v n) -> p b v n", b=3, v=4); B_vr = [Buf("vr0"), Buf("vr1"), Buf("vr2")]
        VM = A1.bf16(4 * 128); VMv = VM.rearrange("p (v n) -> p v n", v=4); B_vm = Buf("vm")
        H1 = [A1.f32(D) for _ in range(2)]; B_h1 = [Buf("h1a"), Buf("h1b")]
        XS1 = A1.f32(D); B_xs1 = Buf("xs1")
        HN1 = A1.bf16(D); B_hn1 = Buf("hn1")
        HNT1 = [A1.bf16(8 * 128) for _ in range(2)]; HNT1v = [h.rearrange("p (c t) -> p c t", c=8) for h in HNT1]
        B_hnt1 = [Buf("hnt1a"), Buf("hnt1b")]
        QG = [[A1.bf16(8 * 128) for _ in range(2)] for _ in range(2)]
        QGv = [[q.rearrange("p (h t) -> p h t", h=8) for q in qq] for qq in QG]
        B_qg = [[Buf(f"qg{i}{j}") for j in range(2)] for i in range(2)]
        SG1 = [A1.bf16(8 * 128) for _ in range(2)]; SG1v = [x_.rearrange("p (c t) -> p c t", c=8) for x_ in SG1]
        B_sg1 = [Buf("sg1a"), Buf("sg1b")]
        OG1 = A1.bf16(8 * 128); OG1v = OG1.rearrange("p (c t) -> p c t", c=8); B_og1 = Buf("og1")
        EX = [A1.f32(1024) for _ in range(2)]; EXv = [x_.rearrange("p (h t) -> p h t", h=8) for x_ in EX]
        B_ex = [Buf("exa"), Buf("exb")]
        PB = [A1.bf16(1024) for _ in range(2)]; PBv = [x_.rearrange("p (h t) -> p h t", h=8) for x_ in PB]
        B_pb = [Buf("pba"), Buf("pbb")]
        B_exh = [[Buf(f"exh{i}{j}") for j in range(2)] for i in range(2)]
        B_pbh = [[Buf(f"pbh{i}{j}") for j in range(2)] for i in range(2)]
        R32 = A1.f32(512); B_r32 = Buf("r32")
        U32 = A1.f32(512); B_u32 = Buf("u32")
        ST1 = A1.f32(8); B_st1 = Buf("st1")
        ST2 = A1.f32(8); B_st2 = Buf("st2")
        JK1 = HN1; B_jk1 = B_hn1
        OUTB = A1.f32(D); B_outb = Buf("outb")
        assert A1.off <= ARENA_F32, A1.off

        PT = ps[7].bitcast(BF16)

        TCH = [(c * 512, min(512, LP - c * 512)) for c in range(5)]
        QCH = [(0, 1), (1, 5), (5, 9), (9, 13), (13, 17)]

        def rmsnorm_block(src32, B_src, gidx, dst_bf, B_dst, st, B_st_, jk, B_jk_):
            act(jk, src32, AF.Square, [B_src], [B_jk_, B_st_], accum_out=st[:, 0:1])
            act(st[:, 1:2], st[:, 0:1], AF.Ln, [B_st_], [B_st_], scale=1.0 / D, bias=EPS)
            act(st[:, 2:3], st[:, 1:2], AF.Exp, [B_st_], [B_st_], scale=-0.5)
            stt("dve", dst_bf, src32, st[:, 2:3], G[:, gidx * D:(gidx + 1) * D], ALU.mult, ALU.mult,
                [B_src, B_st_, B_g], [B_dst])

        def transpose_block(hn_bf, B_hn_, dst3, B_dst):
            for kc in range(8):
                tr(PT[:, kc * 128:(kc + 1) * 128], hn_bf[:, kc * 128:(kc + 1) * 128], [B_hn_], [P[7]])
            cp("dve", dst3, PT.rearrange("p (c t) -> p c t", c=8), [P[7]], [B_dst])

        def proj_fm(psb, Pb, wv, c0, m, t0, n, B_w, hnt_v, B_h):
            for kc in range(8):
                mm(psb[0:m, 0:n], wv[:, kc, c0:c0 + m], hnt_v[:, kc, t0:t0 + n], kc == 0, kc == 7,
                   [B_w, B_h], [Pb])

        for sq in range(n_seq):
            S.barrier()
            wload(WMv, w0in_d, 2048, 416, [B_wm])
            wload(WUQv, wuq_d, 0, 768, [B_wuq])
            dma("pool", WUKV, wukv_d, [], [B_wukv])
            dma("sp", ROPEv[0:32], rope_d.rearrange("c p t -> p c t"), [], [B_rope])
            for i in range(2):
                memset("pool", KT[i], 0.0, [B_kt[i]])
                memset("pool", QT[i], 0.0, [B_qt[i]])
            memset("pool", VP, 0.0, [B_vp])
            for kc in range(8):
                tsc("pool", WKRRv[:, kc, 0:16], WMv[:, kc, 400:416], -1.0, ALU.mult, [B_wm], [B_wkrr])
                cp("pool", WKRRv[:, kc, 16:32], WMv[:, kc, 384:400], [B_wm], [B_wkrr])
            for kc in range(2):
                src = WUQv[:, kc, :].rearrange("p (h d) -> p h d", h=8)
                dst = WUQRv[:, kc, :].rearrange("p (h d) -> p h d", h=8)
                tsc("pool", dst[:, :, 0:16], src[:, :, 80:96], -1.0, ALU.mult, [B_wuq], [B_wuqr])
                cp("pool", dst[:, :, 16:32], src[:, :, 64:80], [B_wuq], [B_wuqr])

            XSs = [XS, E32P]; B_xss = [B_xs, B_e32p]
            HNs = [HN, C32P.bitcast(BF16)[:, 0:D]]; B_hns = [B_hn, B_c32p]
            STs = [ST[:, 0:4], ST[:, 4:8]]; B_sts = [B_st, B_stb]
            for b in range(NB):
                i = b % 2
                xs_, Bx_ = XSs[i], B_xss[i]
                if b == 0:
                    memset("dve", xs_, 0.0, [Bx_])
                    dma("sp", xs_[NPAD:128, :], meta_d, [], [Bx_])
                else:
                    dma("sp", xs_, x_d[sq, (b - 1) * 128:b * 128, :], [], [Bx_])
                rmsnorm_block(xs_, Bx_, 0, HNs[i], B_hns[i], STs[i], B_sts[i], HNs[i], B_hns[i])
                pk = 6 + i
                ptv = ps[pk].bitcast(BF16)
                for kc in range(8):
                    tr(ptv[:, kc * 128:(kc + 1) * 128], HNs[i][:, kc * 128:(kc + 1) * 128], [B_hns[i]], [P[pk]])
                cp("dve", HNTv[:, :, b * 128:(b + 1) * 128], ptv.rearrange("p (c t) -> p c t", c=8), [P[pk]], [B_hnt])

            def attention_pair(kts, B_ks, qts, B_qs, pair_chunk, sb, escale):
                M2 = lambda m_: m_.unsqueeze(1).to_broadcast([128, 2, 128])
                for ci, (ba, bz) in enumerate(QCH):
                    q0, q1 = ba * 128, bz * 128
                    N = q1 - q0
                    if sb:
                        psO, PO = ps[4 + (ci % 2)], P[4 + (ci % 2)]
                    else:
                        ob = 4 if ci % 2 == 0 else 6
                        psO, PO = ps[ob], P[ob]
                        psD, PD = ps[ob + 1], P[ob + 1]
                    mm(psO[:, 0:N], ZEROS, FIN[:, 0:N], True, False, [B_cb, B_fin], [PO])
                    if not sb:
                        mm(psD[:, 0:N], ZEROS, FIN[:, 0:N], True, False, [B_cb, B_fin], [PD])
                    steps = list(range(bz - 1, -1, -1))

                    def geom(j):
                        tq0 = max(q0, j * 128)
                        return tq0, q1 - tq0, tq0 - q0, j >= ba

                    def qk(si):
                        j = steps[si]
                        tq0, n, c0, diag = geom(j)
                        for hh in range(2):
                            bank = hh if sb else 2 * (si % 2) + hh
                            mm(ps[bank][:, 0:n], kts[hh][:, j * 128:(j + 1) * 128], qts[hh][:, tq0:q1], True, True,
                               [B_ks[hh], B_qs[hh]], [P[bank]])

                    if sb:
                        SPV = [(spbv, SPB, B_spbp), (spbv2, SPB2, B_spbp2)]
                        CSB = [(2, 3), (6, 7)]

                        def stage1(si):
                            j = steps[si]
                            tq0, n, c0, diag = geom(j)
                            sv, _, Bs = SPV[si % 2]
                            act(e32v[:, :, 0:n], psP[0][:, :, 0:n], AF.Exp, [P[0], P[1]], [B_e32p])
                            act(sv[:, :, 0:n], e32v[:, :, 0:n], AF.Ln, [B_e32p], [Bs], bias=1.0)
                            if diag:
                                tt("pool", sv[:, :, 0:128], sv[:, :, 0:128], M2(MSTRICT), ALU.mult, [Bs, B_cb], [Bs])

                        def cumsum(si):
                            j = steps[si]
                            tq0, n, c0, diag = geom(j)
                            _, sp2, Bs = SPV[si % 2]
                            cb_ = CSB[si % 2]
                            first = si == 0
                            for hh in range(2):
                                bk = cb_[hh]
                                mm(ps[bk][:, 0:n], kts[hh][:, j * 128:(j + 1) * 128], qts[hh][:, tq0:q1], True, False,
                                   [B_ks[hh], B_qs[hh]], [P[bk]])
                            for hh in range(2):
                                bk = cb_[hh]
                                mm(ps[bk][:, 0:n], NEGU0 if j == 0 else NEGU, sp2[hh][:, 0:n], False, first,
                                   [B_cb, Bs], [P[bk]])
                                if not first:
                                    mm(ps[bk][:, 0:n], NEGONES, C16[hh][:, c0:c0 + n], False, True,
                                       [B_cb, B_c16p], [P[bk]])

                        def carry(si):
                            j = steps[si]
                            tq0, n, c0, diag = geom(j)
                            sv, _, Bs = SPV[si % 2]
                            if j == 0:
                                return
                            if si == 0:
                                memset("pool", C32P, 0.0, [B_c32p])
                            tt("dve", c32v[:, :, c0:c0 + n], c32v[:, :, c0:c0 + n], sv[:, :, 0:n], ALU.add,
                               [B_c32p, Bs], [B_c32p])
                            cp("dve", c16v[:, :, 0:N], c32v[:, :, 0:N], [B_c32p], [B_c16p])

                        def exp2(si):
                            j = steps[si]
                            tq0, n, c0, diag = geom(j)
                            cb_ = CSB[si % 2]
                            e = si % 2
                            act(atbv[e][:, :, 0:n], psP[cb_[0] // 2][:, :, 0:n], AF.Exp, [P[cb_[0]], P[cb_[1]]], [B_atbp[e]])
                            if diag:
                                tt("pool", atbv[e][:, :, 0:128], atbv[e][:, :, 0:128], M2(MSTRICT), ALU.mult,
                                   [B_atbp[e], B_cb], [B_atbp[e]])

                        def av(si):
                            j = steps[si]
                            tq0, n, c0, diag = geom(j)
                            e = si % 2
                            for hh in range(2):
                                mm(psO[:, c0:c0 + n], VPv[:, j, hh, :], ATB[2 * e + hh][:, 0:n], False, j == 0 and hh == 1,
                                   [B_vp, B_atbp[e]], [PO])

                        ns = len(steps)
                        qk(0)
                        stage1(0)
                        if ns > 1:
                            qk(1)
                        for si in range(ns):
                            cumsum(si)
                            carry(si)
                            if si + 1 < ns:
                                stage1(si + 1)
                            if si + 2 < ns:
                                qk(si + 2)
                            exp2(si)
                            av(si)
                    else:
                        qk(0)
                    for si, j in (enumerate(steps) if not sb else []):
                        tq0, n, c0, diag = geom(j)
                        first = si == 0
                        last = j == 0
                        if True:
                            e = si % 2
                            act(atbv[e][:, :, 0:n], psP[e][:, :, 0:n], AF.Exp, [P[2 * e], P[2 * e + 1]], [B_atbp[e]],
                                scale=escale)
                            if diag:
                                tt("pool", atbv[e][:, :, 0:128], atbv[e][:, :, 0:128], M2(MINCL), ALU.mult,
                                   [B_atbp[e], B_cb], [B_atbp[e]])
                            if not last:
                                qk(si + 1)
                            mm(psO[:, c0:c0 + n], VPv[:, j, 0, :], ATB[2 * e][:, 0:n], False, last,
                               [B_vp, B_atbp[e]], [PO])
                            mm(psD[:, c0:c0 + n], VPv[:, j, 1, :], ATB[2 * e + 1][:, 0:n], False, last,
                               [B_vp, B_atbp[e]], [PD])
                    if sb:
                        tt("dve", OGv[:, pair_chunk, q0:q1], psO[:, 0:N], SG[:, q0:q1], ALU.mult,
                           [PO, B_sg], [B_og])
                    else:
                        t32, B_t = T32[ci % 2], B_t32[ci % 2]
                        u32 = E32[ci % 2]
                        lo, hi = slice(0, 64), slice(64, 128)
                        act(t32[hi, 0:N], psO[hi, 0:N], AF.Ln, [PO], [B_t], bias=1e-30)
                        act(t32[lo, 0:N], psD[lo, 0:N], AF.Ln, [PD], [B_t], bias=1e-30)
                        act(t32[:, 0:N], t32[:, 0:N], AF.Exp, [B_t], [B_t], scale=-1.0)
                        cp("dve", u32[lo, 0:N], t32[hi, 0:N], [B_t], [B_e32p])
                        cp("dve", u32[hi, 0:N], t32[lo, 0:N], [B_t], [B_e32p])
                        tt("dve", u32[:, 0:N], u32[:, 0:N], SG[:, q0:q1], ALU.mult, [B_e32p, B_sg], [B_e32p])
                        tt("dve", OGv[lo, pair_chunk, q0:q1], psO[lo, 0:N], u32[lo, 0:N], ALU.mult,
                           [PO, B_e32p], [B_og])
                        tt("dve", OGv[hi, pair_chunk, q0:q1], psD[hi, 0:N], u32[hi, 0:N], ALU.mult,
                           [PD, B_e32p], [B_og])

            def attention_pair_sb(kts, B_ks, qts, B_qs, pair_chunk):
                M2 = lambda m_: m_.unsqueeze(1).to_broadcast([128, 2, 128])
                SPV = [(spbv, SPB, B_spbp), (spbv2, SPB2, B_spbp2)]
                CSB = [(2, 3), (6, 7)]

                class Chunk:
                    pass

                def mk_chunk(ci):
                    ba, bz = QCH[ci]
                    q0, q1 = ba * 128, bz * 128
                    N = q1 - q0
                    psO, PO = ps[4 + (ci % 2)], P[4 + (ci % 2)]
                    steps = list(range(bz - 1, -1, -1))
                    ns = len(steps)

                    def geom(si):
                        j = steps[si]
                        tq0 = max(q0, j * 128)
                        return j, tq0, q1 - tq0, tq0 - q0, j >= ba

                    def qk(si):
                        j, tq0, n, c0, diag = geom(si)
                        for hh in range(2):
                            mm(ps[hh][:, 0:n], kts[hh][:, j * 128:(j + 1) * 128], qts[hh][:, tq0:q1], True, True,
                               [B_ks[hh], B_qs[hh]], [P[hh]])

                    def stage1(si):
                        j, tq0, n, c0, diag = geom(si)
                        sv, _, Bs = SPV[si % 2]
                        act(e32v[:, :, 0:n], psP[0][:, :, 0:n], AF.Exp, [P[0], P[1]], [B_e32p])
                        act(sv[:, :, 0:n], e32v[:, :, 0:n], AF.Ln, [B_e32p], [Bs], bias=1.0)
                        if diag:
                            tt("pool", sv[:, :, 0:128], sv[:, :, 0:128], M2(MSTRICT), ALU.mult, [Bs, B_cb], [Bs])

                    def cumsum(si):
                        j, tq0, n, c0, diag = geom(si)
                        _, sp2, Bs = SPV[si % 2]
                        cb_ = CSB[si % 2]
                        first = si == 0
                        for hh in range(2):
                            bk = cb_[hh]
                            mm(ps[bk][:, 0:n], kts[hh][:, j * 128:(j + 1) * 128], qts[hh][:, tq0:q1], True, False,
                               [B_ks[hh], B_qs[hh]], [P[bk]])
                        for hh in range(2):
                            bk = cb_[hh]
                            mm(ps[bk][:, 0:n], NEGU0 if j == 0 else NEGU, sp2[hh][:, 0:n], False, first,
                               [B_cb, Bs], [P[bk]])
                            if not first:
                                mm(ps[bk][:, 0:n], NEGONES, C16[hh][:, c0:c0 + n], False, True,
                                   [B_cb, B_c16p], [P[bk]])

                    def carry(si):
                        j, tq0, n, c0, diag = geom(si)
                        sv, _, Bs = SPV[si % 2]
                        if j == 0:
                            return
                        if si == 0:
                            memset("pool", C32P, 0.0, [B_c32p])
                        tt("dve", c32v[:, :, c0:c0 + n], c32v[:, :, c0:c0 + n], sv[:, :, 0:n], ALU.add,
                           [B_c32p, Bs], [B_c32p])
                        cp("dve", c16v[:, :, 0:N], c32v[:, :, 0:N], [B_c32p], [B_c16p])

                    def exp2(si):
                        j, tq0, n, c0, diag = geom(si)
                        cb_ = CSB[si % 2]
                        e = si % 2
                        act(atbv[e][:, :, 0:n], psP[cb_[0] // 2][:, :, 0:n], AF.Exp, [P[cb_[0]], P[cb_[1]]], [B_atbp[e]])
                        if diag:
                            tt("pool", atbv[e][:, :, 0:128], atbv[e][:, :, 0:128], M2(MSTRICT), ALU.mult,
                               [B_atbp[e], B_cb], [B_atbp[e]])

                    def av(si):
                        j, tq0, n, c0, diag = geom(si)
                        e = si % 2
                        for hh in range(2):
                            mm(psO[:, c0:c0 + n], VPv[:, j, hh, :], ATB[2 * e + hh][:, 0:n], False, j == 0 and hh == 1,
                               [B_vp, B_atbp[e]], [PO])

                    def prologue():
                        mm(psO[:, 0:N], ZEROS, FIN[:, 0:N], True, False, [B_cb, B_fin], [PO])
                        qk(0)
                        stage1(0)
                        if ns > 1:
                            qk(1)

                    def finalize():
                        tt("dve", OGv[:, pair_chunk, q0:q1], psO[:, 0:N], SG[:, q0:q1], ALU.mult,
                           [PO, B_sg], [B_og])

                    c = Chunk()
                    c.ns, c.qk, c.stage1, c.cumsum, c.carry, c.exp2, c.av = ns, qk, stage1, cumsum, carry, exp2, av
                    c.prologue, c.finalize = prologue, finalize
                    return c

                chunks = [mk_chunk(ci) for ci in range(len(QCH))]
                chunks[0].prologue()
                chunks[0].cumsum(0)
                for ci, ch in enumerate(chunks):
                    nxt = chunks[ci + 1] if ci + 1 < len(chunks) else None
                    for si in range(ch.ns):
                        lastst = si == ch.ns - 1
                        ch.carry(si)
                        if si + 1 < ch.ns:
                            ch.stage1(si + 1)
                        if si + 2 < ch.ns:
                            ch.qk(si + 2)
                        if lastst and nxt is not None:
                            nxt.prologue()
                        ch.exp2(si)
                        if not lastst:
                            ch.cumsum(si + 1)
                        elif nxt is not None:
                            nxt.cumsum(0)
                        ch.av(si)
                    ch.finalize()

            bk_rot = [0]
            BK_ORDER = [[6, 7]]

            def nbk():
                bk_rot[0] = (bk_rot[0] + 1) % len(BK_ORDER[0])
                return BK_ORDER[0][bk_rot[0]]

            def build_v(lhs_fn, rhs_fn, nk, reads):
                for b0 in range(0, NB, 4):
                    nb4 = min(4, NB - b0)
                    k_ = nbk()
                    for i in range(nb4):
                        b = b0 + i
                        for kc in range(nk):
                            mm(ps[k_][:, i * 128:(i + 1) * 128], lhs_fn(kc, b), rhs_fn(kc), kc == 0, kc == nk - 1,
                               reads, [P[k_]])
                    pv = ps[k_][:, 0:nb4 * 128].rearrange("p (b n) -> p b n", b=nb4)
                    cp("dve", VPv[:, b0:b0 + nb4, 0, 0:64], pv[:, :, 0:64], [P[k_]], [B_vp])
                    cp("dve", VPv[:, b0:b0 + nb4, 1, 64:128], pv[:, :, 64:128], [P[k_]], [B_vp])

            BK_ORDER[0] = [6, 7, 0, 1, 2, 3]

            def pj(wv, c0, m, t0, n, B_w, src_v, B_src, nkc=8):
                k_ = nbk()
                for kc in range(nkc):
                    mm(ps[k_][0:m, 0:n], wv[:, kc, c0:c0 + m], src_v[:, kc, t0:t0 + n], kc == 0, kc == nkc - 1,
                       [B_w, B_src], [P[k_]])
                return ps[k_], P[k_]

            for p in range(4):
                ws, wsv, B_w = WS[p % 2], WSv[p % 2], B_ws[p % 2]
                wv4 = ws.rearrange("p (c f n) -> p c f n", c=8, f=4)
                srcv = w0in_d.rearrange("(kc p) n -> p kc n", p=128)
                for f in range(4):
                    dma("pool", wv4[:, :, f, :], srcv[:, :, f * 512 + p * 128:f * 512 + (p + 1) * 128], [], [B_w])
                for (t0, n) in TCH:
                    pa, Pa = pj(wsv, 128, 128, t0, n, B_w, HNTv, B_hnt)
                    cp("act", KT[0][:, t0:t0 + n], pa[:, 0:n], [Pa], [B_kt[0]])
                    pa, Pa = pj(wsv, 0, 128, t0, n, B_w, HNTv, B_hnt)
                    tsc("dve", QT[0][0:64, t0:t0 + n], pa[0:64, 0:n], 0.125, ALU.mult, [Pa], [B_qt[0]])
                    tsc("dve", QT[1][64:128, t0:t0 + n], pa[64:128, 0:n], 0.125, ALU.mult, [Pa], [B_qt[1]])
                    pa, Pa = pj(wsv, 384, 128, t0, n, B_w, HNTv, B_hnt)
                    act(SG[:, t0:t0 + n], pa[:, 0:n], AF.Silu, [Pa], [B_sg])
                build_v(lambda kc, b: HNTv[:, kc, b * 128:(b + 1) * 128], lambda kc, wsv=wsv: wsv[:, kc, 256:384], 8,
                        [B_hnt, B_w])
                attention_pair_sb([KT[0], KT[0]], [B_kt[0], B_kt[0]], QT, B_qt, p)

            memset("pool", VPv[:, :, 0, 64:128], 1.0, [B_vp])
            memset("pool", VPv[:, :, 1, 0:64], 1.0, [B_vp])
            BK_ORDER[0] = [6, 7, 0, 1, 2, 3]
            for i in range(2):
                memset("pool", KT[i], 0.0, [B_kt[i]])
                memset("pool", QT[i], 0.0, [B_qt[i]])
                memset("pool", KT[i][96:97, 0:NPAD], -30000.0, [B_kt[i]])
                memset("pool", QT[i][96:97, :], 1.0, [B_qt[i]])
            CKVNv1 = CKVN.rearrange("p (c t) -> p c t", c=1)
            for (t0, n) in TCH:
                for cc in range(2):
                    pa, Pa = pj(WMv, cc * 128, 128, t0, n, B_wm, HNTv, B_hnt)
                    cp("dve", T32[cc][:, 0:n], pa[:, 0:n], [Pa], [B_t32[cc]])
                    act(ATB[cc][:, 0:n], pa[:, 0:n], AF.Square, [Pa], [B_atb[cc]])
                k_ = nbk()
                for cc in range(2):
                    mm(ps[k_][:, 0:n], ONES, ATB[cc][:, 0:n], cc == 0, cc == 1, [B_cb, B_atb[cc]], [P[k_]])
                act(E32[0][:, 0:n], ps[k_][:, 0:n], AF.Ln, [P[k_]], [B_e32[0]], scale=1.0 / 256, bias=EPS)
                act(E32[0][:, 0:n], E32[0][:, 0:n], AF.Exp, [B_e32[0]], [B_e32[0]], scale=-0.5)
                for cc in range(2):
                    stt("dve", CQNv[:, cc, t0:t0 + n], T32[cc][:, 0:n], NG[:, cc:cc + 1], E32[0][:, 0:n],
                        ALU.mult, ALU.mult, [B_t32[cc], B_ng, B_e32[0]], [B_cqn])
                pa, Pa = pj(WMv, 256, 128, t0, n, B_wm, HNTv, B_hnt)
                cp("dve", T32[0][:, 0:n], pa[:, 0:n], [Pa], [B_t32[0]])
                act(ATB[2][:, 0:n], pa[:, 0:n], AF.Square, [Pa], [B_atb[2]])
                k_ = nbk()
                mm(ps[k_][:, 0:n], ONES, ATB[2][:, 0:n], True, True, [B_cb, B_atb[2]], [P[k_]])
                act(E32[1][:, 0:n], ps[k_][:, 0:n], AF.Ln, [P[k_]], [B_e32[1]], scale=1.0 / 128, bias=EPS)
                act(E32[1][:, 0:n], E32[1][:, 0:n], AF.Exp, [B_e32[1]], [B_e32[1]], scale=-0.5)
                stt("dve", CKVN[:, t0:t0 + n], T32[0][:, 0:n], NG[:, 2:3], E32[1][:, 0:n],
                    ALU.mult, ALU.mult, [B_t32[0], B_ng, B_e32[1]], [B_ckvn])
                pa, Pa = pj(WMv, 384, 32, t0, n, B_wm, HNTv, B_hnt)
                pb_, Pb_ = pj(WKRRv, 0, 32, t0, n, B_wkrr, HNTv, B_hnt)
                tt("dve", T32[0][0:32, 0:n], pa[0:32, 0:n], ROPEv[0:32, 0, t0:t0 + n], ALU.mult,
                   [Pa, B_rope], [B_t32[0]])
                tt("dve", T32[1][0:32, 0:n], pb_[0:32, 0:n], ROPEv[0:32, 1, t0:t0 + n], ALU.mult,
                   [Pb_, B_rope], [B_t32[1]])
                tt("dve", KR[0:32, t0:t0 + n], T32[0][0:32, 0:n], T32[1][0:32, 0:n], ALU.add,
                   [B_t32[0], B_t32[1]], [B_kr])

            WUKVv = WUKV.rearrange("p (h a d) -> p h a d", h=8, a=2)
            for p in range(4):
                ws, wsv, B_w = WS[p % 2], WSv[p % 2], B_ws[p % 2]
                wload(wsv[:, :, 0:128], w0in_d, 2464 + p * 128, 128, [B_w])
                for (t0, n) in TCH:
                    pa, Pa = pj(wsv, 0, 128, t0, n, B_w, HNTv, B_hnt)
                    act(SG[:, t0:t0 + n], pa[:, 0:n], AF.Silu, [Pa], [B_sg])
                    for hh in range(2):
                        h = 2 * p + hh
                        k_ = nbk()
                        mm(ps[k_][0:64, 0:n], WUKVv[:, h, 0, :], CKVN[:, t0:t0 + n], True, True,
                           [B_wukv, B_ckvn], [P[k_]])
                        cp("act", KT[hh][0:64, t0:t0 + n], ps[k_][0:64, 0:n], [P[k_]], [B_kt[hh]])
                        cp("pool", KT[hh][64:96, t0:t0 + n], KR[0:32, t0:t0 + n], [B_kr], [B_kt[hh]])
                        pa, Pa = pj(WUQv, h * 96, 64, t0, n, B_wuq, CQNv, B_cqn, nkc=2)
                        cp("act", QT[hh][0:64, t0:t0 + n], pa[0:64, 0:n], [Pa], [B_qt[hh]])
                        px, Px = pj(WUQv, h * 96 + 64, 32, t0, n, B_wuq, CQNv, B_cqn, nkc=2)
                        pr_, Pr_ = pj(WUQRv, h * 32, 32, t0, n, B_wuqr, CQNv, B_cqn, nkc=2)
                        tt("dve", T32[0][0:32, 0:n], px[0:32, 0:n], ROPEv[0:32, 0, t0:t0 + n], ALU.mult,
                           [Px, B_rope], [B_t32[0]])
                        tt("dve", T32[1][0:32, 0:n], pr_[0:32, 0:n], ROPEv[0:32, 1, t0:t0 + n], ALU.mult,
                           [Pr_, B_rope], [B_t32[1]])
                        tt("dve", QT[hh][64:96, t0:t0 + n], T32[0][0:32, 0:n], T32[1][0:32, 0:n], ALU.add,
                           [B_t32[0], B_t32[1]], [B_qt[hh]])
                build_v(lambda kc, b: CKVN[:, b * 128:(b + 1) * 128], lambda kc, p=p: WUKVv[:, 2 * p:2 * p + 2, 1, :], 1,
                        [B_ckvn, B_wukv])
                if p == 3:
                    assert pers_end + 4096 + 9216 <= A0_KT_START - (128 + 768 + 256 + 512)
                    dead = [B_hnt, B_ws[0], B_ws[1], B_wm]
                    for c in range(0, 1024, 512):
                        wload(W0Ov[:, :, c:c + 512], w0out_d, c, 512, [B_w0o] + dead)
                    for c in range(0, OD_IN, 576):
                        wload(W1Iv[:, :, c:c + 576], w1in_d, c, 576, [B_w1i] + dead)
                attention_pair(KT, B_kt, QT, B_qt, 4 + p, False, 96.0 ** -0.5)

            if debug_h1:
                for c in range(8):
                    dma("pool", dbg2_d[sq, :, c * LP:(c + 1) * LP], OG[:, c * LP:(c + 1) * LP], [B_og], [])
            S.barrier()
            for c in range(0, 1024, 512):
                wload(W1Ov[:, :, c:c + 512], w1out_d, c, 512, [B_w1o])
            dma("sp", SWEv, swae_d.rearrange("a p h r -> p a h r"), [], [B_swe])
            dma("sp", EMv[0:16], em_d, [], [B_em])
            dma("sp", CMv[0:16], cm_d, [], [B_cm])
            sk2 = sinks_d.rearrange("(p two) -> two p", two=2)
            S.op("sp", lambda e: e.dma_start(out=ESK[0:64, :], in_=sk2[0:1, :].broadcast_to([64, 8]),
                                             allow_slow_non_contiguous=True), writes=[B_esk], dma=True)
            S.op("sp", lambda e: e.dma_start(out=ESK[64:128, :], in_=sk2[1:2, :].broadcast_to([64, 8]),
                                             allow_slow_non_contiguous=True), writes=[B_esk], dma=True)
            act(ESK, ESK, AF.Exp, [B_esk], [B_esk])
            for par in range(2):
                for g in range(2):
                    memset("pool", QG[par][g], 0.0, [B_qg[par][g]])
            memset("pool", VM, 0.0, [B_vm])
            memset("pool", VR, 0.0, B_vr)

            rot = [0]

            def pbank():
                rot[0] ^= 1
                return 6 + rot[0]

            def front(b):
                par = b % 2
                h1, Bh1 = H1[par], B_h1[par]
                hv, Bhv = HNT1v[par], B_hnt1[par]
                if b == 0:
                    memset("dve", XS1, 0.0, [B_xs1])
                    dma("sp", XS1[NPAD:128, :], meta_d, [], [B_xs1])
                else:
                    dma("sp", XS1, x_d[sq, (b - 1) * 128:b * 128, :], [], [B_xs1])
                for hf in range(2):
                    for kc in range(8):
                        mm(ps[4 + hf][:, :], OGv[:, kc, b * 128:(b + 1) * 128], W0Ov[:, kc, hf * 512:(hf + 1) * 512],
                           kc == 0, kc == 7, [B_og, B_w0o], [P[4 + hf]])
                    tt("dve", h1[:, hf * 512:(hf + 1) * 512], ps[4 + hf][:, :], XS1[:, hf * 512:(hf + 1) * 512],
                       ALU.add, [P[4 + hf], B_xs1], [Bh1])
                if debug_h1:
                    dma("sp", dbg_d[sq, b * 128:(b + 1) * 128, :], h1, [Bh1], [])
                yield
                rmsnorm_block(h1, Bh1, 1, HN1, B_hn1, ST1, B_st1, JK1, B_jk1)
                pk = pbank()
                ptv = ps[pk].bitcast(BF16)
                for kc in range(8):
                    tr(ptv[:, kc * 128:(kc + 1) * 128], HN1[:, kc * 128:(kc + 1) * 128], [B_hn1], [P[pk]])
                cp("dve", hv, ptv.rearrange("p (c t) -> p c t", c=8), [P[pk]], [Bhv])
                yield
                pk = pbank()
                for kc in range(8):
                    mm(ps[pk][:, 0:128], W1Iv[:, kc, 1024:1152], hv[:, kc, :], kc == 0, kc == 7,
                       [B_w1i, Bhv], [P[pk]])
                cp("act", KT1[:, b * 128:(b + 1) * 128], ps[pk][:, 0:128], [P[pk]], [B_kt1[b]])
                slot = b % 3
                pk = pbank()
                if b == 0:
                    for kc in range(8):
                        mm(ps[pk][0:16, 0:128], hv[:, kc, NPAD:128], W1Iv[:, kc, 1152:1280], kc == 0, kc == 7,
                           [Bhv, B_w1i], [P[pk]])
                    for kh in range(2):
                        cp("dve", VMv[0:16, 2 * kh, 0:64], ps[pk][0:16, kh * 64:(kh + 1) * 64], [P[pk]], [B_vm])
                        cp("dve", VMv[0:16, 2 * kh + 1, 64:128], ps[pk][0:16, kh * 64:(kh + 1) * 64], [P[pk]], [B_vm])
                    return
                for kc in range(8):
                    mm(ps[pk][:, 0:128], hv[:, kc, :], W1Iv[:, kc, 1152:1280], kc == 0, kc == 7,
                       [Bhv, B_w1i], [P[pk]])
                for kh in range(2):
                    cp("dve", VRv[:, slot, 2 * kh, 0:64], ps[pk][:, kh * 64:(kh + 1) * 64], [P[pk]], [B_vr[slot]])
                    cp("dve", VRv[:, slot, 2 * kh + 1, 64:128], ps[pk][:, kh * 64:(kh + 1) * 64], [P[pk]], [B_vr[slot]])
                yield
                for q4 in range(2):
                    pk = pbank()
                    for i in range(4):
                        pr = q4 * 4 + i
                        for kc in range(8):
                            mm(ps[pk][:, i * 128:(i + 1) * 128], W1Iv[:, kc, pr * 128:(pr + 1) * 128], hv[:, kc, :],
                               kc == 0, kc == 7, [B_w1i, Bhv], [P[pk]])
                    g = q4
                    gs = slice(g * 64, g * 64 + 64)
                    pv = ps[pk].rearrange("p (i t) -> p i t", i=4)
                    qv = QGv[par][g][gs].rearrange("p (i two) t -> p two i t", two=2)
                    cp("act" if g == 0 else "dve", qv[:, 0], pv[0:64], [P[pk]], [B_qg[par][g]])
                    cp("dve" if g == 0 else "act", qv[:, 1], pv[64:128], [P[pk]], [B_qg[par][g]])
                    yield
                for c4 in range(2):
                    pk = pbank()
                    for i in range(4):
                        cc = c4 * 4 + i
                        for kc in range(8):
                            mm(ps[pk][:, i * 128:(i + 1) * 128], W1Iv[:, kc, 1280 + cc * 128:1280 + (cc + 1) * 128],
                               hv[:, kc, :], kc == 0, kc == 7, [B_w1i, Bhv], [P[pk]])
                    act(SG1v[par][:, c4 * 4:(c4 + 1) * 4, :], ps[pk].rearrange("p (i t) -> p i t", i=4), AF.Silu,
                        [P[pk]], [B_sg1[par]])
                    yield

            def back_parts(b):
                par = b % 2
                slot = b % 3
                h1, Bh1 = H1[par], B_h1[par]
                tiles = []
                for g in range(2):
                    if b >= 2:
                        tiles.append((g, "prev", KT1[:, (b - 1) * 128:b * 128], 128, (b - 1) % 3, B_kt1[b - 1]))
                    tiles.append((g, "cur", KT1[:, b * 128:(b + 1) * 128], 128, slot, B_kt1[b]))
                    tiles.append((g, "meta", KT1[:, NPAD:128], 16, None, B_kt1[0]))
                nt = len(tiles)

                def qk(ti):
                    g, kind, kk, nk, vs, Bk = tiles[ti]
                    for hf in range(2):
                        mm(ps[hf][0:nk, :], kk, QGv[par][g][:, hf * 4:(hf + 1) * 4, :], True, True,
                           [Bk, B_qg[par][g]], [P[hf]])

                def soft(ti):
                    g, kind, kk, nk, vs, Bk = tiles[ti]
                    e = ti % 2
                    exv, pbv = EXv[e], PBv[e]
                    for hf in range(2):
                        act(EX[e][0:nk, hf * 512:(hf + 1) * 512], ps[hf][0:nk, :], AF.Exp, [P[hf]], [B_exh[e][hf]],
                            scale=0.125)
                        if kind != "meta":
                            a = 0 if kind == "prev" else 1
                            hs = slice(hf * 4, (hf + 1) * 4)
                            tt("pool" if hf == 0 else "dve", pbv[:, hs, :], exv[:, hs, :],
                               SWEv[:, a, g * 8 + hf * 4:g * 8 + (hf + 1) * 4, :], ALU.mult,
                               [B_exh[e][hf], B_swe], [B_pbh[e][hf]])
                    if kind == "meta":
                        tt("dve", exv[0:16], exv[0:16], EMv[0:16, g * 8:(g + 1) * 8, :], ALU.mult,
                           B_exh[e] + [B_em], B_exh[e])
                        tt("dve", pbv[0:16], exv[0:16],
                           CMv[0:16, b, g * 8:(g + 1) * 8].unsqueeze(2).to_broadcast([16, 8, 128]), ALU.mult,
                           B_exh[e] + [B_cm], B_pbh[e])

                def prologue():
                    qk(0)
                    soft(0)
                    if nt > 1:
                        qk(1)

                def tile_gen():
                    for ti, (g, kind, kk, nk, vs, Bk) in enumerate(tiles):
                        e = ti % 2
                        pbv, B_p = PBv[e], B_pbh[e]
                        if kind == "meta":
                            va, vb2 = VMv[0:16, 2 * g, :], VMv[0:16, 2 * g + 1, :]
                            B_v = B_vm
                        else:
                            va, vb2 = VRv[:, vs, 2 * g, :], VRv[:, vs, 2 * g + 1, :]
                            B_v = B_vr[vs]
                        pe_ = pbv[0:nk].rearrange("p (q two) t -> p two q t", two=2)
                        gfirst = kind == ("prev" if b >= 2 else "cur")
                        glast = kind == "meta"
                        mm(ps[2][:, :], va, pe_[:, 0], gfirst, False, [B_v] + B_p, [P[2]])
                        mm(ps[2][:, :], vb2, pe_[:, 1], False, glast, [B_v] + B_p, [P[2]])
                        mm(ps[3][:, :], ONESA[0:nk, :], pe_[:, 0], gfirst, False, [B_cb] + B_p, [P[3]])
                        mm(ps[3][:, :], ONESB[0:nk, :], pe_[:, 1], False, glast, [B_cb] + B_p, [P[3]])
                        if ti + 1 < nt:
                            soft(ti + 1)
                        if ti + 2 < nt:
                            qk(ti + 2)
                        if glast:
                            R3 = R32.rearrange("p (q t) -> p q t", q=4)
                            U3 = U32.rearrange("p (q t) -> p q t", q=4)
                            tt("dve", R3, ps[3].rearrange("p (q t) -> p q t", q=4),
                               ESK[:, g * 4:(g + 1) * 4].unsqueeze(2).to_broadcast([128, 4, 128]), ALU.add,
                               [P[3], B_esk], [B_r32])
                            act(R32, R32, AF.Ln, [B_r32], [B_r32])
                            act(R32, R32, AF.Exp, [B_r32], [B_r32], scale=-1.0)
                            tt("pool", U3, R3, SG1v[par][:, g * 4:(g + 1) * 4, :], ALU.mult, [B_r32, B_sg1[par]], [B_u32])
                            tt("dve", OG1v[:, g * 4:(g + 1) * 4, :], ps[2].rearrange("p (q t) -> p q t", q=4), U3,
                               ALU.mult, [P[2], B_u32], [B_og1])
                        yield

                def tail():
                    for hf in range(2):
                        for kc in range(8):
                            mm(ps[4 + hf][:, :], OG1v[:, kc, :], W1Ov[:, kc, hf * 512:(hf + 1) * 512],
                               kc == 0, kc == 7, [B_og1, B_w1o], [P[4 + hf]])
                        tt("dve", h1[:, hf * 512:(hf + 1) * 512], ps[4 + hf][:, :], h1[:, hf * 512:(hf + 1) * 512],
                           ALU.add, [P[4 + hf], Bh1], [Bh1])
                    rmsnorm_block(h1, Bh1, 2, OUTB, B_outb, ST2, B_st2, JK1, B_jk1)
                    dma("sp", out_d[sq, (b - 1) * 128:b * 128, :], OUTB, [B_outb], [])

                return prologue, tile_gen, tail

            def drain(gen):
                for _ in gen:
                    pass

            drain(front(0))
            drain(front(1))
            parts = back_parts(1)
            parts[0]()
            for b in range(1, NB):
                prologue, tile_gen, tail = parts
                alive = [tile_gen()]
                if b + 1 < NB:
                    alive.append(front(b + 1))
                while alive:
                    for gq in list(alive):
                        try:
                            next(gq)
                        except StopIteration:
                            alive.remove(gq)
                if b + 1 < NB:
                    parts = back_parts(b + 1)
                    parts[0]()
                tail()

        S.finalize()
        with nc.Block() as block:
            @block.tensor
            def _(e):
                S.emit_engine("pe", e, sems, dma_sems)

            @block.scalar
            def _(e):
                S.emit_engine("act", e, sems, dma_sems)

            @block.vector
            def _(e):
                S.emit_engine("dve", e, sems, dma_sems)

            @block.gpsimd
            def _(e):
                S.emit_engine("pool", e, sems, dma_sems)

            @block.sync
            def _(e):
                S.emit_engine("sp", e, sems, dma_sems, final_wait=True)
    return nc


_NC_CACHE = {}


def _common_inputs(meta, norm_g, final_g, ev_w_in, ev_q_norm_g, ev_kv_norm_g, ev_w_uq, ev_w_ukv,
                   ev_w_out, od_w_in, od_sinks, od_w_out):
    cb, rope, swae, em, cm = _const_tables()
    f = lambda a: np.ascontiguousarray(np.asarray(a, dtype=np.float32))
    return {
        "meta": f(meta),
        "gains": f(np.concatenate([np.asarray(norm_g), np.asarray(final_g)[None, :]], 0)),
        "w0in": f(ev_w_in[0]), "qng": f(ev_q_norm_g[0]), "kvng": f(ev_kv_norm_g[0]),
        "wuq": f(ev_w_uq[0]), "wukv": f(ev_w_ukv[0]), "w0out": f(ev_w_out[0]),
        "w1in": f(od_w_in[0]), "sinks": f(od_sinks[0]), "w1out": f(od_w_out[0]),
        "cb": cb, "rope": f(rope), "swae": f(swae), "em": f(em), "cm": f(cm),
    }


def kernel(x, meta, norm_g, final_g, ev_w_in, ev_q_norm_g, ev_kv_norm_g, ev_w_uq, ev_w_ukv,
           ev_w_out, od_w_in, od_sinks, od_w_out):
    n = 8
    x = np.asarray(x, dtype=np.float32)
    common = _common_inputs(meta, norm_g, final_g, ev_w_in, ev_q_norm_g, ev_kv_norm_g, ev_w_uq,
                            ev_w_ukv, ev_w_out, od_w_in, od_sinks, od_w_out)
    if "nc" not in _NC_CACHE:
        _NC_CACHE["nc"] = build_nc()
    nc = _NC_CACHE["nc"]
    in_maps = []
    for c in range(n):
        m = dict(common)
        m["x"] = np.ascontiguousarray(x[c * SEQ_PER_CORE:(c + 1) * SEQ_PER_CORE])
        in_maps.append(m)
    res = run_bass_kernel_spmd(nc, in_maps, core_ids=list(range(n)))
    return np.concatenate([r["out"] for r in res.results], axis=0)
```

```python
import numpy as np
from contextlib import ExitStack
import concourse.bass as bass
import concourse.mybir as mybir
from concourse.bass_utils import run_bass_kernel_spmd

F32 = mybir.dt.float32
BF16 = mybir.dt.bfloat16
AF = mybir.ActivationFunctionType
ALU = mybir.AluOpType

ENGS = ["pe", "act", "dve", "pool", "sp"]

D = 1024
LP = 2176
NB = 17
NPAD = 112
EPS = 1e-6
SEQ_PER_CORE = 2
EV_IN = 2976
OD_IN = 2304


class Buf:
    __slots__ = ("name", "last_w", "readers", "excl")

    def __init__(self, name, excl=False):
        self.name = name
        self.last_w = None
        self.readers = []
        self.excl = excl


class Op:
    __slots__ = ("eng", "fn", "deps", "signal", "is_dma", "dsem", "dval", "cnt", "prev_dma")

    def __init__(self, eng, fn, is_dma):
        self.eng = eng
        self.fn = fn
        self.deps = []
        self.signal = False
        self.is_dma = is_dma
        self.dsem = None
        self.dval = 0
        self.cnt = 0
        self.prev_dma = None


class Sched:
    def __init__(self, n_dma_sems=8):
        self.ops = {e: [] for e in ENGS}
        self.n_dma_sems = n_dma_sems
        self.dma_rr = {e: 0 for e in ENGS}
        self.dma_cnt = {}
        self.dma_last = {}

    def op(self, eng, fn, reads=(), writes=(), dma=False):
        o = Op(eng, fn, dma)
        deps = {}
        for b in reads:
            if b.last_w is not None:
                deps[id(b.last_w)] = b.last_w
            if b.excl:
                for r in b.readers:
                    if r.eng != eng:
                        deps[id(r)] = r
        for b in writes:
            if b.last_w is not None:
                deps[id(b.last_w)] = b.last_w
            for r in b.readers:
                deps[id(r)] = r
        final = []
        for d in deps.values():
            if d is o:
                continue
            if d.eng == "pe" and eng == "pe" and (not d.is_dma) and (not dma):
                continue
            final.append(d)
            d.signal = True
        o.deps = final
        for b in reads:
            if not dma:
                b.readers = [r for r in b.readers if r.is_dma or r.eng != eng]
            b.readers.append(o)
        for b in writes:
            b.last_w = o
            b.readers = []
        if dma:
            k = self.dma_rr[eng]
            self.dma_rr[eng] = (k + 1) % self.n_dma_sems
            key = (eng, k)
            self.dma_cnt[key] = self.dma_cnt.get(key, 0) + 1
            o.dsem = key
            o.dval = 16 * self.dma_cnt[key]
            o.prev_dma = self.dma_last.get(key)
            self.dma_last[key] = o
        self.ops[eng].append(o)
        return o

    def barrier(self):
        lasts = []
        for e in ENGS:
            for o in reversed(self.ops[e]):
                if (not o.is_dma) and o.fn is not None:
                    lasts.append(o)
                    break
        dmas = list(self.dma_last.values())
        for e in ENGS:
            m = Op(e, None, False)
            m.deps = [d for d in lasts if d.eng != e] + dmas
            for d in m.deps:
                d.signal = True
            self.ops[e].append(m)

    def finalize(self):
        for e in ENGS:
            c = 0
            for o in self.ops[e]:
                if o.is_dma or o.fn is None:
                    continue
                if o.signal:
                    c += 1
                    o.cnt = c

    def emit_engine(self, eng_name, e, sems, dma_sems, final_wait=False):
        waited = {}

        def wait(key, sem, val):
            if val <= 0:
                return
            if waited.get(key, 0) < val:
                e.wait_ge(sem, val)
                waited[key] = val

        for o in self.ops[eng_name]:
            for d in o.deps:
                if d.is_dma:
                    wait(d.dsem, dma_sems[d.dsem], d.dval)
                else:
                    wait(d.eng, sems[d.eng], d.cnt)
            if o.is_dma and o.prev_dma is not None:
                wait(o.dsem, dma_sems[o.dsem], o.prev_dma.dval)
            if o.fn is None:
                continue
            ins = o.fn(e)
            if o.is_dma:
                ins.then_inc(dma_sems[o.dsem], 16)
            elif o.signal:
                ins.then_inc(sems[eng_name], 1)
        if final_wait:
            for key, o in self.dma_last.items():
                wait(key, dma_sems[key], o.dval)


def _const_tables():
    r = np.arange(128)
    s = r[:, None]
    t = r[None, :]
    ident = (s == t).astype(np.float32)
    negU = -(s >= t).astype(np.float32)
    negU0 = negU * (s >= NPAD)
    negOnes = -np.ones((128, 128), np.float32)
    negOnes0 = negOnes * (s >= NPAD)
    ones = np.ones((128, 128), np.float32)
    mstrict = (s < t).astype(np.float32)
    mincl = (s <= t).astype(np.float32)
    onesA = np.concatenate([np.ones((128, 64)), np.zeros((128, 64))], 1).astype(np.float32)
    onesB = np.concatenate([np.zeros((128, 64)), np.ones((128, 64))], 1).astype(np.float32)
    zeros = np.zeros((128, 128), np.float32)
    cb = np.concatenate([ident, negU, negU0, negOnes, negOnes0, ones, mstrict, mincl,
                         onesA, onesB, zeros], axis=1)
    half = 16
    inv = 10000.0 ** (-np.arange(half, dtype=np.float64) / half)
    pos = (np.arange(LP) - NPAD).astype(np.float64)
    ang = inv[:, None] * pos[None, :]
    cos = np.concatenate([np.cos(ang), np.cos(ang)], 0).astype(np.float32)
    sin = np.concatenate([np.sin(ang), np.sin(ang)], 0).astype(np.float32)
    rope = np.stack([cos, sin], 0)
    H = 16
    slopes = 2.0 ** (-8.0 * (np.arange(H, dtype=np.float64) + 1.0) / H)
    sl = slopes[None, :, None]
    dprev = (128 + r[None, None, :] - r[:, None, None]).astype(np.float64)
    eprev = np.where(dprev < 128, np.exp(-sl * dprev), 0.0)
    dcur = (r[None, None, :] - r[:, None, None]).astype(np.float64)
    ecur = np.where(dcur >= 0, np.exp(-sl * np.maximum(dcur, 0)), 0.0)
    m = np.arange(16)
    dm = (16 + r[None, None, :] - m[:, None, None]).astype(np.float64)
    em = np.exp(-sl * dm)
    n = np.arange(NB)
    cm = np.exp(-slopes[None, None, :] * 128.0 * np.maximum(n - 1, 0)[None, :, None])
    cm = np.broadcast_to(cm, (16, NB, H))
    swa_e = np.stack([eprev, ecur], 0).astype(np.float32)
    return (cb.astype(np.float32), rope, swa_e, em.astype(np.float32),
            np.ascontiguousarray(cm).astype(np.float32))


def build_nc(debug_h1=False, n_seq=SEQ_PER_CORE):
    nc = bass.Bass("TRN2", target_bir_lowering=False)
    dt = nc.dram_tensor
    x_d = dt("x", [n_seq, 2048, D], F32, kind="ExternalInput").ap()
    meta_d = dt("meta", [16, D], F32, kind="ExternalInput").ap()
    gains_d = dt("gains", [3, D], F32, kind="ExternalInput").ap()
    w0in_d = dt("w0in", [D, EV_IN], F32, kind="ExternalInput").ap()
    qng_d = dt("qng", [256], F32, kind="ExternalInput").ap()
    kvng_d = dt("kvng", [128], F32, kind="ExternalInput").ap()
    wuq_d = dt("wuq", [256, 768], F32, kind="ExternalInput").ap()
    wukv_d = dt("wukv", [128, 1024], F32, kind="ExternalInput").ap()
    w0out_d = dt("w0out", [D, D], F32, kind="ExternalInput").ap()
    w1in_d = dt("w1in", [D, OD_IN], F32, kind="ExternalInput").ap()
    sinks_d = dt("sinks", [16], F32, kind="ExternalInput").ap()
    w1out_d = dt("w1out", [D, D], F32, kind="ExternalInput").ap()
    cb_d = dt("cb", [128, 11 * 128], F32, kind="ExternalInput").ap()
    rope_d = dt("rope", [2, 32, LP], F32, kind="ExternalInput").ap()
    swae_d = dt("swae", [2, 128, 16, 128], F32, kind="ExternalInput").ap()
    em_d = dt("em", [16, 16, 128], F32, kind="ExternalInput").ap()
    cm_d = dt("cm", [16, NB, 16], F32, kind="ExternalInput").ap()
    out_d = dt("out", [n_seq, 2048, D], F32, kind="ExternalOutput").ap()
    if debug_h1:
        dbg_d = dt("dbg", [n_seq, LP, D], F32, kind="ExternalOutput").ap()
        dbg2_d = dt("dbg2", [n_seq, 128, 8 * LP], F32, kind="ExternalOutput").ap()

    S = Sched()
    with ExitStack() as es:
        ARENA_F32 = 53000
        arena = es.enter_context(nc.sbuf_tensor("arena", [128, ARENA_F32], F32))
        psq = [es.enter_context(nc.psum_tensor(f"psq{i}", [128, 1024], F32)) for i in range(4)]
        ps = [psq[i // 2][:, (i % 2) * 512:(i % 2 + 1) * 512] for i in range(8)]
        psP = [q_.rearrange("p (h n) -> p h n", h=2) for q_ in psq]
        sems = {e: es.enter_context(nc.semaphore(f"s_{e}")) for e in ENGS}
        dma_sems = {}
        for e in ["sp", "pool"]:
            for k in range(S.n_dma_sems):
                dma_sems[(e, k)] = es.enter_context(nc.semaphore(f"d_{e}{k}"))
        P = [Buf(f"ps{i}", excl=True) for i in range(8)]

        class Arena:
            def __init__(self, start=0):
                self.off = start

            def f32(self, n):
                a = arena[:, self.off:self.off + n]
                self.off += n
                return a

            def bf16(self, n):
                assert n % 2 == 0
                a = arena[:, self.off:self.off + n // 2].bitcast(BF16)
                self.off += n // 2
                return a

        A = Arena(0)
        CB = A.bf16(11 * 128)
        cb_v = lambda i: CB[:, i * 128:(i + 1) * 128]
        IDENT, NEGU, NEGU0, NEGONES, NEGONES0, ONES, MSTRICT, MINCL, ONESA, ONESB, ZEROS = [cb_v(i) for i in range(11)]
        G = A.f32(3 * D)
        NG = A.f32(4)
        FIN = A.bf16(512)
        OG = A.bf16(8 * LP)
        OGv = OG.rearrange("p (c t) -> p c t", c=8)
        pers_end = A.off
        B_cb, B_g, B_ng, B_fin, B_og = Buf("cb"), Buf("g"), Buf("ng"), Buf("fin"), Buf("og")

        def mm(out, lhsT, rhs, start, stop, reads, writes):
            S.op("pe", lambda e: e.matmul(out, lhsT=lhsT, rhs=rhs, start=start, stop=stop),
                 reads=reads, writes=writes)

        def tr(out, in_, reads, writes):
            S.op("pe", lambda e: e.transpose(out, in_, IDENT), reads=list(reads) + [B_cb], writes=writes)

        def act(out, in_, func, reads, writes, scale=1.0, bias=0.0, accum_out=None, eng="act"):
            if accum_out is None:
                S.op("act", lambda e: e.activation(out=out, in_=in_, func=func, bias=bias, scale=scale),
                     reads=reads, writes=writes)
            else:
                S.op("act", lambda e: e.activation(out=out, in_=in_, func=func, bias=bias, scale=scale,
                                                   accum_out=accum_out), reads=reads, writes=writes)

        def tt(eng, out, in0, in1, op, reads, writes):
            S.op(eng, lambda e: e.tensor_tensor(out=out, in0=in0, in1=in1, op=op), reads=reads, writes=writes)

        def tsc(eng, out, in0, s1, op0, reads, writes, s2=None, op1=None):
            if op1 is None:
                S.op(eng, lambda e: e.tensor_scalar(out=out, in0=in0, scalar1=s1, scalar2=None, op0=op0),
                     reads=reads, writes=writes)
            else:
                S.op(eng, lambda e: e.tensor_scalar(out=out, in0=in0, scalar1=s1, scalar2=s2, op0=op0, op1=op1),
                     reads=reads, writes=writes)

        def stt(eng, out, in0, scalar, in1, op0, op1, reads, writes):
            S.op(eng, lambda e: e.scalar_tensor_tensor(out=out, in0=in0, scalar=scalar, in1=in1, op0=op0, op1=op1),
                 reads=reads, writes=writes)

        def cp(eng, out, in_, reads, writes):
            if eng == "act":
                S.op("act", lambda e: e.copy(out=out, in_=in_), reads=reads, writes=writes)
            else:
                S.op(eng, lambda e: e.tensor_copy(out=out, in_=in_), reads=reads, writes=writes)

        def memset(eng, ap, val, writes):
            S.op(eng, lambda e: e.memset(ap, val), writes=writes)

        def dma(eng, out, in_, reads, writes):
            S.op(eng, lambda e: e.dma_start(out=out, in_=in_), reads=reads, writes=writes, dma=True)

        def recip(out, in_, reads, writes):
            S.op("dve", lambda e: e.reciprocal(out=out, in_=in_), reads=reads, writes=writes)

        def wload(dst3, src2, c0, ncols, writes, reads=()):
            kc = src2.shape[0] // 128
            srcv = src2.rearrange("(kc p) n -> p kc n", p=128)
            dma("pool", dst3, srcv[:, :, c0:c0 + ncols], reads, writes)

        dma("pool", CB, cb_d, [], [B_cb])
        for i in range(3):
            dma("sp", G[:, i * D:(i + 1) * D], gains_d[i:i + 1, :].broadcast_to([128, D]), [], [B_g])
        for c in range(2):
            dma("sp", NG[:, c:c + 1], qng_d[c * 128:(c + 1) * 128].rearrange("(p o) -> p o", o=1), [], [B_ng])
        dma("sp", NG[:, 2:3], kvng_d.rearrange("(p o) -> p o", o=1), [], [B_ng])
        memset("dve", FIN, 1.0, [B_fin])

        A0 = Arena(pers_end)
        HNT = A0.bf16(8 * LP); HNTv = HNT.rearrange("p (c t) -> p c t", c=8); B_hnt = Buf("hnt")
        WS = [A0.bf16(8 * 512) for _ in range(2)]; B_ws = [Buf("ws0"), Buf("ws1")]
        WSv = [w.rearrange("p (c n) -> p c n", c=8) for w in WS]
        WM = A0.bf16(8 * 416); WMv = WM.rearrange("p (c n) -> p c n", c=8); B_wm = Buf("wm")
        WKRR = A0.bf16(8 * 32); WKRRv = WKRR.rearrange("p (c n) -> p c n", c=8); B_wkrr = Buf("wkrr")
        WUQ = A0.bf16(2 * 768); WUQv = WUQ.rearrange("p (c n) -> p c n", c=2); B_wuq = Buf("wuq")
        WUQR = A0.bf16(2 * 256); WUQRv = WUQR.rearrange("p (c n) -> p c n", c=2); B_wuqr = Buf("wuqr")
        WUKV = A0.bf16(1024); B_wukv = Buf("wukv")
        A0_KT_START = A0.off
        KT = [A0.bf16(LP) for _ in range(2)]; B_kt = [Buf("kt0"), Buf("kt1")]
        QT = [A0.bf16(LP) for _ in range(2)]; B_qt = [Buf("qt0"), Buf("qt1")]
        VP = A0.bf16(NB * 256); VPv = VP.rearrange("p (b v n) -> p b v n", b=NB, v=2); B_vp = Buf("vp")
        SG = A0.bf16(LP); B_sg = Buf("sg")
        CQN = A0.bf16(2 * LP); CQNv = CQN.rearrange("p (c t) -> p c t", c=2); B_cqn = Buf("cqn")
        CKVN = A0.bf16(LP); B_ckvn = Buf("ckvn")
        KR = A0.bf16(LP); B_kr = Buf("kr")
        ROPE = A0.f32(2 * LP); ROPEv = ROPE.rearrange("p (c t) -> p c t", c=2); B_rope = Buf("rope")
        def pairbuf(ap):
            return ap.rearrange("p (h n) -> p h n", h=2), [ap[:, 0:512], ap[:, 512:1024]]
        E32P = A0.f32(1024); e32v, E32 = pairbuf(E32P); B_e32p = Buf("e32p"); B_e32 = [B_e32p, B_e32p]
        SPBP = A0.bf16(1024); spbv, SPB = pairbuf(SPBP); B_spbp = Buf("spbp"); B_spb = [B_spbp, B_spbp]
        SPBP2 = A0.bf16(1024); spbv2, SPB2 = pairbuf(SPBP2); B_spbp2 = Buf("spbp2")
        C32P = A0.f32(1024); c32v, C32 = pairbuf(C32P); B_c32p = Buf("c32p")
        C16P = A0.bf16(1024); c16v, C16 = pairbuf(C16P); B_c16p = Buf("c16p")
        ATBP = [A0.bf16(1024) for _ in range(2)]
        atbv = [pairbuf(a_)[0] for a_ in ATBP]
        ATB = [pairbuf(ATBP[i // 2])[1][i % 2] for i in range(4)]
        B_atbp = [Buf("atbp0"), Buf("atbp1")]; B_atb = [B_atbp[i // 2] for i in range(4)]
        XS = A0.f32(D); B_xs = Buf("xs")
        T32 = [XS[:, 0:512], XS[:, 512:1024]]; B_t32 = [Buf("t32a"), Buf("t32b")]
        HN = A0.bf16(D); B_hn = Buf("hn")
        ST = A0.f32(8); B_st = Buf("st"); B_stb = Buf("stb")
        JK = HN; B_jk = B_hn
        l0_end = A0.off
        assert l0_end <= ARENA_F32, l0_end

        A1 = Arena(pers_end)
        W0O = A1.bf16(8 * D); W0Ov = W0O.rearrange("p (c n) -> p c n", c=8); B_w0o = Buf("w0o")
        W1I = A1.bf16(8 * OD_IN); W1Iv = W1I.rearrange("p (c n) -> p c n", c=8); B_w1i = Buf("w1i")
        W1O = A1.bf16(8 * D); W1Ov = W1O.rearrange("p (c n) -> p c n", c=8); B_w1o = Buf("w1o")
        SWE = A1.f32(2 * 16 * 128); SWEv = SWE.rearrange("p (a h r) -> p a h r", a=2, h=16); B_swe = Buf("swe")
        EM = A1.f32(16 * 128); EMv = EM.rearrange("p (h r) -> p h r", h=16); B_em = Buf("em")
        CM = A1.f32(NB * 16); CMv = CM.rearrange("p (n h) -> p n h", n=NB); B_cm = Buf("cm")
        ESK = A1.f32(8); B_esk = Buf("esk")
        KT1 = A1.bf16(LP); B_kt1 = [Buf(f"kt1l{i}") for i in range(NB)]
        VR = A1.bf16(3 * 4 * 128); VRv = VR.rearrange("p (b v n) -> p b v n", b=3, v=4); B_vr = [Buf("vr0"), Buf("vr1"), Buf("vr2")]
        VM = A1.bf16(4 * 128); VMv = VM.rearrange("p (v n) -> p v n", v=4); B_vm = Buf("vm")
        H1 = [A1.f32(D) for _ in range(2)]; B_h1 = [Buf("h1a"), Buf("h1b")]
        XS1 = A1.f32(D); B_xs1 = Buf("xs1")
        HN1 = A1.bf16(D); B_hn1 = Buf("hn1")
        HNT1 = [A1.bf16(8 * 128) for _ in range(2)]; HNT1v = [h.rearrange("p (c t) -> p c t", c=8) for h in HNT1]
        B_hnt1 = [Buf("hnt1a"), Buf("hnt1b")]
        QG = [[A1.bf16(8 * 128) for _ in range(2)] for _ in range(2)]
        QGv = [[q.rearrange("p (h t) -> p h t", h=8) for q in qq] for qq in QG]
        B_qg = [[Buf(f"qg{i}{j}") for j in range(2)] for i in range(2)]
        SG1 = [A1.bf16(8 * 128) for _ in range(2)]; SG1v = [x_.rearrange("p (c t) -> p c t", c=8) for x_ in SG1]
        B_sg1 = [Buf("sg1a"), Buf("sg1b")]
        OG1 = A1.bf16(8 * 128); OG1v = OG1.rearrange("p (c t) -> p c t", c=8); B_og1 = Buf("og1")
        EX = [A1.f32(1024) for _ in range(2)]; EXv = [x_.rearrange("p (h t) -> p h t", h=8) for x_ in EX]
        B_ex = [Buf("exa"), Buf("exb")]
        PB = [A1.bf16(1024) for _ in range(2)]; PBv = [x_.rearrange("p (h t) -> p h t", h=8) for x_ in PB]
        B_pb = [Buf("pba"), Buf("pbb")]
        B_exh = [[Buf(f"exh{i}{j}") for j in range(2)] for i in range(2)]
        B_pbh = [[Buf(f"pbh{i}{j}") for j in range(2)] for i in range(2)]
        R32 = A1.f32(512); B_r32 = Buf("r32")
        U32 = A1.f32(512); B_u32 = Buf("u32")
        ST1 = A1.f32(8); B_st1 = Buf("st1")
        ST2 = A1.f32(8); B_st2 = Buf("st2")
        JK1 = HN1; B_jk1 = B_hn1
        OUTB = A1.f32(D); B_outb = Buf("outb")
        assert A1.off <= ARENA_F32, A1.off

        PT = ps[7].bitcast(BF16)

        TCH = [(c * 512, min(512, LP - c * 512)) for c in range(5)]
        QCH = [(0, 1), (1, 5), (5, 9), (9, 13), (13, 17)]

        def rmsnorm_block(src32, B_src, gidx, dst_bf, B_dst, st, B_st_, jk, B_jk_):
            act(jk, src32, AF.Square, [B_src], [B_jk_, B_st_], accum_out=st[:, 0:1])
            act(st[:, 1:2], st[:, 0:1], AF.Ln, [B_st_], [B_st_], scale=1.0 / D, bias=EPS)
            act(st[:, 2:3], st[:, 1:2], AF.Exp, [B_st_], [B_st_], scale=-0.5)
            stt("dve", dst_bf, src32, st[:, 2:3], G[:, gidx * D:(gidx + 1) * D], ALU.mult, ALU.mult,
                [B_src, B_st_, B_g], [B_dst])

        def transpose_block(hn_bf, B_hn_, dst3, B_dst):
            for kc in range(8):
                tr(PT[:, kc * 128:(kc + 1) * 128], hn_bf[:, kc * 128:(kc + 1) * 128], [B_hn_], [P[7]])
            cp("dve", dst3, PT.rearrange("p (c t) -> p c t", c=8), [P[7]], [B_dst])

        def proj_fm(psb, Pb, wv, c0, m, t0, n, B_w, hnt_v, B_h):
            for kc in range(8):
                mm(psb[0:m, 0:n], wv[:, kc, c0:c0 + m], hnt_v[:, kc, t0:t0 + n], kc == 0, kc == 7,
                   [B_w, B_h], [Pb])

        for sq in range(n_seq):
            S.barrier()
            wload(WMv, w0in_d, 2048, 416, [B_wm])
            wload(WUQv, wuq_d, 0, 768, [B_wuq])
            dma("pool", WUKV, wukv_d, [], [B_wukv])
            dma("sp", ROPEv[0:32], rope_d.rearrange("c p t -> p c t"), [], [B_rope])
            for i in range(2):
                memset("pool", KT[i], 0.0, [B_kt[i]])
                memset("pool", QT[i], 0.0, [B_qt[i]])
            memset("pool", VP, 0.0, [B_vp])
            for kc in range(8):
                tsc("pool", WKRRv[:, kc, 0:16], WMv[:, kc, 400:416], -1.0, ALU.mult, [B_wm], [B_wkrr])
                cp("pool", WKRRv[:, kc, 16:32], WMv[:, kc, 384:400], [B_wm], [B_wkrr])
            for kc in range(2):
                src = WUQv[:, kc, :].rearrange("p (h d) -> p h d", h=8)
                dst = WUQRv[:, kc, :].rearrange("p (h d) -> p h d", h=8)
                tsc("pool", dst[:, :, 0:16], src[:, :, 80:96], -1.0, ALU.mult, [B_wuq], [B_wuqr])
                cp("pool", dst[:, :, 16:32], src[:, :, 64:80], [B_wuq], [B_wuqr])

            XSs = [XS, E32P]; B_xss = [B_xs, B_e32p]
            HNs = [HN, C32P.bitcast(BF16)[:, 0:D]]; B_hns = [B_hn, B_c32p]
            STs = [ST[:, 0:4], ST[:, 4:8]]; B_sts = [B_st, B_stb]
            for b in range(NB):
                i = b % 2
                xs_, Bx_ = XSs[i], B_xss[i]
                if b == 0:
                    memset("dve", xs_, 0.0, [Bx_])
                    dma("sp", xs_[NPAD:128, :], meta_d, [], [Bx_])
                else:
                    dma("sp", xs_, x_d[sq, (b - 1) * 128:b * 128, :], [], [Bx_])
                rmsnorm_block(xs_, Bx_, 0, HNs[i], B_hns[i], STs[i], B_sts[i], HNs[i], B_hns[i])
                pk = 6 + i
                ptv = ps[pk].bitcast(BF16)
                for kc in range(8):
                    tr(ptv[:, kc * 128:(kc + 1) * 128], HNs[i][:, kc * 128:(kc + 1) * 128], [B_hns[i]], [P[pk]])
                cp("dve", HNTv[:, :, b * 128:(b + 1) * 128], ptv.rearrange("p (c t) -> p c t", c=8), [P[pk]], [B_hnt])

            def attention_pair(kts, B_ks, qts, B_qs, pair_chunk, sb, escale):
                M2 = lambda m_: m_.unsqueeze(1).to_broadcast([128, 2, 128])
                for ci, (ba, bz) in enumerate(QCH):
                    q0, q1 = ba * 128, bz * 128
                    N = q1 - q0
                    if sb:
                        psO, PO = ps[4 + (ci % 2)], P[4 + (ci % 2)]
                    else:
                        ob = 4 if ci % 2 == 0 else 6
                        psO, PO = ps[ob], P[ob]
                        psD, PD = ps[ob + 1], P[ob + 1]
                    mm(psO[:, 0:N], ZEROS, FIN[:, 0:N], True, False, [B_cb, B_fin], [PO])
                    if not sb:
                        mm(psD[:, 0:N], ZEROS, FIN[:, 0:N], True, False, [B_cb, B_fin], [PD])
                    steps = list(range(bz - 1, -1, -1))

                    def geom(j):
                        tq0 = max(q0, j * 128)
                        return tq0, q1 - tq0, tq0 - q0, j >= ba

                    def qk(si):
                        j = steps[si]
                        tq0, n, c0, diag = geom(j)
                        for hh in range(2):
                            bank = hh if sb else 2 * (si % 2) + hh
                            mm(ps[bank][:, 0:n], kts[hh][:, j * 128:(j + 1) * 128], qts[hh][:, tq0:q1], True, True,
                               [B_ks[hh], B_qs[hh]], [P[bank]])

                    if sb:
                        SPV = [(spbv, SPB, B_spbp), (spbv2, SPB2, B_spbp2)]
                        CSB = [(2, 3), (6, 7)]

                        def stage1(si):
                            j = steps[si]
                            tq0, n, c0, diag = geom(j)
                            sv, _, Bs = SPV[si % 2]
                            act(e32v[:, :, 0:n], psP[0][:, :, 0:n], AF.Exp, [P[0], P[1]], [B_e32p])
                            act(sv[:, :, 0:n], e32v[:, :, 0:n], AF.Ln, [B_e32p], [Bs], bias=1.0)
                            if diag:
                                tt("pool", sv[:, :, 0:128], sv[:, :, 0:128], M2(MSTRICT), ALU.mult, [Bs, B_cb], [Bs])

                        def cumsum(si):
                            j = steps[si]
                            tq0, n, c0, diag = geom(j)
                            _, sp2, Bs = SPV[si % 2]
                            cb_ = CSB[si % 2]
                            first = si == 0
                            for hh in range(2):
                                bk = cb_[hh]
                                mm(ps[bk][:, 0:n], kts[hh][:, j * 128:(j + 1) * 128], qts[hh][:, tq0:q1], True, False,
                                   [B_ks[hh], B_qs[hh]], [P[bk]])
                            for hh in range(2):
                                bk = cb_[hh]
                                mm(ps[bk][:, 0:n], NEGU0 if j == 0 else NEGU, sp2[hh][:, 0:n], False, first,
                                   [B_cb, Bs], [P[bk]])
                                if not first:
                                    mm(ps[bk][:, 0:n], NEGONES, C16[hh][:, c0:c0 + n], False, True,
                                       [B_cb, B_c16p], [P[bk]])

                        def carry(si):
                            j = steps[si]
                            tq0, n, c0, diag = geom(j)
                            sv, _, Bs = SPV[si % 2]
                            if j == 0:
                                return
                            if si == 0:
                                memset("pool", C32P, 0.0, [B_c32p])
                            tt("dve", c32v[:, :, c0:c0 + n], c32v[:, :, c0:c0 + n], sv[:, :, 0:n], ALU.add,
                               [B_c32p, Bs], [B_c32p])
                            cp("dve", c16v[:, :, 0:N], c32v[:, :, 0:N], [B_c32p], [B_c16p])

                        def exp2(si):
                            j = steps[si]
                            tq0, n, c0, diag = geom(j)
                            cb_ = CSB[si % 2]
                            e = si % 2
                            act(atbv[e][:, :, 0:n], psP[cb_[0] // 2][:, :, 0:n], AF.Exp, [P[cb_[0]], P[cb_[1]]], [B_atbp[e]])
                            if diag:
                                tt("pool", atbv[e][:, :, 0:128], atbv[e][:, :, 0:128], M2(MSTRICT), ALU.mult,
                                   [B_atbp[e], B_cb], [B_atbp[e]])

                        def av(si):
                            j = steps[si]
                            tq0, n, c0, diag = geom(j)
                            e = si % 2
                            for hh in range(2):
                                mm(psO[:, c0:c0 + n], VPv[:, j, hh, :], ATB[2 * e + hh][:, 0:n], False, j == 0 and hh == 1,
                                   [B_vp, B_atbp[e]], [PO])

                        ns = len(steps)
                        qk(0)
                        stage1(0)
                        if ns > 1:
                            qk(1)
                        for si in range(ns):
                            cumsum(si)
                            carry(si)
                            if si + 1 < ns:
                                stage1(si + 1)
                            if si + 2 < ns:
                                qk(si + 2)
                            exp2(si)
                            av(si)
                    else:
                        qk(0)
                    for si, j in (enumerate(steps) if not sb else []):
                        tq0, n, c0, diag = geom(j)
                        first = si == 0
                        last = j == 0
                        if True:
                            e = si % 2
                            act(atbv[e][:, :, 0:n], psP[e][:, :, 0:n], AF.Exp, [P[2 * e], P[2 * e + 1]], [B_atbp[e]],
                                scale=escale)
                            if diag:
                                tt("pool", atbv[e][:, :, 0:128], atbv[e][:, :, 0:128], M2(MINCL), ALU.mult,
                                   [B_atbp[e], B_cb], [B_atbp[e]])
                            if not last:
                                qk(si + 1)
                            mm(psO[:, c0:c0 + n], VPv[:, j, 0, :], ATB[2 * e][:, 0:n], False, last,
                               [B_vp, B_atbp[e]], [PO])
                            mm(psD[:, c0:c0 + n], VPv[:, j, 1, :], ATB[2 * e + 1][:, 0:n], False, last,
                               [B_vp, B_atbp[e]], [PD])
                    if sb:
                        tt("dve", OGv[:, pair_chunk, q0:q1], psO[:, 0:N], SG[:, q0:q1], ALU.mult,
                           [PO, B_sg], [B_og])
                    else:
                        t32, B_t = T32[ci % 2], B_t32[ci % 2]
                        u32 = E32[ci % 2]
                        lo, hi = slice(0, 64), slice(64, 128)
                        act(t32[hi, 0:N], psO[hi, 0:N], AF.Ln, [PO], [B_t], bias=1e-30)
                        act(t32[lo, 0:N], psD[lo, 0:N], AF.Ln, [PD], [B_t], bias=1e-30)
                        act(t32[:, 0:N], t32[:, 0:N], AF.Exp, [B_t], [B_t], scale=-1.0)
                        cp("dve", u32[lo, 0:N], t32[hi, 0:N], [B_t], [B_e32p])
                        cp("dve", u32[hi, 0:N], t32[lo, 0:N], [B_t], [B_e32p])
                        tt("dve", u32[:, 0:N], u32[:, 0:N], SG[:, q0:q1], ALU.mult, [B_e32p, B_sg], [B_e32p])
                        tt("dve", OGv[lo, pair_chunk, q0:q1], psO[lo, 0:N], u32[lo, 0:N], ALU.mult,
                           [PO, B_e32p], [B_og])
                        tt("dve", OGv[hi, pair_chunk, q0:q1], psD[hi, 0:N], u32[hi, 0:N], ALU.mult,
                           [PD, B_e32p], [B_og])

            def attention_pair_sb(kts, B_ks, qts, B_qs, pair_chunk):
                M2 = lambda m_: m_.unsqueeze(1).to_broadcast([128, 2, 128])
                SPV = [(spbv, SPB, B_spbp), (spbv2, SPB2, B_spbp2)]
                CSB = [(2, 3), (6, 7)]

                class Chunk:
                    pass

                def mk_chunk(ci):
                    ba, bz = QCH[ci]
                    q0, q1 = ba * 128, bz * 128
                    N = q1 - q0
                    psO, PO = ps[4 + (ci % 2)], P[4 + (ci % 2)]
                    steps = list(range(bz - 1, -1, -1))
                    ns = len(steps)

                    def geom(si):
                        j = steps[si]
                        tq0 = max(q0, j * 128)
                        return j, tq0, q1 - tq0, tq0 - q0, j >= ba

                    def qk(si):
                        j, tq0, n, c0, diag = geom(si)
                        for hh in range(2):
                            mm(ps[hh][:, 0:n], kts[hh][:, j * 128:(j + 1) * 128], qts[hh][:, tq0:q1], True, True,
                               [B_ks[hh], B_qs[hh]], [P[hh]])

                    def stage1(si):
                        j, tq0, n, c0, diag = geom(si)
                        sv, _, Bs = SPV[si % 2]
                        act(e32v[:, :, 0:n], psP[0][:, :, 0:n], AF.Exp, [P[0], P[1]], [B_e32p])
                        act(sv[:, :, 0:n], e32v[:, :, 0:n], AF.Ln, [B_e32p], [Bs], bias=1.0)
                        if diag:
                            tt("pool", sv[:, :, 0:128], sv[:, :, 0:128], M2(MSTRICT), ALU.mult, [Bs, B_cb], [Bs])

                    def cumsum(si):
                        j, tq0, n, c0, diag = geom(si)
                        _, sp2, Bs = SPV[si % 2]
                        cb_ = CSB[si % 2]
                        first = si == 0
                        for hh in range(2):
                            bk = cb_[hh]
                            mm(ps[bk][:, 0:n], kts[hh][:, j * 128:(j + 1) * 128], qts[hh][:, tq0:q1], True, False,
                               [B_ks[hh], B_qs[hh]], [P[bk]])
                        for hh in range(2):
                            bk = cb_[hh]
                            mm(ps[bk][:, 0:n], NEGU0 if j == 0 else NEGU, sp2[hh][:, 0:n], False, first,
                               [B_cb, Bs], [P[bk]])
                            if not first:
                                mm(ps[bk][:, 0:n], NEGONES, C16[hh][:, c0:c0 + n], False, True,
                                   [B_cb, B_c16p], [P[bk]])

                    def carry(si):
                        j, tq0, n, c0, diag = geom(si)
                        sv, _, Bs = SPV[si % 2]
                        if j == 0:
                            return
                        if si == 0:
                            memset("pool", C32P, 0.0, [B_c32p])
                        tt("dve", c32v[:, :, c0:c0 + n], c32v[:, :, c0:c0 + n], sv[:, :, 0:n], ALU.add,
                           [B_c32p, Bs], [B_c32p])
                        cp("dve", c16v[:, :, 0:N], c32v[:, :, 0:N], [B_c32p], [B_c16p])

                    def exp2(si):
                        j, tq0, n, c0, diag = geom(si)
                        cb_ = CSB[si % 2]
                        e = si % 2
                        act(atbv[e][:, :, 0:n], psP[cb_[0] // 2][:, :, 0:n], AF.Exp, [P[cb_[0]], P[cb_[1]]], [B_atbp[e]])
                        if diag:
                            tt("pool", atbv[e][:, :, 0:128], atbv[e][:, :, 0:128], M2(MSTRICT), ALU.mult,
                               [B_atbp[e], B_cb], [B_atbp[e]])

                    def av(si):
                        j, tq0, n, c0, diag = geom(si)
                        e = si % 2
                        for hh in range(2):
                            mm(psO[:, c0:c0 + n], VPv[:, j, hh, :], ATB[2 * e + hh][:, 0:n], False, j == 0 and hh == 1,
                               [B_vp, B_atbp[e]], [PO])

                    def prologue():
                        mm(psO[:, 0:N], ZEROS, FIN[:, 0:N], True, False, [B_cb, B_fin], [PO])
                        qk(0)
                        stage1(0)
                        if ns > 1:
                            qk(1)

                    def finalize():
                        tt("dve", OGv[:, pair_chunk, q0:q1], psO[:, 0:N], SG[:, q0:q1], ALU.mult,
                           [PO, B_sg], [B_og])

                    c = Chunk()
                    c.ns, c.qk, c.stage1, c.cumsum, c.carry, c.exp2, c.av = ns, qk, stage1, cumsum, carry, exp2, av
                    c.prologue, c.finalize = prologue, finalize
                    return c

                chunks = [mk_chunk(ci) for ci in range(len(QCH))]
                chunks[0].prologue()
                chunks[0].cumsum(0)
                for ci, ch in enumerate(chunks):
                    nxt = chunks[ci + 1] if ci + 1 < len(chunks) else None
                    for si in range(ch.ns):
                        lastst = si == ch.ns - 1
                        ch.carry(si)
                        if si + 1 < ch.ns:
                            ch.stage1(si + 1)
                        if si + 2 < ch.ns:
                            ch.qk(si + 2)
                        if lastst and nxt is not None:
                            nxt.prologue()
                        ch.exp2(si)
                        if not lastst:
                            ch.cumsum(si + 1)
                        elif nxt is not None:
                            nxt.cumsum(0)
                        ch.av(si)
                    ch.finalize()

            bk_rot = [0]
            BK_ORDER = [[6, 7]]

            def nbk():
                bk_rot[0] = (bk_rot[0] + 1) % len(BK_ORDER[0])
                return BK_ORDER[0][bk_rot[0]]

            def build_v(lhs_fn, rhs_fn, nk, reads):
                for b0 in range(0, NB, 4):
                    nb4 = min(4, NB - b0)
                    k_ = nbk()
                    for i in range(nb4):
                        b = b0 + i
                        for kc in range(nk):
                            mm(ps[k_][:, i * 128:(i + 1) * 128], lhs_fn(kc, b), rhs_fn(kc), kc == 0, kc == nk - 1,
                               reads, [P[k_]])
                    pv = ps[k_][:, 0:nb4 * 128].rearrange("p (b n) -> p b n", b=nb4)
                    cp("dve", VPv[:, b0:b0 + nb4, 0, 0:64], pv[:, :, 0:64], [P[k_]], [B_vp])
                    cp("dve", VPv[:, b0:b0 + nb4, 1, 64:128], pv[:, :, 64:128], [P[k_]], [B_vp])

            BK_ORDER[0] = [6, 7, 0, 1, 2, 3]

            def pj(wv, c0, m, t0, n, B_w, src_v, B_src, nkc=8):
                k_ = nbk()
                for kc in range(nkc):
                    mm(ps[k_][0:m, 0:n], wv[:, kc, c0:c0 + m], src_v[:, kc, t0:t0 + n], kc == 0, kc == nkc - 1,
                       [B_w, B_src], [P[k_]])
                return ps[k_], P[k_]

            for p in range(4):
                ws, wsv, B_w = WS[p % 2], WSv[p % 2], B_ws[p % 2]
                wv4 = ws.rearrange("p (c f n) -> p c f n", c=8, f=4)
                srcv = w0in_d.rearrange("(kc p) n -> p kc n", p=128)
                for f in range(4):
                    dma("pool", wv4[:, :, f, :], srcv[:, :, f * 512 + p * 128:f * 512 + (p + 1) * 128], [], [B_w])
                for (t0, n) in TCH:
                    pa, Pa = pj(wsv, 128, 128, t0, n, B_w, HNTv, B_hnt)
                    cp("act", KT[0][:, t0:t0 + n], pa[:, 0:n], [Pa], [B_kt[0]])
                    pa, Pa = pj(wsv, 0, 128, t0, n, B_w, HNTv, B_hnt)
                    tsc("dve", QT[0][0:64, t0:t0 + n], pa[0:64, 0:n], 0.125, ALU.mult, [Pa], [B_qt[0]])
                    tsc("dve", QT[1][64:128, t0:t0 + n], pa[64:128, 0:n], 0.125, ALU.mult, [Pa], [B_qt[1]])
                    pa, Pa = pj(wsv, 384, 128, t0, n, B_w, HNTv, B_hnt)
                    act(SG[:, t0:t0 + n], pa[:, 0:n], AF.Silu, [Pa], [B_sg])
                build_v(lambda kc, b: HNTv[:, kc, b * 128:(b + 1) * 128], lambda kc, wsv=wsv: wsv[:, kc, 256:384], 8,
                        [B_hnt, B_w])
                attention_pair_sb([KT[0], KT[0]], [B_kt[0], B_kt[0]], QT, B_qt, p)

            memset("pool", VPv[:, :, 0, 64:128], 1.0, [B_vp])
            memset("pool", VPv[:, :, 1, 0:64], 1.0, [B_vp])
            BK_ORDER[0] = [6, 7, 0, 1, 2, 3]
            for i in range(2):
                memset("pool", KT[i], 0.0, [B_kt[i]])
                memset("pool", QT[i], 0.0, [B_qt[i]])
                memset("pool", KT[i][96:97, 0:NPAD], -30000.0, [B_kt[i]])
                memset("pool", QT[i][96:97, :], 1.0, [B_qt[i]])
            CKVNv1 = CKVN.rearrange("p (c t) -> p c t", c=1)
            for (t0, n) in TCH:
                for cc in range(2):
                    pa, Pa = pj(WMv, cc * 128, 128, t0, n, B_wm, HNTv, B_hnt)
                    cp("dve", T32[cc][:, 0:n], pa[:, 0:n], [Pa], [B_t32[cc]])
                    act(ATB[cc][:, 0:n], pa[:, 0:n], AF.Square, [Pa], [B_atb[cc]])
                k_ = nbk()
                for cc in range(2):
                    mm(ps[k_][:, 0:n], ONES, ATB[cc][:, 0:n], cc == 0, cc == 1, [B_cb, B_atb[cc]], [P[k_]])
                act(E32[0][:, 0:n], ps[k_][:, 0:n], AF.Ln, [P[k_]], [B_e32[0]], scale=1.0 / 256, bias=EPS)
                act(E32[0][:, 0:n], E32[0][:, 0:n], AF.Exp, [B_e32[0]], [B_e32[0]], scale=-0.5)
                for cc in range(2):
                    stt("dve", CQNv[:, cc, t0:t0 + n], T32[cc][:, 0:n], NG[:, cc:cc + 1], E32[0][:, 0:n],
                        ALU.mult, ALU.mult, [B_t32[cc], B_ng, B_e32[0]], [B_cqn])
                pa, Pa = pj(WMv, 256, 128, t0, n, B_wm, HNTv, B_hnt)
                cp("dve", T32[0][:, 0:n], pa[:, 0:n], [Pa], [B_t32[0]])
                act(ATB[2][:, 0:n], pa[:, 0:n], AF.Square, [Pa], [B_atb[2]])
                k_ = nbk()
                mm(ps[k_][:, 0:n], ONES, ATB[2][:, 0:n], True, True, [B_cb, B_atb[2]], [P[k_]])
                act(E32[1][:, 0:n], ps[k_][:, 0:n], AF.Ln, [P[k_]], [B_e32[1]], scale=1.0 / 128, bias=EPS)
                act(E32[1][:, 0:n], E32[1][:, 0:n], AF.Exp, [B_e32[1]], [B_e32[1]], scale=-0.5)
                stt("dve", CKVN[:, t0:t0 + n], T32[0][:, 0:n], NG[:, 2:3], E32[1][:, 0:n],
                    ALU.mult, ALU.mult, [B_t32[0], B_ng, B_e32[1]], [B_ckvn])
                pa, Pa = pj(WMv, 384, 32, t0, n, B_wm, HNTv, B_hnt)
                pb_, Pb_ = pj(WKRRv, 0, 32, t0, n, B_wkrr, HNTv, B_hnt)
                tt("dve", T32[0][0:32, 0:n], pa[0:32, 0:n], ROPEv[0:32, 0, t0:t0 + n], ALU.mult,
                   [Pa, B_rope], [B_t32[0]])
                tt("dve", T32[1][0:32, 0:n], pb_[0:32, 0:n], ROPEv[0:32, 1, t0:t0 + n], ALU.mult,
                   [Pb_, B_rope], [B_t32[1]])
                tt("dve", KR[0:32, t0:t0 + n], T32[0][0:32, 0:n], T32[1][0:32, 0:n], ALU.add,
                   [B_t32[0], B_t32[1]], [B_kr])

            WUKVv = WUKV.rearrange("p (h a d) -> p h a d", h=8, a=2)
            for p in range(4):
                ws, wsv, B_w = WS[p % 2], WSv[p % 2], B_ws[p % 2]
                wload(wsv[:, :, 0:128], w0in_d, 2464 + p * 128, 128, [B_w])
                for (t0, n) in TCH:
                    pa, Pa = pj(wsv, 0, 128, t0, n, B_w, HNTv, B_hnt)
                    act(SG[:, t0:t0 + n], pa[:, 0:n], AF.Silu, [Pa], [B_sg])
                    for hh in range(2):
                        h = 2 * p + hh
                        k_ = nbk()
                        mm(ps[k_][0:64, 0:n], WUKVv[:, h, 0, :], CKVN[:, t0:t0 + n], True, True,
                           [B_wukv, B_ckvn], [P[k_]])
                        cp("act", KT[hh][0:64, t0:t0 + n], ps[k_][0:64, 0:n], [P[k_]], [B_kt[hh]])
                        cp("pool", KT[hh][64:96, t0:t0 + n], KR[0:32, t0:t0 + n], [B_kr], [B_kt[hh]])
                        pa, Pa = pj(WUQv, h * 96, 64, t0, n, B_wuq, CQNv, B_cqn, nkc=2)
                        cp("act", QT[hh][0:64, t0:t0 + n], pa[0:64, 0:n], [Pa], [B_qt[hh]])
                        px, Px = pj(WUQv, h * 96 + 64, 32, t0, n, B_wuq, CQNv, B_cqn, nkc=2)
                        pr_, Pr_ = pj(WUQRv, h * 32, 32, t0, n, B_wuqr, CQNv, B_cqn, nkc=2)
                        tt("dve", T32[0][0:32, 0:n], px[0:32, 0:n], ROPEv[0:32, 0, t0:t0 + n], ALU.mult,
                           [Px, B_rope], [B_t32[0]])
                        tt("dve", T32[1][0:32, 0:n], pr_[0:32, 0:n], ROPEv[0:32, 1, t0:t0 + n], ALU.mult,
                           [Pr_, B_rope], [B_t32[1]])
                        tt("dve", QT[hh][64:96, t0:t0 + n], T32[0][0:32, 0:n], T32[1][0:32, 0:n], ALU.add,
                           [B_t32[0], B_t32[1]], [B_qt[hh]])
                build_v(lambda kc, b: CKVN[:, b * 128:(b + 1) * 128], lambda kc, p=p: WUKVv[:, 2 * p:2 * p + 2, 1, :], 1,
                        [B_ckvn, B_wukv])
                if p == 3:
                    assert pers_end + 4096 + 9216 <= A0_KT_START - (128 + 768 + 256 + 512)
                    dead = [B_hnt, B_ws[0], B_ws[1], B_wm]
                    for c in range(0, 1024, 512):
                        wload(W0Ov[:, :, c:c + 512], w0out_d, c, 512, [B_w0o] + dead)
                    for c in range(0, OD_IN, 576):
                        wload(W1Iv[:, :, c:c + 576], w1in_d, c, 576, [B_w1i] + dead)
                attention_pair(KT, B_kt, QT, B_qt, 4 + p, False, 96.0 ** -0.5)

            if debug_h1:
                for c in range(8):
                    dma("pool", dbg2_d[sq, :, c * LP:(c + 1) * LP], OG[:, c * LP:(c + 1) * LP], [B_og], [])
            S.barrier()
            for c in range(0, 1024, 512):
                wload(W1Ov[:, :, c:c + 512], w1out_d, c, 512, [B_w1o])
            dma("sp", SWEv, swae_d.rearrange("a p h r -> p a h r"), [], [B_swe])
            dma("sp", EMv[0:16], em_d, [], [B_em])
            dma("sp", CMv[0:16], cm_d, [], [B_cm])
            sk2 = sinks_d.rearrange("(p two) -> two p", two=2)
            S.op("sp", lambda e: e.dma_start(out=ESK[0:64, :], in_=sk2[0:1, :].broadcast_to([64, 8]),
                                             allow_slow_non_contiguous=True), writes=[B_esk], dma=True)
            S.op("sp", lambda e: e.dma_start(out=ESK[64:128, :], in_=sk2[1:2, :].broadcast_to([64, 8]),
                                             allow_slow_non_contiguous=True), writes=[B_esk], dma=True)
            act(ESK, ESK, AF.Exp, [B_esk], [B_esk])
            for par in range(2):
                for g in range(2):
                    memset("pool", QG[par][g], 0.0, [B_qg[par][g]])
            memset("pool", VM, 0.0, [B_vm])
            memset("pool", VR, 0.0, B_vr)

            rot = [0]

            def pbank():
                rot[0] ^= 1
                return 6 + rot[0]

            def front(b):
                par = b % 2
                h1, Bh1 = H1[par], B_h1[par]
                hv, Bhv = HNT1v[par], B_hnt1[par]
                if b == 0:
                    memset("dve", XS1, 0.0, [B_xs1])
                    dma("sp", XS1[NPAD:128, :], meta_d, [], [B_xs1])
                else:
                    dma("sp", XS1, x_d[sq, (b - 1) * 128:b * 128, :], [], [B_xs1])
                for hf in range(2):
                    for kc in range(8):
                        mm(ps[4 + hf][:, :], OGv[:, kc, b * 128:(b + 1) * 128], W0Ov[:, kc, hf * 512:(hf + 1) * 512],
                           kc == 0, kc == 7, [B_og, B_w0o], [P[4 + hf]])
                    tt("dve", h1[:, hf * 512:(hf + 1) * 512], ps[4 + hf][:, :], XS1[:, hf * 512:(hf + 1) * 512],
                       ALU.add, [P[4 + hf], B_xs1], [Bh1])
                if debug_h1:
                    dma("sp", dbg_d[sq, b * 128:(b + 1) * 128, :], h1, [Bh1], [])
                yield
                rmsnorm_block(h1, Bh1, 1, HN1, B_hn1, ST1, B_st1, JK1, B_jk1)
                pk = pbank()
                ptv = ps[pk].bitcast(BF16)
                for kc in range(8):
                    tr(ptv[:, kc * 128:(kc + 1) * 128], HN1[:, kc * 128:(kc + 1) * 128], [B_hn1], [P[pk]])
                cp("dve", hv, ptv.rearrange("p (c t) -> p c t", c=8), [P[pk]], [Bhv])
                yield
                pk = pbank()
                for kc in range(8):
                    mm(ps[pk][:, 0:128], W1Iv[:, kc, 1024:1152], hv[:, kc, :], kc == 0, kc == 7,
                       [B_w1i, Bhv], [P[pk]])
                cp("act", KT1[:, b * 128:(b + 1) * 128], ps[pk][:, 0:128], [P[pk]], [B_kt1[b]])
                slot = b % 3
                pk = pbank()
                if b == 0:
                    for kc in range(8):
                        mm(ps[pk][0:16, 0:128], hv[:, kc, NPAD:128], W1Iv[:, kc, 1152:1280], kc == 0, kc == 7,
                           [Bhv, B_w1i], [P[pk]])
                    for kh in range(2):
                        cp("dve", VMv[0:16, 2 * kh, 0:64], ps[pk][0:16, kh * 64:(kh + 1) * 64], [P[pk]], [B_vm])
                        cp("dve", VMv[0:16, 2 * kh + 1, 64:128], ps[pk][0:16, kh * 64:(kh + 1) * 64], [P[pk]], [B_vm])
                    return
                for kc in range(8):
                    mm(ps[pk][:, 0:128], hv[:, kc, :], W1Iv[:, kc, 1152:1280], kc == 0, kc == 7,
                       [Bhv, B_w1i], [P[pk]])
                for kh in range(2):
                    cp("dve", VRv[:, slot, 2 * kh, 0:64], ps[pk][:, kh * 64:(kh + 1) * 64], [P[pk]], [B_vr[slot]])
                    cp("dve", VRv[:, slot, 2 * kh + 1, 64:128], ps[pk][:, kh * 64:(kh + 1) * 64], [P[pk]], [B_vr[slot]])
                yield
                for q4 in range(2):
                    pk = pbank()
                    for i in range(4):
                        pr = q4 * 4 + i
                        for kc in range(8):
                            mm(ps[pk][:, i * 128:(i + 1) * 128], W1Iv[:, kc, pr * 128:(pr + 1) * 128], hv[:, kc, :],
                               kc == 0, kc == 7, [B_w1i, Bhv], [P[pk]])
                    g = q4
                    gs = slice(g * 64, g * 64 + 64)
                    pv = ps[pk].rearrange("p (i t) -> p i t", i=4)
                    qv = QGv[par][g][gs].rearrange("p (i two) t -> p two i t", two=2)
                    cp("act" if g == 0 else "dve", qv[:, 0], pv[0:64], [P[pk]], [B_qg[par][g]])
                    cp("dve" if g == 0 else "act", qv[:, 1], pv[64:128], [P[pk]], [B_qg[par][g]])
                    yield
                for c4 in range(2):
                    pk = pbank()
                    for i in range(4):
                        cc = c4 * 4 + i
                        for kc in range(8):
                            mm(ps[pk][:, i * 128:(i + 1) * 128], W1Iv[:, kc, 1280 + cc * 128:1280 + (cc + 1) * 128],
                               hv[:, kc, :], kc == 0, kc == 7, [B_w1i, Bhv], [P[pk]])
                    act(SG1v[par][:, c4 * 4:(c4 + 1) * 4, :], ps[pk].rearrange("p (i t) -> p i t", i=4), AF.Silu,
                        [P[pk]], [B_sg1[par]])
                    yield

            def back_parts(b):
                par = b % 2
                slot = b % 3
                h1, Bh1 = H1[par], B_h1[par]
                tiles = []
                for g in range(2):
                    if b >= 2:
                        tiles.append((g, "prev", KT1[:, (b - 1) * 128:b * 128], 128, (b - 1) % 3, B_kt1[b - 1]))
                    tiles.append((g, "cur", KT1[:, b * 128:(b + 1) * 128], 128, slot, B_kt1[b]))
                    tiles.append((g, "meta", KT1[:, NPAD:128], 16, None, B_kt1[0]))
                nt = len(tiles)

                def qk(ti):
                    g, kind, kk, nk, vs, Bk = tiles[ti]
                    for hf in range(2):
                        mm(ps[hf][0:nk, :], kk, QGv[par][g][:, hf * 4:(hf + 1) * 4, :], True, True,
                           [Bk, B_qg[par][g]], [P[hf]])

                def soft(ti):
                    g, kind, kk, nk, vs, Bk = tiles[ti]
                    e = ti % 2
                    exv, pbv = EXv[e], PBv[e]
                    for hf in range(2):
                        act(EX[e][0:nk, hf * 512:(hf + 1) * 512], ps[hf][0:nk, :], AF.Exp, [P[hf]], [B_exh[e][hf]],
                            scale=0.125)
                        if kind != "meta":
                            a = 0 if kind == "prev" else 1
                            hs = slice(hf * 4, (hf + 1) * 4)
                            tt("pool" if hf == 0 else "dve", pbv[:, hs, :], exv[:, hs, :],
                               SWEv[:, a, g * 8 + hf * 4:g * 8 + (hf + 1) * 4, :], ALU.mult,
                               [B_exh[e][hf], B_swe], [B_pbh[e][hf]])
                    if kind == "meta":
                        tt("dve", exv[0:16], exv[0:16], EMv[0:16, g * 8:(g + 1) * 8, :], ALU.mult,
                           B_exh[e] + [B_em], B_exh[e])
                        tt("dve", pbv[0:16], exv[0:16],
                           CMv[0:16, b, g * 8:(g + 1) * 8].unsqueeze(2).to_broadcast([16, 8, 128]), ALU.mult,
                           B_exh[e] + [B_cm], B_pbh[e])

                def prologue():
                    qk(0)
                    soft(0)
                    if nt > 1:
                        qk(1)

                def tile_gen():
                    for ti, (g, kind, kk, nk, vs, Bk) in enumerate(tiles):
                        e = ti % 2
                        pbv, B_p = PBv[e], B_pbh[e]
                        if kind == "meta":
                            va, vb2 = VMv[0:16, 2 * g, :], VMv[0:16, 2 * g + 1, :]
                            B_v = B_vm
                        else:
                            va, vb2 = VRv[:, vs, 2 * g, :], VRv[:, vs, 2 * g + 1, :]
                            B_v = B_vr[vs]
                        pe_ = pbv[0:nk].rearrange("p (q two) t -> p two q t", two=2)
                        gfirst = kind == ("prev" if b >= 2 else "cur")
                        glast = kind == "meta"
                        mm(ps[2][:, :], va, pe_[:, 0], gfirst, False, [B_v] + B_p, [P[2]])
                        mm(ps[2][:, :], vb2, pe_[:, 1], False, glast, [B_v] + B_p, [P[2]])
                        mm(ps[3][:, :], ONESA[0:nk, :], pe_[:, 0], gfirst, False, [B_cb] + B_p, [P[3]])
                        mm(ps[3][:, :], ONESB[0:nk, :], pe_[:, 1], False, glast, [B_cb] + B_p, [P[3]])
                        if ti + 1 < nt:
                            soft(ti + 1)
                        if ti + 2 < nt:
                            qk(ti + 2)
                        if glast:
                            R3 = R32.rearrange("p (q t) -> p q t", q=4)
                            U3 = U32.rearrange("p (q t) -> p q t", q=4)
                            tt("dve", R3, ps[3].rearrange("p (q t) -> p q t", q=4),
                               ESK[:, g * 4:(g + 1) * 4].unsqueeze(2).to_broadcast([128, 4, 128]), ALU.add,
                               [P[3], B_esk], [B_r32])
                            tt("dve", U3, ps[2].rearrange("p (q t) -> p q t", q=4), SG1v[par][:, g * 4:(g + 1) * 4, :],
                               ALU.mult, [P[2], B_sg1[par]], [B_u32])
                            act(R32, R32, AF.Ln, [B_r32], [B_r32])
                            act(R32, R32, AF.Exp, [B_r32], [B_r32], scale=-1.0)
                            tt("dve", OG1v[:, g * 4:(g + 1) * 4, :], U3, R3, ALU.mult, [B_u32, B_r32], [B_og1])
                        yield

                def tail():
                    for hf in range(2):
                        for kc in range(8):
                            mm(ps[4 + hf][:, :], OG1v[:, kc, :], W1Ov[:, kc, hf * 512:(hf + 1) * 512],
                               kc == 0, kc == 7, [B_og1, B_w1o], [P[4 + hf]])
                        tt("dve", h1[:, hf * 512:(hf + 1) * 512], ps[4 + hf][:, :], h1[:, hf * 512:(hf + 1) * 512],
                           ALU.add, [P[4 + hf], Bh1], [Bh1])
                    rmsnorm_block(h1, Bh1, 2, OUTB, B_outb, ST2, B_st2, JK1, B_jk1)
                    dma("sp", out_d[sq, (b - 1) * 128:b * 128, :], OUTB, [B_outb], [])

                return prologue, tile_gen, tail

            def drain(gen):
                for _ in gen:
                    pass

            drain(front(0))
            drain(front(1))
            parts = back_parts(1)
            parts[0]()
            for b in range(1, NB):
                prologue, tile_gen, tail = parts
                alive = [tile_gen()]
                if b + 1 < NB:
                    alive.append(front(b + 1))
                while alive:
                    for gq in list(alive):
                        try:
                            next(gq)
                        except StopIteration:
                            alive.remove(gq)
                if b + 1 < NB:
                    parts = back_parts(b + 1)
                    parts[0]()
                tail()

        S.finalize()
        with nc.Block() as block:
            @block.tensor
            def _(e):
                S.emit_engine("pe", e, sems, dma_sems)

            @block.scalar
            def _(e):
                S.emit_engine("act", e, sems, dma_sems)

            @block.vector
            def _(e):
                S.emit_engine("dve", e, sems, dma_sems)

            @block.gpsimd
            def _(e):
                S.emit_engine("pool", e, sems, dma_sems)

            @block.sync
            def _(e):
                S.emit_engine("sp", e, sems, dma_sems, final_wait=True)
    return nc


_NC_CACHE = {}


def _common_inputs(meta, norm_g, final_g, ev_w_in, ev_q_norm_g, ev_kv_norm_g, ev_w_uq, ev_w_ukv,
                   ev_w_out, od_w_in, od_sinks, od_w_out):
    cb, rope, swae, em, cm = _const_tables()
    f = lambda a: np.ascontiguousarray(np.asarray(a, dtype=np.float32))
    return {
        "meta": f(meta),
        "gains": f(np.concatenate([np.asarray(norm_g), np.asarray(final_g)[None, :]], 0)),
        "w0in": f(ev_w_in[0]), "qng": f(ev_q_norm_g[0]), "kvng": f(ev_kv_norm_g[0]),
        "wuq": f(ev_w_uq[0]), "wukv": f(ev_w_ukv[0]), "w0out": f(ev_w_out[0]),
        "w1in": f(od_w_in[0]), "sinks": f(od_sinks[0]), "w1out": f(od_w_out[0]),
        "cb": cb, "rope": f(rope), "swae": f(swae), "em": f(em), "cm": f(cm),
    }


def kernel(x, meta, norm_g, final_g, ev_w_in, ev_q_norm_g, ev_kv_norm_g, ev_w_uq, ev_w_ukv,
           ev_w_out, od_w_in, od_sinks, od_w_out):
    n = 8
    x = np.asarray(x, dtype=np.float32)
    common = _common_inputs(meta, norm_g, final_g, ev_w_in, ev_q_norm_g, ev_kv_norm_g, ev_w_uq,
                            ev_w_ukv, ev_w_out, od_w_in, od_sinks, od_w_out)
    if "nc" not in _NC_CACHE:
        _NC_CACHE["nc"] = build_nc()
    nc = _NC_CACHE["nc"]
    in_maps = []
    for c in range(n):
        m = dict(common)
        m["x"] = np.ascontiguousarray(x[c * SEQ_PER_CORE:(c + 1) * SEQ_PER_CORE])
        in_maps.append(m)
    res = run_bass_kernel_spmd(nc, in_maps, core_ids=list(range(n)))
    return np.concatenate([r["out"] for r in res.results], axis=0)
```

```python
import numpy as np
from contextlib import ExitStack
import concourse.bass as bass
import concourse.mybir as mybir
from concourse.bass_utils import run_bass_kernel_spmd

F32 = mybir.dt.float32
BF16 = mybir.dt.bfloat16
AF = mybir.ActivationFunctionType
ALU = mybir.AluOpType

ENGS = ["pe", "act", "dve", "pool", "sp"]

D = 1024
LP = 2176
NB = 17
NPAD = 112
EPS = 1e-6
SEQ_PER_CORE = 2
EV_IN = 2976
OD_IN = 2304


class Buf:
    __slots__ = ("name", "last_w", "readers", "excl")

    def __init__(self, name, excl=False):
        self.name = name
        self.last_w = None
        self.readers = []
        self.excl = excl


class Op:
    __slots__ = ("eng", "fn", "deps", "signal", "is_dma", "dsem", "dval", "cnt", "prev_dma")

    def __init__(self, eng, fn, is_dma):
        self.eng = eng
        self.fn = fn
        self.deps = []
        self.signal = False
        self.is_dma = is_dma
        self.dsem = None
        self.dval = 0
        self.cnt = 0
        self.prev_dma = None


class Sched:
    def __init__(self, n_dma_sems=8):
        self.ops = {e: [] for e in ENGS}
        self.n_dma_sems = n_dma_sems
        self.dma_rr = {e: 0 for e in ENGS}
        self.dma_cnt = {}
        self.dma_last = {}

    def op(self, eng, fn, reads=(), writes=(), dma=False):
        o = Op(eng, fn, dma)
        deps = {}
        for b in reads:
            if b.last_w is not None:
                deps[id(b.last_w)] = b.last_w
            if b.excl:
                for r in b.readers:
                    if r.eng != eng:
                        deps[id(r)] = r
        for b in writes:
            if b.last_w is not None:
                deps[id(b.last_w)] = b.last_w
            for r in b.readers:
                deps[id(r)] = r
        final = []
        for d in deps.values():
            if d is o:
                continue
            if d.eng == "pe" and eng == "pe" and (not d.is_dma) and (not dma):
                continue
            final.append(d)
            d.signal = True
        o.deps = final
        for b in reads:
            if not dma:
                b.readers = [r for r in b.readers if r.is_dma or r.eng != eng]
            b.readers.append(o)
        for b in writes:
            b.last_w = o
            b.readers = []
        if dma:
            k = self.dma_rr[eng]
            self.dma_rr[eng] = (k + 1) % self.n_dma_sems
            key = (eng, k)
            self.dma_cnt[key] = self.dma_cnt.get(key, 0) + 1
            o.dsem = key
            o.dval = 16 * self.dma_cnt[key]
            o.prev_dma = self.dma_last.get(key)
            self.dma_last[key] = o
        self.ops[eng].append(o)
        return o

    def barrier(self):
        lasts = []
        for e in ENGS:
            for o in reversed(self.ops[e]):
                if (not o.is_dma) and o.fn is not None:
                    lasts.append(o)
                    break
        dmas = list(self.dma_last.values())
        for e in ENGS:
            m = Op(e, None, False)
            m.deps = [d for d in lasts if d.eng != e] + dmas
            for d in m.deps:
                d.signal = True
            self.ops[e].append(m)

    def finalize(self):
        for e in ENGS:
            c = 0
            for o in self.ops[e]:
                if o.is_dma or o.fn is None:
                    continue
                if o.signal:
                    c += 1
                    o.cnt = c

    def emit_engine(self, eng_name, e, sems, dma_sems, final_wait=False):
        waited = {}

        def wait(key, sem, val):
            if val <= 0:
                return
            if waited.get(key, 0) < val:
                e.wait_ge(sem, val)
                waited[key] = val

        for o in self.ops[eng_name]:
            for d in o.deps:
                if d.is_dma:
                    wait(d.dsem, dma_sems[d.dsem], d.dval)
                else:
                    wait(d.eng, sems[d.eng], d.cnt)
            if o.is_dma and o.prev_dma is not None:
                wait(o.dsem, dma_sems[o.dsem], o.prev_dma.dval)
            if o.fn is None:
                continue
            ins = o.fn(e)
            if o.is_dma:
                ins.then_inc(dma_sems[o.dsem], 16)
            elif o.signal:
                ins.then_inc(sems[eng_name], 1)
        if final_wait:
            for key, o in self.dma_last.items():
                wait(key, dma_sems[key], o.dval)


def _const_tables():
    r = np.arange(128)
    s = r[:, None]
    t = r[None, :]
    ident = (s == t).astype(np.float32)
    negU = -(s >= t).astype(np.float32)
    negU0 = negU * (s >= NPAD)
    negOnes = -np.ones((128, 128), np.float32)
    negOnes0 = negOnes * (s >= NPAD)
    ones = np.ones((128, 128), np.float32)
    mstrict = (s < t).astype(np.float32)
    mincl = (s <= t).astype(np.float32)
    onesA = np.concatenate([np.ones((128, 64)), np.zeros((128, 64))], 1).astype(np.float32)
    onesB = np.concatenate([np.zeros((128, 64)), np.ones((128, 64))], 1).astype(np.float32)
    zeros = np.zeros((128, 128), np.float32)
    cb = np.concatenate([ident, negU, negU0, negOnes, negOnes0, ones, mstrict, mincl,
                         onesA, onesB, zeros], axis=1)
    half = 16
    inv = 10000.0 ** (-np.arange(half, dtype=np.float64) / half)
    pos = (np.arange(LP) - NPAD).astype(np.float64)
    ang = inv[:, None] * pos[None, :]
    cos = np.concatenate([np.cos(ang), np.cos(ang)], 0).astype(np.float32)
    sin = np.concatenate([np.sin(ang), np.sin(ang)], 0).astype(np.float32)
    rope = np.stack([cos, sin], 0)
    H = 16
    slopes = 2.0 ** (-8.0 * (np.arange(H, dtype=np.float64) + 1.0) / H)
    sl = slopes[None, :, None]
    dprev = (128 + r[None, None, :] - r[:, None, None]).astype(np.float64)
    eprev = np.where(dprev < 128, np.exp(-sl * dprev), 0.0)
    dcur = (r[None, None, :] - r[:, None, None]).astype(np.float64)
    ecur = np.where(dcur >= 0, np.exp(-sl * np.maximum(dcur, 0)), 0.0)
    m = np.arange(16)
    dm = (16 + r[None, None, :] - m[:, None, None]).astype(np.float64)
    em = np.exp(-sl * dm)
    n = np.arange(NB)
    cm = np.exp(-slopes[None, None, :] * 128.0 * np.maximum(n - 1, 0)[None, :, None])
    cm = np.broadcast_to(cm, (16, NB, H))
    swa_e = np.stack([eprev, ecur], 0).astype(np.float32)
    return (cb.astype(np.float32), rope, swa_e, em.astype(np.float32),
            np.ascontiguousarray(cm).astype(np.float32))


def build_nc(debug_h1=False, n_seq=SEQ_PER_CORE):
    nc = bass.Bass("TRN2", target_bir_lowering=False)
    dt = nc.dram_tensor
    x_d = dt("x", [n_seq, 2048, D], F32, kind="ExternalInput").ap()
    meta_d = dt("meta", [16, D], F32, kind="ExternalInput").ap()
    gains_d = dt("gains", [3, D], F32, kind="ExternalInput").ap()
    w0in_d = dt("w0in", [D, EV_IN], F32, kind="ExternalInput").ap()
    qng_d = dt("qng", [256], F32, kind="ExternalInput").ap()
    kvng_d = dt("kvng", [128], F32, kind="ExternalInput").ap()
    wuq_d = dt("wuq", [256, 768], F32, kind="ExternalInput").ap()
    wukv_d = dt("wukv", [128, 1024], F32, kind="ExternalInput").ap()
    w0out_d = dt("w0out", [D, D], F32, kind="ExternalInput").ap()
    w1in_d = dt("w1in", [D, OD_IN], F32, kind="ExternalInput").ap()
    sinks_d = dt("sinks", [16], F32, kind="ExternalInput").ap()
    w1out_d = dt("w1out", [D, D], F32, kind="ExternalInput").ap()
    cb_d = dt("cb", [128, 11 * 128], F32, kind="ExternalInput").ap()
    rope_d = dt("rope", [2, 32, LP], F32, kind="ExternalInput").ap()
    swae_d = dt("swae", [2, 128, 16, 128], F32, kind="ExternalInput").ap()
    em_d = dt("em", [16, 16, 128], F32, kind="ExternalInput").ap()
    cm_d = dt("cm", [16, NB, 16], F32, kind="ExternalInput").ap()
    out_d = dt("out", [n_seq, 2048, D], F32, kind="ExternalOutput").ap()
    if debug_h1:
        dbg_d = dt("dbg", [n_seq, LP, D], F32, kind="ExternalOutput").ap()
        dbg2_d = dt("dbg2", [n_seq, 128, 8 * LP], F32, kind="ExternalOutput").ap()

    S = Sched()
    with ExitStack() as es:
        ARENA_F32 = 53000
        arena = es.enter_context(nc.sbuf_tensor("arena", [128, ARENA_F32], F32))
        psq = [es.enter_context(nc.psum_tensor(f"psq{i}", [128, 1024], F32)) for i in range(4)]
        ps = [psq[i // 2][:, (i % 2) * 512:(i % 2 + 1) * 512] for i in range(8)]
        psP = [q_.rearrange("p (h n) -> p h n", h=2) for q_ in psq]
        sems = {e: es.enter_context(nc.semaphore(f"s_{e}")) for e in ENGS}
        dma_sems = {}
        for e in ["sp", "pool"]:
            for k in range(S.n_dma_sems):
                dma_sems[(e, k)] = es.enter_context(nc.semaphore(f"d_{e}{k}"))
        P = [Buf(f"ps{i}", excl=True) for i in range(8)]

        class Arena:
            def __init__(self, start=0):
                self.off = start

            def f32(self, n):
                a = arena[:, self.off:self.off + n]
                self.off += n
                return a

            def bf16(self, n):
                assert n % 2 == 0
                a = arena[:, self.off:self.off + n // 2].bitcast(BF16)
                self.off += n // 2
                return a

        A = Arena(0)
        CB = A.bf16(11 * 128)
        cb_v = lambda i: CB[:, i * 128:(i + 1) * 128]
        IDENT, NEGU, NEGU0, NEGONES, NEGONES0, ONES, MSTRICT, MINCL, ONESA, ONESB, ZEROS = [cb_v(i) for i in range(11)]
        G = A.f32(3 * D)
        NG = A.f32(4)
        FIN = A.bf16(512)
        OG = A.bf16(8 * LP)
        OGv = OG.rearrange("p (c t) -> p c t", c=8)
        pers_end = A.off
        B_cb, B_g, B_ng, B_fin, B_og = Buf("cb"), Buf("g"), Buf("ng"), Buf("fin"), Buf("og")

        def mm(out, lhsT, rhs, start, stop, reads, writes):
            S.op("pe", lambda e: e.matmul(out, lhsT=lhsT, rhs=rhs, start=start, stop=stop),
                 reads=reads, writes=writes)

        def tr(out, in_, reads, writes):
            S.op("pe", lambda e: e.transpose(out, in_, IDENT), reads=list(reads) + [B_cb], writes=writes)

        def act(out, in_, func, reads, writes, scale=1.0, bias=0.0, accum_out=None, eng="act"):
            if accum_out is None:
                S.op("act", lambda e: e.activation(out=out, in_=in_, func=func, bias=bias, scale=scale),
                     reads=reads, writes=writes)
            else:
                S.op("act", lambda e: e.activation(out=out, in_=in_, func=func, bias=bias, scale=scale,
                                                   accum_out=accum_out), reads=reads, writes=writes)

        def tt(eng, out, in0, in1, op, reads, writes):
            S.op(eng, lambda e: e.tensor_tensor(out=out, in0=in0, in1=in1, op=op), reads=reads, writes=writes)

        def tsc(eng, out, in0, s1, op0, reads, writes, s2=None, op1=None):
            if op1 is None:
                S.op(eng, lambda e: e.tensor_scalar(out=out, in0=in0, scalar1=s1, scalar2=None, op0=op0),
                     reads=reads, writes=writes)
            else:
                S.op(eng, lambda e: e.tensor_scalar(out=out, in0=in0, scalar1=s1, scalar2=s2, op0=op0, op1=op1),
                     reads=reads, writes=writes)

        def stt(eng, out, in0, scalar, in1, op0, op1, reads, writes):
            S.op(eng, lambda e: e.scalar_tensor_tensor(out=out, in0=in0, scalar=scalar, in1=in1, op0=op0, op1=op1),
                 reads=reads, writes=writes)

        def cp(eng, out, in_, reads, writes):
            if eng == "act":
                S.op("act", lambda e: e.copy(out=out, in_=in_), reads=reads, writes=writes)
            else:
                S.op(eng, lambda e: e.tensor_copy(out=out, in_=in_), reads=reads, writes=writes)

        def memset(eng, ap, val, writes):
            S.op(eng, lambda e: e.memset(ap, val), writes=writes)

        def dma(eng, out, in_, reads, writes):
            S.op(eng, lambda e: e.dma_start(out=out, in_=in_), reads=reads, writes=writes, dma=True)

        def recip(out, in_, reads, writes):
            S.op("dve", lambda e: e.reciprocal(out=out, in_=in_), reads=reads, writes=writes)

        def wload(dst3, src2, c0, ncols, writes, reads=()):
            kc = src2.shape[0] // 128
            srcv = src2.rearrange("(kc p) n -> p kc n", p=128)
            dma("pool", dst3, srcv[:, :, c0:c0 + ncols], reads, writes)

        dma("pool", CB, cb_d, [], [B_cb])
        for i in range(3):
            dma("sp", G[:, i * D:(i + 1) * D], gains_d[i:i + 1, :].broadcast_to([128, D]), [], [B_g])
        for c in range(2):
            dma("sp", NG[:, c:c + 1], qng_d[c * 128:(c + 1) * 128].rearrange("(p o) -> p o", o=1), [], [B_ng])
        dma("sp", NG[:, 2:3], kvng_d.rearrange("(p o) -> p o", o=1), [], [B_ng])
        memset("dve", FIN, 1.0, [B_fin])

        A0 = Arena(pers_end)
        HNT = A0.bf16(8 * LP); HNTv = HNT.rearrange("p (c t) -> p c t", c=8); B_hnt = Buf("hnt")
        WS = [A0.bf16(8 * 512) for _ in range(2)]; B_ws = [Buf("ws0"), Buf("ws1")]
        WSv = [w.rearrange("p (c n) -> p c n", c=8) for w in WS]
        WM = A0.bf16(8 * 416); WMv = WM.rearrange("p (c n) -> p c n", c=8); B_wm = Buf("wm")
        WKRR = A0.bf16(8 * 32); WKRRv = WKRR.rearrange("p (c n) -> p c n", c=8); B_wkrr = Buf("wkrr")
        WUQ = A0.bf16(2 * 768); WUQv = WUQ.rearrange("p (c n) -> p c n", c=2); B_wuq = Buf("wuq")
        WUQR = A0.bf16(2 * 256); WUQRv = WUQR.rearrange("p (c n) -> p c n", c=2); B_wuqr = Buf("wuqr")
        WUKV = A0.bf16(1024); B_wukv = Buf("wukv")
        A0_KT_START = A0.off
        KT = [A0.bf16(LP) for _ in range(2)]; B_kt = [Buf("kt0"), Buf("kt1")]
        QT = [A0.bf16(LP) for _ in range(2)]; B_qt = [Buf("qt0"), Buf("qt1")]
        VP = A0.bf16(NB * 256); VPv = VP.rearrange("p (b v n) -> p b v n", b=NB, v=2); B_vp = Buf("vp")
        SG = A0.bf16(LP); B_sg = Buf("sg")
        CQN = A0.bf16(2 * LP); CQNv = CQN.rearrange("p (c t) -> p c t", c=2); B_cqn = Buf("cqn")
        CKVN = A0.bf16(LP); B_ckvn = Buf("ckvn")
        KR = A0.bf16(LP); B_kr = Buf("kr")
        ROPE = A0.f32(2 * LP); ROPEv = ROPE.rearrange("p (c t) -> p c t", c=2); B_rope = Buf("rope")
        def pairbuf(ap):
            return ap.rearrange("p (h n) -> p h n", h=2), [ap[:, 0:512], ap[:, 512:1024]]
        E32P = A0.f32(1024); e32v, E32 = pairbuf(E32P); B_e32p = Buf("e32p"); B_e32 = [B_e32p, B_e32p]
        SPBP = A0.bf16(1024); spbv, SPB = pairbuf(SPBP); B_spbp = Buf("spbp"); B_spb = [B_spbp, B_spbp]
        SPBP2 = A0.bf16(1024); spbv2, SPB2 = pairbuf(SPBP2); B_spbp2 = Buf("spbp2")
        C32P = A0.f32(1024); c32v, C32 = pairbuf(C32P); B_c32p = Buf("c32p")
        C16P = A0.bf16(1024); c16v, C16 = pairbuf(C16P); B_c16p = Buf("c16p")
        ATBP = [A0.bf16(1024) for _ in range(2)]
        atbv = [pairbuf(a_)[0] for a_ in ATBP]
        ATB = [pairbuf(ATBP[i // 2])[1][i % 2] for i in range(4)]
        B_atbp = [Buf("atbp0"), Buf("atbp1")]; B_atb = [B_atbp[i // 2] for i in range(4)]
        XS = A0.f32(D); B_xs = Buf("xs")
        T32 = [XS[:, 0:512], XS[:, 512:1024]]; B_t32 = [Buf("t32a"), Buf("t32b")]
        HN = A0.bf16(D); B_hn = Buf("hn")
        ST = A0.f32(8); B_st = Buf("st"); B_stb = Buf("stb")
        JK = HN; B_jk = B_hn
        l0_end = A0.off
        assert l0_end <= ARENA_F32, l0_end

        A1 = Arena(pers_end)
        W0O = A1.bf16(8 * D); W0Ov = W0O.rearrange("p (c n) -> p c n", c=8); B_w0o = Buf("w0o")
        W1I = A1.bf16(8 * OD_IN); W1Iv = W1I.rearrange("p (c n) -> p c n", c=8); B_w1i = Buf("w1i")
        W1O = A1.bf16(8 * D); W1Ov = W1O.rearrange("p (c n) -> p c n", c=8); B_w1o = Buf("w1o")
        SWE = A1.f32(2 * 16 * 128); SWEv = SWE.rearrange("p (a h r) -> p a h r", a=2, h=16); B_swe = Buf("swe")
        EM = A1.f32(16 * 128); EMv = EM.rearrange("p (h r) -> p h r", h=16); B_em = Buf("em")
        CM = A1.f32(NB * 16); CMv = CM.rearrange("p (n h) -> p n h", n=NB); B_cm = Buf("cm")
        ESK = A1.f32(8); B_esk = Buf("esk")
        KT1 = A1.bf16(LP); B_kt1 = [Buf(f"kt1l{i}") for i in range(NB)]
        VR = A1.bf16(3 * 4 * 128); VRv = VR.rearrange("p (b v n) -> p b v n", b=3, v=4); B_vr = [Buf("vr0"), Buf("vr1"), Buf("vr2")]
        VM = A1.bf16(4 * 128); VMv = VM.rearrange("p (v n) -> p v n", v=4); B_vm = Buf("vm")
        H1 = [A1.f32(D) for _ in range(2)]; B_h1 = [Buf("h1a"), Buf("h1b")]
        XS1 = A1.f32(D); B_xs1 = Buf("xs1")
        HN1 = A1.bf16(D); B_hn1 = Buf("hn1")
        HNT1 = [A1.bf16(8 * 128) for _ in range(2)]; HNT1v = [h.rearrange("p (c t) -> p c t", c=8) for h in HNT1]
        B_hnt1 = [Buf("hnt1a"), Buf("hnt1b")]
        QG = [[A1.bf16(8 * 128) for _ in range(2)] for _ in range(2)]
        QGv = [[q.rearrange("p (h t) -> p h t", h=8) for q in qq] for qq in QG]
        B_qg = [[Buf(f"qg{i}{j}") for j in range(2)] for i in range(2)]
        SG1 = [A1.bf16(8 * 128) for _ in range(2)]; SG1v = [x_.rearrange("p (c t) -> p c t", c=8) for x_ in SG1]
        B_sg1 = [Buf("sg1a"), Buf("sg1b")]
        OG1 = A1.bf16(8 * 128); OG1v = OG1.rearrange("p (c t) -> p c t", c=8); B_og1 = Buf("og1")
        EX = [A1.f32(1024) for _ in range(2)]; EXv = [x_.rearrange("p (h t) -> p h t", h=8) for x_ in EX]
        B_ex = [Buf("exa"), Buf("exb")]
        PB = [A1.bf16(1024) for _ in range(2)]; PBv = [x_.rearrange("p (h t) -> p h t", h=8) for x_ in PB]
        B_pb = [Buf("pba"), Buf("pbb")]
        B_exh = [[Buf(f"exh{i}{j}") for j in range(2)] for i in range(2)]
        B_pbh = [[Buf(f"pbh{i}{j}") for j in range(2)] for i in range(2)]
        R32 = A1.f32(512); B_r32 = Buf("r32")
        U32 = A1.f32(512); B_u32 = Buf("u32")
        ST1 = A1.f32(8); B_st1 = Buf("st1")
        ST2 = A1.f32(8); B_st2 = Buf("st2")
        JK1 = HN1; B_jk1 = B_hn1
        OUTB = A1.f32(D); B_outb = Buf("outb")
        assert A1.off <= ARENA_F32, A1.off

        PT = ps[7].bitcast(BF16)

        TCH = [(c * 512, min(512, LP - c * 512)) for c in range(5)]
        QCH = [(0, 1), (1, 5), (5, 9), (9, 13), (13, 17)]

        def rmsnorm_block(src32, B_src, gidx, dst_bf, B_dst, st, B_st_, jk, B_jk_):
            act(jk, src32, AF.Square, [B_src], [B_jk_, B_st_], accum_out=st[:, 0:1])
            act(st[:, 1:2], st[:, 0:1], AF.Ln, [B_st_], [B_st_], scale=1.0 / D, bias=EPS)
            act(st[:, 2:3], st[:, 1:2], AF.Exp, [B_st_], [B_st_], scale=-0.5)
            stt("dve", dst_bf, src32, st[:, 2:3], G[:, gidx * D:(gidx + 1) * D], ALU.mult, ALU.mult,
                [B_src, B_st_, B_g], [B_dst])

        def transpose_block(hn_bf, B_hn_, dst3, B_dst):
            for kc in range(8):
                tr(PT[:, kc * 128:(kc + 1) * 128], hn_bf[:, kc * 128:(kc + 1) * 128], [B_hn_], [P[7]])
            cp("dve", dst3, PT.rearrange("p (c t) -> p c t", c=8), [P[7]], [B_dst])

        def proj_fm(psb, Pb, wv, c0, m, t0, n, B_w, hnt_v, B_h):
            for kc in range(8):
                mm(psb[0:m, 0:n], wv[:, kc, c0:c0 + m], hnt_v[:, kc, t0:t0 + n], kc == 0, kc == 7,
                   [B_w, B_h], [Pb])

        for sq in range(n_seq):
            S.barrier()
            wload(WMv, w0in_d, 2048, 416, [B_wm])
            wload(WUQv, wuq_d, 0, 768, [B_wuq])
            dma("pool", WUKV, wukv_d, [], [B_wukv])
            dma("sp", ROPEv[0:32], rope_d.rearrange("c p t -> p c t"), [], [B_rope])
            for i in range(2):
                memset("pool", KT[i], 0.0, [B_kt[i]])
                memset("pool", QT[i], 0.0, [B_qt[i]])
            memset("pool", VP, 0.0, [B_vp])
            for kc in range(8):
                tsc("pool", WKRRv[:, kc, 0:16], WMv[:, kc, 400:416], -1.0, ALU.mult, [B_wm], [B_wkrr])
                cp("pool", WKRRv[:, kc, 16:32], WMv[:, kc, 384:400], [B_wm], [B_wkrr])
            for kc in range(2):
                src = WUQv[:, kc, :].rearrange("p (h d) -> p h d", h=8)
                dst = WUQRv[:, kc, :].rearrange("p (h d) -> p h d", h=8)
                tsc("pool", dst[:, :, 0:16], src[:, :, 80:96], -1.0, ALU.mult, [B_wuq], [B_wuqr])
                cp("pool", dst[:, :, 16:32], src[:, :, 64:80], [B_wuq], [B_wuqr])

            XSs = [XS, E32P]; B_xss = [B_xs, B_e32p]
            HNs = [HN, C32P.bitcast(BF16)[:, 0:D]]; B_hns = [B_hn, B_c32p]
            STs = [ST[:, 0:4], ST[:, 4:8]]; B_sts = [B_st, B_stb]
            for b in range(NB):
                i = b % 2
                xs_, Bx_ = XSs[i], B_xss[i]
                if b == 0:
                    memset("dve", xs_, 0.0, [Bx_])
                    dma("sp", xs_[NPAD:128, :], meta_d, [], [Bx_])
                else:
                    dma("sp", xs_, x_d[sq, (b - 1) * 128:b * 128, :], [], [Bx_])
                rmsnorm_block(xs_, Bx_, 0, HNs[i], B_hns[i], STs[i], B_sts[i], HNs[i], B_hns[i])
                pk = 6 + i
                ptv = ps[pk].bitcast(BF16)
                for kc in range(8):
                    tr(ptv[:, kc * 128:(kc + 1) * 128], HNs[i][:, kc * 128:(kc + 1) * 128], [B_hns[i]], [P[pk]])
                cp("dve", HNTv[:, :, b * 128:(b + 1) * 128], ptv.rearrange("p (c t) -> p c t", c=8), [P[pk]], [B_hnt])

            def attention_pair(kts, B_ks, qts, B_qs, pair_chunk, sb, escale):
                M2 = lambda m_: m_.unsqueeze(1).to_broadcast([128, 2, 128])
                for ci, (ba, bz) in enumerate(QCH):
                    q0, q1 = ba * 128, bz * 128
                    N = q1 - q0
                    if sb:
                        psO, PO = ps[4 + (ci % 2)], P[4 + (ci % 2)]
                    else:
                        ob = 4 if ci % 2 == 0 else 6
                        psO, PO = ps[ob], P[ob]
                        psD, PD = ps[ob + 1], P[ob + 1]
                    mm(psO[:, 0:N], ZEROS, FIN[:, 0:N], True, False, [B_cb, B_fin], [PO])
                    if not sb:
                        mm(psD[:, 0:N], ZEROS, FIN[:, 0:N], True, False, [B_cb, B_fin], [PD])
                    steps = list(range(bz - 1, -1, -1))

                    def geom(j):
                        tq0 = max(q0, j * 128)
                        return tq0, q1 - tq0, tq0 - q0, j >= ba

                    def qk(si):
                        j = steps[si]
                        tq0, n, c0, diag = geom(j)
                        for hh in range(2):
                            bank = hh if sb else 2 * (si % 2) + hh
                            mm(ps[bank][:, 0:n], kts[hh][:, j * 128:(j + 1) * 128], qts[hh][:, tq0:q1], True, True,
                               [B_ks[hh], B_qs[hh]], [P[bank]])

                    if sb:
                        SPV = [(spbv, SPB, B_spbp), (spbv2, SPB2, B_spbp2)]
                        CSB = [(2, 3), (6, 7)]

                        def stage1(si):
                            j = steps[si]
                            tq0, n, c0, diag = geom(j)
                            sv, _, Bs = SPV[si % 2]
                            act(e32v[:, :, 0:n], psP[0][:, :, 0:n], AF.Exp, [P[0], P[1]], [B_e32p])
                            act(sv[:, :, 0:n], e32v[:, :, 0:n], AF.Ln, [B_e32p], [Bs], bias=1.0)
                            if diag:
                                tt("pool", sv[:, :, 0:128], sv[:, :, 0:128], M2(MSTRICT), ALU.mult, [Bs, B_cb], [Bs])

                        def cumsum(si):
                            j = steps[si]
                            tq0, n, c0, diag = geom(j)
                            _, sp2, Bs = SPV[si % 2]
                            cb_ = CSB[si % 2]
                            first = si == 0
                            for hh in range(2):
                                bk = cb_[hh]
                                mm(ps[bk][:, 0:n], kts[hh][:, j * 128:(j + 1) * 128], qts[hh][:, tq0:q1], True, False,
                                   [B_ks[hh], B_qs[hh]], [P[bk]])
                            for hh in range(2):
                                bk = cb_[hh]
                                mm(ps[bk][:, 0:n], NEGU0 if j == 0 else NEGU, sp2[hh][:, 0:n], False, first,
                                   [B_cb, Bs], [P[bk]])
                                if not first:
                                    mm(ps[bk][:, 0:n], NEGONES, C16[hh][:, c0:c0 + n], False, True,
                                       [B_cb, B_c16p], [P[bk]])

                        def carry(si):
                            j = steps[si]
                            tq0, n, c0, diag = geom(j)
                            sv, _, Bs = SPV[si % 2]
                            if j == 0:
                                return
                            if si == 0:
                                memset("pool", C32P, 0.0, [B_c32p])
                            tt("dve", c32v[:, :, c0:c0 + n], c32v[:, :, c0:c0 + n], sv[:, :, 0:n], ALU.add,
                               [B_c32p, Bs], [B_c32p])
                            cp("dve", c16v[:, :, 0:N], c32v[:, :, 0:N], [B_c32p], [B_c16p])

                        def exp2(si):
                            j = steps[si]
                            tq0, n, c0, diag = geom(j)
                            cb_ = CSB[si % 2]
                            e = si % 2
                            act(atbv[e][:, :, 0:n], psP[cb_[0] // 2][:, :, 0:n], AF.Exp, [P[cb_[0]], P[cb_[1]]], [B_atbp[e]])
                            if diag:
                                tt("pool", atbv[e][:, :, 0:128], atbv[e][:, :, 0:128], M2(MSTRICT), ALU.mult,
                                   [B_atbp[e], B_cb], [B_atbp[e]])

                        def av(si):
                            j = steps[si]
                            tq0, n, c0, diag = geom(j)
                            e = si % 2
                            for hh in range(2):
                                mm(psO[:, c0:c0 + n], VPv[:, j, hh, :], ATB[2 * e + hh][:, 0:n], False, j == 0 and hh == 1,
                                   [B_vp, B_atbp[e]], [PO])

                        ns = len(steps)
                        qk(0)
                        stage1(0)
                        if ns > 1:
                            qk(1)
                        for si in range(ns):
                            cumsum(si)
                            carry(si)
                            if si + 1 < ns:
                                stage1(si + 1)
                            if si + 2 < ns:
                                qk(si + 2)
                            exp2(si)
                            av(si)
                    else:
                        qk(0)
                    for si, j in (enumerate(steps) if not sb else []):
                        tq0, n, c0, diag = geom(j)
                        first = si == 0
                        last = j == 0
                        if True:
                            e = si % 2
                            act(atbv[e][:, :, 0:n], psP[e][:, :, 0:n], AF.Exp, [P[2 * e], P[2 * e + 1]], [B_atbp[e]],
                                scale=escale)
                            if diag:
                                tt("pool", atbv[e][:, :, 0:128], atbv[e][:, :, 0:128], M2(MINCL), ALU.mult,
                                   [B_atbp[e], B_cb], [B_atbp[e]])
                            if not last:
                                qk(si + 1)
                            mm(psO[:, c0:c0 + n], VPv[:, j, 0, :], ATB[2 * e][:, 0:n], False, last,
                               [B_vp, B_atbp[e]], [PO])
                            mm(psD[:, c0:c0 + n], VPv[:, j, 1, :], ATB[2 * e + 1][:, 0:n], False, last,
                               [B_vp, B_atbp[e]], [PD])
                    if sb:
                        tt("dve", OGv[:, pair_chunk, q0:q1], psO[:, 0:N], SG[:, q0:q1], ALU.mult,
                           [PO, B_sg], [B_og])
                    else:
                        t32, B_t = T32[ci % 2], B_t32[ci % 2]
                        u32 = E32[ci % 2]
                        lo, hi = slice(0, 64), slice(64, 128)
                        act(t32[hi, 0:N], psO[hi, 0:N], AF.Ln, [PO], [B_t], bias=1e-30)
                        act(t32[lo, 0:N], psD[lo, 0:N], AF.Ln, [PD], [B_t], bias=1e-30)
                        act(t32[:, 0:N], t32[:, 0:N], AF.Exp, [B_t], [B_t], scale=-1.0)
                        cp("dve", u32[lo, 0:N], t32[hi, 0:N], [B_t], [B_e32p])
                        cp("dve", u32[hi, 0:N], t32[lo, 0:N], [B_t], [B_e32p])
                        tt("dve", u32[:, 0:N], u32[:, 0:N], SG[:, q0:q1], ALU.mult, [B_e32p, B_sg], [B_e32p])
                        tt("dve", OGv[lo, pair_chunk, q0:q1], psO[lo, 0:N], u32[lo, 0:N], ALU.mult,
                           [PO, B_e32p], [B_og])
                        tt("dve", OGv[hi, pair_chunk, q0:q1], psD[hi, 0:N], u32[hi, 0:N], ALU.mult,
                           [PD, B_e32p], [B_og])

            def attention_pair_sb(kts, B_ks, qts, B_qs, pair_chunk):
                M2 = lambda m_: m_.unsqueeze(1).to_broadcast([128, 2, 128])
                SPV = [(spbv, SPB, B_spbp), (spbv2, SPB2, B_spbp2)]
                CSB = [(2, 3), (6, 7)]

                class Chunk:
                    pass

                def mk_chunk(ci):
                    ba, bz = QCH[ci]
                    q0, q1 = ba * 128, bz * 128
                    N = q1 - q0
                    psO, PO = ps[4 + (ci % 2)], P[4 + (ci % 2)]
                    steps = list(range(bz - 1, -1, -1))
                    ns = len(steps)

                    def geom(si):
                        j = steps[si]
                        tq0 = max(q0, j * 128)
                        return j, tq0, q1 - tq0, tq0 - q0, j >= ba

                    def qk(si):
                        j, tq0, n, c0, diag = geom(si)
                        for hh in range(2):
                            mm(ps[hh][:, 0:n], kts[hh][:, j * 128:(j + 1) * 128], qts[hh][:, tq0:q1], True, True,
                               [B_ks[hh], B_qs[hh]], [P[hh]])

                    def stage1(si):
                        j, tq0, n, c0, diag = geom(si)
                        sv, _, Bs = SPV[si % 2]
                        act(e32v[:, :, 0:n], psP[0][:, :, 0:n], AF.Exp, [P[0], P[1]], [B_e32p])
                        act(sv[:, :, 0:n], e32v[:, :, 0:n], AF.Ln, [B_e32p], [Bs], bias=1.0)
                        if diag:
                            tt("pool", sv[:, :, 0:128], sv[:, :, 0:128], M2(MSTRICT), ALU.mult, [Bs, B_cb], [Bs])

                    def cumsum(si):
                        j, tq0, n, c0, diag = geom(si)
                        _, sp2, Bs = SPV[si % 2]
                        cb_ = CSB[si % 2]
                        first = si == 0
                        for hh in range(2):
                            bk = cb_[hh]
                            mm(ps[bk][:, 0:n], kts[hh][:, j * 128:(j + 1) * 128], qts[hh][:, tq0:q1], True, False,
                               [B_ks[hh], B_qs[hh]], [P[bk]])
                        for hh in range(2):
                            bk = cb_[hh]
                            mm(ps[bk][:, 0:n], NEGU0 if j == 0 else NEGU, sp2[hh][:, 0:n], False, first,
                               [B_cb, Bs], [P[bk]])
                            if not first:
                                mm(ps[bk][:, 0:n], NEGONES, C16[hh][:, c0:c0 + n], False, True,
                                   [B_cb, B_c16p], [P[bk]])

                    def carry(si):
                        j, tq0, n, c0, diag = geom(si)
                        sv, _, Bs = SPV[si % 2]
                        if j == 0:
                            return
                        if si == 0:
                            memset("pool", C32P, 0.0, [B_c32p])
                        tt("dve", c32v[:, :, c0:c0 + n], c32v[:, :, c0:c0 + n], sv[:, :, 0:n], ALU.add,
                           [B_c32p, Bs], [B_c32p])
                        cp("dve", c16v[:, :, 0:N], c32v[:, :, 0:N], [B_c32p], [B_c16p])

                    def exp2(si):
                        j, tq0, n, c0, diag = geom(si)
                        cb_ = CSB[si % 2]
                        e = si % 2
                        act(atbv[e][:, :, 0:n], psP[cb_[0] // 2][:, :, 0:n], AF.Exp, [P[cb_[0]], P[cb_[1]]], [B_atbp[e]])
                        if diag:
                            tt("pool", atbv[e][:, :, 0:128], atbv[e][:, :, 0:128], M2(MSTRICT), ALU.mult,
                               [B_atbp[e], B_cb], [B_atbp[e]])

                    def av(si):
                        j, tq0, n, c0, diag = geom(si)
                        e = si % 2
                        for hh in range(2):
                            mm(psO[:, c0:c0 + n], VPv[:, j, hh, :], ATB[2 * e + hh][:, 0:n], False, j == 0 and hh == 1,
                               [B_vp, B_atbp[e]], [PO])

                    def prologue():
                        mm(psO[:, 0:N], ZEROS, FIN[:, 0:N], True, False, [B_cb, B_fin], [PO])
                        qk(0)
                        stage1(0)
                        if ns > 1:
                            qk(1)

                    def finalize():
                        tt("dve", OGv[:, pair_chunk, q0:q1], psO[:, 0:N], SG[:, q0:q1], ALU.mult,
                           [PO, B_sg], [B_og])

                    c = Chunk()
                    c.ns, c.qk, c.stage1, c.cumsum, c.carry, c.exp2, c.av = ns, qk, stage1, cumsum, carry, exp2, av
                    c.prologue, c.finalize = prologue, finalize
                    return c

                chunks = [mk_chunk(ci) for ci in range(len(QCH))]
                chunks[0].prologue()
                chunks[0].cumsum(0)
                for ci, ch in enumerate(chunks):
                    nxt = chunks[ci + 1] if ci + 1 < len(chunks) else None
                    for si in range(ch.ns):
                        lastst = si == ch.ns - 1
                        ch.carry(si)
                        if si + 1 < ch.ns:
                            ch.stage1(si + 1)
                        if si + 2 < ch.ns:
                            ch.qk(si + 2)
                        if lastst and nxt is not None:
                            nxt.prologue()
                        ch.exp2(si)
                        if not lastst:
                            ch.cumsum(si + 1)
                        elif nxt is not None:
                            nxt.cumsum(0)
                        ch.av(si)
                    ch.finalize()

            bk_rot = [0]
            BK_ORDER = [[6, 7]]

            def nbk():
                bk_rot[0] = (bk_rot[0] + 1) % len(BK_ORDER[0])
                return BK_ORDER[0][bk_rot[0]]

            def build_v(lhs_fn, rhs_fn, nk, reads):
                for b0 in range(0, NB, 4):
                    nb4 = min(4, NB - b0)
                    k_ = nbk()
                    for i in range(nb4):
                        b = b0 + i
                        for kc in range(nk):
                            mm(ps[k_][:, i * 128:(i + 1) * 128], lhs_fn(kc, b), rhs_fn(kc), kc == 0, kc == nk - 1,
                               reads, [P[k_]])
                    pv = ps[k_][:, 0:nb4 * 128].rearrange("p (b n) -> p b n", b=nb4)
                    cp("dve", VPv[:, b0:b0 + nb4, 0, 0:64], pv[:, :, 0:64], [P[k_]], [B_vp])
                    cp("dve", VPv[:, b0:b0 + nb4, 1, 64:128], pv[:, :, 64:128], [P[k_]], [B_vp])

            BK_ORDER[0] = [6, 7, 0, 1, 2, 3]

            def pj(wv, c0, m, t0, n, B_w, src_v, B_src, nkc=8):
                k_ = nbk()
                for kc in range(nkc):
                    mm(ps[k_][0:m, 0:n], wv[:, kc, c0:c0 + m], src_v[:, kc, t0:t0 + n], kc == 0, kc == nkc - 1,
                       [B_w, B_src], [P[k_]])
                return ps[k_], P[k_]

            for p in range(4):
                ws, wsv, B_w = WS[p % 2], WSv[p % 2], B_ws[p % 2]
                wv4 = ws.rearrange("p (c f n) -> p c f n", c=8, f=4)
                srcv = w0in_d.rearrange("(kc p) n -> p kc n", p=128)
                for f in range(4):
                    dma("pool", wv4[:, :, f, :], srcv[:, :, f * 512 + p * 128:f * 512 + (p + 1) * 128], [], [B_w])
                for (t0, n) in TCH:
                    pa, Pa = pj(wsv, 128, 128, t0, n, B_w, HNTv, B_hnt)
                    cp("act", KT[0][:, t0:t0 + n], pa[:, 0:n], [Pa], [B_kt[0]])
                    pa, Pa = pj(wsv, 0, 128, t0, n, B_w, HNTv, B_hnt)
                    tsc("dve", QT[0][0:64, t0:t0 + n], pa[0:64, 0:n], 0.125, ALU.mult, [Pa], [B_qt[0]])
                    tsc("dve", QT[1][64:128, t0:t0 + n], pa[64:128, 0:n], 0.125, ALU.mult, [Pa], [B_qt[1]])
                    pa, Pa = pj(wsv, 384, 128, t0, n, B_w, HNTv, B_hnt)
                    act(SG[:, t0:t0 + n], pa[:, 0:n], AF.Silu, [Pa], [B_sg])
                build_v(lambda kc, b: HNTv[:, kc, b * 128:(b + 1) * 128], lambda kc, wsv=wsv: wsv[:, kc, 256:384], 8,
                        [B_hnt, B_w])
                attention_pair_sb([KT[0], KT[0]], [B_kt[0], B_kt[0]], QT, B_qt, p)

            memset("pool", VPv[:, :, 0, 64:128], 1.0, [B_vp])
            memset("pool", VPv[:, :, 1, 0:64], 1.0, [B_vp])
            BK_ORDER[0] = [6, 7, 0, 1, 2, 3]
            for i in range(2):
                memset("pool", KT[i], 0.0, [B_kt[i]])
                memset("pool", QT[i], 0.0, [B_qt[i]])
                memset("pool", KT[i][96:97, 0:NPAD], -30000.0, [B_kt[i]])
                memset("pool", QT[i][96:97, :], 1.0, [B_qt[i]])
            CKVNv1 = CKVN.rearrange("p (c t) -> p c t", c=1)
            for (t0, n) in TCH:
                for cc in range(2):
                    pa, Pa = pj(WMv, cc * 128, 128, t0, n, B_wm, HNTv, B_hnt)
                    cp("dve", T32[cc][:, 0:n], pa[:, 0:n], [Pa], [B_t32[cc]])
                    act(ATB[cc][:, 0:n], pa[:, 0:n], AF.Square, [Pa], [B_atb[cc]])
                k_ = nbk()
                for cc in range(2):
                    mm(ps[k_][:, 0:n], ONES, ATB[cc][:, 0:n], cc == 0, cc == 1, [B_cb, B_atb[cc]], [P[k_]])
                act(E32[0][:, 0:n], ps[k_][:, 0:n], AF.Ln, [P[k_]], [B_e32[0]], scale=1.0 / 256, bias=EPS)
                act(E32[0][:, 0:n], E32[0][:, 0:n], AF.Exp, [B_e32[0]], [B_e32[0]], scale=-0.5)
                for cc in range(2):
                    stt("dve", CQNv[:, cc, t0:t0 + n], T32[cc][:, 0:n], NG[:, cc:cc + 1], E32[0][:, 0:n],
                        ALU.mult, ALU.mult, [B_t32[cc], B_ng, B_e32[0]], [B_cqn])
                pa, Pa = pj(WMv, 256, 128, t0, n, B_wm, HNTv, B_hnt)
                cp("dve", T32[0][:, 0:n], pa[:, 0:n], [Pa], [B_t32[0]])
                act(ATB[2][:, 0:n], pa[:, 0:n], AF.Square, [Pa], [B_atb[2]])
                k_ = nbk()
                mm(ps[k_][:, 0:n], ONES, ATB[2][:, 0:n], True, True, [B_cb, B_atb[2]], [P[k_]])
                act(E32[1][:, 0:n], ps[k_][:, 0:n], AF.Ln, [P[k_]], [B_e32[1]], scale=1.0 / 128, bias=EPS)
                act(E32[1][:, 0:n], E32[1][:, 0:n], AF.Exp, [B_e32[1]], [B_e32[1]], scale=-0.5)
                stt("dve", CKVN[:, t0:t0 + n], T32[0][:, 0:n], NG[:, 2:3], E32[1][:, 0:n],
                    ALU.mult, ALU.mult, [B_t32[0], B_ng, B_e32[1]], [B_ckvn])
                pa, Pa = pj(WMv, 384, 32, t0, n, B_wm, HNTv, B_hnt)
                pb_, Pb_ = pj(WKRRv, 0, 32, t0, n, B_wkrr, HNTv, B_hnt)
                tt("dve", T32[0][0:32, 0:n], pa[0:32, 0:n], ROPEv[0:32, 0, t0:t0 + n], ALU.mult,
                   [Pa, B_rope], [B_t32[0]])
                tt("dve", T32[1][0:32, 0:n], pb_[0:32, 0:n], ROPEv[0:32, 1, t0:t0 + n], ALU.mult,
                   [Pb_, B_rope], [B_t32[1]])
                tt("dve", KR[0:32, t0:t0 + n], T32[0][0:32, 0:n], T32[1][0:32, 0:n], ALU.add,
                   [B_t32[0], B_t32[1]], [B_kr])

            WUKVv = WUKV.rearrange("p (h a d) -> p h a d", h=8, a=2)
            for p in range(4):
                ws, wsv, B_w = WS[p % 2], WSv[p % 2], B_ws[p % 2]
                wload(wsv[:, :, 0:128], w0in_d, 2464 + p * 128, 128, [B_w])
                for (t0, n) in TCH:
                    pa, Pa = pj(wsv, 0, 128, t0, n, B_w, HNTv, B_hnt)
                    act(SG[:, t0:t0 + n], pa[:, 0:n], AF.Silu, [Pa], [B_sg])
                    for hh in range(2):
                        h = 2 * p + hh
                        k_ = nbk()
                        mm(ps[k_][0:64, 0:n], WUKVv[:, h, 0, :], CKVN[:, t0:t0 + n], True, True,
                           [B_wukv, B_ckvn], [P[k_]])
                        cp("act", KT[hh][0:64, t0:t0 + n], ps[k_][0:64, 0:n], [P[k_]], [B_kt[hh]])
                        cp("pool", KT[hh][64:96, t0:t0 + n], KR[0:32, t0:t0 + n], [B_kr], [B_kt[hh]])
                        pa, Pa = pj(WUQv, h * 96, 64, t0, n, B_wuq, CQNv, B_cqn, nkc=2)
                        cp("act", QT[hh][0:64, t0:t0 + n], pa[0:64, 0:n], [Pa], [B_qt[hh]])
                        px, Px = pj(WUQv, h * 96 + 64, 32, t0, n, B_wuq, CQNv, B_cqn, nkc=2)
                        pr_, Pr_ = pj(WUQRv, h * 32, 32, t0, n, B_wuqr, CQNv, B_cqn, nkc=2)
                        tt("dve", T32[0][0:32, 0:n], px[0:32, 0:n], ROPEv[0:32, 0, t0:t0 + n], ALU.mult,
                           [Px, B_rope], [B_t32[0]])
                        tt("dve", T32[1][0:32, 0:n], pr_[0:32, 0:n], ROPEv[0:32, 1, t0:t0 + n], ALU.mult,
                           [Pr_, B_rope], [B_t32[1]])
                        tt("dve", QT[hh][64:96, t0:t0 + n], T32[0][0:32, 0:n], T32[1][0:32, 0:n], ALU.add,
                           [B_t32[0], B_t32[1]], [B_qt[hh]])
                build_v(lambda kc, b: CKVN[:, b * 128:(b + 1) * 128], lambda kc, p=p: WUKVv[:, 2 * p:2 * p + 2, 1, :], 1,
                        [B_ckvn, B_wukv])
                if p == 3:
                    assert pers_end + 4096 + 9216 <= A0_KT_START - (128 + 768 + 256 + 512)
                    dead = [B_hnt, B_ws[0], B_ws[1], B_wm]
                    for c in range(0, 1024, 512):
                        wload(W0Ov[:, :, c:c + 512], w0out_d, c, 512, [B_w0o] + dead)
                    for c in range(0, OD_IN, 576):
                        wload(W1Iv[:, :, c:c + 576], w1in_d, c, 576, [B_w1i] + dead)
                attention_pair(KT, B_kt, QT, B_qt, 4 + p, False, 96.0 ** -0.5)

            if debug_h1:
                for c in range(8):
                    dma("pool", dbg2_d[sq, :, c * LP:(c + 1) * LP], OG[:, c * LP:(c + 1) * LP], [B_og], [])
            S.barrier()
            for c in range(0, 1024, 512):
                wload(W1Ov[:, :, c:c + 512], w1out_d, c, 512, [B_w1o])
            dma("sp", SWEv, swae_d.rearrange("a p h r -> p a h r"), [], [B_swe])
            dma("sp", EMv[0:16], em_d, [], [B_em])
            dma("sp", CMv[0:16], cm_d, [], [B_cm])
            sk2 = sinks_d.rearrange("(p two) -> two p", two=2)
            S.op("sp", lambda e: e.dma_start(out=ESK[0:64, :], in_=sk2[0:1, :].broadcast_to([64, 8]),
                                             allow_slow_non_contiguous=True), writes=[B_esk], dma=True)
            S.op("sp", lambda e: e.dma_start(out=ESK[64:128, :], in_=sk2[1:2, :].broadcast_to([64, 8]),
                                             allow_slow_non_contiguous=True), writes=[B_esk], dma=True)
            act(ESK, ESK, AF.Exp, [B_esk], [B_esk])
            for par in range(2):
                for g in range(2):
                    memset("pool", QG[par][g], 0.0, [B_qg[par][g]])
            memset("pool", VM, 0.0, [B_vm])
            memset("pool", VR, 0.0, B_vr)

            rot = [0]

            def pbank():
                rot[0] ^= 1
                return 6 + rot[0]

            def front(b):
                par = b % 2
                h1, Bh1 = H1[par], B_h1[par]
                hv, Bhv = HNT1v[par], B_hnt1[par]
                if b == 0:
                    memset("dve", XS1, 0.0, [B_xs1])
                    dma("sp", XS1[NPAD:128, :], meta_d, [], [B_xs1])
                else:
                    dma("sp", XS1, x_d[sq, (b - 1) * 128:b * 128, :], [], [B_xs1])
                for hf in range(2):
                    for kc in range(8):
                        mm(ps[4 + hf][:, :], OGv[:, kc, b * 128:(b + 1) * 128], W0Ov[:, kc, hf * 512:(hf + 1) * 512],
                           kc == 0, kc == 7, [B_og, B_w0o], [P[4 + hf]])
                    tt("dve", h1[:, hf * 512:(hf + 1) * 512], ps[4 + hf][:, :], XS1[:, hf * 512:(hf + 1) * 512],
                       ALU.add, [P[4 + hf], B_xs1], [Bh1])
                if debug_h1:
                    dma("sp", dbg_d[sq, b * 128:(b + 1) * 128, :], h1, [Bh1], [])
                yield
                rmsnorm_block(h1, Bh1, 1, HN1, B_hn1, ST1, B_st1, JK1, B_jk1)
                yield
                pk = pbank()
                ptv = ps[pk].bitcast(BF16)
                for kc in range(8):
                    tr(ptv[:, kc * 128:(kc + 1) * 128], HN1[:, kc * 128:(kc + 1) * 128], [B_hn1], [P[pk]])
                cp("dve", hv, ptv.rearrange("p (c t) -> p c t", c=8), [P[pk]], [Bhv])
                yield
                pk = pbank()
                for kc in range(8):
                    mm(ps[pk][:, 0:128], W1Iv[:, kc, 1024:1152], hv[:, kc, :], kc == 0, kc == 7,
                       [B_w1i, Bhv], [P[pk]])
                cp("act", KT1[:, b * 128:(b + 1) * 128], ps[pk][:, 0:128], [P[pk]], [B_kt1[b]])
                slot = b % 3
                pk = pbank()
                if b == 0:
                    for kc in range(8):
                        mm(ps[pk][0:16, 0:128], hv[:, kc, NPAD:128], W1Iv[:, kc, 1152:1280], kc == 0, kc == 7,
                           [Bhv, B_w1i], [P[pk]])
                    for kh in range(2):
                        cp("dve", VMv[0:16, 2 * kh, 0:64], ps[pk][0:16, kh * 64:(kh + 1) * 64], [P[pk]], [B_vm])
                        cp("dve", VMv[0:16, 2 * kh + 1, 64:128], ps[pk][0:16, kh * 64:(kh + 1) * 64], [P[pk]], [B_vm])
                    return
                for kc in range(8):
                    mm(ps[pk][:, 0:128], hv[:, kc, :], W1Iv[:, kc, 1152:1280], kc == 0, kc == 7,
                       [Bhv, B_w1i], [P[pk]])
                for kh in range(2):
                    cp("dve", VRv[:, slot, 2 * kh, 0:64], ps[pk][:, kh * 64:(kh + 1) * 64], [P[pk]], [B_vr[slot]])
                    cp("dve", VRv[:, slot, 2 * kh + 1, 64:128], ps[pk][:, kh * 64:(kh + 1) * 64], [P[pk]], [B_vr[slot]])
                yield
                for q4 in range(2):
                    pk = pbank()
                    for i in range(4):
                        pr = q4 * 4 + i
                        for kc in range(8):
                            mm(ps[pk][:, i * 128:(i + 1) * 128], W1Iv[:, kc, pr * 128:(pr + 1) * 128], hv[:, kc, :],
                               kc == 0, kc == 7, [B_w1i, Bhv], [P[pk]])
                    g = q4
                    gs = slice(g * 64, g * 64 + 64)
                    pv = ps[pk].rearrange("p (i t) -> p i t", i=4)
                    qv = QGv[par][g][gs].rearrange("p (i two) t -> p two i t", two=2)
                    cp("act" if g == 0 else "dve", qv[:, 0], pv[0:64], [P[pk]], [B_qg[par][g]])
                    cp("dve" if g == 0 else "act", qv[:, 1], pv[64:128], [P[pk]], [B_qg[par][g]])
                    yield
                for c4 in range(2):
                    pk = pbank()
                    for i in range(4):
                        cc = c4 * 4 + i
                        for kc in range(8):
                            mm(ps[pk][:, i * 128:(i + 1) * 128], W1Iv[:, kc, 1280 + cc * 128:1280 + (cc + 1) * 128],
                               hv[:, kc, :], kc == 0, kc == 7, [B_w1i, Bhv], [P[pk]])
                    act(SG1v[par][:, c4 * 4:(c4 + 1) * 4, :], ps[pk].rearrange("p (i t) -> p i t", i=4), AF.Silu,
                        [P[pk]], [B_sg1[par]])
                    yield

            def back_parts(b):
                par = b % 2
                slot = b % 3
                h1, Bh1 = H1[par], B_h1[par]
                tiles = []
                for g in range(2):
                    if b >= 2:
                        tiles.append((g, "prev", KT1[:, (b - 1) * 128:b * 128], 128, (b - 1) % 3, B_kt1[b - 1]))
                    tiles.append((g, "cur", KT1[:, b * 128:(b + 1) * 128], 128, slot, B_kt1[b]))
                    tiles.append((g, "meta", KT1[:, NPAD:128], 16, None, B_kt1[0]))
                nt = len(tiles)

                def qk(ti):
                    g, kind, kk, nk, vs, Bk = tiles[ti]
                    for hf in range(2):
                        mm(ps[hf][0:nk, :], kk, QGv[par][g][:, hf * 4:(hf + 1) * 4, :], True, True,
                           [Bk, B_qg[par][g]], [P[hf]])

                def soft(ti):
                    g, kind, kk, nk, vs, Bk = tiles[ti]
                    e = ti % 2
                    exv, pbv = EXv[e], PBv[e]
                    for hf in range(2):
                        act(EX[e][0:nk, hf * 512:(hf + 1) * 512], ps[hf][0:nk, :], AF.Exp, [P[hf]], [B_exh[e][hf]],
                            scale=0.125)
                        if kind != "meta":
                            a = 0 if kind == "prev" else 1
                            hs = slice(hf * 4, (hf + 1) * 4)
                            tt("pool" if hf == 0 else "dve", pbv[:, hs, :], exv[:, hs, :],
                               SWEv[:, a, g * 8 + hf * 4:g * 8 + (hf + 1) * 4, :], ALU.mult,
                               [B_exh[e][hf], B_swe], [B_pbh[e][hf]])
                    if kind == "meta":
                        tt("dve", exv[0:16], exv[0:16], EMv[0:16, g * 8:(g + 1) * 8, :], ALU.mult,
                           B_exh[e] + [B_em], B_exh[e])
                        tt("dve", pbv[0:16], exv[0:16],
                           CMv[0:16, b, g * 8:(g + 1) * 8].unsqueeze(2).to_broadcast([16, 8, 128]), ALU.mult,
                           B_exh[e] + [B_cm], B_pbh[e])

                def prologue():
                    qk(0)
                    soft(0)
                    if nt > 1:
                        qk(1)

                def tile_gen():
                    for ti, (g, kind, kk, nk, vs, Bk) in enumerate(tiles):
                        e = ti % 2
                        pbv, B_p = PBv[e], B_pbh[e]
                        if kind == "meta":
                            va, vb2 = VMv[0:16, 2 * g, :], VMv[0:16, 2 * g + 1, :]
                            B_v = B_vm
                        else:
                            va, vb2 = VRv[:, vs, 2 * g, :], VRv[:, vs, 2 * g + 1, :]
                            B_v = B_vr[vs]
                        pe_ = pbv[0:nk].rearrange("p (q two) t -> p two q t", two=2)
                        gfirst = kind == ("prev" if b >= 2 else "cur")
                        glast = kind == "meta"
                        mm(ps[2][:, :], va, pe_[:, 0], gfirst, False, [B_v] + B_p, [P[2]])
                        mm(ps[2][:, :], vb2, pe_[:, 1], False, glast, [B_v] + B_p, [P[2]])
                        mm(ps[3][:, :], ONESA[0:nk, :], pe_[:, 0], gfirst, False, [B_cb] + B_p, [P[3]])
                        mm(ps[3][:, :], ONESB[0:nk, :], pe_[:, 1], False, glast, [B_cb] + B_p, [P[3]])
                        if ti + 1 < nt:
                            soft(ti + 1)
                        if ti + 2 < nt:
                            qk(ti + 2)
                        if glast:
                            R3 = R32.rearrange("p (q t) -> p q t", q=4)
                            U3 = U32.rearrange("p (q t) -> p q t", q=4)
                            tt("dve", R3, ps[3].rearrange("p (q t) -> p q t", q=4),
                               ESK[:, g * 4:(g + 1) * 4].unsqueeze(2).to_broadcast([128, 4, 128]), ALU.add,
                               [P[3], B_esk], [B_r32])
                            tt("dve", U3, ps[2].rearrange("p (q t) -> p q t", q=4), SG1v[par][:, g * 4:(g + 1) * 4, :],
                               ALU.mult, [P[2], B_sg1[par]], [B_u32])
                            act(R32, R32, AF.Ln, [B_r32], [B_r32])
                            act(R32, R32, AF.Exp, [B_r32], [B_r32], scale=-1.0)
                            tt("dve", OG1v[:, g * 4:(g + 1) * 4, :], U3, R3, ALU.mult, [B_u32, B_r32], [B_og1])
                        yield

                def tail():
                    for hf in range(2):
                        for kc in range(8):
                            mm(ps[4 + hf][:, :], OG1v[:, kc, :], W1Ov[:, kc, hf * 512:(hf + 1) * 512],
                               kc == 0, kc == 7, [B_og1, B_w1o], [P[4 + hf]])
                        tt("dve", h1[:, hf * 512:(hf + 1) * 512], ps[4 + hf][:, :], h1[:, hf * 512:(hf + 1) * 512],
                           ALU.add, [P[4 + hf], Bh1], [Bh1])
                    rmsnorm_block(h1, Bh1, 2, OUTB, B_outb, ST2, B_st2, JK1, B_jk1)
                    dma("sp", out_d[sq, (b - 1) * 128:b * 128, :], OUTB, [B_outb], [])

                return prologue, tile_gen, tail

            def drain(gen):
                for _ in gen:
                    pass

            drain(front(0))
            drain(front(1))
            parts = back_parts(1)
            parts[0]()
            for b in range(1, NB):
                prologue, tile_gen, tail = parts
                alive = [tile_gen()]
                if b + 1 < NB:
                    alive.append(front(b + 1))
                while alive:
                    for gq in list(alive):
                        try:
                            next(gq)
                        except StopIteration:
                            alive.remove(gq)
                if b + 1 < NB:
                    parts = back_parts(b + 1)
                    parts[0]()
                tail()

        S.finalize()
        with nc.Block() as block:
            @block.tensor
            def _(e):
                S.emit_engine("pe", e, sems, dma_sems)

            @block.scalar
            def _(e):
                S.emit_engine("act", e, sems, dma_sems)

            @block.vector
            def _(e):
                S.emit_engine("dve", e, sems, dma_sems)

            @block.gpsimd
            def _(e):
                S.emit_engine("pool", e, sems, dma_sems)

            @block.sync
            def _(e):
                S.emit_engine("sp", e, sems, dma_sems, final_wait=True)
    return nc


_NC_CACHE = {}


def _common_inputs(meta, norm_g, final_g, ev_w_in, ev_q_norm_g, ev_kv_norm_g, ev_w_uq, ev_w_ukv,
                   ev_w_out, od_w_in, od_sinks, od_w_out):
    cb, rope, swae, em, cm = _const_tables()
    f = lambda a: np.ascontiguousarray(np.asarray(a, dtype=np.float32))
    return {
        "meta": f(meta),
        "gains": f(np.concatenate([np.asarray(norm_g), np.asarray(final_g)[None, :]], 0)),
        "w0in": f(ev_w_in[0]), "qng": f(ev_q_norm_g[0]), "kvng": f(ev_kv_norm_g[0]),
        "wuq": f(ev_w_uq[0]), "wukv": f(ev_w_ukv[0]), "w0out": f(ev_w_out[0]),
        "w1in": f(od_w_in[0]), "sinks": f(od_sinks[0]), "w1out": f(od_w_out[0]),
        "cb": cb, "rope": f(rope), "swae": f(swae), "em": f(em), "cm": f(cm),
    }


def kernel(x, meta, norm_g, final_g, ev_w_in, ev_q_norm_g, ev_kv_norm_g, ev_w_uq, ev_w_ukv,
           ev_w_out, od_w_in, od_sinks, od_w_out):
    n = 8
    x = np.asarray(x, dtype=np.float32)
    common = _common_inputs(meta, norm_g, final_g, ev_w_in, ev_q_norm_g, ev_kv_norm_g, ev_w_uq,
                            ev_w_ukv, ev_w_out, od_w_in, od_sinks, od_w_out)
    if "nc" not in _NC_CACHE:
        _NC_CACHE["nc"] = build_nc()
    nc = _NC_CACHE["nc"]
    in_maps = []
    for c in range(n):
        m = dict(common)
        m["x"] = np.ascontiguousarray(x[c * SEQ_PER_CORE:(c + 1) * SEQ_PER_CORE])
        in_maps.append(m)
    res = run_bass_kernel_spmd(nc, in_maps, core_ids=list(range(n)))
    return np.concatenate([r["out"] for r in res.results], axis=0)
```

```python
import numpy as np
from contextlib import ExitStack
import concourse.bass as bass
import concourse.mybir as mybir
from concourse.bass_utils import run_bass_kernel_spmd

F32 = mybir.dt.float32
BF16 = mybir.dt.bfloat16
AF = mybir.ActivationFunctionType
ALU = mybir.AluOpType

ENGS = ["pe", "act", "dve", "pool", "sp"]

D = 1024
LP = 2176
NB = 17
NPAD = 112
EPS = 1e-6
SEQ_PER_CORE = 2
EV_IN = 2976
OD_IN = 2304


class Buf:
    __slots__ = ("name", "last_w", "readers", "excl")

    def __init__(self, name, excl=False):
        self.name = name
        self.last_w = None
        self.readers = []
        self.excl = excl


class Op:
    __slots__ = ("eng", "fn", "deps", "signal", "is_dma", "dsem", "dval", "cnt", "prev_dma")

    def __init__(self, eng, fn, is_dma):
        self.eng = eng
        self.fn = fn
        self.deps = []
        self.signal = False
        self.is_dma = is_dma
        self.dsem = None
        self.dval = 0
        self.cnt = 0
        self.prev_dma = None


class Sched:
    def __init__(self, n_dma_sems=8):
        self.ops = {e: [] for e in ENGS}
        self.n_dma_sems = n_dma_sems
        self.dma_rr = {e: 0 for e in ENGS}
        self.dma_cnt = {}
        self.dma_last = {}

    def op(self, eng, fn, reads=(), writes=(), dma=False):
        o = Op(eng, fn, dma)
        deps = {}
        for b in reads:
            if b.last_w is not None:
                deps[id(b.last_w)] = b.last_w
            if b.excl:
                for r in b.readers:
                    if r.eng != eng:
                        deps[id(r)] = r
        for b in writes:
            if b.last_w is not None:
                deps[id(b.last_w)] = b.last_w
            for r in b.readers:
                deps[id(r)] = r
        final = []
        for d in deps.values():
            if d is o:
                continue
            if d.eng == "pe" and eng == "pe" and (not d.is_dma) and (not dma):
                continue
            final.append(d)
            d.signal = True
        o.deps = final
        for b in reads:
            if not dma:
                b.readers = [r for r in b.readers if r.is_dma or r.eng != eng]
            b.readers.append(o)
        for b in writes:
            b.last_w = o
            b.readers = []
        if dma:
            k = self.dma_rr[eng]
            self.dma_rr[eng] = (k + 1) % self.n_dma_sems
            key = (eng, k)
            self.dma_cnt[key] = self.dma_cnt.get(key, 0) + 1
            o.dsem = key
            o.dval = 16 * self.dma_cnt[key]
            o.prev_dma = self.dma_last.get(key)
            self.dma_last[key] = o
        self.ops[eng].append(o)
        return o

    def barrier(self):
        lasts = []
        for e in ENGS:
            for o in reversed(self.ops[e]):
                if (not o.is_dma) and o.fn is not None:
                    lasts.append(o)
                    break
        dmas = list(self.dma_last.values())
        for e in ENGS:
            m = Op(e, None, False)
            m.deps = [d for d in lasts if d.eng != e] + dmas
            for d in m.deps:
                d.signal = True
            self.ops[e].append(m)

    def finalize(self):
        for e in ENGS:
            c = 0
            for o in self.ops[e]:
                if o.is_dma or o.fn is None:
                    continue
                if o.signal:
                    c += 1
                    o.cnt = c

    def emit_engine(self, eng_name, e, sems, dma_sems, final_wait=False):
        waited = {}

        def wait(key, sem, val):
            if val <= 0:
                return
            if waited.get(key, 0) < val:
                e.wait_ge(sem, val)
                waited[key] = val

        for o in self.ops[eng_name]:
            for d in o.deps:
                if d.is_dma:
                    wait(d.dsem, dma_sems[d.dsem], d.dval)
                else:
                    wait(d.eng, sems[d.eng], d.cnt)
            if o.is_dma and o.prev_dma is not None:
                wait(o.dsem, dma_sems[o.dsem], o.prev_dma.dval)
            if o.fn is None:
                continue
            ins = o.fn(e)
            if o.is_dma:
                ins.then_inc(dma_sems[o.dsem], 16)
            elif o.signal:
                ins.then_inc(sems[eng_name], 1)
        if final_wait:
            for key, o in self.dma_last.items():
                wait(key, dma_sems[key], o.dval)


def _const_tables():
    r = np.arange(128)
    s = r[:, None]
    t = r[None, :]
    ident = (s == t).astype(np.float32)
    negU = -(s >= t).astype(np.float32)
    negU0 = negU * (s >= NPAD)
    negOnes = -np.ones((128, 128), np.float32)
    negOnes0 = negOnes * (s >= NPAD)
    ones = np.ones((128, 128), np.float32)
    mstrict = (s < t).astype(np.float32)
    mincl = (s <= t).astype(np.float32)
    onesA = np.concatenate([np.ones((128, 64)), np.zeros((128, 64))], 1).astype(np.float32)
    onesB = np.concatenate([np.zeros((128, 64)), np.ones((128, 64))], 1).astype(np.float32)
    zeros = np.zeros((128, 128), np.float32)
    cb = np.concatenate([ident, negU, negU0, negOnes, negOnes0, ones, mstrict, mincl,
                         onesA, onesB, zeros], axis=1)
    half = 16
    inv = 10000.0 ** (-np.arange(half, dtype=np.float64) / half)
    pos = (np.arange(LP) - NPAD).astype(np.float64)
    ang = inv[:, None] * pos[None, :]
    cos = np.concatenate([np.cos(ang), np.cos(ang)], 0).astype(np.float32)
    sin = np.concatenate([np.sin(ang), np.sin(ang)], 0).astype(np.float32)
    rope = np.stack([cos, sin], 0)
    H = 16
    slopes = 2.0 ** (-8.0 * (np.arange(H, dtype=np.float64) + 1.0) / H)
    sl = slopes[None, :, None]
    dprev = (128 + r[None, None, :] - r[:, None, None]).astype(np.float64)
    eprev = np.where(dprev < 128, np.exp(-sl * dprev), 0.0)
    dcur = (r[None, None, :] - r[:, None, None]).astype(np.float64)
    ecur = np.where(dcur >= 0, np.exp(-sl * np.maximum(dcur, 0)), 0.0)
    m = np.arange(16)
    dm = (16 + r[None, None, :] - m[:, None, None]).astype(np.float64)
    em = np.exp(-sl * dm)
    n = np.arange(NB)
    cm = np.exp(-slopes[None, None, :] * 128.0 * np.maximum(n - 1, 0)[None, :, None])
    cm = np.broadcast_to(cm, (16, NB, H))
    swa_e = np.stack([eprev, ecur], 0).astype(np.float32)
    return (cb.astype(np.float32), rope, swa_e, em.astype(np.float32),
            np.ascontiguousarray(cm).astype(np.float32))


def build_nc(debug_h1=False, n_seq=SEQ_PER_CORE):
    nc = bass.Bass("TRN2", target_bir_lowering=False)
    dt = nc.dram_tensor
    x_d = dt("x", [n_seq, 2048, D], F32, kind="ExternalInput").ap()
    meta_d = dt("meta", [16, D], F32, kind="ExternalInput").ap()
    gains_d = dt("gains", [3, D], F32, kind="ExternalInput").ap()
    w0in_d = dt("w0in", [D, EV_IN], F32, kind="ExternalInput").ap()
    qng_d = dt("qng", [256], F32, kind="ExternalInput").ap()
    kvng_d = dt("kvng", [128], F32, kind="ExternalInput").ap()
    wuq_d = dt("wuq", [256, 768], F32, kind="ExternalInput").ap()
    wukv_d = dt("wukv", [128, 1024], F32, kind="ExternalInput").ap()
    w0out_d = dt("w0out", [D, D], F32, kind="ExternalInput").ap()
    w1in_d = dt("w1in", [D, OD_IN], F32, kind="ExternalInput").ap()
    sinks_d = dt("sinks", [16], F32, kind="ExternalInput").ap()
    w1out_d = dt("w1out", [D, D], F32, kind="ExternalInput").ap()
    cb_d = dt("cb", [128, 11 * 128], F32, kind="ExternalInput").ap()
    rope_d = dt("rope", [2, 32, LP], F32, kind="ExternalInput").ap()
    swae_d = dt("swae", [2, 128, 16, 128], F32, kind="ExternalInput").ap()
    em_d = dt("em", [16, 16, 128], F32, kind="ExternalInput").ap()
    cm_d = dt("cm", [16, NB, 16], F32, kind="ExternalInput").ap()
    out_d = dt("out", [n_seq, 2048, D], F32, kind="ExternalOutput").ap()
    if debug_h1:
        dbg_d = dt("dbg", [n_seq, LP, D], F32, kind="ExternalOutput").ap()
        dbg2_d = dt("dbg2", [n_seq, 128, 8 * LP], F32, kind="ExternalOutput").ap()

    S = Sched()
    with ExitStack() as es:
        ARENA_F32 = 53000
        arena = es.enter_context(nc.sbuf_tensor("arena", [128, ARENA_F32], F32))
        psq = [es.enter_context(nc.psum_tensor(f"psq{i}", [128, 1024], F32)) for i in range(4)]
        ps = [psq[i // 2][:, (i % 2) * 512:(i % 2 + 1) * 512] for i in range(8)]
        psP = [q_.rearrange("p (h n) -> p h n", h=2) for q_ in psq]
        sems = {e: es.enter_context(nc.semaphore(f"s_{e}")) for e in ENGS}
        dma_sems = {}
        for e in ["sp", "pool"]:
            for k in range(S.n_dma_sems):
                dma_sems[(e, k)] = es.enter_context(nc.semaphore(f"d_{e}{k}"))
        P = [Buf(f"ps{i}", excl=True) for i in range(8)]

        class Arena:
            def __init__(self, start=0):
                self.off = start

            def f32(self, n):
                a = arena[:, self.off:self.off + n]
                self.off += n
                return a

            def bf16(self, n):
                assert n % 2 == 0
                a = arena[:, self.off:self.off + n // 2].bitcast(BF16)
                self.off += n // 2
                return a

        A = Arena(0)
        CB = A.bf16(11 * 128)
        cb_v = lambda i: CB[:, i * 128:(i + 1) * 128]
        IDENT, NEGU, NEGU0, NEGONES, NEGONES0, ONES, MSTRICT, MINCL, ONESA, ONESB, ZEROS = [cb_v(i) for i in range(11)]
        G = A.f32(3 * D)
        NG = A.f32(4)
        FIN = A.bf16(512)
        OG = A.bf16(8 * LP)
        OGv = OG.rearrange("p (c t) -> p c t", c=8)
        pers_end = A.off
        B_cb, B_g, B_ng, B_fin, B_og = Buf("cb"), Buf("g"), Buf("ng"), Buf("fin"), Buf("og")

        def mm(out, lhsT, rhs, start, stop, reads, writes):
            S.op("pe", lambda e: e.matmul(out, lhsT=lhsT, rhs=rhs, start=start, stop=stop),
                 reads=reads, writes=writes)

        def tr(out, in_, reads, writes):
            S.op("pe", lambda e: e.transpose(out, in_, IDENT), reads=list(reads) + [B_cb], writes=writes)

        def act(out, in_, func, reads, writes, scale=1.0, bias=0.0, accum_out=None, eng="act"):
            if accum_out is None:
                S.op("act", lambda e: e.activation(out=out, in_=in_, func=func, bias=bias, scale=scale),
                     reads=reads, writes=writes)
            else:
                S.op("act", lambda e: e.activation(out=out, in_=in_, func=func, bias=bias, scale=scale,
                                                   accum_out=accum_out), reads=reads, writes=writes)

        def tt(eng, out, in0, in1, op, reads, writes):
            S.op(eng, lambda e: e.tensor_tensor(out=out, in0=in0, in1=in1, op=op), reads=reads, writes=writes)

        def tsc(eng, out, in0, s1, op0, reads, writes, s2=None, op1=None):
            if op1 is None:
                S.op(eng, lambda e: e.tensor_scalar(out=out, in0=in0, scalar1=s1, scalar2=None, op0=op0),
                     reads=reads, writes=writes)
            else:
                S.op(eng, lambda e: e.tensor_scalar(out=out, in0=in0, scalar1=s1, scalar2=s2, op0=op0, op1=op1),
                     reads=reads, writes=writes)

        def stt(eng, out, in0, scalar, in1, op0, op1, reads, writes):
            S.op(eng, lambda e: e.scalar_tensor_tensor(out=out, in0=in0, scalar=scalar, in1=in1, op0=op0, op1=op1),
                 reads=reads, writes=writes)

        def cp(eng, out, in_, reads, writes):
            if eng == "act":
                S.op("act", lambda e: e.copy(out=out, in_=in_), reads=reads, writes=writes)
            else:
                S.op(eng, lambda e: e.tensor_copy(out=out, in_=in_), reads=reads, writes=writes)

        def memset(eng, ap, val, writes):
            S.op(eng, lambda e: e.memset(ap, val), writes=writes)

        def dma(eng, out, in_, reads, writes):
            S.op(eng, lambda e: e.dma_start(out=out, in_=in_), reads=reads, writes=writes, dma=True)

        def recip(out, in_, reads, writes):
            S.op("dve", lambda e: e.reciprocal(out=out, in_=in_), reads=reads, writes=writes)

        def wload(dst3, src2, c0, ncols, writes, reads=()):
            kc = src2.shape[0] // 128
            srcv = src2.rearrange("(kc p) n -> p kc n", p=128)
            dma("pool", dst3, srcv[:, :, c0:c0 + ncols], reads, writes)

        dma("pool", CB, cb_d, [], [B_cb])
        for i in range(3):
            dma("sp", G[:, i * D:(i + 1) * D], gains_d[i:i + 1, :].broadcast_to([128, D]), [], [B_g])
        for c in range(2):
            dma("sp", NG[:, c:c + 1], qng_d[c * 128:(c + 1) * 128].rearrange("(p o) -> p o", o=1), [], [B_ng])
        dma("sp", NG[:, 2:3], kvng_d.rearrange("(p o) -> p o", o=1), [], [B_ng])
        memset("dve", FIN, 1.0, [B_fin])

        A0 = Arena(pers_end)
        HNT = A0.bf16(8 * LP); HNTv = HNT.rearrange("p (c t) -> p c t", c=8); B_hnt = Buf("hnt")
        WS = [A0.bf16(8 * 512) for _ in range(2)]; B_ws = [Buf("ws0"), Buf("ws1")]
        WSv = [w.rearrange("p (c n) -> p c n", c=8) for w in WS]
        WM = A0.bf16(8 * 416); WMv = WM.rearrange("p (c n) -> p c n", c=8); B_wm = Buf("wm")
        WKRR = A0.bf16(8 * 32); WKRRv = WKRR.rearrange("p (c n) -> p c n", c=8); B_wkrr = Buf("wkrr")
        WUQ = A0.bf16(2 * 768); WUQv = WUQ.rearrange("p (c n) -> p c n", c=2); B_wuq = Buf("wuq")
        WUQR = A0.bf16(2 * 256); WUQRv = WUQR.rearrange("p (c n) -> p c n", c=2); B_wuqr = Buf("wuqr")
        WUKV = A0.bf16(1024); B_wukv = Buf("wukv")
        A0_KT_START = A0.off
        KT = [A0.bf16(LP) for _ in range(2)]; B_kt = [Buf("kt0"), Buf("kt1")]
        QT = [A0.bf16(LP) for _ in range(2)]; B_qt = [Buf("qt0"), Buf("qt1")]
        VP = A0.bf16(NB * 256); VPv = VP.rearrange("p (b v n) -> p b v n", b=NB, v=2); B_vp = Buf("vp")
        SG = A0.bf16(LP); B_sg = Buf("sg")
        CQN = A0.bf16(2 * LP); CQNv = CQN.rearrange("p (c t) -> p c t", c=2); B_cqn = Buf("cqn")
        CKVN = A0.bf16(LP); B_ckvn = Buf("ckvn")
        KR = A0.bf16(LP); B_kr = Buf("kr")
        ROPE = A0.f32(2 * LP); ROPEv = ROPE.rearrange("p (c t) -> p c t", c=2); B_rope = Buf("rope")
        def pairbuf(ap):
            return ap.rearrange("p (h n) -> p h n", h=2), [ap[:, 0:512], ap[:, 512:1024]]
        E32P = A0.f32(1024); e32v, E32 = pairbuf(E32P); B_e32p = Buf("e32p"); B_e32 = [B_e32p, B_e32p]
        SPBP = A0.bf16(1024); spbv, SPB = pairbuf(SPBP); B_spbp = Buf("spbp"); B_spb = [B_spbp, B_spbp]
        SPBP2 = A0.bf16(1024); spbv2, SPB2 = pairbuf(SPBP2); B_spbp2 = Buf("spbp2")
        C32P = A0.f32(1024); c32v, C32 = pairbuf(C32P); B_c32p = Buf("c32p")
        C16P = A0.bf16(1024); c16v, C16 = pairbuf(C16P); B_c16p = Buf("c16p")
        ATBP = [A0.bf16(1024) for _ in range(2)]
        atbv = [pairbuf(a_)[0] for a_ in ATBP]
        ATB = [pairbuf(ATBP[i // 2])[1][i % 2] for i in range(4)]
        B_atbp = [Buf("atbp0"), Buf("atbp1")]; B_atb = [B_atbp[i // 2] for i in range(4)]
        XS = A0.f32(D); B_xs = Buf("xs")
        T32 = [XS[:, 0:512], XS[:, 512:1024]]; B_t32 = [Buf("t32a"), Buf("t32b")]
        HN = A0.bf16(D); B_hn = Buf("hn")
        ST = A0.f32(8); B_st = Buf("st"); B_stb = Buf("stb")
        JK = HN; B_jk = B_hn
        l0_end = A0.off
        assert l0_end <= ARENA_F32, l0_end

        A1 = Arena(pers_end)
        W0O = A1.bf16(8 * D); W0Ov = W0O.rearrange("p (c n) -> p c n", c=8); B_w0o = Buf("w0o")
        W1I = A1.bf16(8 * OD_IN); W1Iv = W1I.rearrange("p (c n) -> p c n", c=8); B_w1i = Buf("w1i")
        W1O = A1.bf16(8 * D); W1Ov = W1O.rearrange("p (c n) -> p c n", c=8); B_w1o = Buf("w1o")
        SWE = A1.f32(2 * 16 * 128); SWEv = SWE.rearrange("p (a h r) -> p a h r", a=2, h=16); B_swe = Buf("swe")
        EM = A1.f32(16 * 128); EMv = EM.rearrange("p (h r) -> p h r", h=16); B_em = Buf("em")
        CM = A1.f32(NB * 16); CMv = CM.rearrange("p (n h) -> p n h", n=NB); B_cm = Buf("cm")
        ESK = A1.f32(8); B_esk = Buf("esk")
        KT1 = A1.bf16(LP); B_kt1 = [Buf(f"kt1l{i}") for i in range(NB)]
        VR = A1.bf16(3 * 4 * 128); VRv = VR.rearrange("p (b v n) -> p b v n", b=3, v=4); B_vr = [Buf("vr0"), Buf("vr1"), Buf("vr2")]
        VM = A1.bf16(4 * 128); VMv = VM.rearrange("p (v n) -> p v n", v=4); B_vm = Buf("vm")
        H1 = [A1.f32(D) for _ in range(2)]; B_h1 = [Buf("h1a"), Buf("h1b")]
        XS1 = A1.f32(D); B_xs1 = Buf("xs1")
        HN1 = A1.bf16(D); B_hn1 = Buf("hn1")
        HNT1 = [A1.bf16(8 * 128) for _ in range(2)]; HNT1v = [h.rearrange("p (c t) -> p c t", c=8) for h in HNT1]
        B_hnt1 = [Buf("hnt1a"), Buf("hnt1b")]
        QG = [[A1.bf16(8 * 128) for _ in range(2)] for _ in range(2)]
        QGv = [[q.rearrange("p (h t) -> p h t", h=8) for q in qq] for qq in QG]
        B_qg = [[Buf(f"qg{i}{j}") for j in range(2)] for i in range(2)]
        SG1 = [A1.bf16(8 * 128) for _ in range(2)]; SG1v = [x_.rearrange("p (c t) -> p c t", c=8) for x_ in SG1]
        B_sg1 = [Buf("sg1a"), Buf("sg1b")]
        OG1 = A1.bf16(8 * 128); OG1v = OG1.rearrange("p (c t) -> p c t", c=8); B_og1 = Buf("og1")
        EX = [A1.f32(1024) for _ in range(2)]; EXv = [x_.rearrange("p (h t) -> p h t", h=8) for x_ in EX]
        B_ex = [Buf("exa"), Buf("exb")]
        PB = [A1.bf16(1024) for _ in range(2)]; PBv = [x_.rearrange("p (h t) -> p h t", h=8) for x_ in PB]
        B_pb = [Buf("pba"), Buf("pbb")]
        B_exh = [[Buf(f"exh{i}{j}") for j in range(2)] for i in range(2)]
        B_pbh = [[Buf(f"pbh{i}{j}") for j in range(2)] for i in range(2)]
        R32 = A1.f32(512); B_r32 = Buf("r32")
        U32 = A1.f32(512); B_u32 = Buf("u32")
        ST1 = A1.f32(8); B_st1 = Buf("st1")
        ST2 = A1.f32(8); B_st2 = Buf("st2")
        JK1 = HN1; B_jk1 = B_hn1
        OUTB = A1.f32(D); B_outb = Buf("outb")
        assert A1.off <= ARENA_F32, A1.off

        PT = ps[7].bitcast(BF16)

        TCH = [(c * 512, min(512, LP - c * 512)) for c in range(5)]
        QCH = [(0, 1), (1, 5), (5, 9), (9, 13), (13, 17)]

        def rmsnorm_block(src32, B_src, gidx, dst_bf, B_dst, st, B_st_, jk, B_jk_):
            act(jk, src32, AF.Square, [B_src], [B_jk_, B_st_], accum_out=st[:, 0:1])
            act(st[:, 1:2], st[:, 0:1], AF.Ln, [B_st_], [B_st_], scale=1.0 / D, bias=EPS)
            act(st[:, 2:3], st[:, 1:2], AF.Exp, [B_st_], [B_st_], scale=-0.5)
            stt("dve", dst_bf, src32, st[:, 2:3], G[:, gidx * D:(gidx + 1) * D], ALU.mult, ALU.mult,
                [B_src, B_st_, B_g], [B_dst])

        def transpose_block(hn_bf, B_hn_, dst3, B_dst):
            for kc in range(8):
                tr(PT[:, kc * 128:(kc + 1) * 128], hn_bf[:, kc * 128:(kc + 1) * 128], [B_hn_], [P[7]])
            cp("dve", dst3, PT.rearrange("p (c t) -> p c t", c=8), [P[7]], [B_dst])

        def proj_fm(psb, Pb, wv, c0, m, t0, n, B_w, hnt_v, B_h):
            for kc in range(8):
                mm(psb[0:m, 0:n], wv[:, kc, c0:c0 + m], hnt_v[:, kc, t0:t0 + n], kc == 0, kc == 7,
                   [B_w, B_h], [Pb])

        for sq in range(n_seq):
            S.barrier()
            wload(WMv, w0in_d, 2048, 416, [B_wm])
            wload(WUQv, wuq_d, 0, 768, [B_wuq])
            dma("pool", WUKV, wukv_d, [], [B_wukv])
            dma("sp", ROPEv[0:32], rope_d.rearrange("c p t -> p c t"), [], [B_rope])
            for i in range(2):
                memset("pool", KT[i], 0.0, [B_kt[i]])
                memset("pool", QT[i], 0.0, [B_qt[i]])
            memset("pool", VP, 0.0, [B_vp])
            for kc in range(8):
                tsc("pool", WKRRv[:, kc, 0:16], WMv[:, kc, 400:416], -1.0, ALU.mult, [B_wm], [B_wkrr])
                cp("pool", WKRRv[:, kc, 16:32], WMv[:, kc, 384:400], [B_wm], [B_wkrr])
            for kc in range(2):
                src = WUQv[:, kc, :].rearrange("p (h d) -> p h d", h=8)
                dst = WUQRv[:, kc, :].rearrange("p (h d) -> p h d", h=8)
                tsc("pool", dst[:, :, 0:16], src[:, :, 80:96], -1.0, ALU.mult, [B_wuq], [B_wuqr])
                cp("pool", dst[:, :, 16:32], src[:, :, 64:80], [B_wuq], [B_wuqr])

            XSs = [XS, E32P]; B_xss = [B_xs, B_e32p]
            HNs = [HN, C32P.bitcast(BF16)[:, 0:D]]; B_hns = [B_hn, B_c32p]
            STs = [ST[:, 0:4], ST[:, 4:8]]; B_sts = [B_st, B_stb]
            for b in range(NB):
                i = b % 2
                xs_, Bx_ = XSs[i], B_xss[i]
                if b == 0:
                    memset("dve", xs_, 0.0, [Bx_])
                    dma("sp", xs_[NPAD:128, :], meta_d, [], [Bx_])
                else:
                    dma("sp", xs_, x_d[sq, (b - 1) * 128:b * 128, :], [], [Bx_])
                rmsnorm_block(xs_, Bx_, 0, HNs[i], B_hns[i], STs[i], B_sts[i], HNs[i], B_hns[i])
                pk = 6 + i
                ptv = ps[pk].bitcast(BF16)
                for kc in range(8):
                    tr(ptv[:, kc * 128:(kc + 1) * 128], HNs[i][:, kc * 128:(kc + 1) * 128], [B_hns[i]], [P[pk]])
                cp("dve", HNTv[:, :, b * 128:(b + 1) * 128], ptv.rearrange("p (c t) -> p c t", c=8), [P[pk]], [B_hnt])

            def attention_pair(kts, B_ks, qts, B_qs, pair_chunk, sb, escale):
                M2 = lambda m_: m_.unsqueeze(1).to_broadcast([128, 2, 128])
                for ci, (ba, bz) in enumerate(QCH):
                    q0, q1 = ba * 128, bz * 128
                    N = q1 - q0
                    if sb:
                        psO, PO = ps[4 + (ci % 2)], P[4 + (ci % 2)]
                    else:
                        ob = 4 if ci % 2 == 0 else 6
                        psO, PO = ps[ob], P[ob]
                        psD, PD = ps[ob + 1], P[ob + 1]
                    mm(psO[:, 0:N], ZEROS, FIN[:, 0:N], True, False, [B_cb, B_fin], [PO])
                    if not sb:
                        mm(psD[:, 0:N], ZEROS, FIN[:, 0:N], True, False, [B_cb, B_fin], [PD])
                    steps = list(range(bz - 1, -1, -1))

                    def geom(j):
                        tq0 = max(q0, j * 128)
                        return tq0, q1 - tq0, tq0 - q0, j >= ba

                    def qk(si):
                        j = steps[si]
                        tq0, n, c0, diag = geom(j)
                        for hh in range(2):
                            bank = hh if sb else 2 * (si % 2) + hh
                            mm(ps[bank][:, 0:n], kts[hh][:, j * 128:(j + 1) * 128], qts[hh][:, tq0:q1], True, True,
                               [B_ks[hh], B_qs[hh]], [P[bank]])

                    if sb:
                        SPV = [(spbv, SPB, B_spbp), (spbv2, SPB2, B_spbp2)]
                        CSB = [(2, 3), (6, 7)]

                        def stage1(si):
                            j = steps[si]
                            tq0, n, c0, diag = geom(j)
                            sv, _, Bs = SPV[si % 2]
                            act(e32v[:, :, 0:n], psP[0][:, :, 0:n], AF.Exp, [P[0], P[1]], [B_e32p])
                            act(sv[:, :, 0:n], e32v[:, :, 0:n], AF.Ln, [B_e32p], [Bs], bias=1.0)
                            if diag:
                                tt("pool", sv[:, :, 0:128], sv[:, :, 0:128], M2(MSTRICT), ALU.mult, [Bs, B_cb], [Bs])

                        def cumsum(si):
                            j = steps[si]
                            tq0, n, c0, diag = geom(j)
                            _, sp2, Bs = SPV[si % 2]
                            cb_ = CSB[si % 2]
                            first = si == 0
                            for hh in range(2):
                                bk = cb_[hh]
                                mm(ps[bk][:, 0:n], kts[hh][:, j * 128:(j + 1) * 128], qts[hh][:, tq0:q1], True, False,
                                   [B_ks[hh], B_qs[hh]], [P[bk]])
                            for hh in range(2):
                                bk = cb_[hh]
                                mm(ps[bk][:, 0:n], NEGU0 if j == 0 else NEGU, sp2[hh][:, 0:n], False, first,
                                   [B_cb, Bs], [P[bk]])
                                if not first:
                                    mm(ps[bk][:, 0:n], NEGONES, C16[hh][:, c0:c0 + n], False, True,
                                       [B_cb, B_c16p], [P[bk]])

                        def carry(si):
                            j = steps[si]
                            tq0, n, c0, diag = geom(j)
                            sv, _, Bs = SPV[si % 2]
                            if j == 0:
                                return
                            if si == 0:
                                memset("pool", C32P, 0.0, [B_c32p])
                            tt("dve", c32v[:, :, c0:c0 + n], c32v[:, :, c0:c0 + n], sv[:, :, 0:n], ALU.add,
                               [B_c32p, Bs], [B_c32p])
                            cp("dve", c16v[:, :, 0:N], c32v[:, :, 0:N], [B_c32p], [B_c16p])

                        def exp2(si):
                            j = steps[si]
                            tq0, n, c0, diag = geom(j)
                            cb_ = CSB[si % 2]
                            e = si % 2
                            act(atbv[e][:, :, 0:n], psP[cb_[0] // 2][:, :, 0:n], AF.Exp, [P[cb_[0]], P[cb_[1]]], [B_atbp[e]])
                            if diag:
                                tt("pool", atbv[e][:, :, 0:128], atbv[e][:, :, 0:128], M2(MSTRICT), ALU.mult,
                                   [B_atbp[e], B_cb], [B_atbp[e]])

                        def av(si):
                            j = steps[si]
                            tq0, n, c0, diag = geom(j)
                            e = si % 2
                            for hh in range(2):
                                mm(psO[:, c0:c0 + n], VPv[:, j, hh, :], ATB[2 * e + hh][:, 0:n], False, j == 0 and hh == 1,
                                   [B_vp, B_atbp[e]], [PO])

                        ns = len(steps)
                        qk(0)
                        stage1(0)
                        if ns > 1:
                            qk(1)
                        for si in range(ns):
                            cumsum(si)
                            carry(si)
                            if si + 1 < ns:
                                stage1(si + 1)
                            if si + 2 < ns:
                                qk(si + 2)
                            exp2(si)
                            av(si)
                    else:
                        qk(0)
                    for si, j in (enumerate(steps) if not sb else []):
                        tq0, n, c0, diag = geom(j)
                        first = si == 0
                        last = j == 0
                        if True:
                            e = si % 2
                            act(atbv[e][:, :, 0:n], psP[e][:, :, 0:n], AF.Exp, [P[2 * e], P[2 * e + 1]], [B_atbp[e]],
                                scale=escale)
                            if diag:
                                tt("dve", atbv[e][:, :, 0:128], atbv[e][:, :, 0:128], M2(MINCL), ALU.mult,
                                   [B_atbp[e], B_cb], [B_atbp[e]])
                            if not last:
                                qk(si + 1)
                            mm(psO[:, c0:c0 + n], VPv[:, j, 0, :], ATB[2 * e][:, 0:n], False, last,
                               [B_vp, B_atbp[e]], [PO])
                            mm(psD[:, c0:c0 + n], VPv[:, j, 1, :], ATB[2 * e + 1][:, 0:n], False, last,
                               [B_vp, B_atbp[e]], [PD])
                    if sb:
                        tt("dve", OGv[:, pair_chunk, q0:q1], psO[:, 0:N], SG[:, q0:q1], ALU.mult,
                           [PO, B_sg], [B_og])
                    else:
                        t32, B_t = T32[ci % 2], B_t32[ci % 2]
                        u32 = E32[ci % 2]
                        lo, hi = slice(0, 64), slice(64, 128)
                        act(t32[hi, 0:N], psO[hi, 0:N], AF.Ln, [PO], [B_t], bias=1e-30)
                        act(t32[lo, 0:N], psD[lo, 0:N], AF.Ln, [PD], [B_t], bias=1e-30)
                        act(t32[:, 0:N], t32[:, 0:N], AF.Exp, [B_t], [B_t], scale=-1.0)
                        cp("dve", u32[lo, 0:N], t32[hi, 0:N], [B_t], [B_e32p])
                        cp("dve", u32[hi, 0:N], t32[lo, 0:N], [B_t], [B_e32p])
                        tt("dve", u32[:, 0:N], u32[:, 0:N], SG[:, q0:q1], ALU.mult, [B_e32p, B_sg], [B_e32p])
                        tt("dve", OGv[lo, pair_chunk, q0:q1], psO[lo, 0:N], u32[lo, 0:N], ALU.mult,
                           [PO, B_e32p], [B_og])
                        tt("dve", OGv[hi, pair_chunk, q0:q1], psD[hi, 0:N], u32[hi, 0:N], ALU.mult,
                           [PD, B_e32p], [B_og])

            def attention_pair_sb(kts, B_ks, qts, B_qs, pair_chunk):
                M2 = lambda m_: m_.unsqueeze(1).to_broadcast([128, 2, 128])
                SPV = [(spbv, SPB, B_spbp), (spbv2, SPB2, B_spbp2)]
                CSB = [(2, 3), (6, 7)]

                class Chunk:
                    pass

                def mk_chunk(ci):
                    ba, bz = QCH[ci]
                    q0, q1 = ba * 128, bz * 128
                    N = q1 - q0
                    psO, PO = ps[4 + (ci % 2)], P[4 + (ci % 2)]
                    steps = list(range(bz - 1, -1, -1))
                    ns = len(steps)

                    def geom(si):
                        j = steps[si]
                        tq0 = max(q0, j * 128)
                        return j, tq0, q1 - tq0, tq0 - q0, j >= ba

                    def qk(si):
                        j, tq0, n, c0, diag = geom(si)
                        for hh in range(2):
                            mm(ps[hh][:, 0:n], kts[hh][:, j * 128:(j + 1) * 128], qts[hh][:, tq0:q1], True, True,
                               [B_ks[hh], B_qs[hh]], [P[hh]])

                    def stage1(si):
                        j, tq0, n, c0, diag = geom(si)
                        sv, _, Bs = SPV[si % 2]
                        act(e32v[:, :, 0:n], psP[0][:, :, 0:n], AF.Exp, [P[0], P[1]], [B_e32p])
                        act(sv[:, :, 0:n], e32v[:, :, 0:n], AF.Ln, [B_e32p], [Bs], bias=1.0)
                        if diag:
                            tt("pool", sv[:, :, 0:128], sv[:, :, 0:128], M2(MSTRICT), ALU.mult, [Bs, B_cb], [Bs])

                    def cumsum(si):
                        j, tq0, n, c0, diag = geom(si)
                        _, sp2, Bs = SPV[si % 2]
                        cb_ = CSB[si % 2]
                        first = si == 0
                        for hh in range(2):
                            bk = cb_[hh]
                            mm(ps[bk][:, 0:n], kts[hh][:, j * 128:(j + 1) * 128], qts[hh][:, tq0:q1], True, False,
                               [B_ks[hh], B_qs[hh]], [P[bk]])
                        for hh in range(2):
                            bk = cb_[hh]
                            mm(ps[bk][:, 0:n], NEGU0 if j == 0 else NEGU, sp2[hh][:, 0:n], False, first,
                               [B_cb, Bs], [P[bk]])
                            if not first:
                                mm(ps[bk][:, 0:n], NEGONES, C16[hh][:, c0:c0 + n], False, True,
                                   [B_cb, B_c16p], [P[bk]])

                    def carry(si):
                        j, tq0, n, c0, diag = geom(si)
                        sv, _, Bs = SPV[si % 2]
                        if j == 0:
                            return
                        if si == 0:
                            memset("pool", C32P, 0.0, [B_c32p])
                        tt("dve", c32v[:, :, c0:c0 + n], c32v[:, :, c0:c0 + n], sv[:, :, 0:n], ALU.add,
                           [B_c32p, Bs], [B_c32p])
                        cp("dve", c16v[:, :, 0:N], c32v[:, :, 0:N], [B_c32p], [B_c16p])

                    def exp2(si):
                        j, tq0, n, c0, diag = geom(si)
                        cb_ = CSB[si % 2]
                        e = si % 2
                        act(atbv[e][:, :, 0:n], psP[cb_[0] // 2][:, :, 0:n], AF.Exp, [P[cb_[0]], P[cb_[1]]], [B_atbp[e]])
                        if diag:
                            tt("dve", atbv[e][:, :, 0:128], atbv[e][:, :, 0:128], M2(MSTRICT), ALU.mult,
                               [B_atbp[e], B_cb], [B_atbp[e]])

                    def av(si):
                        j, tq0, n, c0, diag = geom(si)
                        e = si % 2
                        for hh in range(2):
                            mm(psO[:, c0:c0 + n], VPv[:, j, hh, :], ATB[2 * e + hh][:, 0:n], False, j == 0 and hh == 1,
                               [B_vp, B_atbp[e]], [PO])

                    def prologue():
                        mm(psO[:, 0:N], ZEROS, FIN[:, 0:N], True, False, [B_cb, B_fin], [PO])
                        qk(0)
                        stage1(0)
                        if ns > 1:
                            qk(1)

                    def finalize():
                        tt("dve", OGv[:, pair_chunk, q0:q1], psO[:, 0:N], SG[:, q0:q1], ALU.mult,
                           [PO, B_sg], [B_og])

                    c = Chunk()
                    c.ns, c.qk, c.stage1, c.cumsum, c.carry, c.exp2, c.av = ns, qk, stage1, cumsum, carry, exp2, av
                    c.prologue, c.finalize = prologue, finalize
                    return c

                chunks = [mk_chunk(ci) for ci in range(len(QCH))]
                chunks[0].prologue()
                chunks[0].cumsum(0)
                for ci, ch in enumerate(chunks):
                    nxt = chunks[ci + 1] if ci + 1 < len(chunks) else None
                    for si in range(ch.ns):
                        lastst = si == ch.ns - 1
                        ch.carry(si)
                        if si + 1 < ch.ns:
                            ch.stage1(si + 1)
                        if si + 2 < ch.ns:
                            ch.qk(si + 2)
                        if lastst and nxt is not None:
                            nxt.prologue()
                        ch.exp2(si)
                        if not lastst:
                            ch.cumsum(si + 1)
                        elif nxt is not None:
                            nxt.cumsum(0)
                        ch.av(si)
                    ch.finalize()

            bk_rot = [0]
            BK_ORDER = [[6, 7]]

            def nbk():
                bk_rot[0] = (bk_rot[0] + 1) % len(BK_ORDER[0])
                return BK_ORDER[0][bk_rot[0]]

            def build_v(lhs_fn, rhs_fn, nk, reads):
                for b0 in range(0, NB, 4):
                    nb4 = min(4, NB - b0)
                    k_ = nbk()
                    for i in range(nb4):
                        b = b0 + i
                        for kc in range(nk):
                            mm(ps[k_][:, i * 128:(i + 1) * 128], lhs_fn(kc, b), rhs_fn(kc), kc == 0, kc == nk - 1,
                               reads, [P[k_]])
                    pv = ps[k_][:, 0:nb4 * 128].rearrange("p (b n) -> p b n", b=nb4)
                    cp("dve", VPv[:, b0:b0 + nb4, 0, 0:64], pv[:, :, 0:64], [P[k_]], [B_vp])
                    cp("dve", VPv[:, b0:b0 + nb4, 1, 64:128], pv[:, :, 64:128], [P[k_]], [B_vp])

            BK_ORDER[0] = [6, 7, 0, 1, 2, 3]

            def pj(wv, c0, m, t0, n, B_w, src_v, B_src, nkc=8):
                k_ = nbk()
                for kc in range(nkc):
                    mm(ps[k_][0:m, 0:n], wv[:, kc, c0:c0 + m], src_v[:, kc, t0:t0 + n], kc == 0, kc == nkc - 1,
                       [B_w, B_src], [P[k_]])
                return ps[k_], P[k_]

            for p in range(4):
                ws, wsv, B_w = WS[p % 2], WSv[p % 2], B_ws[p % 2]
                wv4 = ws.rearrange("p (c f n) -> p c f n", c=8, f=4)
                srcv = w0in_d.rearrange("(kc p) n -> p kc n", p=128)
                for f in range(4):
                    dma("pool", wv4[:, :, f, :], srcv[:, :, f * 512 + p * 128:f * 512 + (p + 1) * 128], [], [B_w])
                for (t0, n) in TCH:
                    pa, Pa = pj(wsv, 128, 128, t0, n, B_w, HNTv, B_hnt)
                    cp("act", KT[0][:, t0:t0 + n], pa[:, 0:n], [Pa], [B_kt[0]])
                    pa, Pa = pj(wsv, 0, 128, t0, n, B_w, HNTv, B_hnt)
                    tsc("dve", QT[0][0:64, t0:t0 + n], pa[0:64, 0:n], 0.125, ALU.mult, [Pa], [B_qt[0]])
                    tsc("dve", QT[1][64:128, t0:t0 + n], pa[64:128, 0:n], 0.125, ALU.mult, [Pa], [B_qt[1]])
                    pa, Pa = pj(wsv, 384, 128, t0, n, B_w, HNTv, B_hnt)
                    act(SG[:, t0:t0 + n], pa[:, 0:n], AF.Silu, [Pa], [B_sg])
                build_v(lambda kc, b: HNTv[:, kc, b * 128:(b + 1) * 128], lambda kc, wsv=wsv: wsv[:, kc, 256:384], 8,
                        [B_hnt, B_w])
                attention_pair_sb([KT[0], KT[0]], [B_kt[0], B_kt[0]], QT, B_qt, p)

            memset("pool", VPv[:, :, 0, 64:128], 1.0, [B_vp])
            memset("pool", VPv[:, :, 1, 0:64], 1.0, [B_vp])
            BK_ORDER[0] = [6, 7, 0, 1, 2, 3]
            for i in range(2):
                memset("pool", KT[i], 0.0, [B_kt[i]])
                memset("pool", QT[i], 0.0, [B_qt[i]])
                memset("pool", KT[i][96:97, 0:NPAD], -30000.0, [B_kt[i]])
                memset("pool", QT[i][96:97, :], 1.0, [B_qt[i]])
            CKVNv1 = CKVN.rearrange("p (c t) -> p c t", c=1)
            for (t0, n) in TCH:
                for cc in range(2):
                    pa, Pa = pj(WMv, cc * 128, 128, t0, n, B_wm, HNTv, B_hnt)
                    cp("dve", T32[cc][:, 0:n], pa[:, 0:n], [Pa], [B_t32[cc]])
                    act(ATB[cc][:, 0:n], pa[:, 0:n], AF.Square, [Pa], [B_atb[cc]])
                k_ = nbk()
                for cc in range(2):
                    mm(ps[k_][:, 0:n], ONES, ATB[cc][:, 0:n], cc == 0, cc == 1, [B_cb, B_atb[cc]], [P[k_]])
                act(E32[0][:, 0:n], ps[k_][:, 0:n], AF.Ln, [P[k_]], [B_e32[0]], scale=1.0 / 256, bias=EPS)
                act(E32[0][:, 0:n], E32[0][:, 0:n], AF.Exp, [B_e32[0]], [B_e32[0]], scale=-0.5)
                for cc in range(2):
                    stt("dve", CQNv[:, cc, t0:t0 + n], T32[cc][:, 0:n], NG[:, cc:cc + 1], E32[0][:, 0:n],
                        ALU.mult, ALU.mult, [B_t32[cc], B_ng, B_e32[0]], [B_cqn])
                pa, Pa = pj(WMv, 256, 128, t0, n, B_wm, HNTv, B_hnt)
                cp("dve", T32[0][:, 0:n], pa[:, 0:n], [Pa], [B_t32[0]])
                act(ATB[2][:, 0:n], pa[:, 0:n], AF.Square, [Pa], [B_atb[2]])
                k_ = nbk()
                mm(ps[k_][:, 0:n], ONES, ATB[2][:, 0:n], True, True, [B_cb, B_atb[2]], [P[k_]])
                act(E32[1][:, 0:n], ps[k_][:, 0:n], AF.Ln, [P[k_]], [B_e32[1]], scale=1.0 / 128, bias=EPS)
                act(E32[1][:, 0:n], E32[1][:, 0:n], AF.Exp, [B_e32[1]], [B_e32[1]], scale=-0.5)
                stt("dve", CKVN[:, t0:t0 + n], T32[0][:, 0:n], NG[:, 2:3], E32[1][:, 0:n],
                    ALU.mult, ALU.mult, [B_t32[0], B_ng, B_e32[1]], [B_ckvn])
                pa, Pa = pj(WMv, 384, 32, t0, n, B_wm, HNTv, B_hnt)
                pb_, Pb_ = pj(WKRRv, 0, 32, t0, n, B_wkrr, HNTv, B_hnt)
                tt("dve", T32[0][0:32, 0:n], pa[0:32, 0:n], ROPEv[0:32, 0, t0:t0 + n], ALU.mult,
                   [Pa, B_rope], [B_t32[0]])
                tt("dve", T32[1][0:32, 0:n], pb_[0:32, 0:n], ROPEv[0:32, 1, t0:t0 + n], ALU.mult,
                   [Pb_, B_rope], [B_t32[1]])
                tt("dve", KR[0:32, t0:t0 + n], T32[0][0:32, 0:n], T32[1][0:32, 0:n], ALU.add,
                   [B_t32[0], B_t32[1]], [B_kr])

            WUKVv = WUKV.rearrange("p (h a d) -> p h a d", h=8, a=2)
            for p in range(4):
                ws, wsv, B_w = WS[p % 2], WSv[p % 2], B_ws[p % 2]
                wload(wsv[:, :, 0:128], w0in_d, 2464 + p * 128, 128, [B_w])
                for (t0, n) in TCH:
                    pa, Pa = pj(wsv, 0, 128, t0, n, B_w, HNTv, B_hnt)
                    act(SG[:, t0:t0 + n], pa[:, 0:n], AF.Silu, [Pa], [B_sg])
                    for hh in range(2):
                        h = 2 * p + hh
                        k_ = nbk()
                        mm(ps[k_][0:64, 0:n], WUKVv[:, h, 0, :], CKVN[:, t0:t0 + n], True, True,
                           [B_wukv, B_ckvn], [P[k_]])
                        cp("act", KT[hh][0:64, t0:t0 + n], ps[k_][0:64, 0:n], [P[k_]], [B_kt[hh]])
                        cp("pool", KT[hh][64:96, t0:t0 + n], KR[0:32, t0:t0 + n], [B_kr], [B_kt[hh]])
                        pa, Pa = pj(WUQv, h * 96, 64, t0, n, B_wuq, CQNv, B_cqn, nkc=2)
                        cp("act", QT[hh][0:64, t0:t0 + n], pa[0:64, 0:n], [Pa], [B_qt[hh]])
                        px, Px = pj(WUQv, h * 96 + 64, 32, t0, n, B_wuq, CQNv, B_cqn, nkc=2)
                        pr_, Pr_ = pj(WUQRv, h * 32, 32, t0, n, B_wuqr, CQNv, B_cqn, nkc=2)
                        tt("dve", T32[0][0:32, 0:n], px[0:32, 0:n], ROPEv[0:32, 0, t0:t0 + n], ALU.mult,
                           [Px, B_rope], [B_t32[0]])
                        tt("dve", T32[1][0:32, 0:n], pr_[0:32, 0:n], ROPEv[0:32, 1, t0:t0 + n], ALU.mult,
                           [Pr_, B_rope], [B_t32[1]])
                        tt("dve", QT[hh][64:96, t0:t0 + n], T32[0][0:32, 0:n], T32[1][0:32, 0:n], ALU.add,
                           [B_t32[0], B_t32[1]], [B_qt[hh]])
                build_v(lambda kc, b: CKVN[:, b * 128:(b + 1) * 128], lambda kc, p=p: WUKVv[:, 2 * p:2 * p + 2, 1, :], 1,
                        [B_ckvn, B_wukv])
                if p == 3:
                    assert pers_end + 4096 + 9216 <= A0_KT_START - (128 + 768 + 256 + 512)
                    dead = [B_hnt, B_ws[0], B_ws[1], B_wm]
                    for c in range(0, 1024, 512):
                        wload(W0Ov[:, :, c:c + 512], w0out_d, c, 512, [B_w0o] + dead)
                    for c in range(0, OD_IN, 576):
                        wload(W1Iv[:, :, c:c + 576], w1in_d, c, 576, [B_w1i] + dead)
                attention_pair(KT, B_kt, QT, B_qt, 4 + p, False, 96.0 ** -0.5)

            if debug_h1:
                for c in range(8):
                    dma("pool", dbg2_d[sq, :, c * LP:(c + 1) * LP], OG[:, c * LP:(c + 1) * LP], [B_og], [])
            S.barrier()
            for c in range(0, 1024, 512):
                wload(W1Ov[:, :, c:c + 512], w1out_d, c, 512, [B_w1o])
            dma("sp", SWEv, swae_d.rearrange("a p h r -> p a h r"), [], [B_swe])
            dma("sp", EMv[0:16], em_d, [], [B_em])
            dma("sp", CMv[0:16], cm_d, [], [B_cm])
            sk2 = sinks_d.rearrange("(p two) -> two p", two=2)
            S.op("sp", lambda e: e.dma_start(out=ESK[0:64, :], in_=sk2[0:1, :].broadcast_to([64, 8]),
                                             allow_slow_non_contiguous=True), writes=[B_esk], dma=True)
            S.op("sp", lambda e: e.dma_start(out=ESK[64:128, :], in_=sk2[1:2, :].broadcast_to([64, 8]),
                                             allow_slow_non_contiguous=True), writes=[B_esk], dma=True)
            act(ESK, ESK, AF.Exp, [B_esk], [B_esk])
            for par in range(2):
                for g in range(2):
                    memset("pool", QG[par][g], 0.0, [B_qg[par][g]])
            memset("pool", VM, 0.0, [B_vm])
            memset("pool", VR, 0.0, B_vr)

            rot = [0]

            def pbank():
                rot[0] ^= 1
                return 6 + rot[0]

            def front(b):
                par = b % 2
                h1, Bh1 = H1[par], B_h1[par]
                hv, Bhv = HNT1v[par], B_hnt1[par]
                if b == 0:
                    memset("dve", XS1, 0.0, [B_xs1])
                    dma("sp", XS1[NPAD:128, :], meta_d, [], [B_xs1])
                else:
                    dma("sp", XS1, x_d[sq, (b - 1) * 128:b * 128, :], [], [B_xs1])
                for hf in range(2):
                    for kc in range(8):
                        mm(ps[4 + hf][:, :], OGv[:, kc, b * 128:(b + 1) * 128], W0Ov[:, kc, hf * 512:(hf + 1) * 512],
                           kc == 0, kc == 7, [B_og, B_w0o], [P[4 + hf]])
                    tt("dve", h1[:, hf * 512:(hf + 1) * 512], ps[4 + hf][:, :], XS1[:, hf * 512:(hf + 1) * 512],
                       ALU.add, [P[4 + hf], B_xs1], [Bh1])
                if debug_h1:
                    dma("sp", dbg_d[sq, b * 128:(b + 1) * 128, :], h1, [Bh1], [])
                yield
                rmsnorm_block(h1, Bh1, 1, HN1, B_hn1, ST1, B_st1, JK1, B_jk1)
                yield
                pk = pbank()
                ptv = ps[pk].bitcast(BF16)
                for kc in range(8):
                    tr(ptv[:, kc * 128:(kc + 1) * 128], HN1[:, kc * 128:(kc + 1) * 128], [B_hn1], [P[pk]])
                cp("dve", hv, ptv.rearrange("p (c t) -> p c t", c=8), [P[pk]], [Bhv])
                yield
                pk = pbank()
                for kc in range(8):
                    mm(ps[pk][:, 0:128], W1Iv[:, kc, 1024:1152], hv[:, kc, :], kc == 0, kc == 7,
                       [B_w1i, Bhv], [P[pk]])
                cp("act", KT1[:, b * 128:(b + 1) * 128], ps[pk][:, 0:128], [P[pk]], [B_kt1[b]])
                slot = b % 3
                pk = pbank()
                if b == 0:
                    for kc in range(8):
                        mm(ps[pk][0:16, 0:128], hv[:, kc, NPAD:128], W1Iv[:, kc, 1152:1280], kc == 0, kc == 7,
                           [Bhv, B_w1i], [P[pk]])
                    for kh in range(2):
                        cp("dve", VMv[0:16, 2 * kh, 0:64], ps[pk][0:16, kh * 64:(kh + 1) * 64], [P[pk]], [B_vm])
                        cp("dve", VMv[0:16, 2 * kh + 1, 64:128], ps[pk][0:16, kh * 64:(kh + 1) * 64], [P[pk]], [B_vm])
                    return
                for kc in range(8):
                    mm(ps[pk][:, 0:128], hv[:, kc, :], W1Iv[:, kc, 1152:1280], kc == 0, kc == 7,
                       [Bhv, B_w1i], [P[pk]])
                for kh in range(2):
                    cp("dve", VRv[:, slot, 2 * kh, 0:64], ps[pk][:, kh * 64:(kh + 1) * 64], [P[pk]], [B_vr[slot]])
                    cp("dve", VRv[:, slot, 2 * kh + 1, 64:128], ps[pk][:, kh * 64:(kh + 1) * 64], [P[pk]], [B_vr[slot]])
                yield
                for q4 in range(2):
                    pk = pbank()
                    for i in range(4):
                        pr = q4 * 4 + i
                        for kc in range(8):
                            mm(ps[pk][:, i * 128:(i + 1) * 128], W1Iv[:, kc, pr * 128:(pr + 1) * 128], hv[:, kc, :],
                               kc == 0, kc == 7, [B_w1i, Bhv], [P[pk]])
                    g = q4
                    gs = slice(g * 64, g * 64 + 64)
                    pv = ps[pk].rearrange("p (i t) -> p i t", i=4)
                    qv = QGv[par][g][gs].rearrange("p (i two) t -> p two i t", two=2)
                    cp("act" if g == 0 else "dve", qv[:, 0], pv[0:64], [P[pk]], [B_qg[par][g]])
                    cp("dve" if g == 0 else "act", qv[:, 1], pv[64:128], [P[pk]], [B_qg[par][g]])
                    yield
                for c4 in range(2):
                    pk = pbank()
                    for i in range(4):
                        cc = c4 * 4 + i
                        for kc in range(8):
                            mm(ps[pk][:, i * 128:(i + 1) * 128], W1Iv[:, kc, 1280 + cc * 128:1280 + (cc + 1) * 128],
                               hv[:, kc, :], kc == 0, kc == 7, [B_w1i, Bhv], [P[pk]])
                    act(SG1v[par][:, c4 * 4:(c4 + 1) * 4, :], ps[pk].rearrange("p (i t) -> p i t", i=4), AF.Silu,
                        [P[pk]], [B_sg1[par]])
                    yield

            def back_parts(b):
                par = b % 2
                slot = b % 3
                h1, Bh1 = H1[par], B_h1[par]
                tiles = []
                for g in range(2):
                    if b >= 2:
                        tiles.append((g, "prev", KT1[:, (b - 1) * 128:b * 128], 128, (b - 1) % 3, B_kt1[b - 1]))
                    tiles.append((g, "cur", KT1[:, b * 128:(b + 1) * 128], 128, slot, B_kt1[b]))
                    tiles.append((g, "meta", KT1[:, NPAD:128], 16, None, B_kt1[0]))
                nt = len(tiles)

                def qk(ti):
                    g, kind, kk, nk, vs, Bk = tiles[ti]
                    for hf in range(2):
                        mm(ps[hf][0:nk, :], kk, QGv[par][g][:, hf * 4:(hf + 1) * 4, :], True, True,
                           [Bk, B_qg[par][g]], [P[hf]])

                def soft(ti):
                    g, kind, kk, nk, vs, Bk = tiles[ti]
                    e = ti % 2
                    exv, pbv = EXv[e], PBv[e]
                    for hf in range(2):
                        act(EX[e][0:nk, hf * 512:(hf + 1) * 512], ps[hf][0:nk, :], AF.Exp, [P[hf]], [B_exh[e][hf]],
                            scale=0.125)
                        if kind != "meta":
                            a = 0 if kind == "prev" else 1
                            hs = slice(hf * 4, (hf + 1) * 4)
                            tt("pool" if hf == 0 else "dve", pbv[:, hs, :], exv[:, hs, :],
                               SWEv[:, a, g * 8 + hf * 4:g * 8 + (hf + 1) * 4, :], ALU.mult,
                               [B_exh[e][hf], B_swe], [B_pbh[e][hf]])
                    if kind == "meta":
                        tt("dve", exv[0:16], exv[0:16], EMv[0:16, g * 8:(g + 1) * 8, :], ALU.mult,
                           B_exh[e] + [B_em], B_exh[e])
                        tt("dve", pbv[0:16], exv[0:16],
                           CMv[0:16, b, g * 8:(g + 1) * 8].unsqueeze(2).to_broadcast([16, 8, 128]), ALU.mult,
                           B_exh[e] + [B_cm], B_pbh[e])

                def prologue():
                    qk(0)
                    soft(0)
                    if nt > 1:
                        qk(1)

                def tile_gen():
                    for ti, (g, kind, kk, nk, vs, Bk) in enumerate(tiles):
                        e = ti % 2
                        pbv, B_p = PBv[e], B_pbh[e]
                        if kind == "meta":
                            va, vb2 = VMv[0:16, 2 * g, :], VMv[0:16, 2 * g + 1, :]
                            B_v = B_vm
                        else:
                            va, vb2 = VRv[:, vs, 2 * g, :], VRv[:, vs, 2 * g + 1, :]
                            B_v = B_vr[vs]
                        pe_ = pbv[0:nk].rearrange("p (q two) t -> p two q t", two=2)
                        gfirst = kind == ("prev" if b >= 2 else "cur")
                        glast = kind == "meta"
                        mm(ps[2][:, :], va, pe_[:, 0], gfirst, False, [B_v] + B_p, [P[2]])
                        mm(ps[2][:, :], vb2, pe_[:, 1], False, glast, [B_v] + B_p, [P[2]])
                        mm(ps[3][:, :], ONESA[0:nk, :], pe_[:, 0], gfirst, False, [B_cb] + B_p, [P[3]])
                        mm(ps[3][:, :], ONESB[0:nk, :], pe_[:, 1], False, glast, [B_cb] + B_p, [P[3]])
                        if ti + 1 < nt:
                            soft(ti + 1)
                        if ti + 2 < nt:
                            qk(ti + 2)
                        if glast:
                            R3 = R32.rearrange("p (q t) -> p q t", q=4)
                            U3 = U32.rearrange("p (q t) -> p q t", q=4)
                            tt("dve", R3, ps[3].rearrange("p (q t) -> p q t", q=4),
                               ESK[:, g * 4:(g + 1) * 4].unsqueeze(2).to_broadcast([128, 4, 128]), ALU.add,
                               [P[3], B_esk], [B_r32])
                            tt("dve", U3, ps[2].rearrange("p (q t) -> p q t", q=4), SG1v[par][:, g * 4:(g + 1) * 4, :],
                               ALU.mult, [P[2], B_sg1[par]], [B_u32])
                            act(R32, R32, AF.Ln, [B_r32], [B_r32])
                            act(R32, R32, AF.Exp, [B_r32], [B_r32], scale=-1.0)
                            tt("dve", OG1v[:, g * 4:(g + 1) * 4, :], U3, R3, ALU.mult, [B_u32, B_r32], [B_og1])
                        yield

                def tail():
                    for hf in range(2):
                        for kc in range(8):
                            mm(ps[4 + hf][:, :], OG1v[:, kc, :], W1Ov[:, kc, hf * 512:(hf + 1) * 512],
                               kc == 0, kc == 7, [B_og1, B_w1o], [P[4 + hf]])
                        tt("dve", h1[:, hf * 512:(hf + 1) * 512], ps[4 + hf][:, :], h1[:, hf * 512:(hf + 1) * 512],
                           ALU.add, [P[4 + hf], Bh1], [Bh1])
                    rmsnorm_block(h1, Bh1, 2, OUTB, B_outb, ST2, B_st2, JK1, B_jk1)
                    dma("sp", out_d[sq, (b - 1) * 128:b * 128, :], OUTB, [B_outb], [])

                return prologue, tile_gen, tail

            def drain(gen):
                for _ in gen:
                    pass

            drain(front(0))
            drain(front(1))
            parts = back_parts(1)
            parts[0]()
            for b in range(1, NB):
                prologue, tile_gen, tail = parts
                alive = [tile_gen()]
                if b + 1 < NB:
                    alive.append(front(b + 1))
                while alive:
                    for gq in list(alive):
                        try:
                            next(gq)
                        except StopIteration:
                            alive.remove(gq)
                if b + 1 < NB:
                    parts = back_parts(b + 1)
                    parts[0]()
                tail()

        S.finalize()
        with nc.Block() as block:
            @block.tensor
            def _(e):
                S.emit_engine("pe", e, sems, dma_sems)

            @block.scalar
            def _(e):
                S.emit_engine("act", e, sems, dma_sems)

            @block.vector
            def _(e):
                S.emit_engine("dve", e, sems, dma_sems)

            @block.gpsimd
            def _(e):
                S.emit_engine("pool", e, sems, dma_sems)

            @block.sync
            def _(e):
                S.emit_engine("sp", e, sems, dma_sems, final_wait=True)
    return nc


_NC_CACHE = {}


def _common_inputs(meta, norm_g, final_g, ev_w_in, ev_q_norm_g, ev_kv_norm_g, ev_w_uq, ev_w_ukv,
                   ev_w_out, od_w_in, od_sinks, od_w_out):
    cb, rope, swae, em, cm = _const_tables()
    f = lambda a: np.ascontiguousarray(np.asarray(a, dtype=np.float32))
    return {
        "meta": f(meta),
        "gains": f(np.concatenate([np.asarray(norm_g), np.asarray(final_g)[None, :]], 0)),
        "w0in": f(ev_w_in[0]), "qng": f(ev_q_norm_g[0]), "kvng": f(ev_kv_norm_g[0]),
        "wuq": f(ev_w_uq[0]), "wukv": f(ev_w_ukv[0]), "w0out": f(ev_w_out[0]),
        "w1in": f(od_w_in[0]), "sinks": f(od_sinks[0]), "w1out": f(od_w_out[0]),
        "cb": cb, "rope": f(rope), "swae": f(swae), "em": f(em), "cm": f(cm),
    }


def kernel(x, meta, norm_g, final_g, ev_w_in, ev_q_norm_g, ev_kv_norm_g, ev_w_uq, ev_w_ukv,
           ev_w_out, od_w_in, od_sinks, od_w_out):
    n = 8
    x = np.asarray(x, dtype=np.float32)
    common = _common_inputs(meta, norm_g, final_g, ev_w_in, ev_q_norm_g, ev_kv_norm_g, ev_w_uq,
                            ev_w_ukv, ev_w_out, od_w_in, od_sinks, od_w_out)
    if "nc" not in _NC_CACHE:
        _NC_CACHE["nc"] = build_nc()
    nc = _NC_CACHE["nc"]
    in_maps = []
    for c in range(n):
        m = dict(common)
        m["x"] = np.ascontiguousarray(x[c * SEQ_PER_CORE:(c + 1) * SEQ_PER_CORE])
        in_maps.append(m)
    res = run_bass_kernel_spmd(nc, in_maps, core_ids=list(range(n)))
    return np.concatenate([r["out"] for r in res.results], axis=0)
```
